# Optimizing a Trainium2 kernel written in Bass

```python
import math
import numpy as np
import jax
import jax.numpy as jnp
from jax import lax

D_MODEL = 1024
BATCH = 4
SEQ = 4096
DEPTH = 4

HEAD_DIM = 64
ROPE_THETA = 10000.0
Q_BLOCK = 128
SWA_Q_HEADS = 8
SWA_KV_HEADS = 2
SWA_WINDOW = 128
NSA_Q_HEADS = 8
NSA_KV_HEADS = 2
CMP_LEN = 32
CMP_STRIDE = 16
CMP_HIDDEN = 128
SEL_LEN = 64
SEL_TOPK = 16
NSA_WINDOW = 512
N_BRANCH = 3
ATTN_IN = (SWA_Q_HEADS + 2 * SWA_KV_HEADS) * HEAD_DIM + (NSA_Q_HEADS + 6 * NSA_KV_HEADS) * HEAD_DIM + N_BRANCH * NSA_Q_HEADS
ATTN_OUT = (SWA_Q_HEADS + NSA_Q_HEADS) * HEAD_DIM
SSM_EXPAND = 2
D_INNER = SSM_EXPAND * D_MODEL
SSM_HEAD_DIM = 64
SSM_HEADS = D_INNER // SSM_HEAD_DIM
SSM_GROUPS = 8
D_STATE = 128
CONV_W = 4
SSD_CHUNK = 128
CONV_CH = D_INNER + 2 * SSM_GROUPS * D_STATE
SSM_IN = D_INNER + CONV_CH + SSM_HEADS
D_FF = 2816
N_EVEN = (DEPTH + 1) // 2
N_ODD = DEPTH // 2
RMS_EPS = 1e-6
NEG_INF = -1e30
SEL_FORCE = 1e4

kernel_name = 'hybrid_swa_nsa_ssd_macaron'


def rmsnorm(x, g):
    xf = x.astype(jnp.float32)
    y = xf * lax.rsqrt(jnp.mean(xf * xf, axis=-1, keepdims=True) + RMS_EPS)
    return (y * g.astype(jnp.float32)).astype(x.dtype)


def swiglu(h, w_gate, w_up, w_down):
    return (jax.nn.silu(h @ w_gate) * (h @ w_up)) @ w_down


def rope_tables(seq):
    inv = 1.0 / (ROPE_THETA ** (jnp.arange(0, HEAD_DIM, 2, dtype=jnp.float32) / HEAD_DIM))
    ang = jnp.arange(seq, dtype=jnp.float32)[:, None] * inv[None, :]
    return jnp.cos(ang), jnp.sin(ang)


def apply_rope(t, cos, sin):
    t1, t2 = jnp.split(t.astype(jnp.float32), 2, axis=-1)
    c = cos[None, :, None, :]
    s = sin[None, :, None, :]
    return jnp.concatenate([t1 * c - t2 * s, t2 * c + t1 * s], axis=-1).astype(t.dtype)


def masked_softmax(s, mask, sink=None):
    s = jnp.where(mask, s, NEG_INF)
    m = jnp.max(s, axis=-1, keepdims=True)
    if sink is not None:
        m = jnp.maximum(m, sink)
    e = jnp.where(mask, jnp.exp(s - m), 0.0)
    den = jnp.sum(e, axis=-1, keepdims=True)
    if sink is not None:
        den = den + jnp.exp(sink - m)
    return e / jnp.maximum(den, 1e-30)


def banded_attention(q, k, v, window, sinks=None):
    bsz, g, r, seq, dh = q.shape
    nq = seq // Q_BLOCK
    nb = -(-window // Q_BLOCK)
    qb = q.reshape(bsz, g, r, nq, Q_BLOCK, dh)
    pad = ((0, 0), (0, 0), (nb * Q_BLOCK, 0), (0, 0))
    kp = jnp.pad(k, pad).reshape(bsz, g, nq + nb, Q_BLOCK, dh)
    vp = jnp.pad(v, pad).reshape(bsz, g, nq + nb, Q_BLOCK, dh)
    kband = jnp.concatenate([kp[:, :, j:j + nq] for j in range(nb + 1)], axis=3)
    vband = jnp.concatenate([vp[:, :, j:j + nq] for j in range(nb + 1)], axis=3)
    qpos = jnp.arange(nq)[:, None] * Q_BLOCK + jnp.arange(Q_BLOCK)[None, :]
    kpos = (jnp.arange(nq)[:, None] - nb) * Q_BLOCK + jnp.arange((nb + 1) * Q_BLOCK)[None, :]
    diff = qpos[:, :, None] - kpos[:, None, :]
    mask = (diff >= 0) & (diff < window) & (kpos[:, None, :] >= 0)
    s = jnp.einsum('bgrnid,bgnkd->bgrnik', qb, kband).astype(jnp.float32) * (dh ** -0.5)
    sink = None if sinks is None else sinks.astype(jnp.float32).reshape(1, g, r, 1, 1, 1)
    p = masked_softmax(s, mask, sink)
    o = jnp.einsum('bgrnik,bgnkd->bgrnid', p.astype(vband.dtype), vband)
    return o.reshape(bsz, g, r, seq, dh)


def compress(t, pos_emb, w1, w2):
    seq = t.shape[2]
    n_cmp = (seq - CMP_LEN) // CMP_STRIDE + 1
    idx = np.arange(n_cmp)[:, None] * CMP_STRIDE + np.arange(CMP_LEN)[None, :]
    blocks = t[:, :, idx] + pos_emb
    flat = blocks.reshape(blocks.shape[0], blocks.shape[1], n_cmp, CMP_LEN * HEAD_DIM)
    return jax.nn.gelu(flat @ w1) @ w2


def cmp_to_sel_weights(seq):
    n_cmp = (seq - CMP_LEN) // CMP_STRIDE + 1
    n_blk = seq // SEL_LEN
    cs = np.arange(n_cmp) * CMP_STRIDE
    ss = np.arange(n_blk) * SEL_LEN
    ov = np.minimum(cs[:, None] + CMP_LEN, ss[None, :] + SEL_LEN) - np.maximum(cs[:, None], ss[None, :])
    return (np.clip(ov, 0, None) / CMP_LEN).astype(np.float32)


def selected_attention(q, k, v, idx):
    bsz, g, r, seq, dh = q.shape
    n_blk = seq // SEL_LEN
    nq = seq // Q_BLOCK
    kb = k.reshape(bsz, g, n_blk, SEL_LEN, dh)
    vb = v.reshape(bsz, g, n_blk, SEL_LEN, dh)
    qs = jnp.moveaxis(q.reshape(bsz, g, r, nq, Q_BLOCK, dh), 3, 0)
    ids = jnp.moveaxis(idx.reshape(bsz, g, nq, Q_BLOCK, idx.shape[-1]), 2, 0)
    qpos = jnp.arange(seq).reshape(nq, Q_BLOCK)
    take = jax.vmap(jax.vmap(lambda blocks, i: blocks[i]))

    def one_block(args):
        qq, ii, pos = args
        kg = take(kb, ii)
        vg = take(vb, ii)
        s = jnp.einsum('bgrqd,bgqnkd->bgrqnk', qq, kg).astype(jnp.float32) * (dh ** -0.5)
        kpos = ii[..., None] * SEL_LEN + jnp.arange(SEL_LEN)
        mask = (kpos <= pos[:, None, None])[:, :, None]
        sh = s.shape
        p = masked_softmax(s.reshape(sh[0], sh[1], sh[2], sh[3], -1),
                           mask.reshape(sh[0], sh[1], 1, sh[3], -1)).reshape(sh)
        return jnp.einsum('bgrqnk,bgqnkd->bgrqd', p.astype(vg.dtype), vg)

    o = lax.map(one_block, (qs, ids, qpos))
    return jnp.moveaxis(o, 0, 3).reshape(bsz, g, r, seq, dh)


def nsa_attention(q, kc, vc, ks, vs, kw, vw, gates, ckp, ckw1, ckw2, cvp, cvw1, cvw2):
    seq = q.shape[3]
    kcmp = compress(kc, ckp, ckw1, ckw2)
    vcmp = compress(vc, cvp, cvw1, cvw2)
    n_cmp = kcmp.shape[2]
    t = jnp.arange(seq)
    cmp_end = jnp.arange(n_cmp) * CMP_STRIDE + CMP_LEN - 1
    s = jnp.einsum('bgrtd,bgcd->bgrtc', q, kcmp).astype(jnp.float32) * (HEAD_DIM ** -0.5)
    p_cmp = masked_softmax(s, cmp_end[None, :] <= t[:, None])
    o_cmp = jnp.einsum('bgrtc,bgcd->bgrtd', p_cmp.astype(vcmp.dtype), vcmp)
    n_blk = seq // SEL_LEN
    imp = jnp.einsum('bgrtc,cj->bgtj', p_cmp, jnp.asarray(cmp_to_sel_weights(seq)))
    cur = (t // SEL_LEN)[:, None]
    j = jnp.arange(n_blk)[None, :]
    valid = j <= cur
    forced = valid & ((j == 0) | (j == cur) | (j == cur - 1))
    score = imp + jnp.where(forced, SEL_FORCE, 0.0) - jnp.where(valid, 0.0, SEL_FORCE)
    _, idx = lax.top_k(score, min(SEL_TOPK, n_blk))
    o_slc = selected_attention(q, ks, vs, idx)
    o_win = banded_attention(q, kw, vw, NSA_WINDOW)
    o = gates[..., 0:1] * o_cmp + gates[..., 1:2] * o_slc + gates[..., 2:3] * o_win
    return o.astype(q.dtype)


def attn_mixer(h, w_in, w_out, sinks, ckp, ckw1, ckw2, cvp, cvw1, cvw2, cos, sin):
    bsz, seq, _ = h.shape
    dq_a = SWA_Q_HEADS * HEAD_DIM
    dkv_a = SWA_KV_HEADS * HEAD_DIM
    dq_b = NSA_Q_HEADS * HEAD_DIM
    dkv_b = NSA_KV_HEADS * HEAD_DIM
    sizes = [dq_a, dkv_a, dkv_a, dq_b] + [dkv_b] * 6
    cuts = [int(c) for c in np.cumsum(sizes)]
    qa, ka, va, qb, kc, vc, ks, vs, kw, vw, gt = jnp.split(h @ w_in, cuts, axis=-1)

    def heads(t):
        return t.reshape(bsz, seq, -1, HEAD_DIM)

    def q_layout(t, n_groups):
        return t.reshape(bsz, seq, n_groups, -1, HEAD_DIM).transpose(0, 2, 3, 1, 4)

    def kv_layout(t):
        return t.transpose(0, 2, 1, 3)

    def rot(t):
        return apply_rope(heads(t), cos, sin)

    o_a = banded_attention(q_layout(rot(qa), SWA_KV_HEADS), kv_layout(rot(ka)), kv_layout(heads(va)),
                           SWA_WINDOW, sinks)
    gates = jax.nn.sigmoid(gt.reshape(bsz, seq, NSA_KV_HEADS, -1, N_BRANCH).transpose(0, 2, 3, 1, 4))
    o_b = nsa_attention(q_layout(rot(qb), NSA_KV_HEADS),
                        kv_layout(rot(kc)), kv_layout(heads(vc)),
                        kv_layout(rot(ks)), kv_layout(heads(vs)),
                        kv_layout(rot(kw)), kv_layout(heads(vw)),
                        gates, ckp, ckw1, ckw2, cvp, cvw1, cvw2)

    def merge(o):
        return o.transpose(0, 3, 1, 2, 4).reshape(bsz, seq, -1)

    return jnp.concatenate([merge(o_a), merge(o_b)], axis=-1) @ w_out


def ssd_scan(x, dt, a, bm, cm):
    bsz, seq, nh, hp = x.shape
    g, n = bm.shape[2], bm.shape[3]
    r = nh // g
    L = SSD_CHUNK
    nc = seq // L
    x = x.astype(jnp.float32)
    xdt = (x * dt[..., None]).reshape(bsz, nc, L, g, r, hp)
    da = (dt * a).reshape(bsz, nc, L, g, r).transpose(0, 3, 4, 1, 2)
    bc = bm.astype(jnp.float32).reshape(bsz, nc, L, g, n)
    cc = cm.astype(jnp.float32).reshape(bsz, nc, L, g, n)
    a_cs = jnp.cumsum(da, axis=-1)
    causal = jnp.tril(jnp.ones((L, L), dtype=bool))
    seg = a_cs[..., :, None] - a_cs[..., None, :]
    lmat = jnp.exp(jnp.where(causal, seg, -jnp.inf))
    cb = jnp.einsum('bclgn,bcsgn->bgcls', cc, bc)
    y_diag = jnp.einsum('bgrcls,bcsgrp->bclgrp', cb[:, :, None] * lmat, xdt)
    decay = jnp.exp(a_cs[..., -1:] - a_cs)
    states = jnp.einsum('bclgn,bgrcl,bclgrp->cbgrpn', bc, decay, xdt)
    chunk_decay = jnp.moveaxis(jnp.exp(a_cs[..., -1]), -1, 0)

    def step(hc, inp):
        s_c, d_c = inp
        return d_c[..., None, None] * hc + s_c, hc

    h0 = jnp.zeros((bsz, g, r, hp, n), jnp.float32)
    _, prev = lax.scan(step, h0, (states, chunk_decay))
    y_off = jnp.einsum('bclgn,cbgrpn,bgrcl->bclgrp', cc, prev, jnp.exp(a_cs))
    return (y_diag + y_off).reshape(bsz, seq, nh, hp)


def gated_rmsnorm(y, z, w):
    gy = y.astype(jnp.float32) * jax.nn.silu(z.astype(jnp.float32))
    gg = gy.reshape(gy.shape[:-1] + (SSM_GROUPS, -1))
    gg = gg * lax.rsqrt(jnp.mean(gg * gg, axis=-1, keepdims=True) + RMS_EPS)
    return (gg.reshape(gy.shape) * w.astype(jnp.float32)).astype(z.dtype)


def mamba2_mixer(h, w_in, conv_w, conv_b, dt_bias, a_log, d_skip, norm_w, w_out):
    bsz, seq, _ = h.shape
    z, xbc, dt = jnp.split(h @ w_in, [D_INNER, D_INNER + CONV_CH], axis=-1)
    xbc = lax.conv_general_dilated(xbc, conv_w[:, None, :], window_strides=(1,),
                                   padding=[(CONV_W - 1, 0)],
                                   dimension_numbers=('NWC', 'WIO', 'NWC'),
                                   feature_group_count=CONV_CH) + conv_b
    xbc = jax.nn.silu(xbc)
    xs, bm, cm = jnp.split(xbc, [D_INNER, D_INNER + SSM_GROUPS * D_STATE], axis=-1)
    dt = jax.nn.softplus(dt.astype(jnp.float32) + dt_bias.astype(jnp.float32))
    a = -jnp.exp(a_log.astype(jnp.float32))
    xh = xs.reshape(bsz, seq, SSM_HEADS, SSM_HEAD_DIM)
    y = ssd_scan(xh, dt, a,
                 bm.reshape(bsz, seq, SSM_GROUPS, D_STATE),
                 cm.reshape(bsz, seq, SSM_GROUPS, D_STATE))
    y = y + d_skip.astype(jnp.float32)[:, None] * xh.astype(jnp.float32)
    y = gated_rmsnorm(y.reshape(bsz, seq, D_INNER), z, norm_w)
    return y @ w_out


def setup_inputs(seed: int = 0) -> dict:
    key = jax.random.key(seed)
    keys = iter(jax.random.split(key, 32))
    ne, no = N_EVEN, N_ODD

    def nrm(shape, scale):
        return jax.random.normal(next(keys), shape, jnp.float32) * scale

    x = nrm((BATCH, SEQ, D_MODEL), 1.0)
    norm_gains = 1.0 + nrm((DEPTH, 3, D_MODEL), 0.02)
    final_norm = 1.0 + nrm((D_MODEL,), 0.02)
    ffn_w_gate = nrm((DEPTH, 2, D_MODEL, D_FF), D_MODEL ** -0.5)
    ffn_w_up = nrm((DEPTH, 2, D_MODEL, D_FF), D_MODEL ** -0.5)
    ffn_w_down = nrm((DEPTH, 2, D_FF, D_MODEL), D_FF ** -0.5)
    attn_w_in = nrm((ne, D_MODEL, ATTN_IN), D_MODEL ** -0.5)
    attn_w_out = nrm((ne, ATTN_OUT, D_MODEL), ATTN_OUT ** -0.5)
    attn_sinks = nrm((ne, SWA_Q_HEADS), 1.0)
    cmp_k_pos = nrm((ne, CMP_LEN, HEAD_DIM), 0.1)
    cmp_k_w1 = nrm((ne, CMP_LEN * HEAD_DIM, CMP_HIDDEN), (CMP_LEN * HEAD_DIM) ** -0.5)
    cmp_k_w2 = nrm((ne, CMP_HIDDEN, HEAD_DIM), CMP_HIDDEN ** -0.5)
    cmp_v_pos = nrm((ne, CMP_LEN, HEAD_DIM), 0.1)
    cmp_v_w1 = nrm((ne, CMP_LEN * HEAD_DIM, CMP_HIDDEN), (CMP_LEN * HEAD_DIM) ** -0.5)
    cmp_v_w2 = nrm((ne, CMP_HIDDEN, HEAD_DIM), CMP_HIDDEN ** -0.5)
    ssm_w_in = nrm((no, D_MODEL, SSM_IN), D_MODEL ** -0.5)
    ssm_conv_w = nrm((no, CONV_W, CONV_CH), CONV_W ** -0.5)
    ssm_conv_b = nrm((no, CONV_CH), 0.02)
    dt0 = jnp.exp(jax.random.uniform(next(keys), (no, SSM_HEADS), jnp.float32,
                                     minval=math.log(1e-3), maxval=math.log(1e-1)))
    ssm_dt_bias = dt0 + jnp.log(-jnp.expm1(-dt0))
    ssm_a_log = jnp.log(jax.random.uniform(next(keys), (no, SSM_HEADS), jnp.float32, minval=1.0, maxval=16.0))
    ssm_d = 1.0 + nrm((no, SSM_HEADS), 0.1)
    ssm_norm = 1.0 + nrm((no, D_INNER), 0.02)
    ssm_w_out = nrm((no, D_INNER, D_MODEL), D_INNER ** -0.5)
    return {'x': x, 'norm_gains': norm_gains, 'final_norm': final_norm,
            'ffn_w_gate': ffn_w_gate, 'ffn_w_up': ffn_w_up, 'ffn_w_down': ffn_w_down,
            'attn_w_in': attn_w_in, 'attn_w_out': attn_w_out, 'attn_sinks': attn_sinks,
            'cmp_k_pos': cmp_k_pos, 'cmp_k_w1': cmp_k_w1, 'cmp_k_w2': cmp_k_w2,
            'cmp_v_pos': cmp_v_pos, 'cmp_v_w1': cmp_v_w1, 'cmp_v_w2': cmp_v_w2,
            'ssm_w_in': ssm_w_in, 'ssm_conv_w': ssm_conv_w, 'ssm_conv_b': ssm_conv_b,
            'ssm_dt_bias': ssm_dt_bias, 'ssm_a_log': ssm_a_log, 'ssm_d': ssm_d,
            'ssm_norm': ssm_norm, 'ssm_w_out': ssm_w_out}


def reference(x, norm_gains, final_norm, ffn_w_gate, ffn_w_up, ffn_w_down,
              attn_w_in, attn_w_out, attn_sinks,
              cmp_k_pos, cmp_k_w1, cmp_k_w2, cmp_v_pos, cmp_v_w1, cmp_v_w2,
              ssm_w_in, ssm_conv_w, ssm_conv_b, ssm_dt_bias, ssm_a_log, ssm_d,
              ssm_norm, ssm_w_out):
    cos, sin = rope_tables(x.shape[1])
    for i in range(DEPTH):
        g = norm_gains[i]
        x = x + 0.5 * swiglu(rmsnorm(x, g[0]), ffn_w_gate[i, 0], ffn_w_up[i, 0], ffn_w_down[i, 0])
        h = rmsnorm(x, g[1])
        j = i // 2
        if i % 2 == 0:
            x = x + attn_mixer(h, attn_w_in[j], attn_w_out[j], attn_sinks[j],
                               cmp_k_pos[j], cmp_k_w1[j], cmp_k_w2[j],
                               cmp_v_pos[j], cmp_v_w1[j], cmp_v_w2[j], cos, sin)
        else:
            x = x + mamba2_mixer(h, ssm_w_in[j], ssm_conv_w[j], ssm_conv_b[j], ssm_dt_bias[j],
                                 ssm_a_log[j], ssm_d[j], ssm_norm[j], ssm_w_out[j])
        x = x + 0.5 * swiglu(rmsnorm(x, g[2]), ffn_w_gate[i, 1], ffn_w_up[i, 1], ffn_w_down[i, 1])
    return rmsnorm(x, final_norm)
```

```python
import contextlib
import numpy as np
import concourse.bass as bass
import concourse.mybir as mybir
from concourse.bass_utils import run_bass_kernel_spmd

F32 = mybir.dt.float32
BF16 = mybir.dt.bfloat16
AF = mybir.ActivationFunctionType
ALU = mybir.AluOpType
AX = mybir.AxisListType

D = 1024
S = 4096
DEPTH = 4
DFF = 2816
NFC = DFF // 128
EPS = 1e-6
NAW = 3608


class Res:
    __slots__ = ("name", "w", "r")

    def __init__(self, name):
        self.name = name
        self.w = None
        self.r = []


class Op:
    __slots__ = ("eng", "fn", "waits", "inc", "dma", "dsem", "dval", "cnt", "pre")


class Prog:
    ENGS = ("pe", "act", "dve", "pool", "sp")

    def __init__(self, nc, stack):
        self.nc = nc
        self.stack = stack
        self.ops = {e: [] for e in self.ENGS}
        self.esem = {e: stack.enter_context(nc.semaphore("es_" + e)) for e in self.ENGS}
        self.nsem = 5
        self.streams = []

    def new_sem(self, name):
        self.nsem += 1
        return self.stack.enter_context(self.nc.semaphore(name))

    def op(self, eng, fn, reads=(), writes=(), dma=None):
        o = Op()
        o.eng = eng
        o.fn = fn
        o.inc = False
        o.dma = dma
        o.cnt = 0
        o.pre = None
        o.dsem = None
        o.dval = 0
        deps = []
        seen = set()

        def add(d, raw):
            if d is None or id(d) in seen:
                return
            if d.dma is None and d.eng == eng:
                if eng == "pe" or not raw:
                    return
            seen.add(id(d))
            deps.append(d)

        for r in reads:
            add(r.w, True)
        for w in writes:
            add(w.w, False)
            for rr in w.r:
                add(rr, False)
        for d in deps:
            if d.dma is None:
                d.inc = True
        o.waits = deps
        if dma is not None:
            sem, val, pre = dma.next()
            o.dsem, o.dval, o.pre = sem, val, pre
            dma.ops.append(o)
        for r in reads:
            r.r.append(o)
        for w in writes:
            w.w = o
            w.r = []
        self.ops[eng].append(o)
        return o

    def barrier(self):
        deps = []
        for e in self.ENGS:
            for o in reversed(self.ops[e]):
                if o.dma is None:
                    deps.append(o)
                    break
        for st in self.streams:
            deps.extend(st.ops[-st.R:])
        for e in self.ENGS:
            o = Op()
            o.eng = e
            o.fn = lambda eng: eng.nop()
            o.inc = False
            o.dma = None
            o.cnt = 0
            o.pre = None
            o.dsem = None
            o.dval = 0
            o.waits = [d for d in deps if not (d.dma is None and d.eng == e)]
            for d in o.waits:
                if d.dma is None:
                    d.inc = True
            self.ops[e].append(o)

    def emit(self):
        nc = self.nc
        for e in self.ENGS:
            c = 0
            for o in self.ops[e]:
                if o.dma is None and o.inc:
                    c += 1
                o.cnt = c
        self.counts = {e: (len(self.ops[e]), self.ops[e][-1].cnt if self.ops[e] else 0) for e in self.ENGS}

        def body_for(ename):
            def body(eng):
                seen = {}

                def wait(sem, val):
                    k = id(sem)
                    if seen.get(k, 0) >= val:
                        return
                    seen[k] = val
                    eng.wait_ge(sem, val)

                for o in self.ops[ename]:
                    for d in o.waits:
                        if d.dma is not None:
                            wait(d.dsem, d.dval)
                        else:
                            wait(self.esem[d.eng], d.cnt)
                    if o.pre is not None and o.pre[1] > 0:
                        wait(o.pre[0], o.pre[1])
                    ins = o.fn(eng)
                    if o.dma is not None:
                        ins.then_inc(o.dsem, 16)
                    elif o.inc:
                        ins.then_inc(self.esem[ename], 1)
            return body

        with nc.Block() as block:
            block.tensor(body_for("pe"))
            block.scalar(body_for("act"))
            block.vector(body_for("dve"))
            block.gpsimd(body_for("pool"))
            block.sync(body_for("sp"))


class DmaStream:
    def __init__(self, prog, name, R):
        self.sems = [prog.new_sem("%s%d" % (name, i)) for i in range(R)]
        self.R = R
        self.k = 0
        self.ops = []
        prog.streams.append(self)

    def next(self):
        k = self.k
        self.k += 1
        sem = self.sems[k % self.R]
        return sem, 16 * (k // self.R + 1), (sem, 16 * (k // self.R))


class Builder:
    def __init__(self, plan):
        self.plan = plan
        self.nc = bass.Bass("TRN2", target_bir_lowering=False)
        self.stack = contextlib.ExitStack()
        self.res_cache = {}

    def R(self, name):
        r = self.res_cache.get(name)
        if r is None:
            r = self.res_cache[name] = Res(name)
        return r

    def dram_in(self, name, shape, dt=F32):
        return self.nc.dram_tensor(name, list(shape), dt, kind="ExternalInput").ap()

    def dram_out(self, name, shape, dt=F32):
        return self.nc.dram_tensor(name, list(shape), dt, kind="ExternalOutput").ap()

    def dram_tmp(self, name, shape, dt):
        return self.nc.dram_tensor(name, list(shape), dt).ap()

    def sb(self, name, shape, dt):
        return self.stack.enter_context(self.nc.sbuf_tensor(name, list(shape), dt))

    def ps(self, name, shape, dt):
        return self.stack.enter_context(self.nc.psum_tensor(name, list(shape), dt))

    def build(self):
        nc = self.nc
        with self.stack:
            self.p = Prog(nc, self.stack)
            self._build()
            self.p.emit()
        return nc

    def _build(self):
        p = self.p
        plan = self.plan
        self.x_in = self.dram_in("x", [S, D])
        self.gains = self.dram_in("gains", [DEPTH * 3 + 1, D])
        self.wg = self.dram_in("ffn_w_gate", [DEPTH, 2, D, DFF])
        self.wu = self.dram_in("ffn_w_up", [DEPTH, 2, D, DFF])
        self.wd = self.dram_in("ffn_w_down", [DEPTH, 2, DFF, D])
        self.ident_in = self.dram_in("ident", [128, 128])
        self.tri_in = self.dram_in("tri", [3, 128, 128])
        self.ssm_w_in = self.dram_in("ssm_w_in", [2, D, 6176])
        self.ssm_w_out = self.dram_in("ssm_w_out", [2, 2048, D])
        self.ssm_cw = self.dram_in("ssm_cw", [2, 128, 32, 4])
        self.ssm_cb = self.dram_in("ssm_cb", [2, 128, 32])
        self.ssm_vec = self.dram_in("ssm_vec", [2, 96])
        self.ssm_norm = self.dram_in("ssm_norm", [2, 2048])
        self.swin = self.dram_tmp("swin", [2, D, 6176], BF16)
        self.attn_wr = self.dram_in("attn_wr", [2, D, NAW])
        self.attn_w_out = self.dram_in("attn_w_out", [2, D, D])
        self.sinks_in = self.dram_in("attn_sinks", [2, 8])
        self.rope_in = self.dram_in("rope", [2, 64, S])
        self.cmp_w1 = self.dram_in("cmp_w1", [2, 2, 2048, 128])
        self.cmp_w2 = self.dram_in("cmp_w2", [2, 2, 128, 64])
        self.cmp_posT = self.dram_in("cmp_posT", [2, 2, 64, 32])
        self.wsel_in = self.dram_in("wsel", [128, 2, 64])
        self.selb_in = self.dram_in("selb", [128, 32, 64])
        self.awin = self.dram_tmp("awin", [2, D, NAW], BF16)
        self.qT = self.dram_tmp("qT", [16, 64, S], BF16)
        self.out = self.dram_out("out", [S, D])
        self.xres = self.dram_tmp("xres", [S, D], F32)
        self.wgt = self.dram_tmp("wgt", [DEPTH, 2, 6, 128, 8, 512], BF16)
        self.wut = self.dram_tmp("wut", [DEPTH, 2, 6, 128, 8, 512], BF16)
        self.wdt = self.dram_tmp("wdt", [DEPTH, 2, DFF, D], BF16)

        self.st_const = DmaStream(p, "dc", 1)
        self.st_prep = DmaStream(p, "dp", 4)
        self.st_x = DmaStream(p, "dx", 4)
        self.st_w = DmaStream(p, "dw", 4)
        self.st_o = DmaStream(p, "do", 4)

        self.ident_f = self.sb("ident_f", [128, 128], F32)
        self.ident_b = self.sb("ident_b", [128, 128], BF16)
        self.gbc = self.sb("gbc", [128, D], F32)
        self.epsb = self.sb("epsb", [128, 1], F32)
        r_ident = self.R("ident")
        p.op("sp", lambda e: e.dma_start(out=self.ident_f[:], in_=self.ident_in), writes=[r_ident], dma=self.st_const)
        p.op("dve", lambda e: e.tensor_copy(out=self.ident_b[:], in_=self.ident_f[:]), reads=[r_ident], writes=[self.R("ident_b")])
        p.op("dve", lambda e: e.memset(self.epsb[:], EPS), writes=[self.R("epsb")])

        self.pbank = [self.ps("pb%d" % i, [128, 512], F32) for i in range(6)]
        self.ptrh = [self.ps("ptrh%d" % i, [128, 512], BF16) for i in range(2)]
        self.r_pb = [self.R("pb%d" % i) for i in range(6)]
        self.r_ptr = [self.R("ptr0"), self.R("ptr1")]

        self.x_src = self.x_in
        for ph in plan:
            kind = ph[0]
            if kind == "prep_ffn":
                self.prep_ffn(ph[1], ph[2])
            elif kind == "ffn":
                self.dbg_stage = ph[3] if len(ph) > 3 else 99
                self.ffn(ph[1], ph[2])
            elif kind == "prep_ssm":
                self.prep_ssm(ph[1])
            elif kind == "ssd":
                self.ssd(ph[1])
            elif kind == "prep_attn":
                self.prep_attn(ph[1])
            elif kind == "attn":
                self.attn_dbg = ph[2] if len(ph) > 2 else None
                self.attn(ph[1])
            elif kind == "final":
                self.final_norm()
            elif kind == "copy_out":
                self.copy_out()
            else:
                raise ValueError(kind)

    def load_gain(self, row):
        p = self.p
        src = self.gains[row:row + 1, :].broadcast_to([128, D])
        p.op("sp", lambda e: e.dma_start(out=self.gbc[:], in_=src), writes=[self.R("gbc")], dma=self.st_const)

    def prep_ffn(self, l, i):
        p = self.p
        for (src, dst, nm) in ((self.wg, self.wgt, "g"), (self.wu, self.wut, "u")):
            for blk in range(6):
                w = 512 if blk < 5 else 256
                s_ap = src[l, i, :, blk * 512:blk * 512 + w].rearrange("(kc p) m -> p kc m", p=128)
                d_ap = dst[l, i, blk, :, :, 0:w]
                p.op("pool", lambda e, s_ap=s_ap, d_ap=d_ap: e.dma_start(out=d_ap, in_=s_ap),
                     writes=[self.R("wt_%s_%d_%d_%d" % (nm, l, i, blk))], dma=self.st_prep)
        for q in range(4):
            rows = DFF // 4
            s_ap = self.wd[l, i, q * rows:(q + 1) * rows, :]
            d_ap = self.wdt[l, i, q * rows:(q + 1) * rows, :]
            p.op("pool", lambda e, s_ap=s_ap, d_ap=d_ap: e.dma_start(out=d_ap, in_=s_ap),
                 writes=[self.R("wt_d_%d_%d_%d" % (l, i, q))], dma=self.st_prep)

    def norm_transpose(self, xsrc, t0, ntile, hT, r_hT, xt_bufs, tag):
        p = self.p
        for j in range(ntile):
            tt = t0 + j
            b = j % 2
            xt, hb, sq, ss, rs = xt_bufs[b]
            rx = self.R("%s_xt%d" % (tag, b))
            rh = self.R("%s_hb%d" % (tag, b))
            rss = self.R("%s_ss%d" % (tag, b))
            rsq = self.R("%s_sq%d" % (tag, b))
            p.op("sp", lambda e, xt=xt, tt=tt: e.dma_start(out=xt[:], in_=xsrc[tt * 128:(tt + 1) * 128, :]),
                 reads=[self.R("xres")], writes=[rx], dma=self.st_x)
            p.op("act", lambda e, xt=xt, sq=sq, ss=ss: e.activation(out=sq[:], in_=xt[:], func=AF.Square, accum_out=ss[:]),
                 reads=[rx], writes=[rsq, rss])
            p.op("act", lambda e, ss=ss, rs=rs: e.activation(out=rs[:], in_=ss[:], func=AF.Sqrt, scale=1.0 / D, bias=self.epsb[:]),
                 reads=[rss, self.R("epsb")], writes=[self.R("%s_rs%d" % (tag, b))])
            p.op("dve", lambda e, rs=rs: e.reciprocal(out=rs[:], in_=rs[:]),
                 reads=[self.R("%s_rs%d" % (tag, b))], writes=[self.R("%s_rs%d" % (tag, b))])
            p.op("dve", lambda e, xt=xt, hb=hb, rs=rs: e.scalar_tensor_tensor(out=hb[:], in0=xt[:], scalar=rs[:], in1=self.gbc[:], op0=ALU.mult, op1=ALU.mult),
                 reads=[rx, self.R("%s_rs%d" % (tag, b)), self.R("gbc")], writes=[rh])
            for half in range(2):
                rp = self.r_ptr[half]

                def tr(e, hb=hb, half=half):
                    ins = None
                    for q in range(4):
                        kc = half * 4 + q
                        ins = e.transpose(out=self.ptrh[half][:, q * 128:(q + 1) * 128],
                                          in_=hb[:, kc * 128:(kc + 1) * 128], identity=self.ident_b[:])
                    return ins
                p.op("pe", tr, reads=[rh, self.R("ident_b")], writes=[rp])
                dst = hT[:, half * 4:(half + 1) * 4, j * 128:(j + 1) * 128]
                srcp = self.ptrh[half][:, :].rearrange("p (q m) -> p q m", q=4)
                eng = "act" if half == 0 else "dve"
                if eng == "act":
                    p.op("act", lambda e, dst=dst, srcp=srcp: e.copy(out=dst, in_=srcp), reads=[rp], writes=[r_hT[j][half]])
                else:
                    p.op("dve", lambda e, dst=dst, srcp=srcp: e.tensor_copy(out=dst, in_=srcp), reads=[rp], writes=[r_hT[j][half]])

    def ffn(self, l, i):
        p = self.p
        nc = self.nc
        TB = 1024
        NTB = S // TB
        xsrc = self.x_src
        with contextlib.ExitStack() as st:
            def sb(name, shape, dt):
                return st.enter_context(nc.sbuf_tensor(name, list(shape), dt))
            tag = "f%d%d" % (l, i)
            wd_sb = sb(tag + "wd", [128, NFC, D], BF16)
            aT = sb(tag + "aT", [128, NFC, TB], BF16)
            hT = sb(tag + "hT", [128, 8, TB], BF16)
            wgu = [(sb(tag + "wg%d" % b, [128, 8, 512], BF16), sb(tag + "wu%d" % b, [128, 8, 512], BF16)) for b in range(2)]
            xt_bufs = [(sb(tag + "xt%d" % b, [128, D], F32), sb(tag + "hb%d" % b, [128, D], BF16),
                        sb(tag + "sq%d" % b, [128, D], BF16), sb(tag + "ss%d" % b, [128, 1], F32),
                        sb(tag + "rs%d" % b, [128, 1], F32)) for b in range(2)]
            sg = [sb(tag + "sg%d" % b, [128, 512], F32) for b in range(2)]
            xo = [sb(tag + "xo%d" % b, [128, D], F32) for b in range(2)]
            r_wd = [self.R(tag + "wd0"), self.R(tag + "wd1")]
            r_aT = self.R(tag + "aT")
            r_hT = [[self.R(tag + "hT%d_%d" % (jj, hh)) for hh in range(2)] for jj in range(TB // 128)]
            r_wg = [self.R(tag + "wgs%d" % b) for b in range(2)]
            r_wu = [self.R(tag + "wus%d" % b) for b in range(2)]
            r_sg = [self.R(tag + "sg%d" % b) for b in range(2)]
            r_xo = [self.R(tag + "xo%d" % b) for b in range(2)]

            self.load_gain(l * 3 + (0 if i == 0 else 2))
            for q in range(2):
                fa, fb = q * 11, (q + 1) * 11
                src = self.wdt[l, i, fa * 128:fb * 128, :].rearrange("(fc p) m -> p fc m", p=128)
                p.op("sp", lambda e, src=src, fa=fa, fb=fb: e.dma_start(out=wd_sb[:, fa:fb, :], in_=src),
                     reads=[self.R("wt_d_%d_%d_%d" % (l, i, 2 * q)), self.R("wt_d_%d_%d_%d" % (l, i, 2 * q + 1))], writes=[r_wd[q]], dma=self.st_w)

            wcount = 0
            for tb in range(NTB):
                t0 = tb * (TB // 128)
                self.norm_transpose(xsrc, t0, TB // 128, hT, r_hT, xt_bufs, tag)
                if self.dbg_stage <= 1:
                    continue
                for blk in range(6):
                    w = 512 if blk < 5 else 256
                    b = wcount % 2
                    wcount += 1
                    wgs, wus = wgu[b]
                    p.op("sp", lambda e, wgs=wgs, blk=blk, w=w: e.dma_start(out=wgs[:, :, 0:w], in_=self.wgt[l, i, blk, :, :, 0:w]),
                         reads=[self.R("wt_g_%d_%d_%d" % (l, i, blk))], writes=[r_wg[b]], dma=self.st_w)
                    p.op("sp", lambda e, wus=wus, blk=blk, w=w: e.dma_start(out=wus[:, :, 0:w], in_=self.wut[l, i, blk, :, :, 0:w]),
                         reads=[self.R("wt_u_%d_%d_%d" % (l, i, blk))], writes=[r_wu[b]], dma=self.st_w)
                    for m in range(w // 128):
                        fc = blk * 4 + m
                        for half in range(TB // 512):
                            pg = (fc * 2 + half) % 2
                            bg, bu = self.pbank[pg * 2], self.pbank[pg * 2 + 1]
                            rg, ru = self.r_pb[pg * 2], self.r_pb[pg * 2 + 1]

                            def mm(e, wt, bank, m=m, half=half):
                                ins = None
                                for kc in range(8):
                                    ins = e.matmul(bank[:, :], lhsT=wt[:, kc, m * 128:(m + 1) * 128],
                                                   rhs=hT[:, kc, half * 512:(half + 1) * 512],
                                                   start=(kc == 0), stop=(kc == 7))
                                return ins
                            hres = [r_hT[half * 4 + jj][hh] for jj in range(4) for hh in range(2)]
                            p.op("pe", lambda e, wgs=wgs, bg=bg, mm=mm: mm(e, wgs, bg), reads=[r_wg[b]] + hres, writes=[rg])
                            p.op("pe", lambda e, wus=wus, bu=bu, mm=mm: mm(e, wus, bu), reads=[r_wu[b]] + hres, writes=[ru])
                            sgb = sg[pg]
                            p.op("act", lambda e, sgb=sgb, bg=bg: e.activation(out=sgb[:], in_=bg[:, :], func=AF.Silu),
                                 reads=[rg], writes=[r_sg[pg]])
                            dst = aT[:, fc, half * 512:(half + 1) * 512]
                            p.op("dve", lambda e, dst=dst, sgb=sgb, bu=bu: e.tensor_tensor(out=dst, in0=sgb[:], in1=bu[:, :], op=ALU.mult),
                                 reads=[r_sg[pg], ru], writes=[r_aT])
                if self.dbg_stage <= 2:
                    continue
                for j in range(TB // 128):
                    tt = t0 + j
                    xb = j % 2
                    xt = xt_bufs[xb][0]
                    rx = self.R("%s_xt%d" % (tag, xb))
                    p.op("sp", lambda e, xt=xt, tt=tt: e.dma_start(out=xt[:], in_=xsrc[tt * 128:(tt + 1) * 128, :]),
                         reads=[self.R("xres")], writes=[rx], dma=self.st_x)
                    for ch in range(2):
                        pb = 4 + (j * 2 + ch) % 2
                        bank, rb = self.pbank[pb], self.r_pb[pb]

                        def mmd(e, bank=bank, j=j, ch=ch):
                            ins = None
                            for fc in range(NFC):
                                ins = e.matmul(bank[:, :], lhsT=aT[:, fc, j * 128:(j + 1) * 128],
                                               rhs=wd_sb[:, fc, ch * 512:(ch + 1) * 512],
                                               start=(fc == 0), stop=(fc == NFC - 1))
                            return ins
                        p.op("pe", mmd, reads=[r_aT] + r_wd, writes=[rb])
                        xob = xo[xb]
                        p.op("dve", lambda e, xob=xob, bank=bank, xt=xt, ch=ch: e.scalar_tensor_tensor(
                            out=xob[:, ch * 512:(ch + 1) * 512], in0=bank[:, :], scalar=0.5, in1=xt[:, ch * 512:(ch + 1) * 512],
                            op0=ALU.mult, op1=ALU.add), reads=[rb, rx], writes=[r_xo[xb]])
                    p.op("sp", lambda e, xob=xob, tt=tt: e.dma_start(out=self.xres[tt * 128:(tt + 1) * 128, :], in_=xob[:]),
                         reads=[r_xo[xb]], writes=[self.R("xres_w%d" % (tt % 4))], dma=self.st_o)
            self.phase_barrier()
        self.x_src = self.xres

    def _store_waits(self):
        st = self.st_o
        return [(st.sems[idx % st.R], 16 * (idx // st.R + 1)) for idx in range(max(0, st.k - st.R), st.k)]

    def phase_barrier(self):
        p = self.p
        sems = self._store_waits()

        def fn(e, sems=sems):
            for sem, val in sems:
                e.wait_ge(sem, val)
            return e.nop()
        p.op("sp", fn, reads=[self.R("xres_w%d" % k) for k in range(4)], writes=[self.R("xres")])
        p.barrier()

    def prep_ssm(self, j):
        p = self.p
        for (c0, c1) in ((0, 2048), (2048, 4096), (4096, 6144), (6144, 6176)):
            for hh in range(2):
                s_ap = self.ssm_w_in[j, hh * 512:(hh + 1) * 512, c0:c1]
                d_ap = self.swin[j, hh * 512:(hh + 1) * 512, c0:c1]
                p.op("pool", lambda e, s_ap=s_ap, d_ap=d_ap: e.dma_start(out=d_ap, in_=s_ap),
                     writes=[self.R("swin%d_%d_%d" % (j, c0, hh))], dma=self.st_prep)

    def swin_res(self, j, c0):
        base = (c0 // 2048) * 2048 if c0 < 6144 else 6144
        return [self.R("swin%d_%d_%d" % (j, base, hh)) for hh in range(2)]

    def ssd(self, l):
        p = self.p
        nc = self.nc
        j = l // 2
        TBK = 256
        NBLK = S // TBK
        NT = TBK // 128
        xsrc = self.x_src
        with contextlib.ExitStack() as st:
            def sb(name, shape, dt):
                return st.enter_context(nc.sbuf_tensor(tag + name, list(shape), dt))
            tag = "s%d" % l
            R = lambda n: self.R(tag + n)
            L1 = sb("L1", [128, 128], F32)
            L2 = sb("L2", [128, 128], F32)
            ONES = sb("ONES", [128, 128], F32)
            cw = sb("cw", [128, 32, 4], F32)
            cb = sb("cb", [128, 32], F32)
            vec = sb("vec", [128, 3, 32], F32)
            a_bc = sb("a_bc", [128, 32], F32)
            Dbc = sb("Dbc", [128, 32, 1], F32)
            nw = sb("nw", [128, 2048], F32)
            wout = sb("wout", [128, 16, 1024], BF16)
            wdt = sb("wdt", [128, 8, 32], BF16)
            halo = sb("halo", [128, 32, 3], F32)
            state = sb("state", [128, 2048], F32)
            state_bf = sb("state_bf", [128, 2048], BF16)
            hT = sb("hT", [128, 8, TBK], BF16)
            wx = [sb("wx%d" % b, [128, 8, 512], BF16) for b in range(2)]
            xin = [sb("xin%d" % b, [128, TBK + 3], F32) for b in range(2)]
            acc = [sb("acc%d" % b, [128, TBK], F32) for b in range(2)]
            xsb = [sb("xsb%d" % b, [128, TBK], BF16) for b in range(2)]
            BT = sb("BT", [128, 8, TBK], BF16)
            CT = sb("CT", [128, 8, TBK], BF16)
            Btok = sb("Btok", [128, NT, 1024], BF16)
            xs_tok = sb("xs_tok", [128, NT, 2048], BF16)
            sz = sb("sz", [128, NT, 2048], BF16)
            xt_bufs = [(sb("xt%d" % b, [128, D], F32), sb("hb%d" % b, [128, D], BF16),
                        sb("sq%d" % b, [128, D], BF16), sb("ss%d" % b, [128, 1], F32),
                        sb("rs%d" % b, [128, 1], F32)) for b in range(2)]
            dtv = sb("dtv", [128, 32], F32)
            dt3 = sb("dt3", [128, 32, 1], F32)
            da = sb("da", [128, 32], F32)
            eall = sb("eall", [128, 96], F32)
            ea3 = sb("ea3", [128, 32, 1], F32)
            w23 = sb("w23", [128, 32, 1], F32)
            xdt = sb("xdt", [128, 2048], BF16)
            xdec = sb("xdec", [128, 2048], BF16)
            ybuf = sb("ybuf", [128, 2048], F32)
            t3 = sb("t3", [128, 2048], F32)
            gnb = sb("gnb", [128, 2048], BF16)
            gnT = sb("gnT", [128, 16, 128], BF16)
            Eg = [sb("Eg%d" % b, [128, 4, 128], F32) for b in range(2)]
            MT = [sb("MT%d" % b, [128, 4, 128], BF16) for b in range(2)]
            GTm = [sb("GTm%d" % b, [128, 1, 128], F32) for b in range(2)]
            Ada = [sb("Ada%d" % b, [128, 128], F32) for b in range(4)]
            ssg = sb("ssg", [128, 8], F32)
            rsg = sb("rsg", [128, 8, 1], F32)
            xo = sb("xo", [128, D], F32)
            xres_t = sb("xres_t", [128, D], F32)

            p.op("sp", lambda e: e.dma_start(out=L1[:], in_=self.tri_in[0]), writes=[R("L1")], dma=self.st_const)
            p.op("sp", lambda e: e.dma_start(out=L2[:], in_=self.tri_in[1]), writes=[R("L2")], dma=self.st_const)
            p.op("sp", lambda e: e.dma_start(out=ONES[:], in_=self.tri_in[2]), writes=[R("ONES")], dma=self.st_const)
            p.op("sp", lambda e: e.dma_start(out=cw[:], in_=self.ssm_cw[j]), writes=[R("cw")], dma=self.st_const)
            p.op("sp", lambda e: e.dma_start(out=cb[:], in_=self.ssm_cb[j]), writes=[R("cb")], dma=self.st_const)
            p.op("sp", lambda e: e.dma_start(out=vec[:].rearrange("p a b -> p (a b)"),
                                             in_=self.ssm_vec[j:j + 1, :].broadcast_to([128, 96])), writes=[R("vec")], dma=self.st_const)
            p.op("sp", lambda e: e.dma_start(out=nw[:], in_=self.ssm_norm[j:j + 1, :].broadcast_to([128, 2048])), writes=[R("nw")], dma=self.st_const)
            for q in range(4):
                src = self.ssm_w_out[j, q * 512:(q + 1) * 512, :].rearrange("(k p) m -> p k m", p=128)
                p.op("pool", lambda e, src=src, q=q: e.dma_start(out=wout[:, q * 4:(q + 1) * 4, :], in_=src), writes=[R("wout%d" % q)], dma=self.st_prep)
            r_wout = [R("wout%d" % q) for q in range(4)]
            p.op("sp", lambda e: e.dma_start(out=wdt[:], in_=self.swin[j, :, 6144:6176].rearrange("(k p) m -> p k m", p=128)),
                 reads=self.swin_res(j, 6144), writes=[R("wdt")], dma=self.st_const)
            p.op("act", lambda e: e.activation(out=a_bc[:], in_=vec[:, 1, :], func=AF.Exp), reads=[R("vec")], writes=[R("a_bc")])
            p.op("dve", lambda e: e.tensor_scalar(out=a_bc[:], in0=a_bc[:], scalar1=-1.0, scalar2=None, op0=ALU.mult), reads=[R("a_bc")], writes=[R("a_bc")])
            p.op("dve", lambda e: e.tensor_copy(out=Dbc[:, :, 0], in_=vec[:, 2, :]), reads=[R("vec")], writes=[R("Dbc")])
            p.op("pool", lambda e: e.memset(halo[:], 0.0), writes=[R("halo")])
            p.op("pool", lambda e: e.memset(state[:], 0.0), writes=[R("state")])
            p.op("pool", lambda e: e.memset(state_bf[:], 0.0), writes=[R("state_bf")])
            self.load_gain(l * 3 + 1)

            r_hT = [[R("hT%d_%d" % (jj, hh)) for hh in range(2)] for jj in range(NT)]
            hres_all = [r_hT[jj][hh] for jj in range(NT) for hh in range(2)]
            wxc = 0
            cvc = 0
            for blk in range(NBLK):
                t0 = blk * NT
                self.norm_transpose(xsrc, t0, NT, hT, r_hT, xt_bufs, tag)
                for wgI in range(8):
                    b = wxc % 2
                    wxc += 1
                    c0 = 2048 + wgI * 512
                    src = self.swin[j, :, c0:c0 + 512].rearrange("(k p) m -> p k m", p=128)
                    p.op("sp", lambda e, src=src, b=b: e.dma_start(out=wx[b][:], in_=src),
                         reads=self.swin_res(j, c0), writes=[R("wx%d" % b)], dma=self.st_w)
                    for m in range(4):
                        cc = wgI * 4 + m
                        pb = cc % 2
                        bank, rb = self.pbank[pb], self.r_pb[pb]

                        def mm(e, b=b, m=m, bank=bank):
                            for kc in range(8):
                                ins = e.matmul(bank[:, 0:TBK], lhsT=wx[b][:, kc, m * 128:(m + 1) * 128], rhs=hT[:, kc, :],
                                               start=(kc == 0), stop=(kc == 7))
                            return ins
                        p.op("pe", mm, reads=[R("wx%d" % b)] + hres_all, writes=[rb])
                        cbuf = cvc % 2
                        cvc += 1
                        xi, ac = xin[cbuf], acc[cbuf]
                        rxi, rac = R("xin%d" % cbuf), R("acc%d" % cbuf)
                        p.op("pool", lambda e, xi=xi, cc=cc: e.tensor_copy(out=xi[:, 0:3], in_=halo[:, cc, :]), reads=[R("halo")], writes=[rxi])
                        p.op("act", lambda e, xi=xi, bank=bank: e.copy(out=xi[:, 3:3 + TBK], in_=bank[:, 0:TBK]), reads=[rb], writes=[rxi])
                        p.op("pool", lambda e, xi=xi, cc=cc: e.tensor_copy(out=halo[:, cc, :], in_=xi[:, TBK:TBK + 3]), reads=[rxi], writes=[R("halo")])
                        p.op("dve", lambda e, xi=xi, ac=ac, cc=cc: e.tensor_scalar(out=ac[:], in0=xi[:, 0:TBK], scalar1=cw[:, cc, 0:1], scalar2=cb[:, cc:cc + 1],
                                                                                  op0=ALU.mult, op1=ALU.add), reads=[rxi, R("cw"), R("cb")], writes=[rac])
                        for w in range(1, 4):
                            p.op("dve", lambda e, xi=xi, ac=ac, cc=cc, w=w: e.scalar_tensor_tensor(out=ac[:], in0=xi[:, w:w + TBK], scalar=cw[:, cc, w:w + 1], in1=ac[:],
                                                                                                 op0=ALU.mult, op1=ALU.add), reads=[rxi, rac, R("cw")], writes=[rac])
                        if cc < 16:
                            xb_, rxb = xsb[cbuf], R("xsb%d" % cbuf)
                            p.op("act", lambda e, ac=ac, xb_=xb_: e.activation(out=xb_[:], in_=ac[:], func=AF.Silu), reads=[rac], writes=[rxb])
                            hp = cc % 2
                            rp = self.r_ptr[hp]

                            def tr(e, xb_=xb_, hp=hp):
                                for q in range(NT):
                                    ins = e.transpose(out=self.ptrh[hp][:, q * 128:(q + 1) * 128], in_=xb_[:, q * 128:(q + 1) * 128], identity=self.ident_b[:])
                                return ins
                            p.op("pe", tr, reads=[rxb, self.R("ident_b")], writes=[rp])
                            dst = xs_tok[:, :, cc * 128:(cc + 1) * 128]
                            srcp = self.ptrh[hp][:, 0:NT * 128].rearrange("p (q m) -> p q m", q=NT)
                            p.op("dve", lambda e, dst=dst, srcp=srcp: e.tensor_copy(out=dst, in_=srcp), reads=[rp], writes=[R("xs_tok")])
                        elif cc < 24:
                            g = cc - 16
                            p.op("act", lambda e, ac=ac, g=g: e.activation(out=BT[:, g, :], in_=ac[:], func=AF.Silu), reads=[rac], writes=[R("BT%d" % g)])
                            hp = cc % 2
                            rp = self.r_ptr[hp]

                            def tr(e, g=g, hp=hp):
                                for q in range(NT):
                                    ins = e.transpose(out=self.ptrh[hp][:, q * 128:(q + 1) * 128], in_=BT[:, g, q * 128:(q + 1) * 128], identity=self.ident_b[:])
                                return ins
                            p.op("pe", tr, reads=[R("BT%d" % g), self.R("ident_b")], writes=[rp])
                            dst = Btok[:, :, g * 128:(g + 1) * 128]
                            srcp = self.ptrh[hp][:, 0:NT * 128].rearrange("p (q m) -> p q m", q=NT)
                            p.op("dve", lambda e, dst=dst, srcp=srcp: e.tensor_copy(out=dst, in_=srcp), reads=[rp], writes=[R("Btok")])
                        else:
                            g = cc - 24
                            p.op("act", lambda e, ac=ac, g=g: e.activation(out=CT[:, g, :], in_=ac[:], func=AF.Silu), reads=[rac], writes=[R("CT%d" % g)])
                for zc in range(4):
                    b = wxc % 2
                    wxc += 1
                    c0 = zc * 512
                    src = self.swin[j, :, c0:c0 + 512].rearrange("(k p) m -> p k m", p=128)
                    p.op("sp", lambda e, src=src, b=b: e.dma_start(out=wx[b][:], in_=src),
                         reads=self.swin_res(j, c0), writes=[R("wx%d" % b)], dma=self.st_w)
                    for q in range(NT):
                        pb = (zc * NT + q) % 2
                        bank, rb = self.pbank[pb], self.r_pb[pb]

                        def mmz(e, b=b, q=q, bank=bank):
                            for kc in range(8):
                                ins = e.matmul(bank[:, :], lhsT=hT[:, kc, q * 128:(q + 1) * 128], rhs=wx[b][:, kc, :], start=(kc == 0), stop=(kc == 7))
                            return ins
                        p.op("pe", mmz, reads=[R("wx%d" % b)] + r_hT[q], writes=[rb])
                        p.op("act", lambda e, q=q, zc=zc, bank=bank: e.activation(out=sz[:, q, zc * 512:(zc + 1) * 512], in_=bank[:, :], func=AF.Silu),
                             reads=[rb], writes=[R("sz%d" % q)])
                for q in range(NT):
                    tt = t0 + q
                    tsl = slice(q * 128, (q + 1) * 128)
                    b0, rb0 = self.pbank[0], self.r_pb[0]

                    def mmdt(e, q=q):
                        for kc in range(8):
                            ins = e.matmul(b0[:, 0:32], lhsT=hT[:, kc, q * 128:(q + 1) * 128], rhs=wdt[:, kc, :], start=(kc == 0), stop=(kc == 7))
                        return ins
                    p.op("pe", mmdt, reads=[R("wdt")] + r_hT[q], writes=[rb0])
                    p.op("dve", lambda e: e.tensor_tensor(out=dtv[:], in0=b0[:, 0:32], in1=vec[:, 0, :], op=ALU.add), reads=[rb0, R("vec")], writes=[R("dtv")])
                    p.op("act", lambda e: e.activation(out=dtv[:], in_=dtv[:], func=AF.Exp), reads=[R("dtv")], writes=[R("dtv")])
                    p.op("act", lambda e: e.activation(out=dt3[:, :, 0], in_=dtv[:], func=AF.Ln, bias=1.0), reads=[R("dtv")], writes=[R("dt3")])
                    p.op("dve", lambda e: e.tensor_tensor(out=da[:], in0=dt3[:, :, 0], in1=a_bc[:], op=ALU.mult), reads=[R("dt3"), R("a_bc")], writes=[R("da")])

                    def mmcs(e):
                        e.matmul(b0[:, 32:64], lhsT=L1[:], rhs=da[:], start=True, stop=True)
                        e.matmul(b0[:, 64:96], lhsT=L2[:], rhs=da[:], start=True, stop=True)
                        return e.matmul(b0[:, 96:128], lhsT=ONES[:], rhs=da[:], start=True, stop=True)
                    p.op("pe", mmcs, reads=[R("da"), R("L1"), R("L2"), R("ONES")], writes=[rb0])
                    p.op("act", lambda e: e.activation(out=eall[:], in_=b0[:, 32:128], func=AF.Exp), reads=[rb0], writes=[R("eall")])
                    p.op("dve", lambda e: e.tensor_copy(out=ea3[:, :, 0], in_=eall[:, 0:32]), reads=[R("eall")], writes=[R("ea3")])
                    p.op("dve", lambda e: e.tensor_tensor(out=w23[:, :, 0], in0=dt3[:, :, 0], in1=eall[:, 32:64], op=ALU.mult), reads=[R("dt3"), R("eall")], writes=[R("w23")])
                    xs3 = xs_tok[:, q, :].rearrange("p (h d) -> p h d", h=32)
                    p.op("dve", lambda e, xs3=xs3: e.tensor_tensor(out=xdt[:].rearrange("p (h d) -> p h d", h=32), in0=xs3, in1=dt3[:].to_broadcast([128, 32, 64]), op=ALU.mult),
                         reads=[R("xs_tok"), R("dt3")], writes=[R("xdt")])
                    p.op("pool", lambda e, xs3=xs3: e.tensor_tensor(out=xdec[:].rearrange("p (h d) -> p h d", h=32), in0=xs3, in1=w23[:].to_broadcast([128, 32, 64]), op=ALU.mult),
                         reads=[R("xs_tok"), R("w23")], writes=[R("xdec")])
                    p.op("pool", lambda e, xs3=xs3: e.tensor_tensor(out=t3[:].rearrange("p (h d) -> p h d", h=32), in0=xs3, in1=Dbc[:].to_broadcast([128, 32, 64]), op=ALU.mult),
                         reads=[R("xs_tok"), R("Dbc")], writes=[R("t3")])
                    for g in range(8):
                        gb = g % 2
                        b1, rb1 = self.pbank[1], self.r_pb[1]
                        gcol = slice((g % 4) * 128, (g % 4 + 1) * 128)
                        p.op("pe", lambda e, g=g, gcol=gcol, tsl=tsl: e.matmul(b1[:, gcol], lhsT=BT[:, g, tsl], rhs=CT[:, g, tsl], start=True, stop=True),
                             reads=[R("BT%d" % g), R("CT%d" % g)], writes=[rb1])
                        p.op("dve", lambda e, gb=gb, gcol=gcol: e.tensor_tensor(out=GTm[gb][:, 0, :], in0=b1[:, gcol], in1=L1[:], op=ALU.mult),
                             reads=[rb1, R("L1")], writes=[R("GTm%d" % gb)])
                        bs, rbs = self.pbank[2 + gb], self.r_pb[2 + gb]
                        for r in range(4):
                            h = 4 * g + r
                            ab = h % 4
                            p.op("pool", lambda e, ab=ab, h=h: e.tensor_scalar(out=Ada[ab][:], in0=L2[:], scalar1=da[:, h:h + 1], scalar2=None, op0=ALU.mult),
                                 reads=[R("L2"), R("da")], writes=[R("Ada%d" % ab)])
                            p.op("pe", lambda e, ab=ab, r=r, bs=bs: e.matmul(bs[:, r * 128:(r + 1) * 128], lhsT=Ada[ab][:], rhs=L1[:], start=True, stop=True),
                                 reads=[R("Ada%d" % ab), R("L1")], writes=[rbs])
                        p.op("act", lambda e, gb=gb, bs=bs: e.activation(out=Eg[gb][:].rearrange("p r l -> p (r l)"), in_=bs[:, :], func=AF.Exp),
                             reads=[rbs], writes=[R("Eg%d" % gb)])
                        p.op("pool", lambda e, gb=gb: e.tensor_tensor(out=MT[gb][:], in0=Eg[gb][:], in1=GTm[gb][:].to_broadcast([128, 4, 128]), op=ALU.mult),
                             reads=[R("Eg%d" % gb), R("GTm%d" % gb)], writes=[R("MT%d" % gb)])
                        by, rby = self.pbank[4 + gb], self.r_pb[4 + gb]

                        def mmy(e, g=g, gb=gb, by=by, tsl=tsl):
                            for r in range(4):
                                h = 4 * g + r
                                e.matmul(by[:, r * 64:(r + 1) * 64], lhsT=MT[gb][:, r, :], rhs=xdt[:, h * 64:(h + 1) * 64], start=True, stop=True)
                            return e.matmul(by[:, 256:512], lhsT=CT[:, g, tsl], rhs=state_bf[:, g * 256:(g + 1) * 256], start=True, stop=True)
                        p.op("pe", mmy, reads=[R("MT%d" % gb), R("xdt"), R("CT%d" % g), R("state_bf")], writes=[rby])
                        p.op("pe", lambda e, g=g, q=q: e.matmul(b0[:, 256:512], lhsT=Btok[:, q, g * 128:(g + 1) * 128], rhs=xdec[:, g * 256:(g + 1) * 256], start=True, stop=True),
                             reads=[R("Btok"), R("xdec")], writes=[rb0])
                        for r in range(4):
                            h = 4 * g + r
                            p.op("dve", lambda e, h=h, r=r: e.scalar_tensor_tensor(out=state[:, h * 64:(h + 1) * 64], in0=state[:, h * 64:(h + 1) * 64],
                                                                                     scalar=eall[:, 64 + h:65 + h], in1=b0[:, 256 + r * 64:256 + (r + 1) * 64],
                                                                                     op0=ALU.mult, op1=ALU.add), reads=[R("state"), R("eall"), rb0], writes=[R("state")])
                        yg = ybuf[:, g * 256:(g + 1) * 256].rearrange("p (r d) -> p r d", r=4)
                        p.op("dve", lambda e, yg=yg, by=by, g=g: e.tensor_tensor(out=yg, in0=by[:, 256:512].rearrange("p (r d) -> p r d", r=4),
                                                                               in1=ea3[:, 4 * g:4 * g + 4, :].to_broadcast([128, 4, 64]), op=ALU.mult),
                             reads=[rby, R("ea3")], writes=[R("ybuf")])
                        p.op("dve", lambda e, g=g, by=by: e.tensor_tensor(out=ybuf[:, g * 256:(g + 1) * 256], in0=ybuf[:, g * 256:(g + 1) * 256], in1=by[:, 0:256], op=ALU.add),
                             reads=[R("ybuf"), rby], writes=[R("ybuf")])
                    p.op("act", lambda e: e.copy(out=state_bf[:], in_=state[:]), reads=[R("state")], writes=[R("state_bf")])
                    p.op("pool", lambda e: e.tensor_tensor(out=ybuf[:], in0=ybuf[:], in1=t3[:], op=ALU.add), reads=[R("ybuf"), R("t3")], writes=[R("ybuf")])
                    p.op("pool", lambda e, q=q: e.tensor_tensor(out=ybuf[:], in0=ybuf[:], in1=sz[:, q, :], op=ALU.mult), reads=[R("ybuf"), R("sz%d" % q)], writes=[R("ybuf")])
                    for g in range(8):
                        p.op("act", lambda e, g=g: e.activation(out=gnb[:, g * 256:(g + 1) * 256], in_=ybuf[:, g * 256:(g + 1) * 256], func=AF.Square, accum_out=ssg[:, g:g + 1]),
                             reads=[R("ybuf")], writes=[R("gnb"), R("ssg")])
                    p.op("act", lambda e: e.activation(out=rsg[:, :, 0], in_=ssg[:], func=AF.Sqrt, scale=1.0 / 256, bias=self.epsb[:]), reads=[R("ssg"), self.R("epsb")], writes=[R("rsg")])
                    p.op("dve", lambda e: e.reciprocal(out=rsg[:, :, 0], in_=rsg[:, :, 0]), reads=[R("rsg")], writes=[R("rsg")])
                    p.op("dve", lambda e: e.tensor_tensor(out=ybuf[:].rearrange("p (g d) -> p g d", g=8), in0=ybuf[:].rearrange("p (g d) -> p g d", g=8),
                                                          in1=rsg[:].to_broadcast([128, 8, 256]), op=ALU.mult), reads=[R("ybuf"), R("rsg")], writes=[R("ybuf")])
                    p.op("pool", lambda e: e.tensor_tensor(out=gnb[:], in0=ybuf[:], in1=nw[:], op=ALU.mult), reads=[R("ybuf"), R("nw"), R("gnb")], writes=[R("gnb")])
                    for qq in range(4):
                        hp = qq % 2
                        rp = self.r_ptr[hp]

                        def trg(e, qq=qq, hp=hp):
                            for m in range(4):
                                k = qq * 4 + m
                                ins = e.transpose(out=self.ptrh[hp][:, m * 128:(m + 1) * 128], in_=gnb[:, k * 128:(k + 1) * 128], identity=self.ident_b[:])
                            return ins
                        p.op("pe", trg, reads=[R("gnb"), self.R("ident_b")], writes=[rp])
                        dst = gnT[:, qq * 4:(qq + 1) * 4, :]
                        srcp = self.ptrh[hp][:, :].rearrange("p (q m) -> p q m", q=4)
                        if hp == 0:
                            p.op("act", lambda e, dst=dst, srcp=srcp: e.copy(out=dst, in_=srcp), reads=[rp], writes=[R("gnT%d" % qq)])
                        else:
                            p.op("dve", lambda e, dst=dst, srcp=srcp: e.tensor_copy(out=dst, in_=srcp), reads=[rp], writes=[R("gnT%d" % qq)])
                    p.op("sp", lambda e, tt=tt: e.dma_start(out=xres_t[:], in_=xsrc[tt * 128:(tt + 1) * 128, :]),
                         reads=[self.R("xres")], writes=[R("xres_t")], dma=self.st_x)
                    for ch in range(2):
                        bo, rbo = self.pbank[1 + ch], self.r_pb[1 + ch]

                        def mmo(e, ch=ch, bo=bo):
                            for k in range(16):
                                ins = e.matmul(bo[:, :], lhsT=gnT[:, k, :], rhs=wout[:, k, ch * 512:(ch + 1) * 512], start=(k == 0), stop=(k == 15))
                            return ins
                        p.op("pe", mmo, reads=[R("gnT%d" % qq) for qq in range(4)] + r_wout, writes=[rbo])
                        p.op("dve", lambda e, ch=ch, bo=bo: e.tensor_tensor(out=xo[:, ch * 512:(ch + 1) * 512], in0=bo[:, :], in1=xres_t[:, ch * 512:(ch + 1) * 512], op=ALU.add),
                             reads=[rbo, R("xres_t")], writes=[R("xo")])
                    p.op("sp", lambda e, tt=tt: e.dma_start(out=self.xres[tt * 128:(tt + 1) * 128, :], in_=xo[:]),
                         reads=[R("xo")], writes=[self.R("xres_w%d" % (tt % 4))], dma=self.st_o)
            self.phase_barrier()
        self.x_src = self.xres

    def prep_attn(self, j):
        p = self.p
        for (c0, c1) in ((0, 2048), (2048, NAW)):
            for hh in range(2):
                s_ap = self.attn_wr[j, hh * 512:(hh + 1) * 512, c0:c1]
                d_ap = self.awin[j, hh * 512:(hh + 1) * 512, c0:c1]
                p.op("pool", lambda e, s_ap=s_ap, d_ap=d_ap: e.dma_start(out=d_ap, in_=s_ap),
                     writes=[self.R("awin%d_%d_%d" % (j, c0, hh))], dma=self.st_prep)

    def awin_res(self, j, c0, c1):
        out = []
        for base in (0, 2048):
            hi = 2048 if base == 0 else NAW
            if c0 < hi and c1 > base:
                out += [self.R("awin%d_%d_%d" % (j, base, hh)) for hh in range(2)]
        return out

    def attn(self, l):
        p = self.p
        nc = self.nc
        j = l // 2
        xsrc = self.x_src
        tag = "a%d" % l
        R = lambda n: self.R(tag + n)
        BIG = 30000.0
        dbg = self.attn_dbg or ""
        use_cmp = ("nocmp" not in dbg)
        use_slc = ("noslc" not in dbg)
        use_win = ("nowin" not in dbg)
        with contextlib.ExitStack() as st_long:
            def sbl(name, shape, dt):
                return st_long.enter_context(nc.sbuf_tensor(tag + name, list(shape), dt))
            kT = sbl("kT", [64, 6, S], BF16)
            Vall = sbl("V", [128, 32, 6, 65], BF16)
            gates = sbl("gates", [128, 32, 24], F32)
            kcmpT = sbl("kcmpT", [64, 2, 256], BF16)
            Vcmp = sbl("Vcmp", [128, 2, 2, 129], BF16)
            esink = sbl("esink", [128, 8], F32)
            p.op("pool", lambda e: e.memset(Vall[:, :, :, 64:65], 1.0), writes=[R("Vones")])
            p.op("sp", lambda e: e.dma_start(out=esink[:], in_=self.sinks_in[j:j + 1, :].broadcast_to([128, 8])), writes=[R("esink")], dma=self.st_const)
            p.op("act", lambda e: e.activation(out=esink[:], in_=esink[:], func=AF.Exp), reads=[R("esink")], writes=[R("esink")])
            self.load_gain(l * 3 + 1)

            with contextlib.ExitStack() as st_x:
                kcT = st_x.enter_context(nc.sbuf_tensor(tag + "kcT", [64, 2, S], BF16))
                vcT = st_x.enter_context(nc.sbuf_tensor(tag + "vcT", [128, S], BF16))
                with contextlib.ExitStack() as st1:
                    def sb(name, shape, dt):
                        return st1.enter_context(nc.sbuf_tensor(tag + name, list(shape), dt))
                    TBK = 512
                    NT = 4
                    hT = sb("hT", [128, 8, TBK], BF16)
                    wb = [sb("wb%d" % b, [128, 8, 512], BF16) for b in range(2)]
                    cosb = sb("cosb", [64, TBK], F32)
                    sinb = sb("sinb", [64, TBK], F32)
                    t1 = [sb("t1_%d" % b, [64, TBK], F32) for b in range(2)]
                    t2 = [sb("t2_%d" % b, [64, TBK], F32) for b in range(2)]
                    qst = [sb("qst%d" % b, [64, TBK], BF16) for b in range(2)]
                    xt_bufs = [(sb("xt%d" % b, [128, D], F32), sb("hb%d" % b, [128, D], BF16),
                                sb("sq%d" % b, [128, D], BF16), sb("ss%d" % b, [128, 1], F32),
                                sb("rs%d" % b, [128, 1], F32)) for b in range(2)]
                    r_hT = [[R("hT%d_%d" % (jj, hh)) for hh in range(2)] for jj in range(NT)]
                    hres_all = [r_hT[jj][hh] for jj in range(NT) for hh in range(2)]
                    wc = 0
                    hc = 0
                    for blk in range(S // TBK):
                        t0 = blk * NT
                        csl = slice(blk * TBK, (blk + 1) * TBK)
                        self.norm_transpose(xsrc, t0, NT, hT, r_hT, xt_bufs, tag)
                        p.op("sp", lambda e, csl=csl: e.dma_start(out=cosb[:], in_=self.rope_in[0, :, csl]), writes=[R("cosb")], dma=self.st_x)
                        p.op("sp", lambda e, csl=csl: e.dma_start(out=sinb[:], in_=self.rope_in[1, :, csl]), writes=[R("sinb")], dma=self.st_x)
                        for wgI in range(6):
                            b = wc % 2
                            wc += 1
                            c0 = wgI * 512
                            src = self.awin[j, :, c0:c0 + 512].rearrange("(k p) m -> p k m", p=128)
                            p.op("sp", lambda e, src=src, b=b: e.dma_start(out=wb[b][:], in_=src),
                                 reads=self.awin_res(j, c0, c0 + 512), writes=[R("wb%d" % b)], dma=self.st_w)
                            for m in range(4):
                                hd = wgI * 4 + m
                                hb_ = hc % 2
                                hc += 1
                                bA, rA = self.pbank[hb_ * 2], self.r_pb[hb_ * 2]
                                bB, rB = self.pbank[hb_ * 2 + 1], self.r_pb[hb_ * 2 + 1]

                                def mm(e, bank, off, b=b, m=m):
                                    for kc in range(8):
                                        ins = e.matmul(bank[0:64, :], lhsT=wb[b][:, kc, m * 128 + off:m * 128 + off + 64], rhs=hT[:, kc, :],
                                                       start=(kc == 0), stop=(kc == 7))
                                    return ins
                                p.op("pe", lambda e, mm=mm, bA=bA: mm(e, bA, 0), reads=[R("wb%d" % b)] + hres_all, writes=[rA])
                                p.op("pe", lambda e, mm=mm, bB=bB: mm(e, bB, 64), reads=[R("wb%d" % b)] + hres_all, writes=[rB])
                                p.op("dve", lambda e, hb_=hb_, bA=bA: e.tensor_tensor(out=t1[hb_][:], in0=bA[0:64, :], in1=cosb[:], op=ALU.mult),
                                     reads=[rA, R("cosb")], writes=[R("t1_%d" % hb_)])
                                p.op("dve", lambda e, hb_=hb_, bB=bB: e.tensor_tensor(out=t2[hb_][:], in0=bB[0:64, :], in1=sinb[:], op=ALU.mult),
                                     reads=[rB, R("sinb")], writes=[R("t2_%d" % hb_)])
                                if hd < 8 or 10 <= hd < 18:
                                    qh = hd if hd < 8 else hd - 10 + 8
                                    p.op("pool", lambda e, hb_=hb_: e.tensor_tensor(out=qst[hb_][:], in0=t1[hb_][:], in1=t2[hb_][:], op=ALU.add),
                                         reads=[R("t1_%d" % hb_), R("t2_%d" % hb_)], writes=[R("qst%d" % hb_)])
                                    p.op("sp", lambda e, hb_=hb_, qh=qh, csl=csl: e.dma_start(out=self.qT[qh, :, csl], in_=qst[hb_][:]),
                                         reads=[R("qst%d" % hb_)], writes=[self.R("qT_%d_%d" % (qh, blk))], dma=self.st_o)
                                else:
                                    if hd < 10:
                                        dst, rd = kT[:, hd - 8, csl], R("kT%d" % (hd - 8))
                                    elif hd < 20:
                                        dst, rd = kcT[:, hd - 18, csl], R("kcT%d" % (hd - 18))
                                    elif hd < 22:
                                        dst, rd = kT[:, 2 + hd - 20, csl], R("kT%d" % (2 + hd - 20))
                                    else:
                                        dst, rd = kT[:, 4 + hd - 22, csl], R("kT%d" % (4 + hd - 22))
                                    p.op("pool", lambda e, hb_=hb_, dst=dst: e.tensor_tensor(out=dst, in0=t1[hb_][:], in1=t2[hb_][:], op=ALU.add),
                                         reads=[R("t1_%d" % hb_), R("t2_%d" % hb_)], writes=[rd])
                        b = wc % 2
                        wc += 1
                        src = self.awin[j, :, 3072:3608].rearrange("(k p) m -> p k m", p=128)
                        src_vc = self.awin[j, :, 3072:3200].rearrange("(k p) m -> p k m", p=128)
                        src_tm = self.awin[j, :, 3200:3608].rearrange("(k p) m -> p k m", p=128)
                        b2 = wc % 2
                        wc += 1
                        p.op("sp", lambda e, b=b, src_vc=src_vc: e.dma_start(out=wb[b][:, :, 0:128], in_=src_vc),
                             reads=self.awin_res(j, 3072, 3200), writes=[R("wb%d" % b)], dma=self.st_w)
                        p.op("sp", lambda e, b2=b2, src_tm=src_tm: e.dma_start(out=wb[b2][:, :, 0:408], in_=src_tm),
                             reads=self.awin_res(j, 3200, 3608), writes=[R("wb%d" % b2)], dma=self.st_w)
                        bA, rA = self.pbank[4], self.r_pb[4]

                        def mmvc(e, b=b, bA=bA):
                            for kc in range(8):
                                ins = e.matmul(bA[:, :], lhsT=wb[b][:, kc, 0:128], rhs=hT[:, kc, :], start=(kc == 0), stop=(kc == 7))
                            return ins
                        p.op("pe", mmvc, reads=[R("wb%d" % b)] + hres_all, writes=[rA])
                        p.op("act", lambda e, bA=bA, csl=csl: e.copy(out=vcT[:, csl], in_=bA[:, :]), reads=[rA], writes=[R("vcT")])
                        for q in range(NT):
                            tt = t0 + q
                            bT, rT = self.pbank[5], self.r_pb[5]

                            def mmtm(e, q=q, b2=b2, bT=bT):
                                for kc in range(8):
                                    ins = e.matmul(bT[:, 0:408], lhsT=hT[:, kc, q * 128:(q + 1) * 128], rhs=wb[b2][:, kc, 0:408], start=(kc == 0), stop=(kc == 7))
                                return ins
                            p.op("pe", mmtm, reads=[R("wb%d" % b2)] + r_hT[q], writes=[rT])
                            p.op("dve", lambda e, tt=tt, bT=bT: e.tensor_copy(out=Vall[:, tt, :, 0:64], in_=bT[:, 0:384].rearrange("p (a d) -> p a d", a=6)),
                                 reads=[rT], writes=[R("Vall")])
                            p.op("act", lambda e, tt=tt, bT=bT: e.activation(out=gates[:, tt, :], in_=bT[:, 384:408], func=AF.Sigmoid),
                                 reads=[rT], writes=[R("gates")])
                    p.barrier()
                with contextlib.ExitStack() as st2:
                    def sb(name, shape, dt):
                        return st2.enter_context(nc.sbuf_tensor(tag + name, list(shape), dt))
                    w1s = sb("w1s", [64, 32, 128], BF16)
                    w2s = sb("w2s", [128, 64], BF16)
                    posf = sb("posf", [64, 32], F32)
                    posb_ = sb("posb", [64, 32, 2], BF16)
                    pbias = sb("pbias", [128, 1], F32)
                    u = sb("u", [128, 256], F32)
                    u2 = sb("u2", [128, 256], F32)
                    sg_ = sb("sgm", [128, 256], F32)
                    gl = sb("gl", [128, 256], BF16)
                    wself = sb("wself", [128, 2, 64], F32)
                    p.op("sp", lambda e: e.dma_start(out=wself[:], in_=self.wsel_in), writes=[R("wself")], dma=self.st_const)
                    p.op("dve", lambda e: e.memset(u[:], 0.0), writes=[R("u")])
                    p.op("pool", lambda e: e.memset(Vcmp[:, :, :, 64:65], 1.0), writes=[R("Vcmp1")])
                    for g in range(2):
                        p.op("dve", lambda e, g=g: e.tensor_copy(out=Vcmp[:, g, :, 65:129], in_=wself[:]), reads=[R("wself")], writes=[R("VcmpW%d" % g)])
                    for kv in range(2):
                        w1_in = self.cmp_w1[j, kv].rearrange("(pp d) h -> d pp h", d=64)
                        p.op("pool", lambda e, w1_in=w1_in: e.dma_start(out=w1s[:], in_=w1_in), writes=[R("w1s")], dma=self.st_prep)
                        p.op("pool", lambda e, kv=kv: e.dma_start(out=w2s[:], in_=self.cmp_w2[j, kv]), writes=[R("w2s")], dma=self.st_prep)
                        p.op("sp", lambda e, kv=kv: e.dma_start(out=posf[:], in_=self.cmp_posT[j, kv]), writes=[R("posf")], dma=self.st_const)
                        p.op("dve", lambda e: e.tensor_copy(out=posb_[:], in_=posf[:].rearrange("p (a o) -> p a o", o=1).to_broadcast([64, 32, 2])), reads=[R("posf")], writes=[R("posb")])
                        b0, rb0 = self.pbank[0], self.r_pb[0]

                        def mmb(e):
                            for pp in range(32):
                                ins = e.matmul(b0[:, 0:2], lhsT=w1s[:, pp, :], rhs=posb_[:, pp, :], start=(pp == 0), stop=(pp == 31))
                            return ins
                        p.op("pe", mmb, reads=[R("w1s"), R("posb")], writes=[rb0])
                        p.op("dve", lambda e: e.tensor_copy(out=pbias[:], in_=b0[:, 0:1]), reads=[rb0], writes=[R("pbias")])
                        for g in range(2):
                            b1, rb1 = self.pbank[1 + g], self.r_pb[1 + g]
                            if kv == 0:
                                srcT = kcT[:, g, :]
                                rsrc = R("kcT%d" % g)
                            else:
                                srcT = vcT[g * 64:(g + 1) * 64, :]
                                rsrc = R("vcT")

                            def mmh(e, srcT=srcT, b1=b1, g=g):
                                for pp in range(32):
                                    ins = e.matmul(b1[:, 0:255], lhsT=w1s[g * 64 * kv:g * 64 * kv + 64, pp, :] if False else w1s[:, pp, :],
                                                   rhs=srcT[:, pp:pp + 16 * 254 + 1:16], start=(pp == 0), stop=(pp == 31))
                                return ins
                            if kv == 1 and g == 1:
                                vtmp = sb("vtmp", [64, S], BF16)
                                p.op("sp", lambda e, vtmp=vtmp: e.dma_start(out=vtmp[:], in_=vcT[64:128, :]), reads=[R("vcT")], writes=[R("vtmp")], dma=self.st_x)
                                srcT2 = vtmp[:, :]

                                def mmh(e, srcT2=srcT2, b1=b1):
                                    for pp in range(32):
                                        ins = e.matmul(b1[:, 0:255], lhsT=w1s[:, pp, :], rhs=srcT2[:, pp:pp + 16 * 254 + 1:16], start=(pp == 0), stop=(pp == 31))
                                    return ins
                                rsrc = R("vtmp")
                            p.op("pe", mmh, reads=[R("w1s"), rsrc], writes=[rb1])
                            p.op("act", lambda e, b1=b1: e.activation(out=u[:, 0:255], in_=b1[:, 0:255], func=AF.Identity, bias=pbias[:]),
                                 reads=[rb1, R("pbias"), R("u")], writes=[R("u")])
                            p.op("dve", lambda e: e.tensor_tensor(out=u2[:], in0=u[:], in1=u[:], op=ALU.mult), reads=[R("u")], writes=[R("u2")])
                            p.op("dve", lambda e: e.tensor_scalar(out=u2[:], in0=u2[:], scalar1=0.044715, scalar2=1.0, op0=ALU.mult, op1=ALU.add), reads=[R("u2")], writes=[R("u2")])
                            p.op("dve", lambda e: e.tensor_tensor(out=u2[:], in0=u2[:], in1=u[:], op=ALU.mult), reads=[R("u2"), R("u")], writes=[R("u2")])
                            p.op("act", lambda e: e.activation(out=sg_[:], in_=u2[:], func=AF.Sigmoid, scale=1.5957691216057308), reads=[R("u2")], writes=[R("sgm")])
                            p.op("dve", lambda e: e.tensor_tensor(out=gl[:], in0=u[:], in1=sg_[:], op=ALU.mult), reads=[R("u"), R("sgm")], writes=[R("gl")])
                            b3, rb3 = self.pbank[3], self.r_pb[3]
                            if kv == 0:
                                p.op("pe", lambda e, b3=b3: e.matmul(b3[0:64, 0:256], lhsT=w2s[:], rhs=gl[:], start=True, stop=True), reads=[R("w2s"), R("gl")], writes=[rb3])
                                p.op("dve", lambda e, g=g, b3=b3: e.tensor_copy(out=kcmpT[:, g, :], in_=b3[0:64, 0:256]), reads=[rb3], writes=[R("kcmpT%d" % g)])
                            else:
                                def mmv(e, b3=b3):
                                    for ct in range(2):
                                        ins = e.matmul(b3[:, ct * 64:(ct + 1) * 64], lhsT=gl[:, ct * 128:(ct + 1) * 128], rhs=w2s[:], start=True, stop=True)
                                    return ins
                                p.op("pe", mmv, reads=[R("w2s"), R("gl")], writes=[rb3])
                                p.op("dve", lambda e, g=g, b3=b3: e.tensor_copy(out=Vcmp[:, g, :, 0:64], in_=b3[:, 0:128].rearrange("p (c d) -> p c d", c=2)),
                                     reads=[rb3], writes=[R("VcmpV%d" % g)])
                    p.barrier()
            with contextlib.ExitStack() as st3:
                def sb(name, shape, dt):
                    return st3.enter_context(nc.sbuf_tensor(tag + name, list(shape), dt))
                wout = sb("wout", [128, 8, D], BF16)
                for q in range(2):
                    src = self.attn_w_out[j, q * 512:(q + 1) * 512, :].rearrange("(k p) m -> p k m", p=128)
                    p.op("pool", lambda e, src=src, q=q: e.dma_start(out=wout[:, q * 4:(q + 1) * 4, :], in_=src), writes=[R("wout%d" % q)], dma=self.st_prep)
                r_wout = [R("wout0"), R("wout1")]
                expand = sb("expand", [64, 32, 128], BF16)
                onesb = sb("onesb", [64, 32 * 128], BF16)
                p.op("pool", lambda e: e.memset(onesb[:], 1.0), writes=[R("onesb")])
                p.op("pool", lambda e: e.affine_select(out=expand[:].rearrange("p a (h m) -> p a h m", h=2), in_=onesb[:].rearrange("p (a h m) -> p a h m", a=32, h=2),
                                                       pattern=[[-2, 32], [-1, 2], [0, 64]], compare_op=ALU.is_equal, fill=0.0, base=0, channel_multiplier=1),
                     reads=[R("onesb")], writes=[R("expand")])
                selb = sb("selb", [128, 32, 64], F32)
                p.op("sp", lambda e: e.dma_start(out=selb[:], in_=self.selb_in), writes=[R("selb")], dma=self.st_const)
                qt = [sb("qt%d" % b, [64, 16, 256], BF16) for b in range(2)]
                Eb = [sb("E%d" % b, [128, 4, 128], BF16) for b in range(3)]
                ot = sb("ot", [128, D], BF16)
                accb = sb("accb", [128, 4, 64], F32)
                imp = sb("imp", [128, 64], F32)
                sc2 = sb("sc2", [128, 64], F32)
                m8 = sb("m8", [128, 8], F32)
                nb = sb("nb", [128, 64], F32)
                nbT4 = sb("nbT4", [64, 4, 128], BF16)
                den = sb("den", [128, 4], F32)
                oT = sb("oT", [128, 8, 128], BF16)
                xres_t = sb("xres_t", [128, D], F32)
                xo = sb("xo", [128, D], F32)
                ec = [0]
                sc_ = [0]

                def score_exp(i, lhsT, lres, rhs, rres, mask, extra=None):
                    sb_i = sc_[0] % 2
                    sc_[0] += 1
                    bank, rb = self.pbank[sb_i], self.r_pb[sb_i]
                    eb = ec[0] % 3
                    ec[0] += 1

                    def mm(e, bank=bank):
                        ins = e.matmul(bank[:, :].rearrange("p (r q) -> p r q", r=4), lhsT=lhsT, rhs=rhs, start=True, stop=(extra is None))
                        if extra is not None:
                            ins = e.matmul(bank[:, :].rearrange("p (r q) -> p r q", r=4), lhsT=extra[0], rhs=extra[1], start=False, stop=True)
                        return ins
                    rr = list(lres) + list(rres) + (list(extra[2]) if extra is not None else [])
                    p.op("pe", mm, reads=rr, writes=[rb])
                    E = Eb[eb]
                    rE = R("E%d" % eb)
                    p.op("act", lambda e, E=E, bank=bank: e.activation(out=E[:].rearrange("p r q -> p (r q)"), in_=bank[:, :], func=AF.Exp, scale=0.125),
                         reads=[rb], writes=[rE])
                    if mask is not None:
                        base, cm, stepq = mask
                        p.op("pool", lambda e, E=E, base=base, cm=cm, stepq=stepq: e.affine_select(
                            out=E[:], in_=E[:], pattern=[[0, 4], [stepq, 128]], compare_op=ALU.is_ge, fill=0.0, base=base, channel_multiplier=cm),
                            reads=[rE], writes=[rE])
                    return E, rE

                def pv(E, rE, vrhs, vres, ncols, first, last):
                    def mm(e):
                        for r in range(4):
                            ins = e.matmul(self.pbank[2 + r][:, 0:ncols], lhsT=E[:, r, :], rhs=vrhs, start=first, stop=last)
                        return ins
                    p.op("pe", mm, reads=[rE] + list(vres), writes=[self.r_pb[2 + r] for r in range(4)])

                CAUSAL = (0, -1, 1)
                PREV = (-1, 1, -1)
                r_O = [self.r_pb[2 + r] for r in range(4)]
                for i in range(32):
                    qb_ = (i // 2) % 2
                    if i % 2 == 0:
                        blk = i // 4
                        src = self.qT[:, :, i * 128:i * 128 + 256].rearrange("h d t -> d h t")
                        p.op("sp", lambda e, src=src, qb_=qb_: e.dma_start(out=qt[qb_][:], in_=src),
                             reads=[self.R("qT_%d_%d" % (h, blk)) for h in range(16)], writes=[R("qt%d" % qb_)], dma=self.st_x)
                    qsl = slice((i % 2) * 128, (i % 2 + 1) * 128)
                    rq = [R("qt%d" % qb_)]
                    p.op("sp", lambda e, i=i: e.dma_start(out=xres_t[:], in_=xsrc[i * 128:(i + 1) * 128, :]),
                         reads=[self.R("xres")], writes=[R("xres_t")], dma=self.st_x)
                    for g in range(2):
                        qa4 = qt[qb_][:, g * 4:(g + 1) * 4, qsl]
                        kts = [kt for kt in (i - 1, i) if kt >= 0]
                        for n, kt in enumerate(kts):
                            E, rE = score_exp(i, kT[:, g, kt * 128:(kt + 1) * 128], [R("kT%d" % g)], qa4, rq, CAUSAL if kt == i else PREV)
                            pv(E, rE, Vall[:, kt, g, :], [R("Vall"), R("Vones")], 65, n == 0, n == len(kts) - 1)
                        for r in range(4):
                            h = g * 4 + r
                            O = self.pbank[2 + r]
                            p.op("dve", lambda e, O=O, r=r, h=h: e.tensor_tensor(out=den[:, r:r + 1], in0=O[:, 64:65], in1=esink[:, h:h + 1], op=ALU.add),
                                 reads=[r_O[r], R("esink")], writes=[R("den")])
                        p.op("dve", lambda e: e.reciprocal(out=den[:], in_=den[:]), reads=[R("den")], writes=[R("den")])
                        for r in range(4):
                            h = g * 4 + r
                            O = self.pbank[2 + r]
                            p.op("dve", lambda e, O=O, r=r, h=h: e.tensor_scalar(out=ot[:, h * 64:(h + 1) * 64], in0=O[:, 0:64], scalar1=den[:, r:r + 1], scalar2=None, op0=ALU.mult),
                                 reads=[r_O[r], R("den")], writes=[R("ot")])
                        qb4 = qt[qb_][:, 8 + g * 4:8 + (g + 1) * 4, qsl]
                        nct = 1 if i < 16 else 2
                        for ct in range(nct):
                            E, rE = score_exp(i, kcmpT[:, g, ct * 128:(ct + 1) * 128], [R("kcmpT%d" % g)], qb4, rq, (128 * i - 2048 * ct - 31, -16, 1))
                            pv(E, rE, Vcmp[:, g, ct, :], [R("VcmpV%d" % g), R("VcmpW%d" % g), R("Vcmp1")], 129, ct == 0, ct == nct - 1)
                        for r in range(4):
                            O = self.pbank[2 + r]
                            p.op("dve", lambda e, O=O, r=r: e.tensor_scalar(out=den[:, r:r + 1], in0=O[:, 64:65], scalar1=1e-30, scalar2=None, op0=ALU.max),
                                 reads=[r_O[r]], writes=[R("den")])
                        p.op("dve", lambda e: e.reciprocal(out=den[:], in_=den[:]), reads=[R("den")], writes=[R("den")])
                        for r in range(4):
                            O = self.pbank[2 + r]
                            if r == 0:
                                p.op("dve", lambda e, O=O, r=r: e.tensor_scalar(out=imp[:], in0=O[:, 65:129], scalar1=den[:, r:r + 1], scalar2=None, op0=ALU.mult),
                                     reads=[r_O[r], R("den")], writes=[R("imp")])
                            else:
                                p.op("dve", lambda e, O=O, r=r: e.scalar_tensor_tensor(out=imp[:], in0=O[:, 65:129], scalar=den[:, r:r + 1], in1=imp[:], op0=ALU.mult, op1=ALU.add),
                                     reads=[r_O[r], R("den"), R("imp")], writes=[R("imp")])
                        gsl = gates[:, i, g * 12:(g + 1) * 12].rearrange("p (r b) -> p r b", b=3)
                        p.op("dve", lambda e, gsl=gsl: e.tensor_tensor(out=den[:], in0=den[:], in1=gsl[:, :, 0], op=ALU.mult), reads=[R("den"), R("gates")], writes=[R("den")])
                        for r in range(4):
                            O = self.pbank[2 + r]
                            p.op("dve", lambda e, O=O, r=r: e.tensor_scalar(out=accb[:, r, :], in0=O[:, 0:64], scalar1=den[:, r:r + 1], scalar2=None, op0=ALU.mult),
                                 reads=[r_O[r], R("den")], writes=[R("accb")])
                        if not use_cmp:
                            p.op("dve", lambda e: e.memset(accb[:], 0.0), reads=[R("accb")], writes=[R("accb")])
                        p.op("dve", lambda e, i=i: e.tensor_tensor(out=imp[:], in0=imp[:], in1=selb[:, i, :], op=ALU.add), reads=[R("imp"), R("selb")], writes=[R("imp")])
                        p.op("dve", lambda e: e.max(out=m8[:], in_=imp[:]), reads=[R("imp")], writes=[R("m8")])
                        p.op("dve", lambda e: e.match_replace(out=sc2[:], in_to_replace=m8[:], in_values=imp[:], imm_value=-3.0e38), reads=[R("imp"), R("m8")], writes=[R("sc2")])
                        p.op("dve", lambda e: e.max(out=m8[:], in_=sc2[:]), reads=[R("sc2"), R("m8")], writes=[R("m8")])
                        p.op("dve", lambda e: e.tensor_scalar(out=nb[:], in0=imp[:], scalar1=m8[:, 7:8], scalar2=-BIG, op0=ALU.is_lt, op1=ALU.mult),
                             reads=[R("imp"), R("m8")], writes=[R("nb")])
                        sb_i = sc_[0] % 2
                        sc_[0] += 1
                        bank, rb = self.pbank[sb_i], self.r_pb[sb_i]
                        p.op("pe", lambda e, bank=bank: e.transpose(out=bank[0:64, 0:128], in_=nb[:], identity=self.ident_f[:]), reads=[R("nb"), self.R("ident")], writes=[rb])
                        p.op("dve", lambda e, bank=bank: e.tensor_copy(out=nbT4[:], in_=bank[0:64, 0:128].rearrange("p (o q) -> p o q", o=1).to_broadcast([64, 4, 128])),
                             reads=[rb], writes=[R("nbT4")])
                        for kt in (range(i + 1) if use_slc else []):
                            E, rE = score_exp(i, kT[:, 2 + g, kt * 128:(kt + 1) * 128], [R("kT%d" % (2 + g))], qb4, rq, CAUSAL if kt == i else None,
                                              extra=(expand[:, kt, :], nbT4[:], [R("expand"), R("nbT4")]))
                            pv(E, rE, Vall[:, kt, 2 + g, :], [R("Vall"), R("Vones")], 65, kt == 0, kt == i)
                        for br, kbase in ((1, None), (2, 4)):
                            if br == 1 and not use_slc:
                                continue
                            if br == 2 and not use_win:
                                for r in range(4):
                                    h = 8 + g * 4 + r
                                    p.op("dve", lambda e, r=r, h=h: e.tensor_copy(out=ot[:, h * 64:(h + 1) * 64], in_=accb[:, r, :]), reads=[R("accb")], writes=[R("ot")])
                                continue
                            if br == 2:
                                kts = [kt for kt in range(i - 4, i + 1) if kt >= 0]
                                for n, kt in enumerate(kts):
                                    mk = CAUSAL if kt == i else (PREV if kt == i - 4 else None)
                                    E, rE = score_exp(i, kT[:, 4 + g, kt * 128:(kt + 1) * 128], [R("kT%d" % (4 + g))], qb4, rq, mk)
                                    pv(E, rE, Vall[:, kt, 4 + g, :], [R("Vall"), R("Vones")], 65, n == 0, n == len(kts) - 1)
                            for r in range(4):
                                O = self.pbank[2 + r]
                                p.op("dve", lambda e, O=O, r=r: e.tensor_copy(out=den[:, r:r + 1], in_=O[:, 64:65]), reads=[r_O[r]], writes=[R("den")])
                            p.op("dve", lambda e: e.reciprocal(out=den[:], in_=den[:]), reads=[R("den")], writes=[R("den")])
                            p.op("dve", lambda e, gsl=gsl, br=br: e.tensor_tensor(out=den[:], in0=den[:], in1=gsl[:, :, br], op=ALU.mult), reads=[R("den"), R("gates")], writes=[R("den")])
                            for r in range(4):
                                O = self.pbank[2 + r]
                                h = 8 + g * 4 + r
                                if br == 1:
                                    p.op("dve", lambda e, O=O, r=r: e.scalar_tensor_tensor(out=accb[:, r, :], in0=O[:, 0:64], scalar=den[:, r:r + 1], in1=accb[:, r, :], op0=ALU.mult, op1=ALU.add),
                                         reads=[r_O[r], R("den"), R("accb")], writes=[R("accb")])
                                else:
                                    p.op("dve", lambda e, O=O, r=r, h=h: e.scalar_tensor_tensor(out=ot[:, h * 64:(h + 1) * 64], in0=O[:, 0:64], scalar=den[:, r:r + 1], in1=accb[:, r, :], op0=ALU.mult, op1=ALU.add),
                                         reads=[r_O[r], R("den"), R("accb")], writes=[R("ot")])
                    for half in range(2):
                        rp = self.r_ptr[half]

                        def tr(e, half=half):
                            for q in range(4):
                                kc = half * 4 + q
                                ins = e.transpose(out=self.ptrh[half][:, q * 128:(q + 1) * 128], in_=ot[:, kc * 128:(kc + 1) * 128], identity=self.ident_b[:])
                            return ins
                        p.op("pe", tr, reads=[R("ot"), self.R("ident_b")], writes=[rp])
                        dst = oT[:, half * 4:(half + 1) * 4, :]
                        srcp = self.ptrh[half][:, :].rearrange("p (q m) -> p q m", q=4)
                        p.op("act", lambda e, dst=dst, srcp=srcp: e.copy(out=dst, in_=srcp), reads=[rp], writes=[R("oT%d" % half)])
                    for ch in range(2):
                        bo, rbo = self.pbank[ch], self.r_pb[ch]

                        def mmo(e, ch=ch, bo=bo):
                            for k in range(8):
                                ins = e.matmul(bo[:, :], lhsT=oT[:, k, :], rhs=wout[:, k, ch * 512:(ch + 1) * 512], start=(k == 0), stop=(k == 7))
                            return ins
                        p.op("pe", mmo, reads=[R("oT0"), R("oT1")] + r_wout, writes=[rbo])
                        p.op("dve", lambda e, ch=ch, bo=bo: e.tensor_tensor(out=xo[:, ch * 512:(ch + 1) * 512], in0=bo[:, :], in1=xres_t[:, ch * 512:(ch + 1) * 512], op=ALU.add),
                             reads=[rbo, R("xres_t")], writes=[R("xo")])
                    if "ot" in dbg:
                        p.op("dve", lambda e: e.tensor_copy(out=xo[:], in_=ot[:]), reads=[R("ot"), R("xo")], writes=[R("xo")])
                    p.op("sp", lambda e, i=i: e.dma_start(out=self.xres[i * 128:(i + 1) * 128, :], in_=xo[:]),
                         reads=[R("xo")], writes=[self.R("xres_w%d" % (i % 4))], dma=self.st_o)
            self.phase_barrier()
        self.x_src = self.xres

    def final_norm(self):
        p = self.p
        nc = self.nc
        xsrc = self.x_src
        with contextlib.ExitStack() as st:
            def sb(name, shape, dt):
                return st.enter_context(nc.sbuf_tensor(name, list(shape), dt))
            self.load_gain(DEPTH * 3)
            bufs = [(sb("fn_x%d" % b, [128, D], F32), sb("fn_sq%d" % b, [128, D], BF16), sb("fn_ss%d" % b, [128, 1], F32),
                     sb("fn_rs%d" % b, [128, 1], F32), sb("fn_o%d" % b, [128, D], F32)) for b in range(2)]
            for tt in range(S // 128):
                b = tt % 2
                xt, sq, ss, rs, ot = bufs[b]
                rx, rss, rrs, ro = [self.R("fn_%s%d" % (n, b)) for n in ("x", "ss", "rs", "o")]
                p.op("sp", lambda e, xt=xt, tt=tt: e.dma_start(out=xt[:], in_=xsrc[tt * 128:(tt + 1) * 128, :]),
                     reads=[self.R("xres")], writes=[rx], dma=self.st_x)
                p.op("act", lambda e, xt=xt, sq=sq, ss=ss: e.activation(out=sq[:], in_=xt[:], func=AF.Square, accum_out=ss[:]),
                     reads=[rx], writes=[self.R("fn_sq%d" % b), rss])
                p.op("act", lambda e, ss=ss, rs=rs: e.activation(out=rs[:], in_=ss[:], func=AF.Sqrt, scale=1.0 / D, bias=self.epsb[:]),
                     reads=[rss, self.R("epsb")], writes=[rrs])
                p.op("dve", lambda e, rs=rs: e.reciprocal(out=rs[:], in_=rs[:]), reads=[rrs], writes=[rrs])
                p.op("dve", lambda e, xt=xt, ot=ot, rs=rs: e.scalar_tensor_tensor(out=ot[:], in0=xt[:], scalar=rs[:], in1=self.gbc[:], op0=ALU.mult, op1=ALU.mult),
                     reads=[rx, rrs, self.R("gbc")], writes=[ro])
                p.op("sp", lambda e, ot=ot, tt=tt: e.dma_start(out=self.out[tt * 128:(tt + 1) * 128, :], in_=ot[:]),
                     reads=[ro], writes=[self.R("out_w%d" % (tt % 4))], dma=self.st_o)
            self.final_wait()

    def copy_out(self):
        p = self.p
        nc = self.nc
        xsrc = self.x_src
        with contextlib.ExitStack() as st:
            bufs = [st.enter_context(nc.sbuf_tensor("co%d" % b, [128, D], F32)) for b in range(2)]
            for tt in range(S // 128):
                b = tt % 2
                rx = self.R("co%d" % b)
                p.op("sp", lambda e, b=b, tt=tt: e.dma_start(out=bufs[b][:], in_=xsrc[tt * 128:(tt + 1) * 128, :]),
                     reads=[self.R("xres")], writes=[rx], dma=self.st_x)
                p.op("sp", lambda e, b=b, tt=tt: e.dma_start(out=self.out[tt * 128:(tt + 1) * 128, :], in_=bufs[b][:]),
                     reads=[rx], writes=[self.R("out_w%d" % (tt % 4))], dma=self.st_o)
            self.final_wait()

    def final_wait(self):
        p = self.p
        sems = self._store_waits()

        def fn(e, sems=sems):
            for sem, val in sems:
                e.wait_ge(sem, val)
            return e.nop()
        p.op("sp", fn, reads=[self.R("out_w%d" % k) for k in range(4)], writes=[self.R("done")])


def full_plan():
    def prep(l):
        out = [("prep_ffn", l, 0)]
        out.append(("prep_attn", l // 2) if l % 2 == 0 else ("prep_ssm", l // 2))
        out.append(("prep_ffn", l, 1))
        return out
    plan = prep(0)
    for l in range(DEPTH):
        if l + 1 < DEPTH:
            plan += prep(l + 1)
        plan.append(("ffn", l, 0))
        plan.append(("attn", l) if l % 2 == 0 else ("ssd", l))
        plan.append(("ffn", l, 1))
    plan.append(("final",))
    return plan


_CACHE = {}


def attn_w_layout(w):
    heads = [(h * 64) for h in range(8)] + [512, 576] + [768 + h * 64 for h in range(8)] + [1280, 1344] + [1536, 1600] + [1792, 1856]
    cols = []
    for c0 in heads:
        cols += list(range(c0, c0 + 64)) + list(range(c0 + 32, c0 + 64)) + list(range(c0, c0 + 32))
    cols += list(range(1408, 1536))
    cols += list(range(640, 768)) + list(range(1664, 1792)) + list(range(1920, 2048)) + list(range(2048, 2072))
    assert len(cols) == NAW
    return np.ascontiguousarray(w[:, :, np.asarray(cols)])


def _rope_tables():
    inv = (1.0 / (np.float32(10000.0) ** (np.arange(0, 64, 2, dtype=np.float32) / np.float32(64)))).astype(np.float32)
    ang = (np.arange(S, dtype=np.float32)[:, None] * inv[None, :]).astype(np.float32)
    c = np.cos(ang).astype(np.float32).T
    s_ = np.sin(ang).astype(np.float32).T
    return np.ascontiguousarray(np.stack([np.concatenate([c, c], 0), np.concatenate([-s_, s_], 0)], 0))


def _wsel():
    n_cmp = (S - 32) // 16 + 1
    cs = np.arange(n_cmp) * 16
    ss = np.arange(S // 64) * 64
    ov = np.minimum(cs[:, None] + 32, ss[None, :] + 64) - np.maximum(cs[:, None], ss[None, :])
    w = np.zeros((256, 64), np.float32)
    w[:n_cmp] = np.clip(ov, 0, None) / 32.0
    return np.ascontiguousarray(w.reshape(2, 128, 64).transpose(1, 0, 2))


def _selb():
    t = np.arange(S)
    cur = (t // 64)[:, None]
    jj = np.arange(64)[None, :]
    valid = jj <= cur
    forced = valid & ((jj == 0) | (jj == cur) | (jj == cur - 1))
    b = np.where(forced, 1e4, 0.0) - np.where(valid, 0.0, 1e4)
    return np.ascontiguousarray(b.astype(np.float32).reshape(32, 128, 64).transpose(1, 0, 2))


ROPE = _rope_tables()
WSEL = _wsel()
SELB = _selb()
_ii = np.arange(128)
TRI = np.stack([(_ii[:, None] <= _ii[None, :]), (_ii[:, None] > _ii[None, :]), np.ones((128, 128), bool)]).astype(np.float32)


def run_plan(plan, inputs, n_cores=8, trace=False):
    key = repr(plan)
    if key not in _CACHE:
        _CACHE[key] = Builder(plan).build()
    nc = _CACHE[key]
    x = np.ascontiguousarray(inputs["x"], dtype=np.float32)
    gains = np.concatenate([np.asarray(inputs["norm_gains"], np.float32).reshape(DEPTH * 3, D),
                            np.asarray(inputs["final_norm"], np.float32).reshape(1, D)], axis=0)
    common = {
        "gains": np.ascontiguousarray(gains),
        "ffn_w_gate": np.ascontiguousarray(inputs["ffn_w_gate"], dtype=np.float32),
        "ffn_w_up": np.ascontiguousarray(inputs["ffn_w_up"], dtype=np.float32),
        "ffn_w_down": np.ascontiguousarray(inputs["ffn_w_down"], dtype=np.float32),
        "ident": np.eye(128, dtype=np.float32),
        "tri": TRI,
        "ssm_w_in": np.ascontiguousarray(inputs["ssm_w_in"], dtype=np.float32),
        "ssm_w_out": np.ascontiguousarray(inputs["ssm_w_out"], dtype=np.float32),
        "ssm_cw": np.ascontiguousarray(np.asarray(inputs["ssm_conv_w"], np.float32).transpose(0, 2, 1).reshape(2, 32, 128, 4).transpose(0, 2, 1, 3)),
        "ssm_cb": np.ascontiguousarray(np.asarray(inputs["ssm_conv_b"], np.float32).reshape(2, 32, 128).transpose(0, 2, 1)),
        "ssm_vec": np.ascontiguousarray(np.stack([np.asarray(inputs["ssm_dt_bias"], np.float32), np.asarray(inputs["ssm_a_log"], np.float32),
                                                  np.asarray(inputs["ssm_d"], np.float32)], axis=1).reshape(2, 96)),
        "ssm_norm": np.ascontiguousarray(inputs["ssm_norm"], dtype=np.float32),
        "attn_wr": attn_w_layout(np.asarray(inputs["attn_w_in"], np.float32)),
        "attn_w_out": np.ascontiguousarray(inputs["attn_w_out"], dtype=np.float32),
        "attn_sinks": np.ascontiguousarray(inputs["attn_sinks"], dtype=np.float32),
        "rope": ROPE,
        "cmp_w1": np.ascontiguousarray(np.stack([np.asarray(inputs["cmp_k_w1"], np.float32), np.asarray(inputs["cmp_v_w1"], np.float32)], axis=1)),
        "cmp_w2": np.ascontiguousarray(np.stack([np.asarray(inputs["cmp_k_w2"], np.float32), np.asarray(inputs["cmp_v_w2"], np.float32)], axis=1)),
        "cmp_posT": np.ascontiguousarray(np.stack([np.asarray(inputs["cmp_k_pos"], np.float32).transpose(0, 2, 1),
                                                   np.asarray(inputs["cmp_v_pos"], np.float32).transpose(0, 2, 1)], axis=1)),
        "wsel": WSEL,
        "selb": SELB,
    }
    in_maps = []
    for c in range(n_cores):
        m = dict(common)
        m["x"] = x[c % 4]
        in_maps.append(m)
    res = run_bass_kernel_spmd(nc, in_maps, core_ids=list(range(n_cores)), trace=trace)
    out = np.stack([res.results[c % n_cores]["out"] for c in range(4)], axis=0)
    return out, res


def kernel(**inputs):
    out, _ = run_plan(full_plan(), inputs)
    return out.astype(np.float32)
```

```python
import contextlib
import numpy as np
import concourse.bass as bass
import concourse.mybir as mybir
from concourse.bass_utils import run_bass_kernel_spmd

F32 = mybir.dt.float32
BF16 = mybir.dt.bfloat16
AF = mybir.ActivationFunctionType
ALU = mybir.AluOpType
AX = mybir.AxisListType

D = 1024
S = 4096
DEPTH = 4
DFF = 2816
NFC = DFF // 128
EPS = 1e-6
NAW = 3608


class Res:
    __slots__ = ("name", "w", "r")

    def __init__(self, name):
        self.name = name
        self.w = None
        self.r = []


class Op:
    __slots__ = ("eng", "fn", "waits", "inc", "dma", "dsem", "dval", "cnt", "pre")


class Prog:
    ENGS = ("pe", "act", "dve", "pool", "sp")

    def __init__(self, nc, stack):
        self.nc = nc
        self.stack = stack
        self.ops = {e: [] for e in self.ENGS}
        self.esem = {e: stack.enter_context(nc.semaphore("es_" + e)) for e in self.ENGS}
        self.nsem = 5
        self.streams = []

    def new_sem(self, name):
        self.nsem += 1
        return self.stack.enter_context(self.nc.semaphore(name))

    def op(self, eng, fn, reads=(), writes=(), dma=None):
        o = Op()
        o.eng = eng
        o.fn = fn
        o.inc = False
        o.dma = dma
        o.cnt = 0
        o.pre = None
        o.dsem = None
        o.dval = 0
        deps = []
        seen = set()

        def add(d, raw):
            if d is None or id(d) in seen:
                return
            if d.dma is None and d.eng == eng:
                if eng == "pe" or not raw:
                    return
            seen.add(id(d))
            deps.append(d)

        for r in reads:
            add(r.w, True)
        for w in writes:
            add(w.w, False)
            for rr in w.r:
                add(rr, False)
        for d in deps:
            if d.dma is None:
                d.inc = True
        o.waits = deps
        if dma is not None:
            sem, val, pre = dma.next()
            o.dsem, o.dval, o.pre = sem, val, pre
            dma.ops.append(o)
        for r in reads:
            r.r.append(o)
        for w in writes:
            w.w = o
            w.r = []
        self.ops[eng].append(o)
        return o

    def barrier(self):
        deps = []
        for e in self.ENGS:
            for o in reversed(self.ops[e]):
                if o.dma is None:
                    deps.append(o)
                    break
        for st in self.streams:
            deps.extend(st.ops[-st.R:])
        for e in self.ENGS:
            o = Op()
            o.eng = e
            o.fn = lambda eng: eng.nop()
            o.inc = False
            o.dma = None
            o.cnt = 0
            o.pre = None
            o.dsem = None
            o.dval = 0
            o.waits = [d for d in deps if not (d.dma is None and d.eng == e)]
            for d in o.waits:
                if d.dma is None:
                    d.inc = True
            self.ops[e].append(o)

    def emit(self):
        nc = self.nc
        for e in self.ENGS:
            c = 0
            for o in self.ops[e]:
                if o.dma is None and o.inc:
                    c += 1
                o.cnt = c
        self.counts = {e: (len(self.ops[e]), self.ops[e][-1].cnt if self.ops[e] else 0) for e in self.ENGS}

        def body_for(ename):
            def body(eng):
                seen = {}

                def wait(sem, val):
                    k = id(sem)
                    if seen.get(k, 0) >= val:
                        return
                    seen[k] = val
                    eng.wait_ge(sem, val)

                for o in self.ops[ename]:
                    for d in o.waits:
                        if d.dma is not None:
                            wait(d.dsem, d.dval)
                        else:
                            wait(self.esem[d.eng], d.cnt)
                    if o.pre is not None and o.pre[1] > 0:
                        wait(o.pre[0], o.pre[1])
                    ins = o.fn(eng)
                    if o.dma is not None:
                        ins.then_inc(o.dsem, 16)
                    elif o.inc:
                        ins.then_inc(self.esem[ename], 1)
            return body

        with nc.Block() as block:
            block.tensor(body_for("pe"))
            block.scalar(body_for("act"))
            block.vector(body_for("dve"))
            block.gpsimd(body_for("pool"))
            block.sync(body_for("sp"))


class DmaStream:
    def __init__(self, prog, name, R):
        self.sems = [prog.new_sem("%s%d" % (name, i)) for i in range(R)]
        self.R = R
        self.k = 0
        self.ops = []
        prog.streams.append(self)

    def next(self):
        k = self.k
        self.k += 1
        sem = self.sems[k % self.R]
        return sem, 16 * (k // self.R + 1), (sem, 16 * (k // self.R))


class Builder:
    def __init__(self, plan):
        self.plan = plan
        self.nc = bass.Bass("TRN2", target_bir_lowering=False)
        self.stack = contextlib.ExitStack()
        self.res_cache = {}

    def R(self, name):
        r = self.res_cache.get(name)
        if r is None:
            r = self.res_cache[name] = Res(name)
        return r

    def dram_in(self, name, shape, dt=F32):
        return self.nc.dram_tensor(name, list(shape), dt, kind="ExternalInput").ap()

    def dram_out(self, name, shape, dt=F32):
        return self.nc.dram_tensor(name, list(shape), dt, kind="ExternalOutput").ap()

    def dram_tmp(self, name, shape, dt):
        return self.nc.dram_tensor(name, list(shape), dt).ap()

    def sb(self, name, shape, dt):
        return self.stack.enter_context(self.nc.sbuf_tensor(name, list(shape), dt))

    def ps(self, name, shape, dt):
        return self.stack.enter_context(self.nc.psum_tensor(name, list(shape), dt))

    def build(self):
        nc = self.nc
        with self.stack:
            self.p = Prog(nc, self.stack)
            self._build()
            self.p.emit()
        return nc

    def _build(self):
        p = self.p
        plan = self.plan
        self.x_in = self.dram_in("x", [S, D])
        self.gains = self.dram_in("gains", [DEPTH * 3 + 1, D])
        self.wg = self.dram_in("ffn_w_gate", [DEPTH, 2, D, DFF])
        self.wu = self.dram_in("ffn_w_up", [DEPTH, 2, D, DFF])
        self.wd = self.dram_in("ffn_w_down", [DEPTH, 2, DFF, D])
        self.ident_in = self.dram_in("ident", [128, 128])
        self.tri_in = self.dram_in("tri", [3, 128, 128])
        self.ssm_w_in = self.dram_in("ssm_w_in", [2, D, 6176])
        self.ssm_w_out = self.dram_in("ssm_w_out", [2, 2048, D])
        self.ssm_cw = self.dram_in("ssm_cw", [2, 128, 32, 4])
        self.ssm_cb = self.dram_in("ssm_cb", [2, 128, 32])
        self.ssm_vec = self.dram_in("ssm_vec", [2, 96])
        self.ssm_norm = self.dram_in("ssm_norm", [2, 2048])
        self.swin = self.dram_tmp("swin", [2, D, 6176], BF16)
        self.attn_wr = self.dram_in("attn_wr", [2, D, NAW])
        self.attn_w_out = self.dram_in("attn_w_out", [2, D, D])
        self.sinks_in = self.dram_in("attn_sinks", [2, 8])
        self.rope_in = self.dram_in("rope", [2, 64, S])
        self.cmp_w1 = self.dram_in("cmp_w1", [2, 2, 2048, 128])
        self.cmp_w2 = self.dram_in("cmp_w2", [2, 2, 128, 64])
        self.cmp_posT = self.dram_in("cmp_posT", [2, 2, 64, 32])
        self.wsel_in = self.dram_in("wsel", [128, 2, 64])
        self.selb_in = self.dram_in("selb", [128, 32, 64])
        self.awin = self.dram_tmp("awin", [2, D, NAW], BF16)
        self.qT = self.dram_tmp("qT", [16, 64, S], BF16)
        self.out = self.dram_out("out", [S, D])
        self.xres = self.dram_tmp("xres", [S, D], F32)
        self.wgt = self.dram_tmp("wgt", [DEPTH, 2, 6, 128, 8, 512], BF16)
        self.wut = self.dram_tmp("wut", [DEPTH, 2, 6, 128, 8, 512], BF16)
        self.wdt = self.dram_tmp("wdt", [DEPTH, 2, DFF, D], BF16)

        self.st_const = DmaStream(p, "dc", 1)
        self.st_prep = DmaStream(p, "dp", 4)
        self.st_x = DmaStream(p, "dx", 4)
        self.st_w = DmaStream(p, "dw", 4)
        self.st_o = DmaStream(p, "do", 4)

        self.ident_f = self.sb("ident_f", [128, 128], F32)
        self.ident_b = self.sb("ident_b", [128, 128], BF16)
        self.gbc = self.sb("gbc", [128, D], F32)
        self.epsb = self.sb("epsb", [128, 1], F32)
        r_ident = self.R("ident")
        p.op("sp", lambda e: e.dma_start(out=self.ident_f[:], in_=self.ident_in), writes=[r_ident], dma=self.st_const)
        p.op("dve", lambda e: e.tensor_copy(out=self.ident_b[:], in_=self.ident_f[:]), reads=[r_ident], writes=[self.R("ident_b")])
        p.op("dve", lambda e: e.memset(self.epsb[:], EPS), writes=[self.R("epsb")])

        self.pbank = [self.ps("pb%d" % i, [128, 512], F32) for i in range(6)]
        self.ptrh = [self.ps("ptrh%d" % i, [128, 512], BF16) for i in range(2)]
        self.r_pb = [self.R("pb%d" % i) for i in range(6)]
        self.r_ptr = [self.R("ptr0"), self.R("ptr1")]

        self.x_src = self.x_in
        for ph in plan:
            kind = ph[0]
            if kind == "prep_ffn":
                self.prep_ffn(ph[1], ph[2])
            elif kind == "ffn":
                self.dbg_stage = ph[3] if len(ph) > 3 else 99
                self.ffn(ph[1], ph[2])
            elif kind == "prep_ssm":
                self.prep_ssm(ph[1])
            elif kind == "ssd":
                self.ssd(ph[1])
            elif kind == "prep_attn":
                self.prep_attn(ph[1])
            elif kind == "attn":
                self.attn_dbg = ph[2] if len(ph) > 2 else None
                self.attn(ph[1])
            elif kind == "final":
                self.final_norm()
            elif kind == "copy_out":
                self.copy_out()
            else:
                raise ValueError(kind)

    def load_gain(self, row):
        p = self.p
        src = self.gains[row:row + 1, :].broadcast_to([128, D])
        p.op("sp", lambda e: e.dma_start(out=self.gbc[:], in_=src), writes=[self.R("gbc")], dma=self.st_const)

    def prep_ffn(self, l, i):
        p = self.p
        for (src, dst, nm) in ((self.wg, self.wgt, "g"), (self.wu, self.wut, "u")):
            for blk in range(6):
                w = 512 if blk < 5 else 256
                s_ap = src[l, i, :, blk * 512:blk * 512 + w].rearrange("(kc p) m -> p kc m", p=128)
                d_ap = dst[l, i, blk, :, :, 0:w]
                p.op("pool", lambda e, s_ap=s_ap, d_ap=d_ap: e.dma_start(out=d_ap, in_=s_ap),
                     writes=[self.R("wt_%s_%d_%d_%d" % (nm, l, i, blk))], dma=self.st_prep)
        for q in range(4):
            rows = DFF // 4
            s_ap = self.wd[l, i, q * rows:(q + 1) * rows, :]
            d_ap = self.wdt[l, i, q * rows:(q + 1) * rows, :]
            p.op("pool", lambda e, s_ap=s_ap, d_ap=d_ap: e.dma_start(out=d_ap, in_=s_ap),
                 writes=[self.R("wt_d_%d_%d_%d" % (l, i, q))], dma=self.st_prep)

    def norm_transpose(self, xsrc, t0, ntile, hT, r_hT, xt_bufs, tag):
        p = self.p
        for j in range(ntile):
            tt = t0 + j
            b = j % 2
            xt, hb, sq, ss, rs = xt_bufs[b]
            rx = self.R("%s_xt%d" % (tag, b))
            rh = self.R("%s_hb%d" % (tag, b))
            rss = self.R("%s_ss%d" % (tag, b))
            rsq = self.R("%s_sq%d" % (tag, b))
            p.op("sp", lambda e, xt=xt, tt=tt: e.dma_start(out=xt[:], in_=xsrc[tt * 128:(tt + 1) * 128, :]),
                 reads=[self.R("xres")], writes=[rx], dma=self.st_x)
            p.op("act", lambda e, xt=xt, sq=sq, ss=ss: e.activation(out=sq[:], in_=xt[:], func=AF.Square, accum_out=ss[:]),
                 reads=[rx], writes=[rsq, rss])
            p.op("act", lambda e, ss=ss, rs=rs: e.activation(out=rs[:], in_=ss[:], func=AF.Sqrt, scale=1.0 / D, bias=self.epsb[:]),
                 reads=[rss, self.R("epsb")], writes=[self.R("%s_rs%d" % (tag, b))])
            p.op("dve", lambda e, rs=rs: e.reciprocal(out=rs[:], in_=rs[:]),
                 reads=[self.R("%s_rs%d" % (tag, b))], writes=[self.R("%s_rs%d" % (tag, b))])
            p.op("dve", lambda e, xt=xt, hb=hb, rs=rs: e.scalar_tensor_tensor(out=hb[:], in0=xt[:], scalar=rs[:], in1=self.gbc[:], op0=ALU.mult, op1=ALU.mult),
                 reads=[rx, self.R("%s_rs%d" % (tag, b)), self.R("gbc")], writes=[rh])
            for half in range(2):
                rp = self.r_ptr[half]

                def tr(e, hb=hb, half=half):
                    ins = None
                    for q in range(4):
                        kc = half * 4 + q
                        ins = e.transpose(out=self.ptrh[half][:, q * 128:(q + 1) * 128],
                                          in_=hb[:, kc * 128:(kc + 1) * 128], identity=self.ident_b[:])
                    return ins
                p.op("pe", tr, reads=[rh, self.R("ident_b")], writes=[rp])
                dst = hT[:, half * 4:(half + 1) * 4, j * 128:(j + 1) * 128]
                srcp = self.ptrh[half][:, :].rearrange("p (q m) -> p q m", q=4)
                eng = "act" if half == 0 else "dve"
                if eng == "act":
                    p.op("act", lambda e, dst=dst, srcp=srcp: e.copy(out=dst, in_=srcp), reads=[rp], writes=[r_hT[j][half]])
                else:
                    p.op("dve", lambda e, dst=dst, srcp=srcp: e.tensor_copy(out=dst, in_=srcp), reads=[rp], writes=[r_hT[j][half]])

    def ffn(self, l, i):
        p = self.p
        nc = self.nc
        TB = 1024
        NTB = S // TB
        xsrc = self.x_src
        with contextlib.ExitStack() as st:
            def sb(name, shape, dt):
                return st.enter_context(nc.sbuf_tensor(name, list(shape), dt))
            tag = "f%d%d" % (l, i)
            wd_sb = sb(tag + "wd", [128, NFC, D], BF16)
            aT = sb(tag + "aT", [128, NFC, TB], BF16)
            hT = sb(tag + "hT", [128, 8, TB], BF16)
            wgu = [(sb(tag + "wg%d" % b, [128, 8, 512], BF16), sb(tag + "wu%d" % b, [128, 8, 512], BF16)) for b in range(2)]
            xt_bufs = [(sb(tag + "xt%d" % b, [128, D], F32), sb(tag + "hb%d" % b, [128, D], BF16),
                        sb(tag + "sq%d" % b, [128, D], BF16), sb(tag + "ss%d" % b, [128, 1], F32),
                        sb(tag + "rs%d" % b, [128, 1], F32)) for b in range(2)]
            sg = [sb(tag + "sg%d" % b, [128, 512], F32) for b in range(2)]
            xo = [sb(tag + "xo%d" % b, [128, D], F32) for b in range(2)]
            r_wd = [self.R(tag + "wd0"), self.R(tag + "wd1")]
            r_aT = self.R(tag + "aT")
            r_hT = [[self.R(tag + "hT%d_%d" % (jj, hh)) for hh in range(2)] for jj in range(TB // 128)]
            r_wg = [self.R(tag + "wgs%d" % b) for b in range(2)]
            r_wu = [self.R(tag + "wus%d" % b) for b in range(2)]
            r_sg = [self.R(tag + "sg%d" % b) for b in range(2)]
            r_xo = [self.R(tag + "xo%d" % b) for b in range(2)]

            self.load_gain(l * 3 + (0 if i == 0 else 2))
            for q in range(2):
                fa, fb = q * 11, (q + 1) * 11
                src = self.wdt[l, i, fa * 128:fb * 128, :].rearrange("(fc p) m -> p fc m", p=128)
                p.op("sp", lambda e, src=src, fa=fa, fb=fb: e.dma_start(out=wd_sb[:, fa:fb, :], in_=src),
                     reads=[self.R("wt_d_%d_%d_%d" % (l, i, 2 * q)), self.R("wt_d_%d_%d_%d" % (l, i, 2 * q + 1))], writes=[r_wd[q]], dma=self.st_w)

            wcount = 0
            for tb in range(NTB):
                t0 = tb * (TB // 128)
                self.norm_transpose(xsrc, t0, TB // 128, hT, r_hT, xt_bufs, tag)
                if self.dbg_stage <= 1:
                    continue
                for blk in range(6):
                    w = 512 if blk < 5 else 256
                    b = wcount % 2
                    wcount += 1
                    wgs, wus = wgu[b]
                    p.op("sp", lambda e, wgs=wgs, blk=blk, w=w: e.dma_start(out=wgs[:, :, 0:w], in_=self.wgt[l, i, blk, :, :, 0:w]),
                         reads=[self.R("wt_g_%d_%d_%d" % (l, i, blk))], writes=[r_wg[b]], dma=self.st_w)
                    p.op("sp", lambda e, wus=wus, blk=blk, w=w: e.dma_start(out=wus[:, :, 0:w], in_=self.wut[l, i, blk, :, :, 0:w]),
                         reads=[self.R("wt_u_%d_%d_%d" % (l, i, blk))], writes=[r_wu[b]], dma=self.st_w)
                    for m in range(w // 128):
                        fc = blk * 4 + m
                        for half in range(TB // 512):
                            pg = (fc * 2 + half) % 2
                            bg, bu = self.pbank[pg * 2], self.pbank[pg * 2 + 1]
                            rg, ru = self.r_pb[pg * 2], self.r_pb[pg * 2 + 1]

                            def mm(e, wt, bank, m=m, half=half):
                                ins = None
                                for kc in range(8):
                                    ins = e.matmul(bank[:, :], lhsT=wt[:, kc, m * 128:(m + 1) * 128],
                                                   rhs=hT[:, kc, half * 512:(half + 1) * 512],
                                                   start=(kc == 0), stop=(kc == 7))
                                return ins
                            hres = [r_hT[half * 4 + jj][hh] for jj in range(4) for hh in range(2)]
                            p.op("pe", lambda e, wgs=wgs, bg=bg, mm=mm: mm(e, wgs, bg), reads=[r_wg[b]] + hres, writes=[rg])
                            p.op("pe", lambda e, wus=wus, bu=bu, mm=mm: mm(e, wus, bu), reads=[r_wu[b]] + hres, writes=[ru])
                            sgb = sg[pg]
                            p.op("act", lambda e, sgb=sgb, bg=bg: e.activation(out=sgb[:], in_=bg[:, :], func=AF.Silu),
                                 reads=[rg], writes=[r_sg[pg]])
                            dst = aT[:, fc, half * 512:(half + 1) * 512]
                            p.op("dve", lambda e, dst=dst, sgb=sgb, bu=bu: e.tensor_tensor(out=dst, in0=sgb[:], in1=bu[:, :], op=ALU.mult),
                                 reads=[r_sg[pg], ru], writes=[r_aT])
                if self.dbg_stage <= 2:
                    continue
                for j in range(TB // 128):
                    tt = t0 + j
                    xb = j % 2
                    xt = xt_bufs[xb][0]
                    rx = self.R("%s_xt%d" % (tag, xb))
                    p.op("sp", lambda e, xt=xt, tt=tt: e.dma_start(out=xt[:], in_=xsrc[tt * 128:(tt + 1) * 128, :]),
                         reads=[self.R("xres")], writes=[rx], dma=self.st_x)
                    for ch in range(2):
                        pb = 4 + (j * 2 + ch) % 2
                        bank, rb = self.pbank[pb], self.r_pb[pb]

                        def mmd(e, bank=bank, j=j, ch=ch):
                            ins = None
                            for fc in range(NFC):
                                ins = e.matmul(bank[:, :], lhsT=aT[:, fc, j * 128:(j + 1) * 128],
                                               rhs=wd_sb[:, fc, ch * 512:(ch + 1) * 512],
                                               start=(fc == 0), stop=(fc == NFC - 1))
                            return ins
                        p.op("pe", mmd, reads=[r_aT] + r_wd, writes=[rb])
                        xob = xo[xb]
                        p.op("dve", lambda e, xob=xob, bank=bank, xt=xt, ch=ch: e.scalar_tensor_tensor(
                            out=xob[:, ch * 512:(ch + 1) * 512], in0=bank[:, :], scalar=0.5, in1=xt[:, ch * 512:(ch + 1) * 512],
                            op0=ALU.mult, op1=ALU.add), reads=[rb, rx], writes=[r_xo[xb]])
                    p.op("sp", lambda e, xob=xob, tt=tt: e.dma_start(out=self.xres[tt * 128:(tt + 1) * 128, :], in_=xob[:]),
                         reads=[r_xo[xb]], writes=[self.R("xres_w%d" % (tt % 4))], dma=self.st_o)
            self.phase_barrier()
        self.x_src = self.xres

    def _store_waits(self):
        st = self.st_o
        return [(st.sems[idx % st.R], 16 * (idx // st.R + 1)) for idx in range(max(0, st.k - st.R), st.k)]

    def phase_barrier(self):
        p = self.p
        sems = self._store_waits()

        def fn(e, sems=sems):
            for sem, val in sems:
                e.wait_ge(sem, val)
            return e.nop()
        p.op("sp", fn, reads=[self.R("xres_w%d" % k) for k in range(4)], writes=[self.R("xres")])
        p.barrier()

    def prep_ssm(self, j):
        p = self.p
        for (c0, c1) in ((0, 2048), (2048, 4096), (4096, 6144), (6144, 6176)):
            for hh in range(2):
                s_ap = self.ssm_w_in[j, hh * 512:(hh + 1) * 512, c0:c1]
                d_ap = self.swin[j, hh * 512:(hh + 1) * 512, c0:c1]
                p.op("pool", lambda e, s_ap=s_ap, d_ap=d_ap: e.dma_start(out=d_ap, in_=s_ap),
                     writes=[self.R("swin%d_%d_%d" % (j, c0, hh))], dma=self.st_prep)

    def swin_res(self, j, c0):
        base = (c0 // 2048) * 2048 if c0 < 6144 else 6144
        return [self.R("swin%d_%d_%d" % (j, base, hh)) for hh in range(2)]

    def ssd(self, l):
        p = self.p
        nc = self.nc
        j = l // 2
        TBK = 256
        NBLK = S // TBK
        NT = TBK // 128
        xsrc = self.x_src
        with contextlib.ExitStack() as st:
            def sb(name, shape, dt):
                return st.enter_context(nc.sbuf_tensor(tag + name, list(shape), dt))
            tag = "s%d" % l
            R = lambda n: self.R(tag + n)
            L1 = sb("L1", [128, 128], F32)
            L2 = sb("L2", [128, 128], F32)
            ONES = sb("ONES", [128, 128], F32)
            cw = sb("cw", [128, 32, 4], F32)
            cb = sb("cb", [128, 32], F32)
            vec = sb("vec", [128, 3, 32], F32)
            a_bc = sb("a_bc", [128, 32], F32)
            Dbc = sb("Dbc", [128, 32, 1], F32)
            nw = sb("nw", [128, 2048], F32)
            wout = sb("wout", [128, 16, 1024], BF16)
            wdt = sb("wdt", [128, 8, 32], BF16)
            halo = sb("halo", [128, 32, 3], F32)
            state = sb("state", [128, 2048], F32)
            state_bf = sb("state_bf", [128, 2048], BF16)
            hT = sb("hT", [128, 8, TBK], BF16)
            wx = [sb("wx%d" % b, [128, 8, 512], BF16) for b in range(2)]
            xin = [sb("xin%d" % b, [128, TBK + 3], F32) for b in range(2)]
            acc = [sb("acc%d" % b, [128, TBK], F32) for b in range(2)]
            xsb = [sb("xsb%d" % b, [128, TBK], BF16) for b in range(2)]
            BT = sb("BT", [128, 8, TBK], BF16)
            CT = sb("CT", [128, 8, TBK], BF16)
            Btok = sb("Btok", [128, NT, 1024], BF16)
            xs_tok = sb("xs_tok", [128, NT, 2048], BF16)
            sz = sb("sz", [128, NT, 2048], BF16)
            xt_bufs = [(sb("xt%d" % b, [128, D], F32), sb("hb%d" % b, [128, D], BF16),
                        sb("sq%d" % b, [128, D], BF16), sb("ss%d" % b, [128, 1], F32),
                        sb("rs%d" % b, [128, 1], F32)) for b in range(2)]
            dtv = sb("dtv", [128, 32], F32)
            dt3 = sb("dt3", [128, 32, 1], F32)
            da = sb("da", [128, 32], F32)
            eall = sb("eall", [128, 96], F32)
            ea3 = sb("ea3", [128, 32, 1], F32)
            w23 = sb("w23", [128, 32, 1], F32)
            xdt = sb("xdt", [128, 2048], BF16)
            xdec = sb("xdec", [128, 2048], BF16)
            ybuf = sb("ybuf", [128, 2048], F32)
            t3 = sb("t3", [128, 2048], F32)
            gnb = sb("gnb", [128, 2048], BF16)
            gnT = sb("gnT", [128, 16, 128], BF16)
            Eg = [sb("Eg%d" % b, [128, 4, 128], F32) for b in range(2)]
            MT = [sb("MT%d" % b, [128, 4, 128], BF16) for b in range(2)]
            GTm = [sb("GTm%d" % b, [128, 1, 128], F32) for b in range(2)]
            Ada4 = [sb("Ada4_%d" % b, [128, 4, 128], F32) for b in range(2)]
            ssg = sb("ssg", [128, 8], F32)
            rsg = sb("rsg", [128, 8, 1], F32)
            xo = sb("xo", [128, D], F32)
            xres_t = sb("xres_t", [128, D], F32)

            p.op("sp", lambda e: e.dma_start(out=L1[:], in_=self.tri_in[0]), writes=[R("L1")], dma=self.st_const)
            p.op("sp", lambda e: e.dma_start(out=L2[:], in_=self.tri_in[1]), writes=[R("L2")], dma=self.st_const)
            p.op("sp", lambda e: e.dma_start(out=ONES[:], in_=self.tri_in[2]), writes=[R("ONES")], dma=self.st_const)
            p.op("sp", lambda e: e.dma_start(out=cw[:], in_=self.ssm_cw[j]), writes=[R("cw")], dma=self.st_const)
            p.op("sp", lambda e: e.dma_start(out=cb[:], in_=self.ssm_cb[j]), writes=[R("cb")], dma=self.st_const)
            p.op("sp", lambda e: e.dma_start(out=vec[:].rearrange("p a b -> p (a b)"),
                                             in_=self.ssm_vec[j:j + 1, :].broadcast_to([128, 96])), writes=[R("vec")], dma=self.st_const)
            p.op("sp", lambda e: e.dma_start(out=nw[:], in_=self.ssm_norm[j:j + 1, :].broadcast_to([128, 2048])), writes=[R("nw")], dma=self.st_const)
            for q in range(4):
                src = self.ssm_w_out[j, q * 512:(q + 1) * 512, :].rearrange("(k p) m -> p k m", p=128)
                p.op("pool", lambda e, src=src, q=q: e.dma_start(out=wout[:, q * 4:(q + 1) * 4, :], in_=src), writes=[R("wout%d" % q)], dma=self.st_prep)
            r_wout = [R("wout%d" % q) for q in range(4)]
            p.op("sp", lambda e: e.dma_start(out=wdt[:], in_=self.swin[j, :, 6144:6176].rearrange("(k p) m -> p k m", p=128)),
                 reads=self.swin_res(j, 6144), writes=[R("wdt")], dma=self.st_const)
            p.op("act", lambda e: e.activation(out=a_bc[:], in_=vec[:, 1, :], func=AF.Exp), reads=[R("vec")], writes=[R("a_bc")])
            p.op("dve", lambda e: e.tensor_scalar(out=a_bc[:], in0=a_bc[:], scalar1=-1.0, scalar2=None, op0=ALU.mult), reads=[R("a_bc")], writes=[R("a_bc")])
            p.op("dve", lambda e: e.tensor_copy(out=Dbc[:, :, 0], in_=vec[:, 2, :]), reads=[R("vec")], writes=[R("Dbc")])
            p.op("pool", lambda e: e.memset(halo[:], 0.0), writes=[R("halo")])
            p.op("pool", lambda e: e.memset(state[:], 0.0), writes=[R("state")])
            p.op("pool", lambda e: e.memset(state_bf[:], 0.0), writes=[R("state_bf")])
            self.load_gain(l * 3 + 1)

            r_hT = [[R("hT%d_%d" % (jj, hh)) for hh in range(2)] for jj in range(NT)]
            hres_all = [r_hT[jj][hh] for jj in range(NT) for hh in range(2)]
            wxc = 0
            cvc = 0
            for blk in range(NBLK):
                t0 = blk * NT
                self.norm_transpose(xsrc, t0, NT, hT, r_hT, xt_bufs, tag)
                for wgI in range(8):
                    b = wxc % 2
                    wxc += 1
                    c0 = 2048 + wgI * 512
                    src = self.swin[j, :, c0:c0 + 512].rearrange("(k p) m -> p k m", p=128)
                    p.op("sp", lambda e, src=src, b=b: e.dma_start(out=wx[b][:], in_=src),
                         reads=self.swin_res(j, c0), writes=[R("wx%d" % b)], dma=self.st_w)
                    for m in range(4):
                        cc = wgI * 4 + m
                        pb = cc % 2
                        bank, rb = self.pbank[pb], self.r_pb[pb]

                        def mm(e, b=b, m=m, bank=bank):
                            for kc in range(8):
                                ins = e.matmul(bank[:, 0:TBK], lhsT=wx[b][:, kc, m * 128:(m + 1) * 128], rhs=hT[:, kc, :],
                                               start=(kc == 0), stop=(kc == 7))
                            return ins
                        p.op("pe", mm, reads=[R("wx%d" % b)] + hres_all, writes=[rb])
                        cbuf = cvc % 2
                        cvc += 1
                        xi, ac = xin[cbuf], acc[cbuf]
                        rxi, rac = R("xin%d" % cbuf), R("acc%d" % cbuf)
                        p.op("pool", lambda e, xi=xi, cc=cc: e.tensor_copy(out=xi[:, 0:3], in_=halo[:, cc, :]), reads=[R("halo")], writes=[rxi])
                        p.op("act", lambda e, xi=xi, bank=bank: e.copy(out=xi[:, 3:3 + TBK], in_=bank[:, 0:TBK]), reads=[rb], writes=[rxi])
                        p.op("pool", lambda e, xi=xi, cc=cc: e.tensor_copy(out=halo[:, cc, :], in_=xi[:, TBK:TBK + 3]), reads=[rxi], writes=[R("halo")])
                        p.op("act", lambda e, xi=xi, ac=ac, cc=cc: e.activation(out=ac[:], in_=xi[:, 0:TBK], func=AF.Identity, scale=cw[:, cc, 0:1], bias=cb[:, cc:cc + 1]),
                             reads=[rxi, R("cw"), R("cb")], writes=[rac])
                        for w in range(1, 4):
                            p.op("dve", lambda e, xi=xi, ac=ac, cc=cc, w=w: e.scalar_tensor_tensor(out=ac[:], in0=xi[:, w:w + TBK], scalar=cw[:, cc, w:w + 1], in1=ac[:],
                                                                                                 op0=ALU.mult, op1=ALU.add), reads=[rxi, rac, R("cw")], writes=[rac])
                        if cc < 16:
                            xb_, rxb = xsb[cbuf], R("xsb%d" % cbuf)
                            p.op("act", lambda e, ac=ac, xb_=xb_: e.activation(out=xb_[:], in_=ac[:], func=AF.Silu), reads=[rac], writes=[rxb])
                            hp = cc % 2
                            rp = self.r_ptr[hp]

                            def tr(e, xb_=xb_, hp=hp):
                                for q in range(NT):
                                    ins = e.transpose(out=self.ptrh[hp][:, q * 128:(q + 1) * 128], in_=xb_[:, q * 128:(q + 1) * 128], identity=self.ident_b[:])
                                return ins
                            p.op("pe", tr, reads=[rxb, self.R("ident_b")], writes=[rp])
                            dst = xs_tok[:, :, cc * 128:(cc + 1) * 128]
                            srcp = self.ptrh[hp][:, 0:NT * 128].rearrange("p (q m) -> p q m", q=NT)
                            p.op("dve", lambda e, dst=dst, srcp=srcp: e.tensor_copy(out=dst, in_=srcp), reads=[rp], writes=[R("xs_tok")])
                        elif cc < 24:
                            g = cc - 16
                            p.op("act", lambda e, ac=ac, g=g: e.activation(out=BT[:, g, :], in_=ac[:], func=AF.Silu), reads=[rac], writes=[R("BT%d" % g)])
                            hp = cc % 2
                            rp = self.r_ptr[hp]

                            def tr(e, g=g, hp=hp):
                                for q in range(NT):
                                    ins = e.transpose(out=self.ptrh[hp][:, q * 128:(q + 1) * 128], in_=BT[:, g, q * 128:(q + 1) * 128], identity=self.ident_b[:])
                                return ins
                            p.op("pe", tr, reads=[R("BT%d" % g), self.R("ident_b")], writes=[rp])
                            dst = Btok[:, :, g * 128:(g + 1) * 128]
                            srcp = self.ptrh[hp][:, 0:NT * 128].rearrange("p (q m) -> p q m", q=NT)
                            p.op("dve", lambda e, dst=dst, srcp=srcp: e.tensor_copy(out=dst, in_=srcp), reads=[rp], writes=[R("Btok")])
                        else:
                            g = cc - 24
                            p.op("act", lambda e, ac=ac, g=g: e.activation(out=CT[:, g, :], in_=ac[:], func=AF.Silu), reads=[rac], writes=[R("CT%d" % g)])
                for zc in range(4):
                    b = wxc % 2
                    wxc += 1
                    c0 = zc * 512
                    src = self.swin[j, :, c0:c0 + 512].rearrange("(k p) m -> p k m", p=128)
                    p.op("sp", lambda e, src=src, b=b: e.dma_start(out=wx[b][:], in_=src),
                         reads=self.swin_res(j, c0), writes=[R("wx%d" % b)], dma=self.st_w)
                    for q in range(NT):
                        pb = (zc * NT + q) % 2
                        bank, rb = self.pbank[pb], self.r_pb[pb]

                        def mmz(e, b=b, q=q, bank=bank):
                            for kc in range(8):
                                ins = e.matmul(bank[:, :], lhsT=hT[:, kc, q * 128:(q + 1) * 128], rhs=wx[b][:, kc, :], start=(kc == 0), stop=(kc == 7))
                            return ins
                        p.op("pe", mmz, reads=[R("wx%d" % b)] + r_hT[q], writes=[rb])
                        p.op("act", lambda e, q=q, zc=zc, bank=bank: e.activation(out=sz[:, q, zc * 512:(zc + 1) * 512], in_=bank[:, :], func=AF.Silu),
                             reads=[rb], writes=[R("sz%d" % q)])
                for q in range(NT):
                    tt = t0 + q
                    tsl = slice(q * 128, (q + 1) * 128)
                    b0, rb0 = self.pbank[0], self.r_pb[0]

                    def mmdt(e, q=q):
                        for kc in range(8):
                            ins = e.matmul(b0[:, 0:32], lhsT=hT[:, kc, q * 128:(q + 1) * 128], rhs=wdt[:, kc, :], start=(kc == 0), stop=(kc == 7))
                        return ins
                    p.op("pe", mmdt, reads=[R("wdt")] + r_hT[q], writes=[rb0])
                    p.op("dve", lambda e: e.tensor_tensor(out=dtv[:], in0=b0[:, 0:32], in1=vec[:, 0, :], op=ALU.add), reads=[rb0, R("vec")], writes=[R("dtv")])
                    p.op("act", lambda e: e.activation(out=dtv[:], in_=dtv[:], func=AF.Exp), reads=[R("dtv")], writes=[R("dtv")])
                    p.op("act", lambda e: e.activation(out=dt3[:, :, 0], in_=dtv[:], func=AF.Ln, bias=1.0), reads=[R("dtv")], writes=[R("dt3")])
                    p.op("dve", lambda e: e.tensor_tensor(out=da[:], in0=dt3[:, :, 0], in1=a_bc[:], op=ALU.mult), reads=[R("dt3"), R("a_bc")], writes=[R("da")])

                    def mmcs(e):
                        e.matmul(b0[:, 32:64], lhsT=L1[:], rhs=da[:], start=True, stop=True)
                        e.matmul(b0[:, 64:96], lhsT=L2[:], rhs=da[:], start=True, stop=True)
                        return e.matmul(b0[:, 96:128], lhsT=ONES[:], rhs=da[:], start=True, stop=True)
                    p.op("pe", mmcs, reads=[R("da"), R("L1"), R("L2"), R("ONES")], writes=[rb0])
                    p.op("act", lambda e: e.activation(out=eall[:], in_=b0[:, 32:128], func=AF.Exp), reads=[rb0], writes=[R("eall")])
                    p.op("dve", lambda e: e.tensor_copy(out=ea3[:, :, 0], in_=eall[:, 0:32]), reads=[R("eall")], writes=[R("ea3")])
                    p.op("dve", lambda e: e.tensor_tensor(out=w23[:, :, 0], in0=dt3[:, :, 0], in1=eall[:, 32:64], op=ALU.mult), reads=[R("dt3"), R("eall")], writes=[R("w23")])
                    xs3 = xs_tok[:, q, :].rearrange("p (h d) -> p h d", h=32)
                    p.op("dve", lambda e, xs3=xs3: e.tensor_tensor(out=xdt[:].rearrange("p (h d) -> p h d", h=32), in0=xs3, in1=dt3[:].to_broadcast([128, 32, 64]), op=ALU.mult),
                         reads=[R("xs_tok"), R("dt3")], writes=[R("xdt")])
                    p.op("pool", lambda e, xs3=xs3: e.tensor_tensor(out=xdec[:].rearrange("p (h d) -> p h d", h=32), in0=xs3, in1=w23[:].to_broadcast([128, 32, 64]), op=ALU.mult),
                         reads=[R("xs_tok"), R("w23")], writes=[R("xdec")])
                    p.op("pool", lambda e, xs3=xs3: e.tensor_tensor(out=t3[:].rearrange("p (h d) -> p h d", h=32), in0=xs3, in1=Dbc[:].to_broadcast([128, 32, 64]), op=ALU.mult),
                         reads=[R("xs_tok"), R("Dbc")], writes=[R("t3")])
                    def stage1(g, tsl=tsl):
                        gb = g % 2
                        b1, rb1 = self.pbank[1], self.r_pb[1]
                        gcol = slice((g % 4) * 128, (g % 4 + 1) * 128)
                        p.op("pe", lambda e, g=g, gcol=gcol, tsl=tsl: e.matmul(b1[:, gcol], lhsT=BT[:, g, tsl], rhs=CT[:, g, tsl], start=True, stop=True),
                             reads=[R("BT%d" % g), R("CT%d" % g)], writes=[rb1])
                        p.op("dve", lambda e, gb=gb, gcol=gcol: e.tensor_tensor(out=GTm[gb][:, 0, :], in0=b1[:, gcol], in1=L1[:], op=ALU.mult),
                             reads=[rb1, R("L1")], writes=[R("GTm%d" % gb)])
                        bs, rbs = self.pbank[2 + gb], self.r_pb[2 + gb]
                        p.op("dve", lambda e, gb=gb, g=g: e.tensor_tensor(out=Ada4[gb][:], in0=L2[:].rearrange("p (o l) -> p o l", o=1).to_broadcast([128, 4, 128]),
                                                                         in1=da[:, 4 * g:4 * g + 4].rearrange("p (r o) -> p r o", o=1).to_broadcast([128, 4, 128]), op=ALU.mult),
                             reads=[R("L2"), R("da")], writes=[R("Ada4_%d" % gb)])

                        def mmseg(e, gb=gb, bs=bs):
                            for r in range(4):
                                ins = e.matmul(bs[:, r * 128:(r + 1) * 128], lhsT=Ada4[gb][:, r, :], rhs=L1[:], start=True, stop=True)
                            return ins
                        p.op("pe", mmseg, reads=[R("Ada4_%d" % gb), R("L1")], writes=[rbs])
                        p.op("act", lambda e, gb=gb, bs=bs: e.activation(out=Eg[gb][:].rearrange("p r l -> p (r l)"), in_=bs[:, :], func=AF.Exp),
                             reads=[rbs], writes=[R("Eg%d" % gb)])
                        p.op("dve", lambda e, gb=gb: e.tensor_tensor(out=MT[gb][:], in0=Eg[gb][:], in1=GTm[gb][:].to_broadcast([128, 4, 128]), op=ALU.mult),
                             reads=[R("Eg%d" % gb), R("GTm%d" % gb)], writes=[R("MT%d" % gb)])

                    def stage2(g, tsl=tsl, q=q):
                        gb = g % 2
                        by, rby = self.pbank[4 + gb], self.r_pb[4 + gb]

                        def mmy(e, g=g, gb=gb, by=by, tsl=tsl):
                            for r in range(4):
                                h = 4 * g + r
                                e.matmul(by[:, r * 64:(r + 1) * 64], lhsT=MT[gb][:, r, :], rhs=xdt[:, h * 64:(h + 1) * 64], start=True, stop=True)
                            return e.matmul(by[:, 256:512], lhsT=CT[:, g, tsl], rhs=state_bf[:, g * 256:(g + 1) * 256], start=True, stop=True)
                        p.op("pe", mmy, reads=[R("MT%d" % gb), R("xdt"), R("CT%d" % g), R("state_bf")], writes=[rby])
                        p.op("pe", lambda e, g=g, q=q: e.matmul(b0[:, 256:512], lhsT=Btok[:, q, g * 128:(g + 1) * 128], rhs=xdec[:, g * 256:(g + 1) * 256], start=True, stop=True),
                             reads=[R("Btok"), R("xdec")], writes=[rb0])
                        for r in range(4):
                            h = 4 * g + r
                            p.op("dve", lambda e, h=h, r=r: e.scalar_tensor_tensor(out=state[:, h * 64:(h + 1) * 64], in0=state[:, h * 64:(h + 1) * 64],
                                                                                     scalar=eall[:, 64 + h:65 + h], in1=b0[:, 256 + r * 64:256 + (r + 1) * 64],
                                                                                     op0=ALU.mult, op1=ALU.add), reads=[R("state"), R("eall"), rb0], writes=[R("state")])
                        yg = ybuf[:, g * 256:(g + 1) * 256].rearrange("p (r d) -> p r d", r=4)
                        p.op("dve", lambda e, yg=yg, by=by, g=g: e.tensor_tensor(out=yg, in0=by[:, 256:512].rearrange("p (r d) -> p r d", r=4),
                                                                               in1=ea3[:, 4 * g:4 * g + 4, :].to_broadcast([128, 4, 64]), op=ALU.mult),
                             reads=[rby, R("ea3")], writes=[R("ybuf")])
                        p.op("dve", lambda e, g=g, by=by: e.tensor_tensor(out=ybuf[:, g * 256:(g + 1) * 256], in0=ybuf[:, g * 256:(g + 1) * 256], in1=by[:, 0:256], op=ALU.add),
                             reads=[R("ybuf"), rby], writes=[R("ybuf")])

                    for g in range(9):
                        if g < 8:
                            stage1(g)
                        if g > 0:
                            stage2(g - 1)
                    p.op("act", lambda e: e.copy(out=state_bf[:], in_=state[:]), reads=[R("state")], writes=[R("state_bf")])
                    p.op("pool", lambda e: e.tensor_tensor(out=ybuf[:], in0=ybuf[:], in1=t3[:], op=ALU.add), reads=[R("ybuf"), R("t3")], writes=[R("ybuf")])
                    p.op("pool", lambda e, q=q: e.tensor_tensor(out=ybuf[:], in0=ybuf[:], in1=sz[:, q, :], op=ALU.mult), reads=[R("ybuf"), R("sz%d" % q)], writes=[R("ybuf")])
                    for g in range(8):
                        p.op("act", lambda e, g=g: e.activation(out=gnb[:, g * 256:(g + 1) * 256], in_=ybuf[:, g * 256:(g + 1) * 256], func=AF.Square, accum_out=ssg[:, g:g + 1]),
                             reads=[R("ybuf")], writes=[R("gnb"), R("ssg")])
                    p.op("act", lambda e: e.activation(out=rsg[:, :, 0], in_=ssg[:], func=AF.Sqrt, scale=1.0 / 256, bias=self.epsb[:]), reads=[R("ssg"), self.R("epsb")], writes=[R("rsg")])
                    p.op("dve", lambda e: e.reciprocal(out=rsg[:, :, 0], in_=rsg[:, :, 0]), reads=[R("rsg")], writes=[R("rsg")])
                    p.op("dve", lambda e: e.tensor_tensor(out=ybuf[:].rearrange("p (g d) -> p g d", g=8), in0=ybuf[:].rearrange("p (g d) -> p g d", g=8),
                                                          in1=rsg[:].to_broadcast([128, 8, 256]), op=ALU.mult), reads=[R("ybuf"), R("rsg")], writes=[R("ybuf")])
                    p.op("pool", lambda e: e.tensor_tensor(out=gnb[:], in0=ybuf[:], in1=nw[:], op=ALU.mult), reads=[R("ybuf"), R("nw"), R("gnb")], writes=[R("gnb")])
                    for qq in range(4):
                        hp = qq % 2
                        rp = self.r_ptr[hp]

                        def trg(e, qq=qq, hp=hp):
                            for m in range(4):
                                k = qq * 4 + m
                                ins = e.transpose(out=self.ptrh[hp][:, m * 128:(m + 1) * 128], in_=gnb[:, k * 128:(k + 1) * 128], identity=self.ident_b[:])
                            return ins
                        p.op("pe", trg, reads=[R("gnb"), self.R("ident_b")], writes=[rp])
                        dst = gnT[:, qq * 4:(qq + 1) * 4, :]
                        srcp = self.ptrh[hp][:, :].rearrange("p (q m) -> p q m", q=4)
                        if hp == 0:
                            p.op("act", lambda e, dst=dst, srcp=srcp: e.copy(out=dst, in_=srcp), reads=[rp], writes=[R("gnT%d" % qq)])
                        else:
                            p.op("dve", lambda e, dst=dst, srcp=srcp: e.tensor_copy(out=dst, in_=srcp), reads=[rp], writes=[R("gnT%d" % qq)])
                    p.op("sp", lambda e, tt=tt: e.dma_start(out=xres_t[:], in_=xsrc[tt * 128:(tt + 1) * 128, :]),
                         reads=[self.R("xres")], writes=[R("xres_t")], dma=self.st_x)
                    for ch in range(2):
                        bo, rbo = self.pbank[1 + ch], self.r_pb[1 + ch]

                        def mmo(e, ch=ch, bo=bo):
                            for k in range(16):
                                ins = e.matmul(bo[:, :], lhsT=gnT[:, k, :], rhs=wout[:, k, ch * 512:(ch + 1) * 512], start=(k == 0), stop=(k == 15))
                            return ins
                        p.op("pe", mmo, reads=[R("gnT%d" % qq) for qq in range(4)] + r_wout, writes=[rbo])
                        p.op("dve", lambda e, ch=ch, bo=bo: e.tensor_tensor(out=xo[:, ch * 512:(ch + 1) * 512], in0=bo[:, :], in1=xres_t[:, ch * 512:(ch + 1) * 512], op=ALU.add),
                             reads=[rbo, R("xres_t")], writes=[R("xo")])
                    p.op("sp", lambda e, tt=tt: e.dma_start(out=self.xres[tt * 128:(tt + 1) * 128, :], in_=xo[:]),
                         reads=[R("xo")], writes=[self.R("xres_w%d" % (tt % 4))], dma=self.st_o)
            self.phase_barrier()
        self.x_src = self.xres

    def prep_attn(self, j):
        p = self.p
        for (c0, c1) in ((0, 2048), (2048, NAW)):
            for hh in range(2):
                s_ap = self.attn_wr[j, hh * 512:(hh + 1) * 512, c0:c1]
                d_ap = self.awin[j, hh * 512:(hh + 1) * 512, c0:c1]
                p.op("pool", lambda e, s_ap=s_ap, d_ap=d_ap: e.dma_start(out=d_ap, in_=s_ap),
                     writes=[self.R("awin%d_%d_%d" % (j, c0, hh))], dma=self.st_prep)

    def awin_res(self, j, c0, c1):
        out = []
        for base in (0, 2048):
            hi = 2048 if base == 0 else NAW
            if c0 < hi and c1 > base:
                out += [self.R("awin%d_%d_%d" % (j, base, hh)) for hh in range(2)]
        return out

    def attn(self, l):
        p = self.p
        nc = self.nc
        j = l // 2
        xsrc = self.x_src
        tag = "a%d" % l
        R = lambda n: self.R(tag + n)
        BIG = 30000.0
        dbg = self.attn_dbg or ""
        use_cmp = ("nocmp" not in dbg)
        use_slc = ("noslc" not in dbg)
        use_win = ("nowin" not in dbg)
        with contextlib.ExitStack() as st_long:
            def sbl(name, shape, dt):
                return st_long.enter_context(nc.sbuf_tensor(tag + name, list(shape), dt))
            kT = sbl("kT", [64, 6, S], BF16)
            Vall = sbl("V", [128, 32, 6, 65], BF16)
            gates = sbl("gates", [128, 32, 24], F32)
            kcmpT = sbl("kcmpT", [64, 2, 256], BF16)
            Vcmp = sbl("Vcmp", [128, 2, 2, 129], BF16)
            esink = sbl("esink", [128, 8], F32)
            p.op("pool", lambda e: e.memset(Vall[:, :, :, 64:65], 1.0), writes=[R("Vones")])
            p.op("sp", lambda e: e.dma_start(out=esink[:], in_=self.sinks_in[j:j + 1, :].broadcast_to([128, 8])), writes=[R("esink")], dma=self.st_const)
            p.op("act", lambda e: e.activation(out=esink[:], in_=esink[:], func=AF.Exp), reads=[R("esink")], writes=[R("esink")])
            self.load_gain(l * 3 + 1)

            with contextlib.ExitStack() as st_x:
                kcT = st_x.enter_context(nc.sbuf_tensor(tag + "kcT", [64, 2, S], BF16))
                vcT = st_x.enter_context(nc.sbuf_tensor(tag + "vcT", [128, S], BF16))
                with contextlib.ExitStack() as st1:
                    def sb(name, shape, dt):
                        return st1.enter_context(nc.sbuf_tensor(tag + name, list(shape), dt))
                    TBK = 512
                    NT = 4
                    hT = sb("hT", [128, 8, TBK], BF16)
                    wb = [sb("wb%d" % b, [128, 8, 512], BF16) for b in range(2)]
                    cosb = sb("cosb", [64, TBK], F32)
                    sinb = sb("sinb", [64, TBK], F32)
                    t1 = [sb("t1_%d" % b, [64, TBK], F32) for b in range(2)]
                    t2 = [sb("t2_%d" % b, [64, TBK], F32) for b in range(2)]
                    qst = [sb("qst%d" % b, [64, TBK], BF16) for b in range(2)]
                    xt_bufs = [(sb("xt%d" % b, [128, D], F32), sb("hb%d" % b, [128, D], BF16),
                                sb("sq%d" % b, [128, D], BF16), sb("ss%d" % b, [128, 1], F32),
                                sb("rs%d" % b, [128, 1], F32)) for b in range(2)]
                    r_hT = [[R("hT%d_%d" % (jj, hh)) for hh in range(2)] for jj in range(NT)]
                    hres_all = [r_hT[jj][hh] for jj in range(NT) for hh in range(2)]
                    wc = 0
                    hc = 0
                    for blk in range(S // TBK):
                        t0 = blk * NT
                        csl = slice(blk * TBK, (blk + 1) * TBK)
                        self.norm_transpose(xsrc, t0, NT, hT, r_hT, xt_bufs, tag)
                        p.op("sp", lambda e, csl=csl: e.dma_start(out=cosb[:], in_=self.rope_in[0, :, csl]), writes=[R("cosb")], dma=self.st_x)
                        p.op("sp", lambda e, csl=csl: e.dma_start(out=sinb[:], in_=self.rope_in[1, :, csl]), writes=[R("sinb")], dma=self.st_x)
                        for wgI in range(6):
                            b = wc % 2
                            wc += 1
                            c0 = wgI * 512
                            src = self.awin[j, :, c0:c0 + 512].rearrange("(k p) m -> p k m", p=128)
                            p.op("sp", lambda e, src=src, b=b: e.dma_start(out=wb[b][:], in_=src),
                                 reads=self.awin_res(j, c0, c0 + 512), writes=[R("wb%d" % b)], dma=self.st_w)
                            for m in range(4):
                                hd = wgI * 4 + m
                                hb_ = hc % 2
                                hc += 1
                                bA, rA = self.pbank[hb_ * 2], self.r_pb[hb_ * 2]
                                bB, rB = self.pbank[hb_ * 2 + 1], self.r_pb[hb_ * 2 + 1]

                                def mm(e, bank, off, b=b, m=m):
                                    for kc in range(8):
                                        ins = e.matmul(bank[0:64, :], lhsT=wb[b][:, kc, m * 128 + off:m * 128 + off + 64], rhs=hT[:, kc, :],
                                                       start=(kc == 0), stop=(kc == 7))
                                    return ins
                                p.op("pe", lambda e, mm=mm, bA=bA: mm(e, bA, 0), reads=[R("wb%d" % b)] + hres_all, writes=[rA])
                                p.op("pe", lambda e, mm=mm, bB=bB: mm(e, bB, 64), reads=[R("wb%d" % b)] + hres_all, writes=[rB])
                                p.op("dve", lambda e, hb_=hb_, bA=bA: e.tensor_tensor(out=t1[hb_][:], in0=bA[0:64, :], in1=cosb[:], op=ALU.mult),
                                     reads=[rA, R("cosb")], writes=[R("t1_%d" % hb_)])
                                p.op("dve", lambda e, hb_=hb_, bB=bB: e.tensor_tensor(out=t2[hb_][:], in0=bB[0:64, :], in1=sinb[:], op=ALU.mult),
                                     reads=[rB, R("sinb")], writes=[R("t2_%d" % hb_)])
                                if hd < 8 or 10 <= hd < 18:
                                    qh = hd if hd < 8 else hd - 10 + 8
                                    p.op("pool", lambda e, hb_=hb_: e.tensor_tensor(out=qst[hb_][:], in0=t1[hb_][:], in1=t2[hb_][:], op=ALU.add),
                                         reads=[R("t1_%d" % hb_), R("t2_%d" % hb_)], writes=[R("qst%d" % hb_)])
                                    p.op("sp", lambda e, hb_=hb_, qh=qh, csl=csl: e.dma_start(out=self.qT[qh, :, csl], in_=qst[hb_][:]),
                                         reads=[R("qst%d" % hb_)], writes=[self.R("qT_%d_%d" % (qh, blk))], dma=self.st_o)
                                else:
                                    if hd < 10:
                                        dst, rd = kT[:, hd - 8, csl], R("kT%d" % (hd - 8))
                                    elif hd < 20:
                                        dst, rd = kcT[:, hd - 18, csl], R("kcT%d" % (hd - 18))
                                    elif hd < 22:
                                        dst, rd = kT[:, 2 + hd - 20, csl], R("kT%d" % (2 + hd - 20))
                                    else:
                                        dst, rd = kT[:, 4 + hd - 22, csl], R("kT%d" % (4 + hd - 22))
                                    p.op("pool", lambda e, hb_=hb_, dst=dst: e.tensor_tensor(out=dst, in0=t1[hb_][:], in1=t2[hb_][:], op=ALU.add),
                                         reads=[R("t1_%d" % hb_), R("t2_%d" % hb_)], writes=[rd])
                        b = wc % 2
                        wc += 1
                        src = self.awin[j, :, 3072:3608].rearrange("(k p) m -> p k m", p=128)
                        src_vc = self.awin[j, :, 3072:3200].rearrange("(k p) m -> p k m", p=128)
                        src_tm = self.awin[j, :, 3200:3608].rearrange("(k p) m -> p k m", p=128)
                        b2 = wc % 2
                        wc += 1
                        p.op("sp", lambda e, b=b, src_vc=src_vc: e.dma_start(out=wb[b][:, :, 0:128], in_=src_vc),
                             reads=self.awin_res(j, 3072, 3200), writes=[R("wb%d" % b)], dma=self.st_w)
                        p.op("sp", lambda e, b2=b2, src_tm=src_tm: e.dma_start(out=wb[b2][:, :, 0:408], in_=src_tm),
                             reads=self.awin_res(j, 3200, 3608), writes=[R("wb%d" % b2)], dma=self.st_w)
                        bA, rA = self.pbank[4], self.r_pb[4]

                        def mmvc(e, b=b, bA=bA):
                            for kc in range(8):
                                ins = e.matmul(bA[:, :], lhsT=wb[b][:, kc, 0:128], rhs=hT[:, kc, :], start=(kc == 0), stop=(kc == 7))
                            return ins
                        p.op("pe", mmvc, reads=[R("wb%d" % b)] + hres_all, writes=[rA])
                        p.op("act", lambda e, bA=bA, csl=csl: e.copy(out=vcT[:, csl], in_=bA[:, :]), reads=[rA], writes=[R("vcT")])
                        for q in range(NT):
                            tt = t0 + q
                            bT, rT = self.pbank[5], self.r_pb[5]

                            def mmtm(e, q=q, b2=b2, bT=bT):
                                for kc in range(8):
                                    ins = e.matmul(bT[:, 0:408], lhsT=hT[:, kc, q * 128:(q + 1) * 128], rhs=wb[b2][:, kc, 0:408], start=(kc == 0), stop=(kc == 7))
                                return ins
                            p.op("pe", mmtm, reads=[R("wb%d" % b2)] + r_hT[q], writes=[rT])
                            p.op("dve", lambda e, tt=tt, bT=bT: e.tensor_copy(out=Vall[:, tt, :, 0:64], in_=bT[:, 0:384].rearrange("p (a d) -> p a d", a=6)),
                                 reads=[rT], writes=[R("Vall")])
                            p.op("act", lambda e, tt=tt, bT=bT: e.activation(out=gates[:, tt, :], in_=bT[:, 384:408], func=AF.Sigmoid),
                                 reads=[rT], writes=[R("gates")])
                    p.barrier()
                with contextlib.ExitStack() as st2:
                    def sb(name, shape, dt):
                        return st2.enter_context(nc.sbuf_tensor(tag + name, list(shape), dt))
                    w1s = sb("w1s", [64, 32, 128], BF16)
                    w2s = sb("w2s", [128, 64], BF16)
                    posf = sb("posf", [64, 32], F32)
                    posb_ = sb("posb", [64, 32, 2], BF16)
                    pbias = sb("pbias", [128, 1], F32)
                    u = sb("u", [128, 256], F32)
                    u2 = sb("u2", [128, 256], F32)
                    sg_ = sb("sgm", [128, 256], F32)
                    gl = sb("gl", [128, 256], BF16)
                    wself = sb("wself", [128, 2, 64], F32)
                    p.op("sp", lambda e: e.dma_start(out=wself[:], in_=self.wsel_in), writes=[R("wself")], dma=self.st_const)
                    p.op("dve", lambda e: e.memset(u[:], 0.0), writes=[R("u")])
                    p.op("pool", lambda e: e.memset(Vcmp[:, :, :, 64:65], 1.0), writes=[R("Vcmp1")])
                    for g in range(2):
                        p.op("dve", lambda e, g=g: e.tensor_copy(out=Vcmp[:, g, :, 65:129], in_=wself[:]), reads=[R("wself")], writes=[R("VcmpW%d" % g)])
                    for kv in range(2):
                        w1_in = self.cmp_w1[j, kv].rearrange("(pp d) h -> d pp h", d=64)
                        p.op("pool", lambda e, w1_in=w1_in: e.dma_start(out=w1s[:], in_=w1_in), writes=[R("w1s")], dma=self.st_prep)
                        p.op("pool", lambda e, kv=kv: e.dma_start(out=w2s[:], in_=self.cmp_w2[j, kv]), writes=[R("w2s")], dma=self.st_prep)
                        p.op("sp", lambda e, kv=kv: e.dma_start(out=posf[:], in_=self.cmp_posT[j, kv]), writes=[R("posf")], dma=self.st_const)
                        p.op("dve", lambda e: e.tensor_copy(out=posb_[:], in_=posf[:].rearrange("p (a o) -> p a o", o=1).to_broadcast([64, 32, 2])), reads=[R("posf")], writes=[R("posb")])
                        b0, rb0 = self.pbank[0], self.r_pb[0]

                        def mmb(e):
                            for pp in range(32):
                                ins = e.matmul(b0[:, 0:2], lhsT=w1s[:, pp, :], rhs=posb_[:, pp, :], start=(pp == 0), stop=(pp == 31))
                            return ins
                        p.op("pe", mmb, reads=[R("w1s"), R("posb")], writes=[rb0])
                        p.op("dve", lambda e: e.tensor_copy(out=pbias[:], in_=b0[:, 0:1]), reads=[rb0], writes=[R("pbias")])
                        for g in range(2):
                            b1, rb1 = self.pbank[1 + g], self.r_pb[1 + g]
                            if kv == 0:
                                srcT = kcT[:, g, :]
                                rsrc = R("kcT%d" % g)
                            else:
                                srcT = vcT[g * 64:(g + 1) * 64, :]
                                rsrc = R("vcT")

                            def mmh(e, srcT=srcT, b1=b1, g=g):
                                for pp in range(32):
                                    ins = e.matmul(b1[:, 0:255], lhsT=w1s[g * 64 * kv:g * 64 * kv + 64, pp, :] if False else w1s[:, pp, :],
                                                   rhs=srcT[:, pp:pp + 16 * 254 + 1:16], start=(pp == 0), stop=(pp == 31))
                                return ins
                            if kv == 1 and g == 1:
                                vtmp = sb("vtmp", [64, S], BF16)
                                p.op("sp", lambda e, vtmp=vtmp: e.dma_start(out=vtmp[:], in_=vcT[64:128, :]), reads=[R("vcT")], writes=[R("vtmp")], dma=self.st_x)
                                srcT2 = vtmp[:, :]

                                def mmh(e, srcT2=srcT2, b1=b1):
                                    for pp in range(32):
                                        ins = e.matmul(b1[:, 0:255], lhsT=w1s[:, pp, :], rhs=srcT2[:, pp:pp + 16 * 254 + 1:16], start=(pp == 0), stop=(pp == 31))
                                    return ins
                                rsrc = R("vtmp")
                            p.op("pe", mmh, reads=[R("w1s"), rsrc], writes=[rb1])
                            p.op("act", lambda e, b1=b1: e.activation(out=u[:, 0:255], in_=b1[:, 0:255], func=AF.Identity, bias=pbias[:]),
                                 reads=[rb1, R("pbias"), R("u")], writes=[R("u")])
                            p.op("dve", lambda e: e.tensor_tensor(out=u2[:], in0=u[:], in1=u[:], op=ALU.mult), reads=[R("u")], writes=[R("u2")])
                            p.op("dve", lambda e: e.tensor_scalar(out=u2[:], in0=u2[:], scalar1=0.044715, scalar2=1.0, op0=ALU.mult, op1=ALU.add), reads=[R("u2")], writes=[R("u2")])
                            p.op("dve", lambda e: e.tensor_tensor(out=u2[:], in0=u2[:], in1=u[:], op=ALU.mult), reads=[R("u2"), R("u")], writes=[R("u2")])
                            p.op("act", lambda e: e.activation(out=sg_[:], in_=u2[:], func=AF.Sigmoid, scale=1.5957691216057308), reads=[R("u2")], writes=[R("sgm")])
                            p.op("dve", lambda e: e.tensor_tensor(out=gl[:], in0=u[:], in1=sg_[:], op=ALU.mult), reads=[R("u"), R("sgm")], writes=[R("gl")])
                            b3, rb3 = self.pbank[3], self.r_pb[3]
                            if kv == 0:
                                p.op("pe", lambda e, b3=b3: e.matmul(b3[0:64, 0:256], lhsT=w2s[:], rhs=gl[:], start=True, stop=True), reads=[R("w2s"), R("gl")], writes=[rb3])
                                p.op("dve", lambda e, g=g, b3=b3: e.tensor_copy(out=kcmpT[:, g, :], in_=b3[0:64, 0:256]), reads=[rb3], writes=[R("kcmpT%d" % g)])
                            else:
                                def mmv(e, b3=b3):
                                    for ct in range(2):
                                        ins = e.matmul(b3[:, ct * 64:(ct + 1) * 64], lhsT=gl[:, ct * 128:(ct + 1) * 128], rhs=w2s[:], start=True, stop=True)
                                    return ins
                                p.op("pe", mmv, reads=[R("w2s"), R("gl")], writes=[rb3])
                                p.op("dve", lambda e, g=g, b3=b3: e.tensor_copy(out=Vcmp[:, g, :, 0:64], in_=b3[:, 0:128].rearrange("p (c d) -> p c d", c=2)),
                                     reads=[rb3], writes=[R("VcmpV%d" % g)])
                    p.barrier()
            with contextlib.ExitStack() as st3:
                def sb(name, shape, dt):
                    return st3.enter_context(nc.sbuf_tensor(tag + name, list(shape), dt))
                wout = sb("wout", [128, 8, D], BF16)
                for q in range(2):
                    src = self.attn_w_out[j, q * 512:(q + 1) * 512, :].rearrange("(k p) m -> p k m", p=128)
                    p.op("pool", lambda e, src=src, q=q: e.dma_start(out=wout[:, q * 4:(q + 1) * 4, :], in_=src), writes=[R("wout%d" % q)], dma=self.st_prep)
                r_wout = [R("wout0"), R("wout1")]
                expand = sb("expand", [64, 32, 128], BF16)
                onesb = sb("onesb", [64, 32 * 128], BF16)
                p.op("pool", lambda e: e.memset(onesb[:], 1.0), writes=[R("onesb")])
                p.op("pool", lambda e: e.affine_select(out=expand[:].rearrange("p a (h m) -> p a h m", h=2), in_=onesb[:].rearrange("p (a h m) -> p a h m", a=32, h=2),
                                                       pattern=[[-2, 32], [-1, 2], [0, 64]], compare_op=ALU.is_equal, fill=0.0, base=0, channel_multiplier=1),
                     reads=[R("onesb")], writes=[R("expand")])
                selb = sb("selb", [128, 32, 64], F32)
                p.op("sp", lambda e: e.dma_start(out=selb[:], in_=self.selb_in), writes=[R("selb")], dma=self.st_const)
                qt = [sb("qt%d" % b, [64, 16, 256], BF16) for b in range(2)]
                Eb = [sb("E%d" % b, [128, 4, 128], BF16) for b in range(3)]
                ot = sb("ot", [128, D], BF16)
                accb = sb("accb", [128, 4, 64], F32)
                imp = sb("imp", [128, 64], F32)
                sc2 = sb("sc2", [128, 64], F32)
                m8 = sb("m8", [128, 8], F32)
                nb = sb("nb", [128, 64], F32)
                nbT4 = sb("nbT4", [64, 4, 128], BF16)
                den = sb("den", [128, 4], F32)
                oT = sb("oT", [128, 8, 128], BF16)
                xres_t = sb("xres_t", [128, D], F32)
                xo = sb("xo", [128, D], F32)
                ec = [0]
                sc_ = [0]

                def score_exp(i, lhsT, lres, rhs, rres, mask, extra=None):
                    sb_i = sc_[0] % 2
                    sc_[0] += 1
                    bank, rb = self.pbank[sb_i], self.r_pb[sb_i]
                    eb = ec[0] % 3
                    ec[0] += 1

                    def mm(e, bank=bank):
                        ins = e.matmul(bank[:, :].rearrange("p (r q) -> p r q", r=4), lhsT=lhsT, rhs=rhs, start=True, stop=(extra is None))
                        if extra is not None:
                            ins = e.matmul(bank[:, :].rearrange("p (r q) -> p r q", r=4), lhsT=extra[0], rhs=extra[1], start=False, stop=True)
                        return ins
                    rr = list(lres) + list(rres) + (list(extra[2]) if extra is not None else [])
                    p.op("pe", mm, reads=rr, writes=[rb])
                    E = Eb[eb]
                    rE = R("E%d" % eb)
                    p.op("act", lambda e, E=E, bank=bank: e.activation(out=E[:].rearrange("p r q -> p (r q)"), in_=bank[:, :], func=AF.Exp, scale=0.125),
                         reads=[rb], writes=[rE])
                    if mask is not None:
                        base, cm, stepq = mask
                        p.op("pool", lambda e, E=E, base=base, cm=cm, stepq=stepq: e.affine_select(
                            out=E[:], in_=E[:], pattern=[[0, 4], [stepq, 128]], compare_op=ALU.is_ge, fill=0.0, base=base, channel_multiplier=cm),
                            reads=[rE], writes=[rE])
                    return E, rE

                def pv(E, rE, vrhs, vres, ncols, first, last):
                    def mm(e):
                        for r in range(4):
                            ins = e.matmul(self.pbank[2 + r][:, 0:ncols], lhsT=E[:, r, :], rhs=vrhs, start=first, stop=last)
                        return ins
                    p.op("pe", mm, reads=[rE] + list(vres), writes=[self.r_pb[2 + r] for r in range(4)])

                CAUSAL = (0, -1, 1)
                PREV = (-1, 1, -1)
                den2 = [den, sb("denB", [128, 4], F32)]
                bc = [0]

                def pv2(E, rE, vrhs, vres, ncols, first, last, par):
                    off = par * 256

                    def mm(e):
                        for r in range(4):
                            ins = e.matmul(self.pbank[2 + r][:, off:off + ncols], lhsT=E[:, r, :], rhs=vrhs, start=first, stop=last)
                        return ins
                    p.op("pe", mm, reads=[rE] + list(vres), writes=[R("O%d_%d" % (r, par)) for r in range(4)] + [self.r_pb[2 + r] for r in range(4)])

                for i in range(32):
                    qb_ = (i // 2) % 2
                    if i % 2 == 0:
                        blk = i // 4
                        src = self.qT[:, :, i * 128:i * 128 + 256].rearrange("h d t -> d h t")
                        p.op("sp", lambda e, src=src, qb_=qb_: e.dma_start(out=qt[qb_][:], in_=src),
                             reads=[self.R("qT_%d_%d" % (h, blk)) for h in range(16)], writes=[R("qt%d" % qb_)], dma=self.st_x)
                    qsl = slice((i % 2) * 128, (i % 2 + 1) * 128)
                    rq = [R("qt%d" % qb_)]
                    p.op("sp", lambda e, i=i: e.dma_start(out=xres_t[:], in_=xsrc[i * 128:(i + 1) * 128, :]),
                         reads=[self.R("xres")], writes=[R("xres_t")], dma=self.st_x)
                    for g in range(2):
                        qa4 = qt[qb_][:, g * 4:(g + 1) * 4, qsl]
                        qb4 = qt[qb_][:, 8 + g * 4:8 + (g + 1) * 4, qsl]
                        gsl = gates[:, i, g * 12:(g + 1) * 12].rearrange("p (r b) -> p r b", b=3)
                        tiles = []
                        kts = [kt for kt in (i - 1, i) if kt >= 0]
                        for n, kt in enumerate(kts):
                            tiles.append(("swa", kT[:, g, kt * 128:(kt + 1) * 128], [R("kT%d" % g)], qa4, CAUSAL if kt == i else PREV, None,
                                          Vall[:, kt, g, :], [R("Vall"), R("Vones")], 65, n == 0, n == len(kts) - 1))
                        nct = 1 if i < 16 else 2
                        if use_cmp or use_slc:
                            for ct in range(nct):
                                tiles.append(("cmp", kcmpT[:, g, ct * 128:(ct + 1) * 128], [R("kcmpT%d" % g)], qb4, (128 * i - 2048 * ct - 31, -16, 1), None,
                                              Vcmp[:, g, ct, :], [R("VcmpV%d" % g), R("VcmpW%d" % g), R("Vcmp1")], 129, ct == 0, ct == nct - 1))
                        if use_win:
                            kts = [kt for kt in range(i - 4, i + 1) if kt >= 0]
                            for n, kt in enumerate(kts):
                                mk = CAUSAL if kt == i else (PREV if kt == i - 4 else None)
                                tiles.append(("win", kT[:, 4 + g, kt * 128:(kt + 1) * 128], [R("kT%d" % (4 + g))], qb4, mk, None,
                                              Vall[:, kt, 4 + g, :], [R("Vall"), R("Vones")], 65, n == 0, n == len(kts) - 1))
                        if use_slc:
                            for kt in range(i + 1):
                                tiles.append(("slc", kT[:, 2 + g, kt * 128:(kt + 1) * 128], [R("kT%d" % (2 + g))], qb4, CAUSAL if kt == i else None,
                                              (expand[:, kt, :], nbT4[:], [R("expand"), R("nbT4")]),
                                              Vall[:, kt, 2 + g, :], [R("Vall"), R("Vones")], 65, kt == 0, kt == i))
                        branches = [br for br in ("swa", "cmp", "win", "slc") if any(t[0] == br for t in tiles)]
                        last_nsa = [br for br in branches if br != "swa" and br != "cmp"]
                        last_nsa = last_nsa[-1] if last_nsa else "cmp"

                        def finish(br, par):
                            dn = den2[par]
                            rdn = R("den%d" % par)
                            rO = [R("O%d_%d" % (r, par)) for r in range(4)]
                            off = par * 256
                            O = [self.pbank[2 + r] for r in range(4)]
                            if br == "swa":
                                for r in range(4):
                                    h = g * 4 + r
                                    p.op("dve", lambda e, r=r, h=h: e.tensor_tensor(out=dn[:, r:r + 1], in0=O[r][:, off + 64:off + 65], in1=esink[:, h:h + 1], op=ALU.add),
                                         reads=[rO[r], R("esink")], writes=[rdn])
                                p.op("dve", lambda e: e.reciprocal(out=dn[:], in_=dn[:]), reads=[rdn], writes=[rdn])
                                for r in range(4):
                                    h = g * 4 + r
                                    p.op("dve", lambda e, r=r, h=h: e.tensor_scalar(out=ot[:, h * 64:(h + 1) * 64], in0=O[r][:, off:off + 64], scalar1=dn[:, r:r + 1], scalar2=None, op0=ALU.mult),
                                         reads=[rO[r], rdn], writes=[R("ot")])
                                return
                            if br == "cmp":
                                for r in range(4):
                                    p.op("dve", lambda e, r=r: e.tensor_scalar(out=dn[:, r:r + 1], in0=O[r][:, off + 64:off + 65], scalar1=1e-30, scalar2=None, op0=ALU.max),
                                         reads=[rO[r]], writes=[rdn])
                                p.op("dve", lambda e: e.reciprocal(out=dn[:], in_=dn[:]), reads=[rdn], writes=[rdn])
                                for r in range(4):
                                    if r == 0:
                                        p.op("dve", lambda e, r=r: e.tensor_scalar(out=imp[:], in0=O[r][:, off + 65:off + 129], scalar1=dn[:, r:r + 1], scalar2=None, op0=ALU.mult),
                                             reads=[rO[r], rdn], writes=[R("imp")])
                                    else:
                                        p.op("dve", lambda e, r=r: e.scalar_tensor_tensor(out=imp[:], in0=O[r][:, off + 65:off + 129], scalar=dn[:, r:r + 1], in1=imp[:], op0=ALU.mult, op1=ALU.add),
                                             reads=[rO[r], rdn, R("imp")], writes=[R("imp")])
                                if use_slc:
                                    p.op("dve", lambda e, i=i: e.tensor_tensor(out=imp[:], in0=imp[:], in1=selb[:, i, :], op=ALU.add), reads=[R("imp"), R("selb")], writes=[R("imp")])
                                    p.op("dve", lambda e: e.max(out=m8[:], in_=imp[:]), reads=[R("imp")], writes=[R("m8")])
                                    p.op("dve", lambda e: e.match_replace(out=sc2[:], in_to_replace=m8[:], in_values=imp[:], imm_value=-3.0e38), reads=[R("imp"), R("m8")], writes=[R("sc2")])
                                    p.op("dve", lambda e: e.max(out=m8[:], in_=sc2[:]), reads=[R("sc2"), R("m8")], writes=[R("m8")])
                                    p.op("dve", lambda e: e.tensor_scalar(out=nb[:], in0=imp[:], scalar1=m8[:, 7:8], scalar2=-BIG, op0=ALU.is_lt, op1=ALU.mult),
                                         reads=[R("imp"), R("m8")], writes=[R("nb")])
                                    sb_i = sc_[0] % 2
                                    sc_[0] += 1
                                    bank, rb = self.pbank[sb_i], self.r_pb[sb_i]
                                    p.op("pe", lambda e, bank=bank: e.transpose(out=bank[0:64, 0:128], in_=nb[:], identity=self.ident_f[:]), reads=[R("nb"), self.R("ident")], writes=[rb])
                                    p.op("dve", lambda e, bank=bank: e.tensor_copy(out=nbT4[:], in_=bank[0:64, 0:128].rearrange("p (o q) -> p o q", o=1).to_broadcast([64, 4, 128])),
                                         reads=[rb], writes=[R("nbT4")])
                                p.op("dve", lambda e, gsl=gsl: e.tensor_tensor(out=dn[:], in0=dn[:], in1=gsl[:, :, 0], op=ALU.mult), reads=[rdn, R("gates")], writes=[rdn])
                                for r in range(4):
                                    h = 8 + g * 4 + r
                                    if not use_cmp:
                                        p.op("dve", lambda e, r=r: e.memset(accb[:, r, :], 0.0), reads=[R("accb")], writes=[R("accb")])
                                    elif last_nsa == "cmp":
                                        p.op("dve", lambda e, r=r, h=h: e.tensor_scalar(out=ot[:, h * 64:(h + 1) * 64], in0=O[r][:, off:off + 64], scalar1=dn[:, r:r + 1], scalar2=None, op0=ALU.mult),
                                             reads=[rO[r], rdn], writes=[R("ot")])
                                    else:
                                        p.op("dve", lambda e, r=r: e.tensor_scalar(out=accb[:, r, :], in0=O[r][:, off:off + 64], scalar1=dn[:, r:r + 1], scalar2=None, op0=ALU.mult),
                                             reads=[rO[r], rdn], writes=[R("accb")])
                                return
                            gi = 2 if br == "win" else 1
                            for r in range(4):
                                p.op("dve", lambda e, r=r: e.tensor_copy(out=dn[:, r:r + 1], in_=O[r][:, off + 64:off + 65]), reads=[rO[r]], writes=[rdn])
                            p.op("dve", lambda e: e.reciprocal(out=dn[:], in_=dn[:]), reads=[rdn], writes=[rdn])
                            p.op("dve", lambda e, gsl=gsl, gi=gi: e.tensor_tensor(out=dn[:], in0=dn[:], in1=gsl[:, :, gi], op=ALU.mult), reads=[rdn, R("gates")], writes=[rdn])
                            for r in range(4):
                                h = 8 + g * 4 + r
                                if br == last_nsa:
                                    p.op("dve", lambda e, r=r, h=h: e.scalar_tensor_tensor(out=ot[:, h * 64:(h + 1) * 64], in0=O[r][:, off:off + 64], scalar=dn[:, r:r + 1], in1=accb[:, r, :], op0=ALU.mult, op1=ALU.add),
                                         reads=[rO[r], rdn, R("accb")], writes=[R("ot")])
                                else:
                                    p.op("dve", lambda e, r=r: e.scalar_tensor_tensor(out=accb[:, r, :], in0=O[r][:, off:off + 64], scalar=dn[:, r:r + 1], in1=accb[:, r, :], op0=ALU.mult, op1=ALU.add),
                                         reads=[rO[r], rdn, R("accb")], writes=[R("accb")])

                        par_of = {}
                        for br in branches:
                            par_of[br] = (bc[0] % 2) if "usepar" in dbg else 0
                            bc[0] += 1
                        pend = None
                        for tl in tiles:
                            br, lhsT, lres, rhs, mask, extra, vrhs, vres, ncols, first, last = tl
                            E, rE = score_exp(i, lhsT, lres, rhs, rq, mask, extra=extra)
                            if "nopipe" in dbg:
                                pv2(E, rE, vrhs, vres, ncols, first, last, par_of[br])
                                if last:
                                    finish(br, par_of[br])
                                continue
                            if pend is not None:
                                pE, prE, ptl = pend
                                pv2(pE, prE, ptl[6], ptl[7], ptl[8], ptl[9], ptl[10], par_of[ptl[0]])
                                if ptl[10]:
                                    finish(ptl[0], par_of[ptl[0]])
                            pend = (E, rE, tl)
                        if "nopipe" not in dbg:
                            pE, prE, ptl = pend
                            pv2(pE, prE, ptl[6], ptl[7], ptl[8], ptl[9], ptl[10], par_of[ptl[0]])
                            finish(ptl[0], par_of[ptl[0]])
                    for half in range(2):
                        rp = self.r_ptr[half]

                        def tr(e, half=half):
                            for q in range(4):
                                kc = half * 4 + q
                                ins = e.transpose(out=self.ptrh[half][:, q * 128:(q + 1) * 128], in_=ot[:, kc * 128:(kc + 1) * 128], identity=self.ident_b[:])
                            return ins
                        p.op("pe", tr, reads=[R("ot"), self.R("ident_b")], writes=[rp])
                        dst = oT[:, half * 4:(half + 1) * 4, :]
                        srcp = self.ptrh[half][:, :].rearrange("p (q m) -> p q m", q=4)
                        p.op("act", lambda e, dst=dst, srcp=srcp: e.copy(out=dst, in_=srcp), reads=[rp], writes=[R("oT%d" % half)])
                    for ch in range(2):
                        bo, rbo = self.pbank[ch], self.r_pb[ch]

                        def mmo(e, ch=ch, bo=bo):
                            for k in range(8):
                                ins = e.matmul(bo[:, :], lhsT=oT[:, k, :], rhs=wout[:, k, ch * 512:(ch + 1) * 512], start=(k == 0), stop=(k == 7))
                            return ins
                        p.op("pe", mmo, reads=[R("oT0"), R("oT1")] + r_wout, writes=[rbo])
                        p.op("dve", lambda e, ch=ch, bo=bo: e.tensor_tensor(out=xo[:, ch * 512:(ch + 1) * 512], in0=bo[:, :], in1=xres_t[:, ch * 512:(ch + 1) * 512], op=ALU.add),
                             reads=[rbo, R("xres_t")], writes=[R("xo")])
                    if "ot" in dbg:
                        p.op("dve", lambda e: e.tensor_copy(out=xo[:], in_=ot[:]), reads=[R("ot"), R("xo")], writes=[R("xo")])
                    p.op("sp", lambda e, i=i: e.dma_start(out=self.xres[i * 128:(i + 1) * 128, :], in_=xo[:]),
                         reads=[R("xo")], writes=[self.R("xres_w%d" % (i % 4))], dma=self.st_o)
            self.phase_barrier()
        self.x_src = self.xres

    def final_norm(self):
        p = self.p
        nc = self.nc
        xsrc = self.x_src
        with contextlib.ExitStack() as st:
            def sb(name, shape, dt):
                return st.enter_context(nc.sbuf_tensor(name, list(shape), dt))
            self.load_gain(DEPTH * 3)
            bufs = [(sb("fn_x%d" % b, [128, D], F32), sb("fn_sq%d" % b, [128, D], BF16), sb("fn_ss%d" % b, [128, 1], F32),
                     sb("fn_rs%d" % b, [128, 1], F32), sb("fn_o%d" % b, [128, D], F32)) for b in range(2)]
            for tt in range(S // 128):
                b = tt % 2
                xt, sq, ss, rs, ot = bufs[b]
                rx, rss, rrs, ro = [self.R("fn_%s%d" % (n, b)) for n in ("x", "ss", "rs", "o")]
                p.op("sp", lambda e, xt=xt, tt=tt: e.dma_start(out=xt[:], in_=xsrc[tt * 128:(tt + 1) * 128, :]),
                     reads=[self.R("xres")], writes=[rx], dma=self.st_x)
                p.op("act", lambda e, xt=xt, sq=sq, ss=ss: e.activation(out=sq[:], in_=xt[:], func=AF.Square, accum_out=ss[:]),
                     reads=[rx], writes=[self.R("fn_sq%d" % b), rss])
                p.op("act", lambda e, ss=ss, rs=rs: e.activation(out=rs[:], in_=ss[:], func=AF.Sqrt, scale=1.0 / D, bias=self.epsb[:]),
                     reads=[rss, self.R("epsb")], writes=[rrs])
                p.op("dve", lambda e, rs=rs: e.reciprocal(out=rs[:], in_=rs[:]), reads=[rrs], writes=[rrs])
                p.op("dve", lambda e, xt=xt, ot=ot, rs=rs: e.scalar_tensor_tensor(out=ot[:], in0=xt[:], scalar=rs[:], in1=self.gbc[:], op0=ALU.mult, op1=ALU.mult),
                     reads=[rx, rrs, self.R("gbc")], writes=[ro])
                p.op("sp", lambda e, ot=ot, tt=tt: e.dma_start(out=self.out[tt * 128:(tt + 1) * 128, :], in_=ot[:]),
                     reads=[ro], writes=[self.R("out_w%d" % (tt % 4))], dma=self.st_o)
            self.final_wait()

    def copy_out(self):
        p = self.p
        nc = self.nc
        xsrc = self.x_src
        with contextlib.ExitStack() as st:
            bufs = [st.enter_context(nc.sbuf_tensor("co%d" % b, [128, D], F32)) for b in range(2)]
            for tt in range(S // 128):
                b = tt % 2
                rx = self.R("co%d" % b)
                p.op("sp", lambda e, b=b, tt=tt: e.dma_start(out=bufs[b][:], in_=xsrc[tt * 128:(tt + 1) * 128, :]),
                     reads=[self.R("xres")], writes=[rx], dma=self.st_x)
                p.op("sp", lambda e, b=b, tt=tt: e.dma_start(out=self.out[tt * 128:(tt + 1) * 128, :], in_=bufs[b][:]),
                     reads=[rx], writes=[self.R("out_w%d" % (tt % 4))], dma=self.st_o)
            self.final_wait()

    def final_wait(self):
        p = self.p
        sems = self._store_waits()

        def fn(e, sems=sems):
            for sem, val in sems:
                e.wait_ge(sem, val)
            return e.nop()
        p.op("sp", fn, reads=[self.R("out_w%d" % k) for k in range(4)], writes=[self.R("done")])


def full_plan():
    def prep(l):
        out = [("prep_ffn", l, 0)]
        out.append(("prep_attn", l // 2) if l % 2 == 0 else ("prep_ssm", l // 2))
        out.append(("prep_ffn", l, 1))
        return out
    plan = prep(0)
    for l in range(DEPTH):
        if l + 1 < DEPTH:
            plan += prep(l + 1)
        plan.append(("ffn", l, 0))
        plan.append(("attn", l) if l % 2 == 0 else ("ssd", l))
        plan.append(("ffn", l, 1))
    plan.append(("final",))
    return plan


_CACHE = {}


def attn_w_layout(w):
    heads = [(h * 64) for h in range(8)] + [512, 576] + [768 + h * 64 for h in range(8)] + [1280, 1344] + [1536, 1600] + [1792, 1856]
    cols = []
    for c0 in heads:
        cols += list(range(c0, c0 + 64)) + list(range(c0 + 32, c0 + 64)) + list(range(c0, c0 + 32))
    cols += list(range(1408, 1536))
    cols += list(range(640, 768)) + list(range(1664, 1792)) + list(range(1920, 2048)) + list(range(2048, 2072))
    assert len(cols) == NAW
    return np.ascontiguousarray(w[:, :, np.asarray(cols)])


def _rope_tables():
    inv = (1.0 / (np.float32(10000.0) ** (np.arange(0, 64, 2, dtype=np.float32) / np.float32(64)))).astype(np.float32)
    ang = (np.arange(S, dtype=np.float32)[:, None] * inv[None, :]).astype(np.float32)
    c = np.cos(ang).astype(np.float32).T
    s_ = np.sin(ang).astype(np.float32).T
    return np.ascontiguousarray(np.stack([np.concatenate([c, c], 0), np.concatenate([-s_, s_], 0)], 0))


def _wsel():
    n_cmp = (S - 32) // 16 + 1
    cs = np.arange(n_cmp) * 16
    ss = np.arange(S // 64) * 64
    ov = np.minimum(cs[:, None] + 32, ss[None, :] + 64) - np.maximum(cs[:, None], ss[None, :])
    w = np.zeros((256, 64), np.float32)
    w[:n_cmp] = np.clip(ov, 0, None) / 32.0
    return np.ascontiguousarray(w.reshape(2, 128, 64).transpose(1, 0, 2))


def _selb():
    t = np.arange(S)
    cur = (t // 64)[:, None]
    jj = np.arange(64)[None, :]
    valid = jj <= cur
    forced = valid & ((jj == 0) | (jj == cur) | (jj == cur - 1))
    b = np.where(forced, 1e4, 0.0) - np.where(valid, 0.0, 1e4)
    return np.ascontiguousarray(b.astype(np.float32).reshape(32, 128, 64).transpose(1, 0, 2))


ROPE = _rope_tables()
WSEL = _wsel()
SELB = _selb()
_ii = np.arange(128)
TRI = np.stack([(_ii[:, None] <= _ii[None, :]), (_ii[:, None] > _ii[None, :]), np.ones((128, 128), bool)]).astype(np.float32)


def run_plan(plan, inputs, n_cores=8, trace=False):
    key = repr(plan)
    if key not in _CACHE:
        _CACHE[key] = Builder(plan).build()
    nc = _CACHE[key]
    x = np.ascontiguousarray(inputs["x"], dtype=np.float32)
    gains = np.concatenate([np.asarray(inputs["norm_gains"], np.float32).reshape(DEPTH * 3, D),
                            np.asarray(inputs["final_norm"], np.float32).reshape(1, D)], axis=0)
    common = {
        "gains": np.ascontiguousarray(gains),
        "ffn_w_gate": np.ascontiguousarray(inputs["ffn_w_gate"], dtype=np.float32),
        "ffn_w_up": np.ascontiguousarray(inputs["ffn_w_up"], dtype=np.float32),
        "ffn_w_down": np.ascontiguousarray(inputs["ffn_w_down"], dtype=np.float32),
        "ident": np.eye(128, dtype=np.float32),
        "tri": TRI,
        "ssm_w_in": np.ascontiguousarray(inputs["ssm_w_in"], dtype=np.float32),
        "ssm_w_out": np.ascontiguousarray(inputs["ssm_w_out"], dtype=np.float32),
        "ssm_cw": np.ascontiguousarray(np.asarray(inputs["ssm_conv_w"], np.float32).transpose(0, 2, 1).reshape(2, 32, 128, 4).transpose(0, 2, 1, 3)),
        "ssm_cb": np.ascontiguousarray(np.asarray(inputs["ssm_conv_b"], np.float32).reshape(2, 32, 128).transpose(0, 2, 1)),
        "ssm_vec": np.ascontiguousarray(np.stack([np.asarray(inputs["ssm_dt_bias"], np.float32), np.asarray(inputs["ssm_a_log"], np.float32),
                                                  np.asarray(inputs["ssm_d"], np.float32)], axis=1).reshape(2, 96)),
        "ssm_norm": np.ascontiguousarray(inputs["ssm_norm"], dtype=np.float32),
        "attn_wr": attn_w_layout(np.asarray(inputs["attn_w_in"], np.float32)),
        "attn_w_out": np.ascontiguousarray(inputs["attn_w_out"], dtype=np.float32),
        "attn_sinks": np.ascontiguousarray(inputs["attn_sinks"], dtype=np.float32),
        "rope": ROPE,
        "cmp_w1": np.ascontiguousarray(np.stack([np.asarray(inputs["cmp_k_w1"], np.float32), np.asarray(inputs["cmp_v_w1"], np.float32)], axis=1)),
        "cmp_w2": np.ascontiguousarray(np.stack([np.asarray(inputs["cmp_k_w2"], np.float32), np.asarray(inputs["cmp_v_w2"], np.float32)], axis=1)),
        "cmp_posT": np.ascontiguousarray(np.stack([np.asarray(inputs["cmp_k_pos"], np.float32).transpose(0, 2, 1),
                                                   np.asarray(inputs["cmp_v_pos"], np.float32).transpose(0, 2, 1)], axis=1)),
        "wsel": WSEL,
        "selb": SELB,
    }
    in_maps = []
    for c in range(n_cores):
        m = dict(common)
        m["x"] = x[c % 4]
        in_maps.append(m)
    res = run_bass_kernel_spmd(nc, in_maps, core_ids=list(range(n_cores)), trace=trace)
    out = np.stack([res.results[c % n_cores]["out"] for c in range(4)], axis=0)
    return out, res


def kernel(**inputs):
    out, _ = run_plan(full_plan(), inputs)
    return out.astype(np.float32)
```

```python
import contextlib
import numpy as np
import concourse.bass as bass
import concourse.mybir as mybir
from concourse.bass_utils import run_bass_kernel_spmd

F32 = mybir.dt.float32
BF16 = mybir.dt.bfloat16
AF = mybir.ActivationFunctionType
ALU = mybir.AluOpType
AX = mybir.AxisListType

D = 1024
S = 4096
DEPTH = 4
DFF = 2816
NFC = DFF // 128
EPS = 1e-6
NAW = 3608


class Res:
    __slots__ = ("name", "w", "r")

    def __init__(self, name):
        self.name = name
        self.w = None
        self.r = []


class Op:
    __slots__ = ("eng", "fn", "waits", "inc", "dma", "dsem", "dval", "cnt", "pre")


class Prog:
    ENGS = ("pe", "act", "dve", "pool", "sp")

    def __init__(self, nc, stack):
        self.nc = nc
        self.stack = stack
        self.ops = {e: [] for e in self.ENGS}
        self.esem = {e: stack.enter_context(nc.semaphore("es_" + e)) for e in self.ENGS}
        self.nsem = 5
        self.streams = []

    def new_sem(self, name):
        self.nsem += 1
        return self.stack.enter_context(self.nc.semaphore(name))

    def op(self, eng, fn, reads=(), writes=(), dma=None):
        o = Op()
        o.eng = eng
        o.fn = fn
        o.inc = False
        o.dma = dma
        o.cnt = 0
        o.pre = None
        o.dsem = None
        o.dval = 0
        deps = []
        seen = set()

        def add(d, raw):
            if d is None or id(d) in seen:
                return
            if d.dma is None and d.eng == eng:
                if eng == "pe" or not raw:
                    return
            seen.add(id(d))
            deps.append(d)

        for r in reads:
            add(r.w, True)
        for w in writes:
            add(w.w, False)
            for rr in w.r:
                add(rr, False)
        for d in deps:
            if d.dma is None:
                d.inc = True
        o.waits = deps
        if dma is not None:
            sem, val, pre = dma.next()
            o.dsem, o.dval, o.pre = sem, val, pre
            dma.ops.append(o)
        for r in reads:
            r.r.append(o)
        for w in writes:
            w.w = o
            w.r = []
        self.ops[eng].append(o)
        return o

    def barrier(self):
        deps = []
        for e in self.ENGS:
            for o in reversed(self.ops[e]):
                if o.dma is None:
                    deps.append(o)
                    break
        for st in self.streams:
            deps.extend(st.ops[-st.R:])
        for e in self.ENGS:
            o = Op()
            o.eng = e
            o.fn = lambda eng: eng.nop()
            o.inc = False
            o.dma = None
            o.cnt = 0
            o.pre = None
            o.dsem = None
            o.dval = 0
            o.waits = [d for d in deps if not (d.dma is None and d.eng == e)]
            for d in o.waits:
                if d.dma is None:
                    d.inc = True
            self.ops[e].append(o)

    def emit(self):
        nc = self.nc
        for e in self.ENGS:
            c = 0
            for o in self.ops[e]:
                if o.dma is None and o.inc:
                    c += 1
                o.cnt = c
        self.counts = {e: (len(self.ops[e]), self.ops[e][-1].cnt if self.ops[e] else 0) for e in self.ENGS}

        def body_for(ename):
            def body(eng):
                seen = {}

                def wait(sem, val):
                    k = id(sem)
                    if seen.get(k, 0) >= val:
                        return
                    seen[k] = val
                    eng.wait_ge(sem, val)

                for o in self.ops[ename]:
                    for d in o.waits:
                        if d.dma is not None:
                            wait(d.dsem, d.dval)
                        else:
                            wait(self.esem[d.eng], d.cnt)
                    if o.pre is not None and o.pre[1] > 0:
                        wait(o.pre[0], o.pre[1])
                    ins = o.fn(eng)
                    if o.dma is not None:
                        ins.then_inc(o.dsem, 16)
                    elif o.inc:
                        ins.then_inc(self.esem[ename], 1)
            return body

        with nc.Block() as block:
            block.tensor(body_for("pe"))
            block.scalar(body_for("act"))
            block.vector(body_for("dve"))
            block.gpsimd(body_for("pool"))
            block.sync(body_for("sp"))


class DmaStream:
    def __init__(self, prog, name, R):
        self.sems = [prog.new_sem("%s%d" % (name, i)) for i in range(R)]
        self.R = R
        self.k = 0
        self.ops = []
        prog.streams.append(self)

    def next(self):
        k = self.k
        self.k += 1
        sem = self.sems[k % self.R]
        return sem, 16 * (k // self.R + 1), (sem, 16 * (k // self.R))


class Builder:
    def __init__(self, plan):
        self.plan = plan
        self.nc = bass.Bass("TRN2", target_bir_lowering=False)
        self.stack = contextlib.ExitStack()
        self.res_cache = {}

    def R(self, name):
        r = self.res_cache.get(name)
        if r is None:
            r = self.res_cache[name] = Res(name)
        return r

    def dram_in(self, name, shape, dt=F32):
        return self.nc.dram_tensor(name, list(shape), dt, kind="ExternalInput").ap()

    def dram_out(self, name, shape, dt=F32):
        return self.nc.dram_tensor(name, list(shape), dt, kind="ExternalOutput").ap()

    def dram_tmp(self, name, shape, dt):
        return self.nc.dram_tensor(name, list(shape), dt).ap()

    def sb(self, name, shape, dt):
        return self.stack.enter_context(self.nc.sbuf_tensor(name, list(shape), dt))

    def ps(self, name, shape, dt):
        return self.stack.enter_context(self.nc.psum_tensor(name, list(shape), dt))

    def build(self):
        nc = self.nc
        with self.stack:
            self.p = Prog(nc, self.stack)
            self._build()
            self.p.emit()
        return nc

    def _build(self):
        p = self.p
        plan = self.plan
        self.x_in = self.dram_in("x", [S, D])
        self.gains = self.dram_in("gains", [DEPTH * 3 + 1, D])
        self.wg = self.dram_in("ffn_w_gate", [DEPTH, 2, D, DFF])
        self.wu = self.dram_in("ffn_w_up", [DEPTH, 2, D, DFF])
        self.wd = self.dram_in("ffn_w_down", [DEPTH, 2, DFF, D])
        self.ident_in = self.dram_in("ident", [128, 128])
        self.tri_in = self.dram_in("tri", [3, 128, 128])
        self.ssm_w_in = self.dram_in("ssm_w_in", [2, D, 6176])
        self.ssm_w_out = self.dram_in("ssm_w_out", [2, 2048, D])
        self.ssm_cw = self.dram_in("ssm_cw", [2, 128, 32, 4])
        self.ssm_cb = self.dram_in("ssm_cb", [2, 128, 32])
        self.ssm_vec = self.dram_in("ssm_vec", [2, 96])
        self.ssm_norm = self.dram_in("ssm_norm", [2, 2048])
        self.swin = self.dram_tmp("swin", [2, D, 6176], BF16)
        self.attn_wr = self.dram_in("attn_wr", [2, D, NAW])
        self.attn_w_out = self.dram_in("attn_w_out", [2, D, D])
        self.sinks_in = self.dram_in("attn_sinks", [2, 8])
        self.rope_in = self.dram_in("rope", [2, 64, S])
        self.cmp_w1 = self.dram_in("cmp_w1", [2, 2, 2048, 128])
        self.cmp_w2 = self.dram_in("cmp_w2", [2, 2, 128, 64])
        self.cmp_posT = self.dram_in("cmp_posT", [2, 2, 64, 32])
        self.wsel_in = self.dram_in("wsel", [128, 2, 64])
        self.selb_in = self.dram_in("selb", [128, 32, 64])
        self.awin = self.dram_tmp("awin", [2, D, NAW], BF16)
        self.qT = self.dram_tmp("qT", [16, 64, S], BF16)
        self.out = self.dram_out("out", [S, D])
        self.xres = self.dram_tmp("xres", [S, D], F32)
        self.wgt = self.dram_tmp("wgt", [DEPTH, 2, 6, 128, 8, 512], BF16)
        self.wut = self.dram_tmp("wut", [DEPTH, 2, 6, 128, 8, 512], BF16)
        self.wdt = self.dram_tmp("wdt", [DEPTH, 2, DFF, D], BF16)

        self.st_const = DmaStream(p, "dc", 1)
        self.st_prep = DmaStream(p, "dp", 4)
        self.st_x = DmaStream(p, "dx", 4)
        self.st_w = DmaStream(p, "dw", 4)
        self.st_o = DmaStream(p, "do", 4)

        self.ident_f = self.sb("ident_f", [128, 128], F32)
        self.ident_b = self.sb("ident_b", [128, 128], BF16)
        self.gbc = self.sb("gbc", [128, D], F32)
        self.epsb = self.sb("epsb", [128, 1], F32)
        r_ident = self.R("ident")
        p.op("sp", lambda e: e.dma_start(out=self.ident_f[:], in_=self.ident_in), writes=[r_ident], dma=self.st_const)
        p.op("dve", lambda e: e.tensor_copy(out=self.ident_b[:], in_=self.ident_f[:]), reads=[r_ident], writes=[self.R("ident_b")])
        p.op("dve", lambda e: e.memset(self.epsb[:], EPS), writes=[self.R("epsb")])

        self.pbank = [self.ps("pb%d" % i, [128, 512], F32) for i in range(8)]
        self.ptrh = [self.pbank[6 + i][:, :].bitcast(BF16)[:, 0:512] for i in range(2)]
        self.r_pb = [self.R("pb%d" % i) for i in range(8)]
        self.r_ptr = [self.r_pb[6], self.r_pb[7]]

        self.x_src = self.x_in
        for ph in plan:
            kind = ph[0]
            if kind == "prep_ffn":
                self.prep_ffn(ph[1], ph[2])
            elif kind == "ffn":
                self.dbg_stage = ph[3] if len(ph) > 3 else 99
                self.ffn(ph[1], ph[2])
            elif kind == "prep_ssm":
                self.prep_ssm(ph[1])
            elif kind == "ssd":
                self.ssd(ph[1])
            elif kind == "prep_attn":
                self.prep_attn(ph[1])
            elif kind == "attn":
                self.attn_dbg = ph[2] if len(ph) > 2 else None
                self.attn(ph[1])
            elif kind == "final":
                self.final_norm()
            elif kind == "copy_out":
                self.copy_out()
            else:
                raise ValueError(kind)

    def load_gain(self, row):
        p = self.p
        src = self.gains[row:row + 1, :].broadcast_to([128, D])
        p.op("sp", lambda e: e.dma_start(out=self.gbc[:], in_=src), writes=[self.R("gbc")], dma=self.st_const)

    def prep_ffn(self, l, i):
        p = self.p
        for (src, dst, nm) in ((self.wg, self.wgt, "g"), (self.wu, self.wut, "u")):
            for blk in range(6):
                w = 512 if blk < 5 else 256
                s_ap = src[l, i, :, blk * 512:blk * 512 + w].rearrange("(kc p) m -> p kc m", p=128)
                d_ap = dst[l, i, blk, :, :, 0:w]
                p.op("pool", lambda e, s_ap=s_ap, d_ap=d_ap: e.dma_start(out=d_ap, in_=s_ap),
                     writes=[self.R("wt_%s_%d_%d_%d" % (nm, l, i, blk))], dma=self.st_prep)
        for q in range(4):
            rows = DFF // 4
            s_ap = self.wd[l, i, q * rows:(q + 1) * rows, :]
            d_ap = self.wdt[l, i, q * rows:(q + 1) * rows, :]
            p.op("pool", lambda e, s_ap=s_ap, d_ap=d_ap: e.dma_start(out=d_ap, in_=s_ap),
                 writes=[self.R("wt_d_%d_%d_%d" % (l, i, q))], dma=self.st_prep)

    def norm_transpose(self, xsrc, t0, ntile, hT, r_hT, xt_bufs, tag):
        p = self.p
        for j in range(ntile):
            tt = t0 + j
            b = j % 2
            xt, hb, sq, ss, rs = xt_bufs[b]
            rx = self.R("%s_xt%d" % (tag, b))
            rh = self.R("%s_hb%d" % (tag, b))
            rss = self.R("%s_ss%d" % (tag, b))
            rsq = self.R("%s_sq%d" % (tag, b))
            p.op("sp", lambda e, xt=xt, tt=tt: e.dma_start(out=xt[:], in_=xsrc[tt * 128:(tt + 1) * 128, :]),
                 reads=[self.R("xres")], writes=[rx], dma=self.st_x)
            p.op("act", lambda e, xt=xt, sq=sq, ss=ss: e.activation(out=sq[:], in_=xt[:], func=AF.Square, accum_out=ss[:]),
                 reads=[rx], writes=[rsq, rss])
            p.op("act", lambda e, ss=ss, rs=rs: e.activation(out=rs[:], in_=ss[:], func=AF.Sqrt, scale=1.0 / D, bias=self.epsb[:]),
                 reads=[rss, self.R("epsb")], writes=[self.R("%s_rs%d" % (tag, b))])
            p.op("dve", lambda e, rs=rs: e.reciprocal(out=rs[:], in_=rs[:]),
                 reads=[self.R("%s_rs%d" % (tag, b))], writes=[self.R("%s_rs%d" % (tag, b))])
            p.op("dve", lambda e, xt=xt, hb=hb, rs=rs: e.scalar_tensor_tensor(out=hb[:], in0=xt[:], scalar=rs[:], in1=self.gbc[:], op0=ALU.mult, op1=ALU.mult),
                 reads=[rx, self.R("%s_rs%d" % (tag, b)), self.R("gbc")], writes=[rh])
            for half in range(2):
                rp = self.r_ptr[half]

                def tr(e, hb=hb, half=half):
                    ins = None
                    for q in range(4):
                        kc = half * 4 + q
                        ins = e.transpose(out=self.ptrh[half][:, q * 128:(q + 1) * 128],
                                          in_=hb[:, kc * 128:(kc + 1) * 128], identity=self.ident_b[:])
                    return ins
                p.op("pe", tr, reads=[rh, self.R("ident_b")], writes=[rp])
                dst = hT[:, half * 4:(half + 1) * 4, j * 128:(j + 1) * 128]
                srcp = self.ptrh[half][:, :].rearrange("p (q m) -> p q m", q=4)
                eng = "act" if half == 0 else "dve"
                if eng == "act":
                    p.op("act", lambda e, dst=dst, srcp=srcp: e.copy(out=dst, in_=srcp), reads=[rp], writes=[r_hT[j][half]])
                else:
                    p.op("dve", lambda e, dst=dst, srcp=srcp: e.tensor_copy(out=dst, in_=srcp), reads=[rp], writes=[r_hT[j][half]])

    def ffn(self, l, i):
        p = self.p
        nc = self.nc
        TB = 1024
        NTB = S // TB
        xsrc = self.x_src
        with contextlib.ExitStack() as st:
            def sb(name, shape, dt):
                return st.enter_context(nc.sbuf_tensor(name, list(shape), dt))
            tag = "f%d%d" % (l, i)
            wd_sb = sb(tag + "wd", [128, NFC, D], BF16)
            aT = sb(tag + "aT", [128, NFC, TB], BF16)
            hT = sb(tag + "hT", [128, 8, TB], BF16)
            wgu = [(sb(tag + "wg%d" % b, [128, 8, 512], BF16), sb(tag + "wu%d" % b, [128, 8, 512], BF16)) for b in range(2)]
            xt_bufs = [(sb(tag + "xt%d" % b, [128, D], F32), sb(tag + "hb%d" % b, [128, D], BF16),
                        sb(tag + "sq%d" % b, [128, D], BF16), sb(tag + "ss%d" % b, [128, 1], F32),
                        sb(tag + "rs%d" % b, [128, 1], F32)) for b in range(2)]
            sg = [sb(tag + "sg%d" % b, [128, 512], F32) for b in range(2)]
            xo = [sb(tag + "xo%d" % b, [128, D], F32) for b in range(2)]
            r_wd = [self.R(tag + "wd0"), self.R(tag + "wd1")]
            r_aT = self.R(tag + "aT")
            r_hT = [[self.R(tag + "hT%d_%d" % (jj, hh)) for hh in range(2)] for jj in range(TB // 128)]
            r_wg = [self.R(tag + "wgs%d" % b) for b in range(2)]
            r_wu = [self.R(tag + "wus%d" % b) for b in range(2)]
            r_sg = [self.R(tag + "sg%d" % b) for b in range(2)]
            r_xo = [self.R(tag + "xo%d" % b) for b in range(2)]

            self.load_gain(l * 3 + (0 if i == 0 else 2))
            for q in range(2):
                fa, fb = q * 11, (q + 1) * 11
                src = self.wdt[l, i, fa * 128:fb * 128, :].rearrange("(fc p) m -> p fc m", p=128)
                p.op("sp", lambda e, src=src, fa=fa, fb=fb: e.dma_start(out=wd_sb[:, fa:fb, :], in_=src),
                     reads=[self.R("wt_d_%d_%d_%d" % (l, i, 2 * q)), self.R("wt_d_%d_%d_%d" % (l, i, 2 * q + 1))], writes=[r_wd[q]], dma=self.st_w)

            wcount = 0
            for tb in range(NTB):
                t0 = tb * (TB // 128)
                self.norm_transpose(xsrc, t0, TB // 128, hT, r_hT, xt_bufs, tag)
                if self.dbg_stage <= 1:
                    continue
                for blk in range(6):
                    w = 512 if blk < 5 else 256
                    b = wcount % 2
                    wcount += 1
                    wgs, wus = wgu[b]
                    p.op("sp", lambda e, wgs=wgs, blk=blk, w=w: e.dma_start(out=wgs[:, :, 0:w], in_=self.wgt[l, i, blk, :, :, 0:w]),
                         reads=[self.R("wt_g_%d_%d_%d" % (l, i, blk))], writes=[r_wg[b]], dma=self.st_w)
                    p.op("sp", lambda e, wus=wus, blk=blk, w=w: e.dma_start(out=wus[:, :, 0:w], in_=self.wut[l, i, blk, :, :, 0:w]),
                         reads=[self.R("wt_u_%d_%d_%d" % (l, i, blk))], writes=[r_wu[b]], dma=self.st_w)
                    for m in range(w // 128):
                        fc = blk * 4 + m
                        for half in range(TB // 512):
                            pg = (fc * 2 + half) % 2
                            bg, bu = self.pbank[pg * 2], self.pbank[pg * 2 + 1]
                            rg, ru = self.r_pb[pg * 2], self.r_pb[pg * 2 + 1]

                            def mm(e, wt, bank, m=m, half=half):
                                ins = None
                                for kc in range(8):
                                    ins = e.matmul(bank[:, :], lhsT=wt[:, kc, m * 128:(m + 1) * 128],
                                                   rhs=hT[:, kc, half * 512:(half + 1) * 512],
                                                   start=(kc == 0), stop=(kc == 7))
                                return ins
                            hres = [r_hT[half * 4 + jj][hh] for jj in range(4) for hh in range(2)]
                            p.op("pe", lambda e, wgs=wgs, bg=bg, mm=mm: mm(e, wgs, bg), reads=[r_wg[b]] + hres, writes=[rg])
                            p.op("pe", lambda e, wus=wus, bu=bu, mm=mm: mm(e, wus, bu), reads=[r_wu[b]] + hres, writes=[ru])
                            sgb = sg[pg]
                            p.op("act", lambda e, sgb=sgb, bg=bg: e.activation(out=sgb[:], in_=bg[:, :], func=AF.Silu),
                                 reads=[rg], writes=[r_sg[pg]])
                            dst = aT[:, fc, half * 512:(half + 1) * 512]
                            p.op("dve", lambda e, dst=dst, sgb=sgb, bu=bu: e.tensor_tensor(out=dst, in0=sgb[:], in1=bu[:, :], op=ALU.mult),
                                 reads=[r_sg[pg], ru], writes=[r_aT])
                if self.dbg_stage <= 2:
                    continue
                for j in range(TB // 128):
                    tt = t0 + j
                    xb = j % 2
                    xt = xt_bufs[xb][0]
                    rx = self.R("%s_xt%d" % (tag, xb))
                    p.op("sp", lambda e, xt=xt, tt=tt: e.dma_start(out=xt[:], in_=xsrc[tt * 128:(tt + 1) * 128, :]),
                         reads=[self.R("xres")], writes=[rx], dma=self.st_x)
                    for ch in range(2):
                        pb = 4 + (j * 2 + ch) % 2
                        bank, rb = self.pbank[pb], self.r_pb[pb]

                        def mmd(e, bank=bank, j=j, ch=ch):
                            ins = None
                            for fc in range(NFC):
                                ins = e.matmul(bank[:, :], lhsT=aT[:, fc, j * 128:(j + 1) * 128],
                                               rhs=wd_sb[:, fc, ch * 512:(ch + 1) * 512],
                                               start=(fc == 0), stop=(fc == NFC - 1))
                            return ins
                        p.op("pe", mmd, reads=[r_aT] + r_wd, writes=[rb])
                        xob = xo[xb]
                        p.op("dve", lambda e, xob=xob, bank=bank, xt=xt, ch=ch: e.scalar_tensor_tensor(
                            out=xob[:, ch * 512:(ch + 1) * 512], in0=bank[:, :], scalar=0.5, in1=xt[:, ch * 512:(ch + 1) * 512],
                            op0=ALU.mult, op1=ALU.add), reads=[rb, rx], writes=[r_xo[xb]])
                    p.op("sp", lambda e, xob=xob, tt=tt: e.dma_start(out=self.xres[tt * 128:(tt + 1) * 128, :], in_=xob[:]),
                         reads=[r_xo[xb]], writes=[self.R("xres_w%d" % (tt % 4))], dma=self.st_o)
            self.phase_barrier()
        self.x_src = self.xres

    def _store_waits(self):
        st = self.st_o
        return [(st.sems[idx % st.R], 16 * (idx // st.R + 1)) for idx in range(max(0, st.k - st.R), st.k)]

    def phase_barrier(self):
        p = self.p
        sems = self._store_waits()

        def fn(e, sems=sems):
            for sem, val in sems:
                e.wait_ge(sem, val)
            return e.nop()
        p.op("sp", fn, reads=[self.R("xres_w%d" % k) for k in range(4)], writes=[self.R("xres")])
        p.barrier()

    def prep_ssm(self, j):
        p = self.p
        for (c0, c1) in ((0, 2048), (2048, 4096), (4096, 6144), (6144, 6176)):
            for hh in range(2):
                s_ap = self.ssm_w_in[j, hh * 512:(hh + 1) * 512, c0:c1]
                d_ap = self.swin[j, hh * 512:(hh + 1) * 512, c0:c1]
                p.op("pool", lambda e, s_ap=s_ap, d_ap=d_ap: e.dma_start(out=d_ap, in_=s_ap),
                     writes=[self.R("swin%d_%d_%d" % (j, c0, hh))], dma=self.st_prep)

    def swin_res(self, j, c0):
        base = (c0 // 2048) * 2048 if c0 < 6144 else 6144
        return [self.R("swin%d_%d_%d" % (j, base, hh)) for hh in range(2)]

    def ssd(self, l):
        p = self.p
        nc = self.nc
        j = l // 2
        TBK = 256
        NBLK = S // TBK
        NT = TBK // 128
        xsrc = self.x_src
        with contextlib.ExitStack() as st:
            def sb(name, shape, dt):
                return st.enter_context(nc.sbuf_tensor(tag + name, list(shape), dt))
            tag = "s%d" % l
            R = lambda n: self.R(tag + n)
            L1 = sb("L1", [128, 128], F32)
            L2 = sb("L2", [128, 128], F32)
            ONES = sb("ONES", [128, 128], F32)
            cw = sb("cw", [128, 32, 4], F32)
            cb = sb("cb", [128, 32], F32)
            vec = sb("vec", [128, 3, 32], F32)
            a_bc = sb("a_bc", [128, 32], F32)
            Dbc = sb("Dbc", [128, 32, 1], F32)
            nw = sb("nw", [128, 2048], F32)
            wout = sb("wout", [128, 16, 1024], BF16)
            wdt = sb("wdt", [128, 8, 32], BF16)
            halo = sb("halo", [128, 32, 3], F32)
            state = sb("state", [128, 2048], F32)
            state_bf = sb("state_bf", [128, 2048], BF16)
            hT = sb("hT", [128, 8, TBK], BF16)
            wx = [sb("wx%d" % b, [128, 8, 512], BF16) for b in range(2)]
            xin = [sb("xin%d" % b, [128, TBK + 3], F32) for b in range(3)]
            acc = [sb("acc%d" % b, [128, TBK], F32) for b in range(3)]
            xsb = [sb("xsb%d" % b, [128, TBK], BF16) for b in range(2)]
            BT = sb("BT", [128, 8, TBK], BF16)
            CT = sb("CT", [128, 8, TBK], BF16)
            Btok = sb("Btok", [128, NT, 1024], BF16)
            xs_tok = sb("xs_tok", [128, NT, 2048], BF16)
            sz = sb("sz", [128, NT, 2048], BF16)
            xt_bufs = [(sb("xt%d" % b, [128, D], F32), sb("hb%d" % b, [128, D], BF16),
                        sb("sq%d" % b, [128, D], BF16), sb("ss%d" % b, [128, 1], F32),
                        sb("rs%d" % b, [128, 1], F32)) for b in range(2)]
            dtv = sb("dtv", [128, 32], F32)
            dt3 = sb("dt3", [128, 32, 1], F32)
            da = sb("da", [128, 32], F32)
            eall = sb("eall", [128, 96], F32)
            ea3 = sb("ea3", [128, 32, 1], F32)
            w23 = sb("w23", [128, 32, 1], F32)
            xdt = sb("xdt", [128, 2048], BF16)
            xdec = sb("xdec", [128, 2048], BF16)
            ybuf = sb("ybuf", [128, 2048], F32)
            t3 = sb("t3", [128, 2048], F32)
            gnb = sb("gnb", [128, 2048], BF16)
            gnT = sb("gnT", [128, 16, 128], BF16)
            Eg = [sb("Eg%d" % b, [128, 4, 128], F32) for b in range(2)]
            MT = [sb("MT%d" % b, [128, 4, 128], BF16) for b in range(2)]
            GTm = [sb("GTm%d" % b, [128, 1, 128], F32) for b in range(2)]
            Ada4 = [sb("Ada4_%d" % b, [128, 4, 128], F32) for b in range(2)]
            ssg = sb("ssg", [128, 8], F32)
            rsg = sb("rsg", [128, 8, 1], F32)
            xo = sb("xo", [128, D], F32)
            xres_t = sb("xres_t", [128, D], F32)

            p.op("sp", lambda e: e.dma_start(out=L1[:], in_=self.tri_in[0]), writes=[R("L1")], dma=self.st_const)
            p.op("sp", lambda e: e.dma_start(out=L2[:], in_=self.tri_in[1]), writes=[R("L2")], dma=self.st_const)
            p.op("sp", lambda e: e.dma_start(out=ONES[:], in_=self.tri_in[2]), writes=[R("ONES")], dma=self.st_const)
            p.op("sp", lambda e: e.dma_start(out=cw[:], in_=self.ssm_cw[j]), writes=[R("cw")], dma=self.st_const)
            p.op("sp", lambda e: e.dma_start(out=cb[:], in_=self.ssm_cb[j]), writes=[R("cb")], dma=self.st_const)
            p.op("sp", lambda e: e.dma_start(out=vec[:].rearrange("p a b -> p (a b)"),
                                             in_=self.ssm_vec[j:j + 1, :].broadcast_to([128, 96])), writes=[R("vec")], dma=self.st_const)
            p.op("sp", lambda e: e.dma_start(out=nw[:], in_=self.ssm_norm[j:j + 1, :].broadcast_to([128, 2048])), writes=[R("nw")], dma=self.st_const)
            for q in range(4):
                src = self.ssm_w_out[j, q * 512:(q + 1) * 512, :].rearrange("(k p) m -> p k m", p=128)
                p.op("pool", lambda e, src=src, q=q: e.dma_start(out=wout[:, q * 4:(q + 1) * 4, :], in_=src), writes=[R("wout%d" % q)], dma=self.st_prep)
            r_wout = [R("wout%d" % q) for q in range(4)]
            p.op("sp", lambda e: e.dma_start(out=wdt[:], in_=self.swin[j, :, 6144:6176].rearrange("(k p) m -> p k m", p=128)),
                 reads=self.swin_res(j, 6144), writes=[R("wdt")], dma=self.st_const)
            p.op("act", lambda e: e.activation(out=a_bc[:], in_=vec[:, 1, :], func=AF.Exp), reads=[R("vec")], writes=[R("a_bc")])
            p.op("dve", lambda e: e.tensor_scalar(out=a_bc[:], in0=a_bc[:], scalar1=-1.0, scalar2=None, op0=ALU.mult), reads=[R("a_bc")], writes=[R("a_bc")])
            p.op("dve", lambda e: e.tensor_copy(out=Dbc[:, :, 0], in_=vec[:, 2, :]), reads=[R("vec")], writes=[R("Dbc")])
            p.op("pool", lambda e: e.memset(halo[:], 0.0), writes=[R("halo%d" % cc_) for cc_ in range(32)])
            p.op("pool", lambda e: e.memset(state[:], 0.0), writes=[R("state")])
            p.op("pool", lambda e: e.memset(state_bf[:], 0.0), writes=[R("state_bf")])
            self.load_gain(l * 3 + 1)

            r_hT = [[R("hT%d_%d" % (jj, hh)) for hh in range(2)] for jj in range(NT)]
            hres_all = [r_hT[jj][hh] for jj in range(NT) for hh in range(2)]
            wxc = 0
            cvc = 0
            pendB = []
            tails = []
            for blk in range(NBLK):
                t0 = blk * NT
                self.norm_transpose(xsrc, t0, NT, hT, r_hT, xt_bufs, tag)
                for wgI in range(8):
                    b = wxc % 2
                    wxc += 1
                    c0 = 2048 + wgI * 512
                    src = self.swin[j, :, c0:c0 + 512].rearrange("(k p) m -> p k m", p=128)
                    p.op("sp", lambda e, src=src, b=b: e.dma_start(out=wx[b][:], in_=src),
                         reads=self.swin_res(j, c0), writes=[R("wx%d" % b)], dma=self.st_w)
                    for m in range(4):
                        cc = wgI * 4 + m
                        pb = cc % 2
                        bank, rb = self.pbank[pb], self.r_pb[pb]

                        def mm(e, b=b, m=m, bank=bank):
                            for kc in range(8):
                                ins = e.matmul(bank[:, 0:TBK], lhsT=wx[b][:, kc, m * 128:(m + 1) * 128], rhs=hT[:, kc, :],
                                               start=(kc == 0), stop=(kc == 7))
                            return ins
                        p.op("pe", mm, reads=[R("wx%d" % b)] + hres_all, writes=[rb])
                        cbuf = cvc % 3
                        cvc += 1
                        xi, ac = xin[cbuf], acc[cbuf]
                        rxi, rac = R("xin%d" % cbuf), R("acc%d" % cbuf)
                        rhalo = R("halo%d" % cc)
                        p.op("pool", lambda e, xi=xi, cc=cc: e.tensor_copy(out=xi[:, 0:3], in_=halo[:, cc, :]), reads=[rhalo], writes=[rxi])
                        p.op("act", lambda e, xi=xi, bank=bank: e.copy(out=xi[:, 3:3 + TBK], in_=bank[:, 0:TBK]), reads=[rb], writes=[rxi])
                        p.op("pool", lambda e, xi=xi, cc=cc: e.tensor_copy(out=halo[:, cc, :], in_=xi[:, TBK:TBK + 3]), reads=[rxi], writes=[rhalo])
                        p.op("act", lambda e, xi=xi, ac=ac, cc=cc: e.activation(out=ac[:], in_=xi[:, 0:TBK], func=AF.Identity, scale=cw[:, cc, 0:1], bias=cb[:, cc:cc + 1]),
                             reads=[rxi, R("cw"), R("cb")], writes=[rac])
                        for w in range(1, 4):
                            p.op("dve", lambda e, xi=xi, ac=ac, cc=cc, w=w: e.scalar_tensor_tensor(out=ac[:], in0=xi[:, w:w + TBK], scalar=cw[:, cc, w:w + 1], in1=ac[:],
                                                                                                 op0=ALU.mult, op1=ALU.add), reads=[rxi, rac, R("cw")], writes=[rac])

                        def stageB(cc=cc, ac=ac, rac=rac, cbuf=cbuf):
                            if cc < 16:
                                xb_, rxb = xsb[cbuf % 2], R("xsb%d" % (cbuf % 2))
                                p.op("act", lambda e, ac=ac, xb_=xb_: e.activation(out=xb_[:], in_=ac[:], func=AF.Silu), reads=[rac], writes=[rxb])
                                hp = cc % 2
                                rp = self.r_ptr[hp]

                                def tr(e, xb_=xb_, hp=hp):
                                    for q in range(NT):
                                        ins = e.transpose(out=self.ptrh[hp][:, q * 128:(q + 1) * 128], in_=xb_[:, q * 128:(q + 1) * 128], identity=self.ident_b[:])
                                    return ins
                                p.op("pe", tr, reads=[rxb, self.R("ident_b")], writes=[rp])
                                dst = xs_tok[:, :, cc * 128:(cc + 1) * 128]
                                srcp = self.ptrh[hp][:, 0:NT * 128].rearrange("p (q m) -> p q m", q=NT)
                                p.op("dve", lambda e, dst=dst, srcp=srcp: e.tensor_copy(out=dst, in_=srcp), reads=[rp], writes=[R("xs_tok")])
                            elif cc < 24:
                                g = cc - 16
                                p.op("act", lambda e, ac=ac, g=g: e.activation(out=BT[:, g, :], in_=ac[:], func=AF.Silu), reads=[rac], writes=[R("BT%d" % g)])
                                hp = cc % 2
                                rp = self.r_ptr[hp]

                                def tr(e, g=g, hp=hp):
                                    for q in range(NT):
                                        ins = e.transpose(out=self.ptrh[hp][:, q * 128:(q + 1) * 128], in_=BT[:, g, q * 128:(q + 1) * 128], identity=self.ident_b[:])
                                    return ins
                                p.op("pe", tr, reads=[R("BT%d" % g), self.R("ident_b")], writes=[rp])
                                dst = Btok[:, :, g * 128:(g + 1) * 128]
                                srcp = self.ptrh[hp][:, 0:NT * 128].rearrange("p (q m) -> p q m", q=NT)
                                p.op("dve", lambda e, dst=dst, srcp=srcp: e.tensor_copy(out=dst, in_=srcp), reads=[rp], writes=[R("Btok")])
                            else:
                                g = cc - 24
                                p.op("act", lambda e, ac=ac, g=g: e.activation(out=CT[:, g, :], in_=ac[:], func=AF.Silu), reads=[rac], writes=[R("CT%d" % g)])
                        pendB.append(stageB)
                        while len(pendB) > 1:
                            pendB.pop(0)()
                        if cc == 3:
                            while tails:
                                tails.pop(0)()
                while pendB:
                    pendB.pop(0)()
                for zc in range(4):
                    b = wxc % 2
                    wxc += 1
                    c0 = zc * 512
                    src = self.swin[j, :, c0:c0 + 512].rearrange("(k p) m -> p k m", p=128)
                    p.op("sp", lambda e, src=src, b=b: e.dma_start(out=wx[b][:], in_=src),
                         reads=self.swin_res(j, c0), writes=[R("wx%d" % b)], dma=self.st_w)
                    for q in range(NT):
                        pb = (zc * NT + q) % 2
                        bank, rb = self.pbank[pb], self.r_pb[pb]

                        def mmz(e, b=b, q=q, bank=bank):
                            for kc in range(8):
                                ins = e.matmul(bank[:, :], lhsT=hT[:, kc, q * 128:(q + 1) * 128], rhs=wx[b][:, kc, :], start=(kc == 0), stop=(kc == 7))
                            return ins
                        p.op("pe", mmz, reads=[R("wx%d" % b)] + r_hT[q], writes=[rb])
                        p.op("act", lambda e, q=q, zc=zc, bank=bank: e.activation(out=sz[:, q, zc * 512:(zc + 1) * 512], in_=bank[:, :], func=AF.Silu),
                             reads=[rb], writes=[R("sz%d" % q)])
                for q in range(NT):
                    tt = t0 + q
                    tsl = slice(q * 128, (q + 1) * 128)
                    b0, rb0 = self.pbank[0], self.r_pb[0]

                    def mmdt(e, q=q):
                        for kc in range(8):
                            ins = e.matmul(b0[:, 0:32], lhsT=hT[:, kc, q * 128:(q + 1) * 128], rhs=wdt[:, kc, :], start=(kc == 0), stop=(kc == 7))
                        return ins
                    p.op("pe", mmdt, reads=[R("wdt")] + r_hT[q], writes=[rb0])
                    p.op("dve", lambda e: e.tensor_tensor(out=dtv[:], in0=b0[:, 0:32], in1=vec[:, 0, :], op=ALU.add), reads=[rb0, R("vec")], writes=[R("dtv")])
                    p.op("act", lambda e: e.activation(out=dtv[:], in_=dtv[:], func=AF.Exp), reads=[R("dtv")], writes=[R("dtv")])
                    p.op("act", lambda e: e.activation(out=dt3[:, :, 0], in_=dtv[:], func=AF.Ln, bias=1.0), reads=[R("dtv")], writes=[R("dt3")])
                    p.op("dve", lambda e: e.tensor_tensor(out=da[:], in0=dt3[:, :, 0], in1=a_bc[:], op=ALU.mult), reads=[R("dt3"), R("a_bc")], writes=[R("da")])

                    def mmcs(e):
                        e.matmul(b0[:, 32:64], lhsT=L1[:], rhs=da[:], start=True, stop=True)
                        e.matmul(b0[:, 64:96], lhsT=L2[:], rhs=da[:], start=True, stop=True)
                        return e.matmul(b0[:, 96:128], lhsT=ONES[:], rhs=da[:], start=True, stop=True)
                    p.op("pe", mmcs, reads=[R("da"), R("L1"), R("L2"), R("ONES")], writes=[rb0])
                    p.op("act", lambda e: e.activation(out=eall[:], in_=b0[:, 32:128], func=AF.Exp), reads=[rb0], writes=[R("eall")])
                    p.op("dve", lambda e: e.tensor_copy(out=ea3[:, :, 0], in_=eall[:, 0:32]), reads=[R("eall")], writes=[R("ea3")])
                    p.op("dve", lambda e: e.tensor_tensor(out=w23[:, :, 0], in0=dt3[:, :, 0], in1=eall[:, 32:64], op=ALU.mult), reads=[R("dt3"), R("eall")], writes=[R("w23")])
                    xs3 = xs_tok[:, q, :].rearrange("p (h d) -> p h d", h=32)
                    p.op("dve", lambda e, xs3=xs3: e.tensor_tensor(out=xdt[:].rearrange("p (h d) -> p h d", h=32), in0=xs3, in1=dt3[:].to_broadcast([128, 32, 64]), op=ALU.mult),
                         reads=[R("xs_tok"), R("dt3")], writes=[R("xdt")])
                    p.op("pool", lambda e, xs3=xs3: e.tensor_tensor(out=xdec[:].rearrange("p (h d) -> p h d", h=32), in0=xs3, in1=w23[:].to_broadcast([128, 32, 64]), op=ALU.mult),
                         reads=[R("xs_tok"), R("w23")], writes=[R("xdec")])
                    p.op("pool", lambda e, xs3=xs3: e.tensor_tensor(out=t3[:].rearrange("p (h d) -> p h d", h=32), in0=xs3, in1=Dbc[:].to_broadcast([128, 32, 64]), op=ALU.mult),
                         reads=[R("xs_tok"), R("Dbc")], writes=[R("t3")])
                    def stage1(g, tsl=tsl):
                        gb = g % 2
                        b1, rb1 = self.pbank[1], self.r_pb[1]
                        gcol = slice((g % 4) * 128, (g % 4 + 1) * 128)
                        p.op("pe", lambda e, g=g, gcol=gcol, tsl=tsl: e.matmul(b1[:, gcol], lhsT=BT[:, g, tsl], rhs=CT[:, g, tsl], start=True, stop=True),
                             reads=[R("BT%d" % g), R("CT%d" % g)], writes=[rb1])
                        p.op("dve", lambda e, gb=gb, gcol=gcol: e.tensor_tensor(out=GTm[gb][:, 0, :], in0=b1[:, gcol], in1=L1[:], op=ALU.mult),
                             reads=[rb1, R("L1")], writes=[R("GTm%d" % gb)])
                        bs, rbs = self.pbank[2 + gb], self.r_pb[2 + gb]
                        p.op("dve", lambda e, gb=gb, g=g: e.tensor_tensor(out=Ada4[gb][:], in0=L2[:].rearrange("p (o l) -> p o l", o=1).to_broadcast([128, 4, 128]),
                                                                         in1=da[:, 4 * g:4 * g + 4].rearrange("p (r o) -> p r o", o=1).to_broadcast([128, 4, 128]), op=ALU.mult),
                             reads=[R("L2"), R("da")], writes=[R("Ada4_%d" % gb)])

                        def mmseg(e, gb=gb, bs=bs):
                            for r in range(4):
                                ins = e.matmul(bs[:, r * 128:(r + 1) * 128], lhsT=Ada4[gb][:, r, :], rhs=L1[:], start=True, stop=True)
                            return ins
                        p.op("pe", mmseg, reads=[R("Ada4_%d" % gb), R("L1")], writes=[rbs])
                        p.op("act", lambda e, gb=gb, bs=bs: e.activation(out=Eg[gb][:].rearrange("p r l -> p (r l)"), in_=bs[:, :], func=AF.Exp),
                             reads=[rbs], writes=[R("Eg%d" % gb)])

                    def stage1b(g):
                        gb = g % 2
                        p.op("dve", lambda e, gb=gb: e.tensor_tensor(out=MT[gb][:], in0=Eg[gb][:], in1=GTm[gb][:].to_broadcast([128, 4, 128]), op=ALU.mult),
                             reads=[R("Eg%d" % gb), R("GTm%d" % gb)], writes=[R("MT%d" % gb)])

                    def stage2(g, tsl=tsl, q=q):
                        gb = g % 2
                        by, rby = self.pbank[4 + gb], self.r_pb[4 + gb]

                        def mmy(e, g=g, gb=gb, by=by, tsl=tsl):
                            for r in range(4):
                                h = 4 * g + r
                                e.matmul(by[:, r * 64:(r + 1) * 64], lhsT=MT[gb][:, r, :], rhs=xdt[:, h * 64:(h + 1) * 64], start=True, stop=True)
                            return e.matmul(by[:, 256:512], lhsT=CT[:, g, tsl], rhs=state_bf[:, g * 256:(g + 1) * 256], start=True, stop=True)
                        p.op("pe", mmy, reads=[R("MT%d" % gb), R("xdt"), R("CT%d" % g), R("state_bf")], writes=[rby])
                        p.op("pe", lambda e, g=g, q=q: e.matmul(b0[:, 256:512], lhsT=Btok[:, q, g * 128:(g + 1) * 128], rhs=xdec[:, g * 256:(g + 1) * 256], start=True, stop=True),
                             reads=[R("Btok"), R("xdec")], writes=[rb0])
                        for r in range(4):
                            h = 4 * g + r
                            p.op("dve", lambda e, h=h, r=r: e.scalar_tensor_tensor(out=state[:, h * 64:(h + 1) * 64], in0=state[:, h * 64:(h + 1) * 64],
                                                                                     scalar=eall[:, 64 + h:65 + h], in1=b0[:, 256 + r * 64:256 + (r + 1) * 64],
                                                                                     op0=ALU.mult, op1=ALU.add), reads=[R("state"), R("eall"), rb0], writes=[R("state")])
                        yg = ybuf[:, g * 256:(g + 1) * 256].rearrange("p (r d) -> p r d", r=4)
                        p.op("dve", lambda e, yg=yg, by=by, g=g: e.tensor_tensor(out=yg, in0=by[:, 256:512].rearrange("p (r d) -> p r d", r=4),
                                                                               in1=ea3[:, 4 * g:4 * g + 4, :].to_broadcast([128, 4, 64]), op=ALU.mult),
                             reads=[rby, R("ea3")], writes=[R("ybuf")])
                        p.op("dve", lambda e, g=g, by=by: e.tensor_tensor(out=ybuf[:, g * 256:(g + 1) * 256], in0=ybuf[:, g * 256:(g + 1) * 256], in1=by[:, 0:256], op=ALU.add),
                             reads=[R("ybuf"), rby], writes=[R("ybuf")])

                    for g in range(10):
                        if g < 8:
                            stage1(g)
                        if 1 <= g <= 8:
                            stage1b(g - 1)
                        if g >= 2:
                            stage2(g - 2)
                        if g == 1:
                            while tails:
                                tails.pop(0)()
                    p.op("act", lambda e: e.copy(out=state_bf[:], in_=state[:]), reads=[R("state")], writes=[R("state_bf")])
                    p.op("pool", lambda e: e.tensor_tensor(out=ybuf[:], in0=ybuf[:], in1=t3[:], op=ALU.add), reads=[R("ybuf"), R("t3")], writes=[R("ybuf")])
                    p.op("pool", lambda e, q=q: e.tensor_tensor(out=ybuf[:], in0=ybuf[:], in1=sz[:, q, :], op=ALU.mult), reads=[R("ybuf"), R("sz%d" % q)], writes=[R("ybuf")])
                    for g in range(8):
                        p.op("act", lambda e, g=g: e.activation(out=gnb[:, g * 256:(g + 1) * 256], in_=ybuf[:, g * 256:(g + 1) * 256], func=AF.Square, accum_out=ssg[:, g:g + 1]),
                             reads=[R("ybuf")], writes=[R("gnb"), R("ssg")])
                    p.op("act", lambda e: e.activation(out=rsg[:, :, 0], in_=ssg[:], func=AF.Sqrt, scale=1.0 / 256, bias=self.epsb[:]), reads=[R("ssg"), self.R("epsb")], writes=[R("rsg")])
                    p.op("dve", lambda e: e.reciprocal(out=rsg[:, :, 0], in_=rsg[:, :, 0]), reads=[R("rsg")], writes=[R("rsg")])
                    p.op("dve", lambda e: e.tensor_tensor(out=ybuf[:].rearrange("p (g d) -> p g d", g=8), in0=ybuf[:].rearrange("p (g d) -> p g d", g=8),
                                                          in1=rsg[:].to_broadcast([128, 8, 256]), op=ALU.mult), reads=[R("ybuf"), R("rsg")], writes=[R("ybuf")])
                    p.op("pool", lambda e: e.tensor_tensor(out=gnb[:], in0=ybuf[:], in1=nw[:], op=ALU.mult), reads=[R("ybuf"), R("nw"), R("gnb")], writes=[R("gnb")])
                    def tail(tt=tt):
                        for qq in range(4):
                            hp = qq % 2
                            rp = self.r_ptr[hp]

                            def trg(e, qq=qq, hp=hp):
                                for m in range(4):
                                    k = qq * 4 + m
                                    ins = e.transpose(out=self.ptrh[hp][:, m * 128:(m + 1) * 128], in_=gnb[:, k * 128:(k + 1) * 128], identity=self.ident_b[:])
                                return ins
                            p.op("pe", trg, reads=[R("gnb"), self.R("ident_b")], writes=[rp])
                            dst = gnT[:, qq * 4:(qq + 1) * 4, :]
                            srcp = self.ptrh[hp][:, :].rearrange("p (q m) -> p q m", q=4)
                            if hp == 0:
                                p.op("act", lambda e, dst=dst, srcp=srcp: e.copy(out=dst, in_=srcp), reads=[rp], writes=[R("gnT%d" % qq)])
                            else:
                                p.op("dve", lambda e, dst=dst, srcp=srcp: e.tensor_copy(out=dst, in_=srcp), reads=[rp], writes=[R("gnT%d" % qq)])
                        p.op("sp", lambda e, tt=tt: e.dma_start(out=xres_t[:], in_=xsrc[tt * 128:(tt + 1) * 128, :]),
                             reads=[self.R("xres")], writes=[R("xres_t")], dma=self.st_x)
                        for ch in range(2):
                            bo, rbo = self.pbank[6 + ch], self.r_pb[6 + ch]

                            def mmo(e, ch=ch, bo=bo):
                                for k in range(16):
                                    ins = e.matmul(bo[:, :], lhsT=gnT[:, k, :], rhs=wout[:, k, ch * 512:(ch + 1) * 512], start=(k == 0), stop=(k == 15))
                                return ins
                            p.op("pe", mmo, reads=[R("gnT%d" % qq) for qq in range(4)] + r_wout, writes=[rbo])
                            p.op("dve", lambda e, ch=ch, bo=bo: e.tensor_tensor(out=xo[:, ch * 512:(ch + 1) * 512], in0=bo[:, :], in1=xres_t[:, ch * 512:(ch + 1) * 512], op=ALU.add),
                                 reads=[rbo, R("xres_t")], writes=[R("xo")])
                        p.op("sp", lambda e, tt=tt: e.dma_start(out=self.xres[tt * 128:(tt + 1) * 128, :], in_=xo[:]),
                             reads=[R("xo")], writes=[self.R("xres_w%d" % (tt % 4))], dma=self.st_o)
                    tails.append(tail)
            while tails:
                tails.pop(0)()
            self.phase_barrier()
        self.x_src = self.xres

    def prep_attn(self, j):
        p = self.p
        for (c0, c1) in ((0, 2048), (2048, NAW)):
            for hh in range(2):
                s_ap = self.attn_wr[j, hh * 512:(hh + 1) * 512, c0:c1]
                d_ap = self.awin[j, hh * 512:(hh + 1) * 512, c0:c1]
                p.op("pool", lambda e, s_ap=s_ap, d_ap=d_ap: e.dma_start(out=d_ap, in_=s_ap),
                     writes=[self.R("awin%d_%d_%d" % (j, c0, hh))], dma=self.st_prep)

    def awin_res(self, j, c0, c1):
        out = []
        for base in (0, 2048):
            hi = 2048 if base == 0 else NAW
            if c0 < hi and c1 > base:
                out += [self.R("awin%d_%d_%d" % (j, base, hh)) for hh in range(2)]
        return out

    def attn(self, l):
        p = self.p
        nc = self.nc
        j = l // 2
        xsrc = self.x_src
        tag = "a%d" % l
        R = lambda n: self.R(tag + n)
        BIG = 30000.0
        dbg = self.attn_dbg or ""
        use_cmp = ("nocmp" not in dbg)
        use_slc = ("noslc" not in dbg)
        use_win = ("nowin" not in dbg)
        with contextlib.ExitStack() as st_long:
            def sbl(name, shape, dt):
                return st_long.enter_context(nc.sbuf_tensor(tag + name, list(shape), dt))
            kT = sbl("kT", [64, 6, S], BF16)
            Vall = sbl("V", [128, 32, 6, 65], BF16)
            gates = sbl("gates", [128, 32, 24], F32)
            kcmpT = sbl("kcmpT", [64, 2, 256], BF16)
            Vcmp = sbl("Vcmp", [128, 2, 2, 129], BF16)
            esink = sbl("esink", [128, 8], F32)
            p.op("pool", lambda e: e.memset(Vall[:, :, :, 64:65], 1.0), writes=[R("Vones")])
            p.op("sp", lambda e: e.dma_start(out=esink[:], in_=self.sinks_in[j:j + 1, :].broadcast_to([128, 8])), writes=[R("esink")], dma=self.st_const)
            p.op("act", lambda e: e.activation(out=esink[:], in_=esink[:], func=AF.Exp), reads=[R("esink")], writes=[R("esink")])
            self.load_gain(l * 3 + 1)

            with contextlib.ExitStack() as st_x:
                kcT = st_x.enter_context(nc.sbuf_tensor(tag + "kcT", [64, 2, S], BF16))
                vcT = st_x.enter_context(nc.sbuf_tensor(tag + "vcT", [128, S], BF16))
                with contextlib.ExitStack() as st1:
                    def sb(name, shape, dt):
                        return st1.enter_context(nc.sbuf_tensor(tag + name, list(shape), dt))
                    TBK = 512
                    NT = 4
                    hT = sb("hT", [128, 8, TBK], BF16)
                    wb = [sb("wb%d" % b, [128, 8, 512], BF16) for b in range(2)]
                    cosb = sb("cosb", [64, TBK], F32)
                    sinb = sb("sinb", [64, TBK], F32)
                    t1 = [sb("t1_%d" % b, [64, TBK], F32) for b in range(2)]
                    t2 = [sb("t2_%d" % b, [64, TBK], F32) for b in range(2)]
                    qst = [sb("qst%d" % b, [64, TBK], BF16) for b in range(2)]
                    xt_bufs = [(sb("xt%d" % b, [128, D], F32), sb("hb%d" % b, [128, D], BF16),
                                sb("sq%d" % b, [128, D], BF16), sb("ss%d" % b, [128, 1], F32),
                                sb("rs%d" % b, [128, 1], F32)) for b in range(2)]
                    r_hT = [[R("hT%d_%d" % (jj, hh)) for hh in range(2)] for jj in range(NT)]
                    hres_all = [r_hT[jj][hh] for jj in range(NT) for hh in range(2)]
                    wc = 0
                    hc = 0
                    for blk in range(S // TBK):
                        t0 = blk * NT
                        csl = slice(blk * TBK, (blk + 1) * TBK)
                        self.norm_transpose(xsrc, t0, NT, hT, r_hT, xt_bufs, tag)
                        p.op("sp", lambda e, csl=csl: e.dma_start(out=cosb[:], in_=self.rope_in[0, :, csl]), writes=[R("cosb")], dma=self.st_x)
                        p.op("sp", lambda e, csl=csl: e.dma_start(out=sinb[:], in_=self.rope_in[1, :, csl]), writes=[R("sinb")], dma=self.st_x)
                        for wgI in range(6):
                            b = wc % 2
                            wc += 1
                            c0 = wgI * 512
                            src = self.awin[j, :, c0:c0 + 512].rearrange("(k p) m -> p k m", p=128)
                            p.op("sp", lambda e, src=src, b=b: e.dma_start(out=wb[b][:], in_=src),
                                 reads=self.awin_res(j, c0, c0 + 512), writes=[R("wb%d" % b)], dma=self.st_w)
                            for m in range(4):
                                hd = wgI * 4 + m
                                hb_ = hc % 2
                                hc += 1
                                bA, rA = self.pbank[hb_ * 2], self.r_pb[hb_ * 2]
                                bB, rB = self.pbank[hb_ * 2 + 1], self.r_pb[hb_ * 2 + 1]

                                def mm(e, bank, off, b=b, m=m):
                                    for kc in range(8):
                                        ins = e.matmul(bank[0:64, :], lhsT=wb[b][:, kc, m * 128 + off:m * 128 + off + 64], rhs=hT[:, kc, :],
                                                       start=(kc == 0), stop=(kc == 7))
                                    return ins
                                p.op("pe", lambda e, mm=mm, bA=bA: mm(e, bA, 0), reads=[R("wb%d" % b)] + hres_all, writes=[rA])
                                p.op("pe", lambda e, mm=mm, bB=bB: mm(e, bB, 64), reads=[R("wb%d" % b)] + hres_all, writes=[rB])
                                p.op("dve", lambda e, hb_=hb_, bA=bA: e.tensor_tensor(out=t1[hb_][:], in0=bA[0:64, :], in1=cosb[:], op=ALU.mult),
                                     reads=[rA, R("cosb")], writes=[R("t1_%d" % hb_)])
                                p.op("dve", lambda e, hb_=hb_, bB=bB: e.tensor_tensor(out=t2[hb_][:], in0=bB[0:64, :], in1=sinb[:], op=ALU.mult),
                                     reads=[rB, R("sinb")], writes=[R("t2_%d" % hb_)])
                                if hd < 8 or 10 <= hd < 18:
                                    qh = hd if hd < 8 else hd - 10 + 8
                                    p.op("pool", lambda e, hb_=hb_: e.tensor_tensor(out=qst[hb_][:], in0=t1[hb_][:], in1=t2[hb_][:], op=ALU.add),
                                         reads=[R("t1_%d" % hb_), R("t2_%d" % hb_)], writes=[R("qst%d" % hb_)])
                                    p.op("sp", lambda e, hb_=hb_, qh=qh, csl=csl: e.dma_start(out=self.qT[qh, :, csl], in_=qst[hb_][:]),
                                         reads=[R("qst%d" % hb_)], writes=[self.R("qT_%d_%d" % (qh, blk))], dma=self.st_o)
                                else:
                                    if hd < 10:
                                        dst, rd = kT[:, hd - 8, csl], R("kT%d" % (hd - 8))
                                    elif hd < 20:
                                        dst, rd = kcT[:, hd - 18, csl], R("kcT%d" % (hd - 18))
                                    elif hd < 22:
                                        dst, rd = kT[:, 2 + hd - 20, csl], R("kT%d" % (2 + hd - 20))
                                    else:
                                        dst, rd = kT[:, 4 + hd - 22, csl], R("kT%d" % (4 + hd - 22))
                                    p.op("pool", lambda e, hb_=hb_, dst=dst: e.tensor_tensor(out=dst, in0=t1[hb_][:], in1=t2[hb_][:], op=ALU.add),
                                         reads=[R("t1_%d" % hb_), R("t2_%d" % hb_)], writes=[rd])
                        b = wc % 2
                        wc += 1
                        src = self.awin[j, :, 3072:3608].rearrange("(k p) m -> p k m", p=128)
                        src_vc = self.awin[j, :, 3072:3200].rearrange("(k p) m -> p k m", p=128)
                        src_tm = self.awin[j, :, 3200:3608].rearrange("(k p) m -> p k m", p=128)
                        b2 = wc % 2
                        wc += 1
                        p.op("sp", lambda e, b=b, src_vc=src_vc: e.dma_start(out=wb[b][:, :, 0:128], in_=src_vc),
                             reads=self.awin_res(j, 3072, 3200), writes=[R("wb%d" % b)], dma=self.st_w)
                        p.op("sp", lambda e, b2=b2, src_tm=src_tm: e.dma_start(out=wb[b2][:, :, 0:408], in_=src_tm),
                             reads=self.awin_res(j, 3200, 3608), writes=[R("wb%d" % b2)], dma=self.st_w)
                        bA, rA = self.pbank[4], self.r_pb[4]

                        def mmvc(e, b=b, bA=bA):
                            for kc in range(8):
                                ins = e.matmul(bA[:, :], lhsT=wb[b][:, kc, 0:128], rhs=hT[:, kc, :], start=(kc == 0), stop=(kc == 7))
                            return ins
                        p.op("pe", mmvc, reads=[R("wb%d" % b)] + hres_all, writes=[rA])
                        p.op("act", lambda e, bA=bA, csl=csl: e.copy(out=vcT[:, csl], in_=bA[:, :]), reads=[rA], writes=[R("vcT")])
                        for q in range(NT):
                            tt = t0 + q
                            bT, rT = self.pbank[5], self.r_pb[5]

                            def mmtm(e, q=q, b2=b2, bT=bT):
                                for kc in range(8):
                                    ins = e.matmul(bT[:, 0:408], lhsT=hT[:, kc, q * 128:(q + 1) * 128], rhs=wb[b2][:, kc, 0:408], start=(kc == 0), stop=(kc == 7))
                                return ins
                            p.op("pe", mmtm, reads=[R("wb%d" % b2)] + r_hT[q], writes=[rT])
                            p.op("dve", lambda e, tt=tt, bT=bT: e.tensor_copy(out=Vall[:, tt, :, 0:64], in_=bT[:, 0:384].rearrange("p (a d) -> p a d", a=6)),
                                 reads=[rT], writes=[R("Vall")])
                            p.op("act", lambda e, tt=tt, bT=bT: e.activation(out=gates[:, tt, :], in_=bT[:, 384:408], func=AF.Sigmoid),
                                 reads=[rT], writes=[R("gates")])
                    p.barrier()
                with contextlib.ExitStack() as st2:
                    def sb(name, shape, dt):
                        return st2.enter_context(nc.sbuf_tensor(tag + name, list(shape), dt))
                    w1s = sb("w1s", [64, 32, 128], BF16)
                    w2s = sb("w2s", [128, 64], BF16)
                    posf = sb("posf", [64, 32], F32)
                    posb_ = sb("posb", [64, 32, 2], BF16)
                    pbias = sb("pbias", [128, 1], F32)
                    u = sb("u", [128, 256], F32)
                    u2 = sb("u2", [128, 256], F32)
                    sg_ = sb("sgm", [128, 256], F32)
                    gl = sb("gl", [128, 256], BF16)
                    wself = sb("wself", [128, 2, 64], F32)
                    p.op("sp", lambda e: e.dma_start(out=wself[:], in_=self.wsel_in), writes=[R("wself")], dma=self.st_const)
                    p.op("dve", lambda e: e.memset(u[:], 0.0), writes=[R("u")])
                    p.op("pool", lambda e: e.memset(Vcmp[:, :, :, 64:65], 1.0), writes=[R("Vcmp1")])
                    for g in range(2):
                        p.op("dve", lambda e, g=g: e.tensor_copy(out=Vcmp[:, g, :, 65:129], in_=wself[:]), reads=[R("wself")], writes=[R("VcmpW%d" % g)])
                    for kv in range(2):
                        w1_in = self.cmp_w1[j, kv].rearrange("(pp d) h -> d pp h", d=64)
                        p.op("pool", lambda e, w1_in=w1_in: e.dma_start(out=w1s[:], in_=w1_in), writes=[R("w1s")], dma=self.st_prep)
                        p.op("pool", lambda e, kv=kv: e.dma_start(out=w2s[:], in_=self.cmp_w2[j, kv]), writes=[R("w2s")], dma=self.st_prep)
                        p.op("sp", lambda e, kv=kv: e.dma_start(out=posf[:], in_=self.cmp_posT[j, kv]), writes=[R("posf")], dma=self.st_const)
                        p.op("dve", lambda e: e.tensor_copy(out=posb_[:], in_=posf[:].rearrange("p (a o) -> p a o", o=1).to_broadcast([64, 32, 2])), reads=[R("posf")], writes=[R("posb")])
                        b0, rb0 = self.pbank[0], self.r_pb[0]

                        def mmb(e):
                            for pp in range(32):
                                ins = e.matmul(b0[:, 0:2], lhsT=w1s[:, pp, :], rhs=posb_[:, pp, :], start=(pp == 0), stop=(pp == 31))
                            return ins
                        p.op("pe", mmb, reads=[R("w1s"), R("posb")], writes=[rb0])
                        p.op("dve", lambda e: e.tensor_copy(out=pbias[:], in_=b0[:, 0:1]), reads=[rb0], writes=[R("pbias")])
                        for g in range(2):
                            b1, rb1 = self.pbank[1 + g], self.r_pb[1 + g]
                            if kv == 0:
                                srcT = kcT[:, g, :]
                                rsrc = R("kcT%d" % g)
                            else:
                                srcT = vcT[g * 64:(g + 1) * 64, :]
                                rsrc = R("vcT")

                            def mmh(e, srcT=srcT, b1=b1, g=g):
                                for pp in range(32):
                                    ins = e.matmul(b1[:, 0:255], lhsT=w1s[g * 64 * kv:g * 64 * kv + 64, pp, :] if False else w1s[:, pp, :],
                                                   rhs=srcT[:, pp:pp + 16 * 254 + 1:16], start=(pp == 0), stop=(pp == 31))
                                return ins
                            if kv == 1 and g == 1:
                                vtmp = sb("vtmp", [64, S], BF16)
                                p.op("sp", lambda e, vtmp=vtmp: e.dma_start(out=vtmp[:], in_=vcT[64:128, :]), reads=[R("vcT")], writes=[R("vtmp")], dma=self.st_x)
                                srcT2 = vtmp[:, :]

                                def mmh(e, srcT2=srcT2, b1=b1):
                                    for pp in range(32):
                                        ins = e.matmul(b1[:, 0:255], lhsT=w1s[:, pp, :], rhs=srcT2[:, pp:pp + 16 * 254 + 1:16], start=(pp == 0), stop=(pp == 31))
                                    return ins
                                rsrc = R("vtmp")
                            p.op("pe", mmh, reads=[R("w1s"), rsrc], writes=[rb1])
                            p.op("act", lambda e, b1=b1: e.activation(out=u[:, 0:255], in_=b1[:, 0:255], func=AF.Identity, bias=pbias[:]),
                                 reads=[rb1, R("pbias"), R("u")], writes=[R("u")])
                            p.op("dve", lambda e: e.tensor_tensor(out=u2[:], in0=u[:], in1=u[:], op=ALU.mult), reads=[R("u")], writes=[R("u2")])
                            p.op("dve", lambda e: e.tensor_scalar(out=u2[:], in0=u2[:], scalar1=0.044715, scalar2=1.0, op0=ALU.mult, op1=ALU.add), reads=[R("u2")], writes=[R("u2")])
                            p.op("dve", lambda e: e.tensor_tensor(out=u2[:], in0=u2[:], in1=u[:], op=ALU.mult), reads=[R("u2"), R("u")], writes=[R("u2")])
                            p.op("act", lambda e: e.activation(out=sg_[:], in_=u2[:], func=AF.Sigmoid, scale=1.5957691216057308), reads=[R("u2")], writes=[R("sgm")])
                            p.op("dve", lambda e: e.tensor_tensor(out=gl[:], in0=u[:], in1=sg_[:], op=ALU.mult), reads=[R("u"), R("sgm")], writes=[R("gl")])
                            b3, rb3 = self.pbank[3], self.r_pb[3]
                            if kv == 0:
                                p.op("pe", lambda e, b3=b3: e.matmul(b3[0:64, 0:256], lhsT=w2s[:], rhs=gl[:], start=True, stop=True), reads=[R("w2s"), R("gl")], writes=[rb3])
                                p.op("dve", lambda e, g=g, b3=b3: e.tensor_copy(out=kcmpT[:, g, :], in_=b3[0:64, 0:256]), reads=[rb3], writes=[R("kcmpT%d" % g)])
                            else:
                                def mmv(e, b3=b3):
                                    for ct in range(2):
                                        ins = e.matmul(b3[:, ct * 64:(ct + 1) * 64], lhsT=gl[:, ct * 128:(ct + 1) * 128], rhs=w2s[:], start=True, stop=True)
                                    return ins
                                p.op("pe", mmv, reads=[R("w2s"), R("gl")], writes=[rb3])
                                p.op("dve", lambda e, g=g, b3=b3: e.tensor_copy(out=Vcmp[:, g, :, 0:64], in_=b3[:, 0:128].rearrange("p (c d) -> p c d", c=2)),
                                     reads=[rb3], writes=[R("VcmpV%d" % g)])
                    p.barrier()
            with contextlib.ExitStack() as st3:
                def sb(name, shape, dt):
                    return st3.enter_context(nc.sbuf_tensor(tag + name, list(shape), dt))
                wout = sb("wout", [128, 8, D], BF16)
                for q in range(2):
                    src = self.attn_w_out[j, q * 512:(q + 1) * 512, :].rearrange("(k p) m -> p k m", p=128)
                    p.op("pool", lambda e, src=src, q=q: e.dma_start(out=wout[:, q * 4:(q + 1) * 4, :], in_=src), writes=[R("wout%d" % q)], dma=self.st_prep)
                r_wout = [R("wout0"), R("wout1")]
                expand = sb("expand", [64, 32, 128], BF16)
                onesb = sb("onesb", [64, 32 * 128], BF16)
                p.op("pool", lambda e: e.memset(onesb[:], 1.0), writes=[R("onesb")])
                p.op("pool", lambda e: e.affine_select(out=expand[:].rearrange("p a (h m) -> p a h m", h=2), in_=onesb[:].rearrange("p (a h m) -> p a h m", a=32, h=2),
                                                       pattern=[[-2, 32], [-1, 2], [0, 64]], compare_op=ALU.is_equal, fill=0.0, base=0, channel_multiplier=1),
                     reads=[R("onesb")], writes=[R("expand")])
                selb = sb("selb", [128, 32, 64], F32)
                p.op("sp", lambda e: e.dma_start(out=selb[:], in_=self.selb_in), writes=[R("selb")], dma=self.st_const)
                qt = [sb("qt%d" % b, [64, 16, 256], BF16) for b in range(2)]
                Eb = [sb("E%d" % b, [128, 4, 128], BF16) for b in range(4)]
                SBANKS = (0, 1, 6)
                maskC = sb("maskC", [128, 4, 128], BF16)
                maskP = sb("maskP", [128, 4, 128], BF16)
                p.op("pool", lambda e: e.memset(maskC[:], 1.0), writes=[R("maskC")])
                p.op("pool", lambda e: e.memset(maskP[:], 1.0), writes=[R("maskP")])
                p.op("pool", lambda e: e.affine_select(out=maskC[:], in_=maskC[:], pattern=[[0, 4], [1, 128]], compare_op=ALU.is_ge, fill=0.0, base=0, channel_multiplier=-1),
                     reads=[R("maskC")], writes=[R("maskC")])
                p.op("pool", lambda e: e.affine_select(out=maskP[:], in_=maskP[:], pattern=[[0, 4], [-1, 128]], compare_op=ALU.is_ge, fill=0.0, base=-1, channel_multiplier=1),
                     reads=[R("maskP")], writes=[R("maskP")])
                ot = sb("ot", [128, D], BF16)
                accb = sb("accb", [128, 4, 64], F32)
                imp = sb("imp", [128, 64], F32)
                sc2 = sb("sc2", [128, 64], F32)
                m8 = sb("m8", [128, 8], F32)
                nb = sb("nb", [128, 64], F32)
                nbT4 = sb("nbT4", [64, 4, 128], BF16)
                den = sb("den", [128, 4], F32)
                oT = sb("oT", [128, 8, 128], BF16)
                xres_t = sb("xres_t", [128, D], F32)
                xo = sb("xo", [128, D], F32)
                ec = [0]
                sc_ = [0]

                def score_exp(i, lhsT, lres, rhs, rres, mask, extra=None):
                    sb_i = SBANKS[sc_[0] % 3]
                    sc_[0] += 1
                    bank, rb = self.pbank[sb_i], self.r_pb[sb_i]
                    eb = ec[0] % 4
                    ec[0] += 1

                    def mm(e, bank=bank):
                        ins = e.matmul(bank[:, :].rearrange("p (r q) -> p r q", r=4), lhsT=lhsT, rhs=rhs, start=True, stop=(extra is None))
                        if extra is not None:
                            ins = e.matmul(bank[:, :].rearrange("p (r q) -> p r q", r=4), lhsT=extra[0], rhs=extra[1], start=False, stop=True)
                        return ins
                    rr = list(lres) + list(rres) + (list(extra[2]) if extra is not None else [])
                    p.op("pe", mm, reads=rr, writes=[rb])
                    E = Eb[eb]
                    rE = R("E%d" % eb)
                    p.op("act", lambda e, E=E, bank=bank: e.activation(out=E[:].rearrange("p r q -> p (r q)"), in_=bank[:, :], func=AF.Exp, scale=0.125),
                         reads=[rb], writes=[rE])
                    if mask is CAUSAL or mask is PREV:
                        mt_, rm_ = (maskC, R("maskC")) if mask is CAUSAL else (maskP, R("maskP"))
                        p.op("dve", lambda e, E=E, mt_=mt_: e.tensor_tensor(out=E[:], in0=E[:], in1=mt_[:], op=ALU.mult), reads=[rE, rm_], writes=[rE])
                    elif mask is not None:
                        base, cm, stepq = mask
                        p.op("pool", lambda e, E=E, base=base, cm=cm, stepq=stepq: e.affine_select(
                            out=E[:], in_=E[:], pattern=[[0, 4], [stepq, 128]], compare_op=ALU.is_ge, fill=0.0, base=base, channel_multiplier=cm),
                            reads=[rE], writes=[rE])
                    return E, rE

                def pv(E, rE, vrhs, vres, ncols, first, last):
                    def mm(e):
                        for r in range(4):
                            ins = e.matmul(self.pbank[2 + r][:, 0:ncols], lhsT=E[:, r, :], rhs=vrhs, start=first, stop=last)
                        return ins
                    p.op("pe", mm, reads=[rE] + list(vres), writes=[self.r_pb[2 + r] for r in range(4)])

                CAUSAL = (0, -1, 1)
                PREV = (-1, 1, -1)
                den2 = [den, sb("denB", [128, 4], F32)]
                bc = [0]

                def pv2(E, rE, vrhs, vres, ncols, first, last, par):
                    off = par * 256

                    def mm(e):
                        for r in range(4):
                            ins = e.matmul(self.pbank[2 + r][:, off:off + ncols], lhsT=E[:, r, :], rhs=vrhs, start=first, stop=last)
                        return ins
                    p.op("pe", mm, reads=[rE] + list(vres), writes=[R("O%d_%d" % (r, par)) for r in range(4)] + [self.r_pb[2 + r] for r in range(4)])

                for i in range(32):
                    qb_ = (i // 2) % 2
                    if i % 2 == 0:
                        blk = i // 4
                        src = self.qT[:, :, i * 128:i * 128 + 256].rearrange("h d t -> d h t")
                        p.op("sp", lambda e, src=src, qb_=qb_: e.dma_start(out=qt[qb_][:], in_=src),
                             reads=[self.R("qT_%d_%d" % (h, blk)) for h in range(16)], writes=[R("qt%d" % qb_)], dma=self.st_x)
                    qsl = slice((i % 2) * 128, (i % 2 + 1) * 128)
                    rq = [R("qt%d" % qb_)]
                    p.op("sp", lambda e, i=i: e.dma_start(out=xres_t[:], in_=xsrc[i * 128:(i + 1) * 128, :]),
                         reads=[self.R("xres")], writes=[R("xres_t")], dma=self.st_x)
                    for g in range(2):
                        qa4 = qt[qb_][:, g * 4:(g + 1) * 4, qsl]
                        qb4 = qt[qb_][:, 8 + g * 4:8 + (g + 1) * 4, qsl]
                        gsl = gates[:, i, g * 12:(g + 1) * 12].rearrange("p (r b) -> p r b", b=3)
                        tiles = []
                        kts = [kt for kt in (i - 1, i) if kt >= 0]
                        for n, kt in enumerate(kts):
                            tiles.append(("swa", kT[:, g, kt * 128:(kt + 1) * 128], [R("kT%d" % g)], qa4, CAUSAL if kt == i else PREV, None,
                                          Vall[:, kt, g, :], [R("Vall"), R("Vones")], 65, n == 0, n == len(kts) - 1))
                        nct = 1 if i < 16 else 2
                        if use_cmp or use_slc:
                            for ct in range(nct):
                                tiles.append(("cmp", kcmpT[:, g, ct * 128:(ct + 1) * 128], [R("kcmpT%d" % g)], qb4, (128 * i - 2048 * ct - 31, -16, 1), None,
                                              Vcmp[:, g, ct, :], [R("VcmpV%d" % g), R("VcmpW%d" % g), R("Vcmp1")], 129, ct == 0, ct == nct - 1))
                        if use_win:
                            kts = [kt for kt in range(i - 4, i + 1) if kt >= 0]
                            for n, kt in enumerate(kts):
                                mk = CAUSAL if kt == i else (PREV if kt == i - 4 else None)
                                tiles.append(("win", kT[:, 4 + g, kt * 128:(kt + 1) * 128], [R("kT%d" % (4 + g))], qb4, mk, None,
                                              Vall[:, kt, 4 + g, :], [R("Vall"), R("Vones")], 65, n == 0, n == len(kts) - 1))
                        if use_slc:
                            for kt in range(i + 1):
                                tiles.append(("slc", kT[:, 2 + g, kt * 128:(kt + 1) * 128], [R("kT%d" % (2 + g))], qb4, CAUSAL if kt == i else None,
                                              (expand[:, kt, :], nbT4[:], [R("expand"), R("nbT4")]),
                                              Vall[:, kt, 2 + g, :], [R("Vall"), R("Vones")], 65, kt == 0, kt == i))
                        branches = [br for br in ("swa", "cmp", "win", "slc") if any(t[0] == br for t in tiles)]
                        last_nsa = [br for br in branches if br != "swa" and br != "cmp"]
                        last_nsa = last_nsa[-1] if last_nsa else "cmp"

                        def finish(br, par):
                            dn = den2[par]
                            rdn = R("den%d" % par)
                            rO = [R("O%d_%d" % (r, par)) for r in range(4)]
                            off = par * 256
                            O = [self.pbank[2 + r] for r in range(4)]
                            if br == "swa":
                                for r in range(4):
                                    h = g * 4 + r
                                    p.op("dve", lambda e, r=r, h=h: e.tensor_tensor(out=dn[:, r:r + 1], in0=O[r][:, off + 64:off + 65], in1=esink[:, h:h + 1], op=ALU.add),
                                         reads=[rO[r], R("esink")], writes=[rdn])
                                p.op("dve", lambda e: e.reciprocal(out=dn[:], in_=dn[:]), reads=[rdn], writes=[rdn])
                                for r in range(4):
                                    h = g * 4 + r
                                    p.op("dve", lambda e, r=r, h=h: e.tensor_scalar(out=ot[:, h * 64:(h + 1) * 64], in0=O[r][:, off:off + 64], scalar1=dn[:, r:r + 1], scalar2=None, op0=ALU.mult),
                                         reads=[rO[r], rdn], writes=[R("ot")])
                                return
                            if br == "cmp":
                                for r in range(4):
                                    p.op("dve", lambda e, r=r: e.tensor_scalar(out=dn[:, r:r + 1], in0=O[r][:, off + 64:off + 65], scalar1=1e-30, scalar2=None, op0=ALU.max),
                                         reads=[rO[r]], writes=[rdn])
                                p.op("dve", lambda e: e.reciprocal(out=dn[:], in_=dn[:]), reads=[rdn], writes=[rdn])
                                for r in range(4):
                                    if r == 0:
                                        p.op("dve", lambda e, r=r: e.tensor_scalar(out=imp[:], in0=O[r][:, off + 65:off + 129], scalar1=dn[:, r:r + 1], scalar2=None, op0=ALU.mult),
                                             reads=[rO[r], rdn], writes=[R("imp")])
                                    else:
                                        p.op("dve", lambda e, r=r: e.scalar_tensor_tensor(out=imp[:], in0=O[r][:, off + 65:off + 129], scalar=dn[:, r:r + 1], in1=imp[:], op0=ALU.mult, op1=ALU.add),
                                             reads=[rO[r], rdn, R("imp")], writes=[R("imp")])
                                if use_slc:
                                    p.op("dve", lambda e, i=i: e.tensor_tensor(out=imp[:], in0=imp[:], in1=selb[:, i, :], op=ALU.add), reads=[R("imp"), R("selb")], writes=[R("imp")])
                                    p.op("dve", lambda e: e.max(out=m8[:], in_=imp[:]), reads=[R("imp")], writes=[R("m8")])
                                    p.op("dve", lambda e: e.match_replace(out=sc2[:], in_to_replace=m8[:], in_values=imp[:], imm_value=-3.0e38), reads=[R("imp"), R("m8")], writes=[R("sc2")])
                                    p.op("dve", lambda e: e.max(out=m8[:], in_=sc2[:]), reads=[R("sc2"), R("m8")], writes=[R("m8")])
                                    p.op("dve", lambda e: e.tensor_scalar(out=nb[:], in0=imp[:], scalar1=m8[:, 7:8], scalar2=-BIG, op0=ALU.is_lt, op1=ALU.mult),
                                         reads=[R("imp"), R("m8")], writes=[R("nb")])
                                    sb_i = SBANKS[sc_[0] % 3]
                                    sc_[0] += 1
                                    bank, rb = self.pbank[sb_i], self.r_pb[sb_i]
                                    p.op("pe", lambda e, bank=bank: e.transpose(out=bank[0:64, 0:128], in_=nb[:], identity=self.ident_f[:]), reads=[R("nb"), self.R("ident")], writes=[rb])
                                    p.op("dve", lambda e, bank=bank: e.tensor_copy(out=nbT4[:], in_=bank[0:64, 0:128].rearrange("p (o q) -> p o q", o=1).to_broadcast([64, 4, 128])),
                                         reads=[rb], writes=[R("nbT4")])
                                p.op("dve", lambda e, gsl=gsl: e.tensor_tensor(out=dn[:], in0=dn[:], in1=gsl[:, :, 0], op=ALU.mult), reads=[rdn, R("gates")], writes=[rdn])
                                for r in range(4):
                                    h = 8 + g * 4 + r
                                    if not use_cmp:
                                        p.op("dve", lambda e, r=r: e.memset(accb[:, r, :], 0.0), reads=[R("accb")], writes=[R("accb")])
                                    elif last_nsa == "cmp":
                                        p.op("dve", lambda e, r=r, h=h: e.tensor_scalar(out=ot[:, h * 64:(h + 1) * 64], in0=O[r][:, off:off + 64], scalar1=dn[:, r:r + 1], scalar2=None, op0=ALU.mult),
                                             reads=[rO[r], rdn], writes=[R("ot")])
                                    else:
                                        p.op("dve", lambda e, r=r: e.tensor_scalar(out=accb[:, r, :], in0=O[r][:, off:off + 64], scalar1=dn[:, r:r + 1], scalar2=None, op0=ALU.mult),
                                             reads=[rO[r], rdn], writes=[R("accb")])
                                return
                            gi = 2 if br == "win" else 1
                            for r in range(4):
                                p.op("dve", lambda e, r=r: e.tensor_copy(out=dn[:, r:r + 1], in_=O[r][:, off + 64:off + 65]), reads=[rO[r]], writes=[rdn])
                            p.op("dve", lambda e: e.reciprocal(out=dn[:], in_=dn[:]), reads=[rdn], writes=[rdn])
                            p.op("dve", lambda e, gsl=gsl, gi=gi: e.tensor_tensor(out=dn[:], in0=dn[:], in1=gsl[:, :, gi], op=ALU.mult), reads=[rdn, R("gates")], writes=[rdn])
                            for r in range(4):
                                h = 8 + g * 4 + r
                                if br == last_nsa:
                                    p.op("dve", lambda e, r=r, h=h: e.scalar_tensor_tensor(out=ot[:, h * 64:(h + 1) * 64], in0=O[r][:, off:off + 64], scalar=dn[:, r:r + 1], in1=accb[:, r, :], op0=ALU.mult, op1=ALU.add),
                                         reads=[rO[r], rdn, R("accb")], writes=[R("ot")])
                                else:
                                    p.op("dve", lambda e, r=r: e.scalar_tensor_tensor(out=accb[:, r, :], in0=O[r][:, off:off + 64], scalar=dn[:, r:r + 1], in1=accb[:, r, :], op0=ALU.mult, op1=ALU.add),
                                         reads=[rO[r], rdn, R("accb")], writes=[R("accb")])

                        par_of = {}
                        for br in branches:
                            par_of[br] = (bc[0] % 2) if "usepar" in dbg else 0
                            bc[0] += 1
                        queue = []
                        LOOK = 0 if "nopipe" in dbg else 2

                        def pop():
                            pE, prE, ptl = queue.pop(0)
                            pv2(pE, prE, ptl[6], ptl[7], ptl[8], ptl[9], ptl[10], par_of[ptl[0]])
                            if ptl[10]:
                                finish(ptl[0], par_of[ptl[0]])
                        for tl in tiles:
                            br, lhsT, lres, rhs, mask, extra, vrhs, vres, ncols, first, last = tl
                            if br == "slc" and first:
                                while any(qq[2][0] == "cmp" for qq in queue):
                                    pop()
                            E, rE = score_exp(i, lhsT, lres, rhs, rq, mask, extra=extra)
                            queue.append((E, rE, tl))
                            while len(queue) > LOOK:
                                pop()
                        while queue:
                            pop()
                    for half in range(2):
                        rp = self.r_ptr[1]

                        def tr(e, half=half):
                            for q in range(4):
                                kc = half * 4 + q
                                ins = e.transpose(out=self.ptrh[1][:, q * 128:(q + 1) * 128], in_=ot[:, kc * 128:(kc + 1) * 128], identity=self.ident_b[:])
                            return ins
                        p.op("pe", tr, reads=[R("ot"), self.R("ident_b")], writes=[rp])
                        dst = oT[:, half * 4:(half + 1) * 4, :]
                        srcp = self.ptrh[1][:, :].rearrange("p (q m) -> p q m", q=4)
                        p.op("act", lambda e, dst=dst, srcp=srcp: e.copy(out=dst, in_=srcp), reads=[rp], writes=[R("oT%d" % half)])
                    for ch in range(2):
                        bo, rbo = self.pbank[ch], self.r_pb[ch]

                        def mmo(e, ch=ch, bo=bo):
                            for k in range(8):
                                ins = e.matmul(bo[:, :], lhsT=oT[:, k, :], rhs=wout[:, k, ch * 512:(ch + 1) * 512], start=(k == 0), stop=(k == 7))
                            return ins
                        p.op("pe", mmo, reads=[R("oT0"), R("oT1")] + r_wout, writes=[rbo])
                        p.op("dve", lambda e, ch=ch, bo=bo: e.tensor_tensor(out=xo[:, ch * 512:(ch + 1) * 512], in0=bo[:, :], in1=xres_t[:, ch * 512:(ch + 1) * 512], op=ALU.add),
                             reads=[rbo, R("xres_t")], writes=[R("xo")])
                    if "ot" in dbg:
                        p.op("dve", lambda e: e.tensor_copy(out=xo[:], in_=ot[:]), reads=[R("ot"), R("xo")], writes=[R("xo")])
                    p.op("sp", lambda e, i=i: e.dma_start(out=self.xres[i * 128:(i + 1) * 128, :], in_=xo[:]),
                         reads=[R("xo")], writes=[self.R("xres_w%d" % (i % 4))], dma=self.st_o)
            self.phase_barrier()
        self.x_src = self.xres

    def final_norm(self):
        p = self.p
        nc = self.nc
        xsrc = self.x_src
        with contextlib.ExitStack() as st:
            def sb(name, shape, dt):
                return st.enter_context(nc.sbuf_tensor(name, list(shape), dt))
            self.load_gain(DEPTH * 3)
            bufs = [(sb("fn_x%d" % b, [128, D], F32), sb("fn_sq%d" % b, [128, D], BF16), sb("fn_ss%d" % b, [128, 1], F32),
                     sb("fn_rs%d" % b, [128, 1], F32), sb("fn_o%d" % b, [128, D], F32)) for b in range(2)]
            for tt in range(S // 128):
                b = tt % 2
                xt, sq, ss, rs, ot = bufs[b]
                rx, rss, rrs, ro = [self.R("fn_%s%d" % (n, b)) for n in ("x", "ss", "rs", "o")]
                p.op("sp", lambda e, xt=xt, tt=tt: e.dma_start(out=xt[:], in_=xsrc[tt * 128:(tt + 1) * 128, :]),
                     reads=[self.R("xres")], writes=[rx], dma=self.st_x)
                p.op("act", lambda e, xt=xt, sq=sq, ss=ss: e.activation(out=sq[:], in_=xt[:], func=AF.Square, accum_out=ss[:]),
                     reads=[rx], writes=[self.R("fn_sq%d" % b), rss])
                p.op("act", lambda e, ss=ss, rs=rs: e.activation(out=rs[:], in_=ss[:], func=AF.Sqrt, scale=1.0 / D, bias=self.epsb[:]),
                     reads=[rss, self.R("epsb")], writes=[rrs])
                p.op("dve", lambda e, rs=rs: e.reciprocal(out=rs[:], in_=rs[:]), reads=[rrs], writes=[rrs])
                p.op("dve", lambda e, xt=xt, ot=ot, rs=rs: e.scalar_tensor_tensor(out=ot[:], in0=xt[:], scalar=rs[:], in1=self.gbc[:], op0=ALU.mult, op1=ALU.mult),
                     reads=[rx, rrs, self.R("gbc")], writes=[ro])
                p.op("sp", lambda e, ot=ot, tt=tt: e.dma_start(out=self.out[tt * 128:(tt + 1) * 128, :], in_=ot[:]),
                     reads=[ro], writes=[self.R("out_w%d" % (tt % 4))], dma=self.st_o)
            self.final_wait()

    def copy_out(self):
        p = self.p
        nc = self.nc
        xsrc = self.x_src
        with contextlib.ExitStack() as st:
            bufs = [st.enter_context(nc.sbuf_tensor("co%d" % b, [128, D], F32)) for b in range(2)]
            for tt in range(S // 128):
                b = tt % 2
                rx = self.R("co%d" % b)
                p.op("sp", lambda e, b=b, tt=tt: e.dma_start(out=bufs[b][:], in_=xsrc[tt * 128:(tt + 1) * 128, :]),
                     reads=[self.R("xres")], writes=[rx], dma=self.st_x)
                p.op("sp", lambda e, b=b, tt=tt: e.dma_start(out=self.out[tt * 128:(tt + 1) * 128, :], in_=bufs[b][:]),
                     reads=[rx], writes=[self.R("out_w%d" % (tt % 4))], dma=self.st_o)
            self.final_wait()

    def final_wait(self):
        p = self.p
        sems = self._store_waits()

        def fn(e, sems=sems):
            for sem, val in sems:
                e.wait_ge(sem, val)
            return e.nop()
        p.op("sp", fn, reads=[self.R("out_w%d" % k) for k in range(4)], writes=[self.R("done")])


def full_plan():
    def prep(l):
        out = [("prep_ffn", l, 0)]
        out.append(("prep_attn", l // 2) if l % 2 == 0 else ("prep_ssm", l // 2))
        out.append(("prep_ffn", l, 1))
        return out
    plan = prep(0)
    for l in range(DEPTH):
        if l + 1 < DEPTH:
            plan += prep(l + 1)
        plan.append(("ffn", l, 0))
        plan.append(("attn", l) if l % 2 == 0 else ("ssd", l))
        plan.append(("ffn", l, 1))
    plan.append(("final",))
    return plan


_CACHE = {}


def attn_w_layout(w):
    heads = [(h * 64) for h in range(8)] + [512, 576] + [768 + h * 64 for h in range(8)] + [1280, 1344] + [1536, 1600] + [1792, 1856]
    cols = []
    for c0 in heads:
        cols += list(range(c0, c0 + 64)) + list(range(c0 + 32, c0 + 64)) + list(range(c0, c0 + 32))
    cols += list(range(1408, 1536))
    cols += list(range(640, 768)) + list(range(1664, 1792)) + list(range(1920, 2048)) + list(range(2048, 2072))
    assert len(cols) == NAW
    return np.ascontiguousarray(w[:, :, np.asarray(cols)])


def _rope_tables():
    inv = (1.0 / (np.float32(10000.0) ** (np.arange(0, 64, 2, dtype=np.float32) / np.float32(64)))).astype(np.float32)
    ang = (np.arange(S, dtype=np.float32)[:, None] * inv[None, :]).astype(np.float32)
    c = np.cos(ang).astype(np.float32).T
    s_ = np.sin(ang).astype(np.float32).T
    return np.ascontiguousarray(np.stack([np.concatenate([c, c], 0), np.concatenate([-s_, s_], 0)], 0))


def _wsel():
    n_cmp = (S - 32) // 16 + 1
    cs = np.arange(n_cmp) * 16
    ss = np.arange(S // 64) * 64
    ov = np.minimum(cs[:, None] + 32, ss[None, :] + 64) - np.maximum(cs[:, None], ss[None, :])
    w = np.zeros((256, 64), np.float32)
    w[:n_cmp] = np.clip(ov, 0, None) / 32.0
    return np.ascontiguousarray(w.reshape(2, 128, 64).transpose(1, 0, 2))


def _selb():
    t = np.arange(S)
    cur = (t // 64)[:, None]
    jj = np.arange(64)[None, :]
    valid = jj <= cur
    forced = valid & ((jj == 0) | (jj == cur) | (jj == cur - 1))
    b = np.where(forced, 1e4, 0.0) - np.where(valid, 0.0, 1e4)
    return np.ascontiguousarray(b.astype(np.float32).reshape(32, 128, 64).transpose(1, 0, 2))


ROPE = _rope_tables()
WSEL = _wsel()
SELB = _selb()
_ii = np.arange(128)
TRI = np.stack([(_ii[:, None] <= _ii[None, :]), (_ii[:, None] > _ii[None, :]), np.ones((128, 128), bool)]).astype(np.float32)


def run_plan(plan, inputs, n_cores=8, trace=False):
    key = repr(plan)
    if key not in _CACHE:
        _CACHE[key] = Builder(plan).build()
    nc = _CACHE[key]
    x = np.ascontiguousarray(inputs["x"], dtype=np.float32)
    gains = np.concatenate([np.asarray(inputs["norm_gains"], np.float32).reshape(DEPTH * 3, D),
                            np.asarray(inputs["final_norm"], np.float32).reshape(1, D)], axis=0)
    common = {
        "gains": np.ascontiguousarray(gains),
        "ffn_w_gate": np.ascontiguousarray(inputs["ffn_w_gate"], dtype=np.float32),
        "ffn_w_up": np.ascontiguousarray(inputs["ffn_w_up"], dtype=np.float32),
        "ffn_w_down": np.ascontiguousarray(inputs["ffn_w_down"], dtype=np.float32),
        "ident": np.eye(128, dtype=np.float32),
        "tri": TRI,
        "ssm_w_in": np.ascontiguousarray(inputs["ssm_w_in"], dtype=np.float32),
        "ssm_w_out": np.ascontiguousarray(inputs["ssm_w_out"], dtype=np.float32),
        "ssm_cw": np.ascontiguousarray(np.asarray(inputs["ssm_conv_w"], np.float32).transpose(0, 2, 1).reshape(2, 32, 128, 4).transpose(0, 2, 1, 3)),
        "ssm_cb": np.ascontiguousarray(np.asarray(inputs["ssm_conv_b"], np.float32).reshape(2, 32, 128).transpose(0, 2, 1)),
        "ssm_vec": np.ascontiguousarray(np.stack([np.asarray(inputs["ssm_dt_bias"], np.float32), np.asarray(inputs["ssm_a_log"], np.float32),
                                                  np.asarray(inputs["ssm_d"], np.float32)], axis=1).reshape(2, 96)),
        "ssm_norm": np.ascontiguousarray(inputs["ssm_norm"], dtype=np.float32),
        "attn_wr": attn_w_layout(np.asarray(inputs["attn_w_in"], np.float32)),
        "attn_w_out": np.ascontiguousarray(inputs["attn_w_out"], dtype=np.float32),
        "attn_sinks": np.ascontiguousarray(inputs["attn_sinks"], dtype=np.float32),
        "rope": ROPE,
        "cmp_w1": np.ascontiguousarray(np.stack([np.asarray(inputs["cmp_k_w1"], np.float32), np.asarray(inputs["cmp_v_w1"], np.float32)], axis=1)),
        "cmp_w2": np.ascontiguousarray(np.stack([np.asarray(inputs["cmp_k_w2"], np.float32), np.asarray(inputs["cmp_v_w2"], np.float32)], axis=1)),
        "cmp_posT": np.ascontiguousarray(np.stack([np.asarray(inputs["cmp_k_pos"], np.float32).transpose(0, 2, 1),
                                                   np.asarray(inputs["cmp_v_pos"], np.float32).transpose(0, 2, 1)], axis=1)),
        "wsel": WSEL,
        "selb": SELB,
    }
    in_maps = []
    for c in range(n_cores):
        m = dict(common)
        m["x"] = x[c % 4]
        in_maps.append(m)
    res = run_bass_kernel_spmd(nc, in_maps, core_ids=list(range(n_cores)), trace=trace)
    out = np.stack([res.results[c % n_cores]["out"] for c in range(4)], axis=0)
    return out, res


def kernel(**inputs):
    out, _ = run_plan(full_plan(), inputs)
    return out.astype(np.float32)
```

```python
import contextlib
import numpy as np
import concourse.bass as bass
import concourse.mybir as mybir
from concourse.bass_utils import run_bass_kernel_spmd

F32 = mybir.dt.float32
BF16 = mybir.dt.bfloat16
AF = mybir.ActivationFunctionType
ALU = mybir.AluOpType
AX = mybir.AxisListType

D = 1024
S = 4096
DEPTH = 4
DFF = 2816
NFC = DFF // 128
EPS = 1e-6
NAW = 3608


class Res:
    __slots__ = ("name", "w", "r")

    def __init__(self, name):
        self.name = name
        self.w = None
        self.r = []


class Op:
    __slots__ = ("eng", "fn", "waits", "inc", "dma", "dsem", "dval", "cnt", "pre")


class Prog:
    ENGS = ("pe", "act", "dve", "pool", "sp")

    def __init__(self, nc, stack):
        self.nc = nc
        self.stack = stack
        self.ops = {e: [] for e in self.ENGS}
        self.esem = {e: stack.enter_context(nc.semaphore("es_" + e)) for e in self.ENGS}
        self.nsem = 5
        self.streams = []

    def new_sem(self, name):
        self.nsem += 1
        return self.stack.enter_context(self.nc.semaphore(name))

    def op(self, eng, fn, reads=(), writes=(), dma=None):
        o = Op()
        o.eng = eng
        o.fn = fn
        o.inc = False
        o.dma = dma
        o.cnt = 0
        o.pre = None
        o.dsem = None
        o.dval = 0
        deps = []
        seen = set()

        def add(d, raw):
            if d is None or id(d) in seen:
                return
            if d.dma is None and d.eng == eng:
                if eng == "pe" or not raw:
                    return
            seen.add(id(d))
            deps.append(d)

        for r in reads:
            add(r.w, True)
        for w in writes:
            add(w.w, False)
            for rr in w.r:
                add(rr, False)
        for d in deps:
            if d.dma is None:
                d.inc = True
        o.waits = deps
        if dma is not None:
            sem, val, pre = dma.next()
            o.dsem, o.dval, o.pre = sem, val, pre
            dma.ops.append(o)
        for r in reads:
            r.r.append(o)
        for w in writes:
            w.w = o
            w.r = []
        self.ops[eng].append(o)
        return o

    def barrier(self):
        deps = []
        for e in self.ENGS:
            for o in reversed(self.ops[e]):
                if o.dma is None:
                    deps.append(o)
                    break
        for st in self.streams:
            deps.extend(st.ops[-st.R:])
        for e in self.ENGS:
            o = Op()
            o.eng = e
            o.fn = lambda eng: eng.nop()
            o.inc = False
            o.dma = None
            o.cnt = 0
            o.pre = None
            o.dsem = None
            o.dval = 0
            o.waits = [d for d in deps if not (d.dma is None and d.eng == e)]
            for d in o.waits:
                if d.dma is None:
                    d.inc = True
            self.ops[e].append(o)

    def emit(self):
        nc = self.nc
        for e in self.ENGS:
            c = 0
            for o in self.ops[e]:
                if o.dma is None and o.inc:
                    c += 1
                o.cnt = c
        self.counts = {e: (len(self.ops[e]), self.ops[e][-1].cnt if self.ops[e] else 0) for e in self.ENGS}

        def body_for(ename):
            def body(eng):
                seen = {}

                def wait(sem, val):
                    k = id(sem)
                    if seen.get(k, 0) >= val:
                        return
                    seen[k] = val
                    eng.wait_ge(sem, val)

                for o in self.ops[ename]:
                    for d in o.waits:
                        if d.dma is not None:
                            wait(d.dsem, d.dval)
                        else:
                            wait(self.esem[d.eng], d.cnt)
                    if o.pre is not None and o.pre[1] > 0:
                        wait(o.pre[0], o.pre[1])
                    ins = o.fn(eng)
                    if o.dma is not None:
                        ins.then_inc(o.dsem, 16)
                    elif o.inc:
                        ins.then_inc(self.esem[ename], 1)
            return body

        with nc.Block() as block:
            block.tensor(body_for("pe"))
            block.scalar(body_for("act"))
            block.vector(body_for("dve"))
            block.gpsimd(body_for("pool"))
            block.sync(body_for("sp"))


class DmaStream:
    def __init__(self, prog, name, R):
        self.sems = [prog.new_sem("%s%d" % (name, i)) for i in range(R)]
        self.R = R
        self.k = 0
        self.ops = []
        prog.streams.append(self)

    def next(self):
        k = self.k
        self.k += 1
        sem = self.sems[k % self.R]
        return sem, 16 * (k // self.R + 1), (sem, 16 * (k // self.R))


class Builder:
    def __init__(self, plan):
        self.plan = plan
        self.nc = bass.Bass("TRN2", target_bir_lowering=False)
        self.stack = contextlib.ExitStack()
        self.res_cache = {}

    def R(self, name):
        r = self.res_cache.get(name)
        if r is None:
            r = self.res_cache[name] = Res(name)
        return r

    def dram_in(self, name, shape, dt=F32):
        return self.nc.dram_tensor(name, list(shape), dt, kind="ExternalInput").ap()

    def dram_out(self, name, shape, dt=F32):
        return self.nc.dram_tensor(name, list(shape), dt, kind="ExternalOutput").ap()

    def dram_tmp(self, name, shape, dt):
        return self.nc.dram_tensor(name, list(shape), dt).ap()

    def sb(self, name, shape, dt):
        return self.stack.enter_context(self.nc.sbuf_tensor(name, list(shape), dt))

    def ps(self, name, shape, dt):
        return self.stack.enter_context(self.nc.psum_tensor(name, list(shape), dt))

    def build(self):
        nc = self.nc
        with self.stack:
            self.p = Prog(nc, self.stack)
            self._build()
            self.p.emit()
        return nc

    def _build(self):
        p = self.p
        plan = self.plan
        self.x_in = self.dram_in("x", [S, D])
        self.gains = self.dram_in("gains", [DEPTH * 3 + 1, D])
        self.wg = self.dram_in("ffn_w_gate", [DEPTH, 2, D, DFF])
        self.wu = self.dram_in("ffn_w_up", [DEPTH, 2, D, DFF])
        self.wd = self.dram_in("ffn_w_down", [DEPTH, 2, DFF, D])
        self.ident_in = self.dram_in("ident", [128, 128])
        self.tri_in = self.dram_in("tri", [3, 128, 128])
        self.ssm_w_in = self.dram_in("ssm_w_in", [2, D, 6176])
        self.ssm_w_out = self.dram_in("ssm_w_out", [2, 2048, D])
        self.ssm_cw = self.dram_in("ssm_cw", [2, 128, 32, 4])
        self.ssm_cb = self.dram_in("ssm_cb", [2, 128, 32])
        self.ssm_vec = self.dram_in("ssm_vec", [2, 96])
        self.ssm_norm = self.dram_in("ssm_norm", [2, 2048])
        self.swin = self.dram_tmp("swin", [2, D, 6176], BF16)
        self.attn_wr = self.dram_in("attn_wr", [2, D, NAW])
        self.attn_w_out = self.dram_in("attn_w_out", [2, D, D])
        self.sinks_in = self.dram_in("attn_sinks", [2, 8])
        self.rope_in = self.dram_in("rope", [2, 64, S])
        self.cmp_w1 = self.dram_in("cmp_w1", [2, 2, 2048, 128])
        self.cmp_w2 = self.dram_in("cmp_w2", [2, 2, 128, 64])
        self.cmp_posT = self.dram_in("cmp_posT", [2, 2, 64, 32])
        self.wsel_in = self.dram_in("wsel", [128, 2, 64])
        self.selb_in = self.dram_in("selb", [128, 32, 64])
        self.awin = self.dram_tmp("awin", [2, D, NAW], BF16)
        self.qT = self.dram_tmp("qT", [16, 64, S], BF16)
        self.out = self.dram_out("out", [S, D])
        self.xres = self.dram_tmp("xres", [S, D], F32)
        self.wgt = self.dram_tmp("wgt", [DEPTH, 2, 6, 128, 8, 512], BF16)
        self.wut = self.dram_tmp("wut", [DEPTH, 2, 6, 128, 8, 512], BF16)
        self.wdt = self.dram_tmp("wdt", [DEPTH, 2, DFF, D], BF16)

        self.st_const = DmaStream(p, "dc", 1)
        self.st_prep = DmaStream(p, "dp", 4)
        self.st_x = DmaStream(p, "dx", 4)
        self.st_w = DmaStream(p, "dw", 4)
        self.st_o = DmaStream(p, "do", 4)

        self.ident_f = self.sb("ident_f", [128, 128], F32)
        self.ident_b = self.sb("ident_b", [128, 128], BF16)
        self.gbc = self.sb("gbc", [128, D], F32)
        self.epsb = self.sb("epsb", [128, 1], F32)
        r_ident = self.R("ident")
        p.op("sp", lambda e: e.dma_start(out=self.ident_f[:], in_=self.ident_in), writes=[r_ident], dma=self.st_const)
        p.op("dve", lambda e: e.tensor_copy(out=self.ident_b[:], in_=self.ident_f[:]), reads=[r_ident], writes=[self.R("ident_b")])
        p.op("dve", lambda e: e.memset(self.epsb[:], EPS), writes=[self.R("epsb")])

        self.pbank = [self.ps("pb%d" % i, [128, 512], F32) for i in range(8)]
        self.ptrh = [self.pbank[6 + i][:, :].bitcast(BF16)[:, 0:512] for i in range(2)]
        self.r_pb = [self.R("pb%d" % i) for i in range(8)]
        self.r_ptr = [self.r_pb[6], self.r_pb[7]]

        self.x_src = self.x_in
        for ph in plan:
            kind = ph[0]
            if kind == "prep_ffn":
                self.prep_ffn(ph[1], ph[2])
            elif kind == "ffn":
                self.dbg_stage = ph[3] if len(ph) > 3 else 99
                self.ffn(ph[1], ph[2])
            elif kind == "prep_ssm":
                self.prep_ssm(ph[1])
            elif kind == "ssd":
                self.ssd(ph[1])
            elif kind == "prep_attn":
                self.prep_attn(ph[1])
            elif kind == "attn":
                self.attn_dbg = ph[2] if len(ph) > 2 else None
                self.attn(ph[1])
            elif kind == "final":
                self.final_norm()
            elif kind == "copy_out":
                self.copy_out()
            else:
                raise ValueError(kind)

    def load_gain(self, row):
        p = self.p
        src = self.gains[row:row + 1, :].broadcast_to([128, D])
        p.op("sp", lambda e: e.dma_start(out=self.gbc[:], in_=src), writes=[self.R("gbc")], dma=self.st_const)

    def prep_ffn(self, l, i):
        p = self.p
        for (src, dst, nm) in ((self.wg, self.wgt, "g"), (self.wu, self.wut, "u")):
            for blk in range(6):
                w = 512 if blk < 5 else 256
                s_ap = src[l, i, :, blk * 512:blk * 512 + w].rearrange("(kc p) m -> p kc m", p=128)
                d_ap = dst[l, i, blk, :, :, 0:w]
                p.op("pool", lambda e, s_ap=s_ap, d_ap=d_ap: e.dma_start(out=d_ap, in_=s_ap),
                     writes=[self.R("wt_%s_%d_%d_%d" % (nm, l, i, blk))], dma=self.st_prep)
        for q in range(4):
            rows = DFF // 4
            s_ap = self.wd[l, i, q * rows:(q + 1) * rows, :]
            d_ap = self.wdt[l, i, q * rows:(q + 1) * rows, :]
            p.op("pool", lambda e, s_ap=s_ap, d_ap=d_ap: e.dma_start(out=d_ap, in_=s_ap),
                 writes=[self.R("wt_d_%d_%d_%d" % (l, i, q))], dma=self.st_prep)

    def norm_transpose(self, xsrc, t0, ntile, hT, r_hT, xt_bufs, tag):
        p = self.p
        for j in range(ntile):
            tt = t0 + j
            b = j % 2
            xt, hb, sq, ss, rs = xt_bufs[b]
            rx = self.R("%s_xt%d" % (tag, b))
            rh = self.R("%s_hb%d" % (tag, b))
            rss = self.R("%s_ss%d" % (tag, b))
            rsq = self.R("%s_sq%d" % (tag, b))
            p.op("sp", lambda e, xt=xt, tt=tt: e.dma_start(out=xt[:], in_=xsrc[tt * 128:(tt + 1) * 128, :]),
                 reads=[self.R("xres")], writes=[rx], dma=self.st_x)
            p.op("act", lambda e, xt=xt, sq=sq, ss=ss: e.activation(out=sq[:], in_=xt[:], func=AF.Square, accum_out=ss[:]),
                 reads=[rx], writes=[rsq, rss])
            p.op("act", lambda e, ss=ss, rs=rs: e.activation(out=rs[:], in_=ss[:], func=AF.Sqrt, scale=1.0 / D, bias=self.epsb[:]),
                 reads=[rss, self.R("epsb")], writes=[self.R("%s_rs%d" % (tag, b))])
            p.op("dve", lambda e, rs=rs: e.reciprocal(out=rs[:], in_=rs[:]),
                 reads=[self.R("%s_rs%d" % (tag, b))], writes=[self.R("%s_rs%d" % (tag, b))])
            p.op("dve", lambda e, xt=xt, hb=hb, rs=rs: e.scalar_tensor_tensor(out=hb[:], in0=xt[:], scalar=rs[:], in1=self.gbc[:], op0=ALU.mult, op1=ALU.mult),
                 reads=[rx, self.R("%s_rs%d" % (tag, b)), self.R("gbc")], writes=[rh])
            for half in range(2):
                rp = self.r_ptr[half]

                def tr(e, hb=hb, half=half):
                    ins = None
                    for q in range(4):
                        kc = half * 4 + q
                        ins = e.transpose(out=self.ptrh[half][:, q * 128:(q + 1) * 128],
                                          in_=hb[:, kc * 128:(kc + 1) * 128], identity=self.ident_b[:])
                    return ins
                p.op("pe", tr, reads=[rh, self.R("ident_b")], writes=[rp])
                dst = hT[:, half * 4:(half + 1) * 4, j * 128:(j + 1) * 128]
                srcp = self.ptrh[half][:, :].rearrange("p (q m) -> p q m", q=4)
                eng = "act" if half == 0 else "dve"
                if eng == "act":
                    p.op("act", lambda e, dst=dst, srcp=srcp: e.copy(out=dst, in_=srcp), reads=[rp], writes=[r_hT[j][half]])
                else:
                    p.op("dve", lambda e, dst=dst, srcp=srcp: e.tensor_copy(out=dst, in_=srcp), reads=[rp], writes=[r_hT[j][half]])

    def ffn(self, l, i):
        p = self.p
        nc = self.nc
        TB = 1024
        NTB = S // TB
        xsrc = self.x_src
        with contextlib.ExitStack() as st:
            def sb(name, shape, dt):
                return st.enter_context(nc.sbuf_tensor(name, list(shape), dt))
            tag = "f%d%d" % (l, i)
            wd_sb = sb(tag + "wd", [128, NFC, D], BF16)
            aT = sb(tag + "aT", [128, NFC, TB], BF16)
            hT = sb(tag + "hT", [128, 8, TB], BF16)
            wgu = [(sb(tag + "wg%d" % b, [128, 8, 512], BF16), sb(tag + "wu%d" % b, [128, 8, 512], BF16)) for b in range(2)]
            xt_bufs = [(sb(tag + "xt%d" % b, [128, D], F32), sb(tag + "hb%d" % b, [128, D], BF16),
                        sb(tag + "sq%d" % b, [128, D], BF16), sb(tag + "ss%d" % b, [128, 1], F32),
                        sb(tag + "rs%d" % b, [128, 1], F32)) for b in range(2)]
            sg = [sb(tag + "sg%d" % b, [128, 512], F32) for b in range(2)]
            xo = [sb(tag + "xo%d" % b, [128, D], F32) for b in range(2)]
            r_wd = [self.R(tag + "wd0"), self.R(tag + "wd1")]
            r_aT = self.R(tag + "aT")
            r_hT = [[self.R(tag + "hT%d_%d" % (jj, hh)) for hh in range(2)] for jj in range(TB // 128)]
            r_wg = [self.R(tag + "wgs%d" % b) for b in range(2)]
            r_wu = [self.R(tag + "wus%d" % b) for b in range(2)]
            r_sg = [self.R(tag + "sg%d" % b) for b in range(2)]
            r_xo = [self.R(tag + "xo%d" % b) for b in range(2)]

            self.load_gain(l * 3 + (0 if i == 0 else 2))
            for q in range(2):
                fa, fb = q * 11, (q + 1) * 11
                src = self.wdt[l, i, fa * 128:fb * 128, :].rearrange("(fc p) m -> p fc m", p=128)
                p.op("sp", lambda e, src=src, fa=fa, fb=fb: e.dma_start(out=wd_sb[:, fa:fb, :], in_=src),
                     reads=[self.R("wt_d_%d_%d_%d" % (l, i, 2 * q)), self.R("wt_d_%d_%d_%d" % (l, i, 2 * q + 1))], writes=[r_wd[q]], dma=self.st_w)

            wcount = 0
            for tb in range(NTB):
                t0 = tb * (TB // 128)
                self.norm_transpose(xsrc, t0, TB // 128, hT, r_hT, xt_bufs, tag)
                if self.dbg_stage <= 1:
                    continue
                for blk in range(6):
                    w = 512 if blk < 5 else 256
                    b = wcount % 2
                    wcount += 1
                    wgs, wus = wgu[b]
                    p.op("sp", lambda e, wgs=wgs, blk=blk, w=w: e.dma_start(out=wgs[:, :, 0:w], in_=self.wgt[l, i, blk, :, :, 0:w]),
                         reads=[self.R("wt_g_%d_%d_%d" % (l, i, blk))], writes=[r_wg[b]], dma=self.st_w)
                    p.op("sp", lambda e, wus=wus, blk=blk, w=w: e.dma_start(out=wus[:, :, 0:w], in_=self.wut[l, i, blk, :, :, 0:w]),
                         reads=[self.R("wt_u_%d_%d_%d" % (l, i, blk))], writes=[r_wu[b]], dma=self.st_w)
                    for m in range(w // 128):
                        fc = blk * 4 + m
                        for half in range(TB // 512):
                            pg = (fc * 2 + half) % 2
                            bg, bu = self.pbank[pg * 2], self.pbank[pg * 2 + 1]
                            rg, ru = self.r_pb[pg * 2], self.r_pb[pg * 2 + 1]

                            def mm(e, wt, bank, m=m, half=half):
                                ins = None
                                for kc in range(8):
                                    ins = e.matmul(bank[:, :], lhsT=wt[:, kc, m * 128:(m + 1) * 128],
                                                   rhs=hT[:, kc, half * 512:(half + 1) * 512],
                                                   start=(kc == 0), stop=(kc == 7))
                                return ins
                            hres = [r_hT[half * 4 + jj][hh] for jj in range(4) for hh in range(2)]
                            p.op("pe", lambda e, wgs=wgs, bg=bg, mm=mm: mm(e, wgs, bg), reads=[r_wg[b]] + hres, writes=[rg])
                            p.op("pe", lambda e, wus=wus, bu=bu, mm=mm: mm(e, wus, bu), reads=[r_wu[b]] + hres, writes=[ru])
                            sgb = sg[pg]
                            p.op("act", lambda e, sgb=sgb, bg=bg: e.activation(out=sgb[:], in_=bg[:, :], func=AF.Silu),
                                 reads=[rg], writes=[r_sg[pg]])
                            dst = aT[:, fc, half * 512:(half + 1) * 512]
                            p.op("dve", lambda e, dst=dst, sgb=sgb, bu=bu: e.tensor_tensor(out=dst, in0=sgb[:], in1=bu[:, :], op=ALU.mult),
                                 reads=[r_sg[pg], ru], writes=[r_aT])
                if self.dbg_stage <= 2:
                    continue
                for j in range(TB // 128):
                    tt = t0 + j
                    xb = j % 2
                    xt = xt_bufs[xb][0]
                    rx = self.R("%s_xt%d" % (tag, xb))
                    p.op("sp", lambda e, xt=xt, tt=tt: e.dma_start(out=xt[:], in_=xsrc[tt * 128:(tt + 1) * 128, :]),
                         reads=[self.R("xres")], writes=[rx], dma=self.st_x)
                    for ch in range(2):
                        pb = 4 + (j * 2 + ch) % 2
                        bank, rb = self.pbank[pb], self.r_pb[pb]

                        def mmd(e, bank=bank, j=j, ch=ch):
                            ins = None
                            for fc in range(NFC):
                                ins = e.matmul(bank[:, :], lhsT=aT[:, fc, j * 128:(j + 1) * 128],
                                               rhs=wd_sb[:, fc, ch * 512:(ch + 1) * 512],
                                               start=(fc == 0), stop=(fc == NFC - 1))
                            return ins
                        p.op("pe", mmd, reads=[r_aT] + r_wd, writes=[rb])
                        xob = xo[xb]
                        p.op("dve", lambda e, xob=xob, bank=bank, xt=xt, ch=ch: e.scalar_tensor_tensor(
                            out=xob[:, ch * 512:(ch + 1) * 512], in0=bank[:, :], scalar=0.5, in1=xt[:, ch * 512:(ch + 1) * 512],
                            op0=ALU.mult, op1=ALU.add), reads=[rb, rx], writes=[r_xo[xb]])
                    p.op("sp", lambda e, xob=xob, tt=tt: e.dma_start(out=self.xres[tt * 128:(tt + 1) * 128, :], in_=xob[:]),
                         reads=[r_xo[xb]], writes=[self.R("xres_w%d" % (tt % 4))], dma=self.st_o)
            self.phase_barrier()
        self.x_src = self.xres

    def _store_waits(self):
        st = self.st_o
        return [(st.sems[idx % st.R], 16 * (idx // st.R + 1)) for idx in range(max(0, st.k - st.R), st.k)]

    def phase_barrier(self):
        p = self.p
        sems = self._store_waits()

        def fn(e, sems=sems):
            for sem, val in sems:
                e.wait_ge(sem, val)
            return e.nop()
        p.op("sp", fn, reads=[self.R("xres_w%d" % k) for k in range(4)], writes=[self.R("xres")])
        p.barrier()

    def prep_ssm(self, j):
        p = self.p
        for (c0, c1) in ((0, 2048), (2048, 4096), (4096, 6144), (6144, 6176)):
            for hh in range(2):
                s_ap = self.ssm_w_in[j, hh * 512:(hh + 1) * 512, c0:c1]
                d_ap = self.swin[j, hh * 512:(hh + 1) * 512, c0:c1]
                p.op("pool", lambda e, s_ap=s_ap, d_ap=d_ap: e.dma_start(out=d_ap, in_=s_ap),
                     writes=[self.R("swin%d_%d_%d" % (j, c0, hh))], dma=self.st_prep)

    def swin_res(self, j, c0):
        base = (c0 // 2048) * 2048 if c0 < 6144 else 6144
        return [self.R("swin%d_%d_%d" % (j, base, hh)) for hh in range(2)]

    def ssd(self, l):
        p = self.p
        nc = self.nc
        j = l // 2
        TBK = 256
        NBLK = S // TBK
        NT = TBK // 128
        xsrc = self.x_src
        with contextlib.ExitStack() as st:
            def sb(name, shape, dt):
                return st.enter_context(nc.sbuf_tensor(tag + name, list(shape), dt))
            tag = "s%d" % l
            R = lambda n: self.R(tag + n)
            L1 = sb("L1", [128, 128], F32)
            L2 = sb("L2", [128, 128], F32)
            ONES = sb("ONES", [128, 128], F32)
            cw = sb("cw", [128, 32, 4], F32)
            cb = sb("cb", [128, 32], F32)
            vec = sb("vec", [128, 3, 32], F32)
            a_bc = sb("a_bc", [128, 32], F32)
            Dbc = sb("Dbc", [128, 32, 1], F32)
            nw = sb("nw", [128, 2048], F32)
            wout = sb("wout", [128, 16, 1024], BF16)
            wdt = sb("wdt", [128, 8, 32], BF16)
            halo = sb("halo", [128, 32, 3], F32)
            state = sb("state", [128, 2048], F32)
            state_bf = sb("state_bf", [128, 2048], BF16)
            hT = sb("hT", [128, 8, TBK], BF16)
            wx = [sb("wx%d" % b, [128, 8, 512], BF16) for b in range(2)]
            xin = [sb("xin%d" % b, [128, TBK + 3], F32) for b in range(3)]
            acc = [sb("acc%d" % b, [128, TBK], F32) for b in range(3)]
            xsb = [sb("xsb%d" % b, [128, TBK], BF16) for b in range(2)]
            BT = sb("BT", [128, 8, TBK], BF16)
            CT = sb("CT", [128, 8, TBK], BF16)
            Btok = sb("Btok", [128, NT, 1024], BF16)
            xs_tok = sb("xs_tok", [128, NT, 2048], BF16)
            sz = sb("sz", [128, NT, 2048], BF16)
            xt_bufs = [(sb("xt%d" % b, [128, D], F32), sb("hb%d" % b, [128, D], BF16),
                        sb("sq%d" % b, [128, D], BF16), sb("ss%d" % b, [128, 1], F32),
                        sb("rs%d" % b, [128, 1], F32)) for b in range(2)]
            dtv = sb("dtv", [128, 32], F32)
            dt3 = sb("dt3", [128, 32, 1], F32)
            da = sb("da", [128, 32], F32)
            eall = sb("eall", [128, 96], F32)
            ea3 = sb("ea3", [128, 32, 1], F32)
            w23 = sb("w23", [128, 32, 1], F32)
            xdt = sb("xdt", [128, 2048], BF16)
            xdec = sb("xdec", [128, 2048], BF16)
            ybuf = sb("ybuf", [128, 2048], F32)
            t3 = sb("t3", [128, 2048], F32)
            gnb = sb("gnb", [128, 2048], BF16)
            gnT = sb("gnT", [128, 16, 128], BF16)
            Eg = [sb("Eg%d" % b, [128, 4, 128], F32) for b in range(2)]
            MT = [sb("MT%d" % b, [128, 4, 128], BF16) for b in range(2)]
            GTm = [sb("GTm%d" % b, [128, 1, 128], F32) for b in range(3)]
            ybufD = [sb("ybufD%d" % b, [128, 256], F32) for b in range(2)]
            Ada4 = [sb("Ada4_%d" % b, [128, 4, 128], F32) for b in range(2)]
            ssg = sb("ssg", [128, 8], F32)
            rsg = sb("rsg", [128, 8, 1], F32)
            xo = sb("xo", [128, D], F32)
            xres_t = sb("xres_t", [128, D], F32)

            p.op("sp", lambda e: e.dma_start(out=L1[:], in_=self.tri_in[0]), writes=[R("L1")], dma=self.st_const)
            p.op("sp", lambda e: e.dma_start(out=L2[:], in_=self.tri_in[1]), writes=[R("L2")], dma=self.st_const)
            p.op("sp", lambda e: e.dma_start(out=ONES[:], in_=self.tri_in[2]), writes=[R("ONES")], dma=self.st_const)
            p.op("sp", lambda e: e.dma_start(out=cw[:], in_=self.ssm_cw[j]), writes=[R("cw")], dma=self.st_const)
            p.op("sp", lambda e: e.dma_start(out=cb[:], in_=self.ssm_cb[j]), writes=[R("cb")], dma=self.st_const)
            p.op("sp", lambda e: e.dma_start(out=vec[:].rearrange("p a b -> p (a b)"),
                                             in_=self.ssm_vec[j:j + 1, :].broadcast_to([128, 96])), writes=[R("vec")], dma=self.st_const)
            p.op("sp", lambda e: e.dma_start(out=nw[:], in_=self.ssm_norm[j:j + 1, :].broadcast_to([128, 2048])), writes=[R("nw")], dma=self.st_const)
            for q in range(4):
                src = self.ssm_w_out[j, q * 512:(q + 1) * 512, :].rearrange("(k p) m -> p k m", p=128)
                p.op("pool", lambda e, src=src, q=q: e.dma_start(out=wout[:, q * 4:(q + 1) * 4, :], in_=src), writes=[R("wout%d" % q)], dma=self.st_prep)
            r_wout = [R("wout%d" % q) for q in range(4)]
            p.op("sp", lambda e: e.dma_start(out=wdt[:], in_=self.swin[j, :, 6144:6176].rearrange("(k p) m -> p k m", p=128)),
                 reads=self.swin_res(j, 6144), writes=[R("wdt")], dma=self.st_const)
            p.op("act", lambda e: e.activation(out=a_bc[:], in_=vec[:, 1, :], func=AF.Exp), reads=[R("vec")], writes=[R("a_bc")])
            p.op("dve", lambda e: e.tensor_scalar(out=a_bc[:], in0=a_bc[:], scalar1=-1.0, scalar2=None, op0=ALU.mult), reads=[R("a_bc")], writes=[R("a_bc")])
            p.op("dve", lambda e: e.tensor_copy(out=Dbc[:, :, 0], in_=vec[:, 2, :]), reads=[R("vec")], writes=[R("Dbc")])
            p.op("pool", lambda e: e.memset(halo[:], 0.0), writes=[R("halo%d" % cc_) for cc_ in range(32)])
            p.op("pool", lambda e: e.memset(state[:], 0.0), writes=[R("state")])
            p.op("pool", lambda e: e.memset(state_bf[:], 0.0), writes=[R("state_bf")])
            self.load_gain(l * 3 + 1)

            r_hT = [[R("hT%d_%d" % (jj, hh)) for hh in range(2)] for jj in range(NT)]
            hres_all = [r_hT[jj][hh] for jj in range(NT) for hh in range(2)]
            wxc = 0
            cvc = 0
            pendB = []
            tails = []
            for blk in range(NBLK):
                t0 = blk * NT
                self.norm_transpose(xsrc, t0, NT, hT, r_hT, xt_bufs, tag)
                for wgI in range(8):
                    b = wxc % 2
                    wxc += 1
                    c0 = 2048 + wgI * 512
                    src = self.swin[j, :, c0:c0 + 512].rearrange("(k p) m -> p k m", p=128)
                    p.op("sp", lambda e, src=src, b=b: e.dma_start(out=wx[b][:], in_=src),
                         reads=self.swin_res(j, c0), writes=[R("wx%d" % b)], dma=self.st_w)
                    for m in range(4):
                        cc = wgI * 4 + m
                        pb = cc % 2
                        bank, rb = self.pbank[pb], self.r_pb[pb]

                        def mm(e, b=b, m=m, bank=bank):
                            for kc in range(8):
                                ins = e.matmul(bank[:, 0:TBK], lhsT=wx[b][:, kc, m * 128:(m + 1) * 128], rhs=hT[:, kc, :],
                                               start=(kc == 0), stop=(kc == 7))
                            return ins
                        p.op("pe", mm, reads=[R("wx%d" % b)] + hres_all, writes=[rb])
                        cbuf = cvc % 3
                        cvc += 1
                        xi, ac = xin[cbuf], acc[cbuf]
                        rxi, rac = R("xin%d" % cbuf), R("acc%d" % cbuf)
                        rhalo = R("halo%d" % cc)
                        p.op("pool", lambda e, xi=xi, cc=cc: e.tensor_copy(out=xi[:, 0:3], in_=halo[:, cc, :]), reads=[rhalo], writes=[rxi])
                        p.op("act", lambda e, xi=xi, bank=bank: e.copy(out=xi[:, 3:3 + TBK], in_=bank[:, 0:TBK]), reads=[rb], writes=[rxi])
                        p.op("pool", lambda e, xi=xi, cc=cc: e.tensor_copy(out=halo[:, cc, :], in_=xi[:, TBK:TBK + 3]), reads=[rxi], writes=[rhalo])
                        p.op("act", lambda e, xi=xi, ac=ac, cc=cc: e.activation(out=ac[:], in_=xi[:, 0:TBK], func=AF.Identity, scale=cw[:, cc, 0:1], bias=cb[:, cc:cc + 1]),
                             reads=[rxi, R("cw"), R("cb")], writes=[rac])
                        for w in range(1, 4):
                            p.op("dve", lambda e, xi=xi, ac=ac, cc=cc, w=w: e.scalar_tensor_tensor(out=ac[:], in0=xi[:, w:w + TBK], scalar=cw[:, cc, w:w + 1], in1=ac[:],
                                                                                                 op0=ALU.mult, op1=ALU.add), reads=[rxi, rac, R("cw")], writes=[rac])

                        def stageB(cc=cc, ac=ac, rac=rac, cbuf=cbuf):
                            if cc < 16:
                                xb_, rxb = xsb[cbuf % 2], R("xsb%d" % (cbuf % 2))
                                p.op("act", lambda e, ac=ac, xb_=xb_: e.activation(out=xb_[:], in_=ac[:], func=AF.Silu), reads=[rac], writes=[rxb])
                                hp = cc % 2
                                rp = self.r_ptr[hp]

                                def tr(e, xb_=xb_, hp=hp):
                                    for q in range(NT):
                                        ins = e.transpose(out=self.ptrh[hp][:, q * 128:(q + 1) * 128], in_=xb_[:, q * 128:(q + 1) * 128], identity=self.ident_b[:])
                                    return ins
                                p.op("pe", tr, reads=[rxb, self.R("ident_b")], writes=[rp])
                                dst = xs_tok[:, :, cc * 128:(cc + 1) * 128]
                                srcp = self.ptrh[hp][:, 0:NT * 128].rearrange("p (q m) -> p q m", q=NT)
                                p.op("dve", lambda e, dst=dst, srcp=srcp: e.tensor_copy(out=dst, in_=srcp), reads=[rp], writes=[R("xs_tok")])
                            elif cc < 24:
                                g = cc - 16
                                p.op("act", lambda e, ac=ac, g=g: e.activation(out=BT[:, g, :], in_=ac[:], func=AF.Silu), reads=[rac], writes=[R("BT%d" % g)])
                                hp = cc % 2
                                rp = self.r_ptr[hp]

                                def tr(e, g=g, hp=hp):
                                    for q in range(NT):
                                        ins = e.transpose(out=self.ptrh[hp][:, q * 128:(q + 1) * 128], in_=BT[:, g, q * 128:(q + 1) * 128], identity=self.ident_b[:])
                                    return ins
                                p.op("pe", tr, reads=[R("BT%d" % g), self.R("ident_b")], writes=[rp])
                                dst = Btok[:, :, g * 128:(g + 1) * 128]
                                srcp = self.ptrh[hp][:, 0:NT * 128].rearrange("p (q m) -> p q m", q=NT)
                                p.op("dve", lambda e, dst=dst, srcp=srcp: e.tensor_copy(out=dst, in_=srcp), reads=[rp], writes=[R("Btok")])
                            else:
                                g = cc - 24
                                p.op("act", lambda e, ac=ac, g=g: e.activation(out=CT[:, g, :], in_=ac[:], func=AF.Silu), reads=[rac], writes=[R("CT%d" % g)])
                        pendB.append(stageB)
                        while len(pendB) > 2:
                            pendB.pop(0)()
                        if cc == 3:
                            while tails:
                                tails.pop(0)()
                while pendB:
                    pendB.pop(0)()
                for zc in range(4):
                    b = wxc % 2
                    wxc += 1
                    c0 = zc * 512
                    src = self.swin[j, :, c0:c0 + 512].rearrange("(k p) m -> p k m", p=128)
                    p.op("sp", lambda e, src=src, b=b: e.dma_start(out=wx[b][:], in_=src),
                         reads=self.swin_res(j, c0), writes=[R("wx%d" % b)], dma=self.st_w)
                    for q in range(NT):
                        pb = (zc * NT + q) % 2
                        bank, rb = self.pbank[pb], self.r_pb[pb]

                        def mmz(e, b=b, q=q, bank=bank):
                            for kc in range(8):
                                ins = e.matmul(bank[:, :], lhsT=hT[:, kc, q * 128:(q + 1) * 128], rhs=wx[b][:, kc, :], start=(kc == 0), stop=(kc == 7))
                            return ins
                        p.op("pe", mmz, reads=[R("wx%d" % b)] + r_hT[q], writes=[rb])
                        p.op("act", lambda e, q=q, zc=zc, bank=bank: e.activation(out=sz[:, q, zc * 512:(zc + 1) * 512], in_=bank[:, :], func=AF.Silu),
                             reads=[rb], writes=[R("sz%d" % q)])
                for q in range(NT):
                    tt = t0 + q
                    tsl = slice(q * 128, (q + 1) * 128)
                    b0, rb0 = self.pbank[0], self.r_pb[0]

                    def mmdt(e, q=q):
                        for kc in range(8):
                            ins = e.matmul(b0[:, 0:32], lhsT=hT[:, kc, q * 128:(q + 1) * 128], rhs=wdt[:, kc, :], start=(kc == 0), stop=(kc == 7))
                        return ins
                    p.op("pe", mmdt, reads=[R("wdt")] + r_hT[q], writes=[rb0])
                    p.op("dve", lambda e: e.tensor_tensor(out=dtv[:], in0=b0[:, 0:32], in1=vec[:, 0, :], op=ALU.add), reads=[rb0, R("vec")], writes=[R("dtv")])
                    p.op("act", lambda e: e.activation(out=dtv[:], in_=dtv[:], func=AF.Exp), reads=[R("dtv")], writes=[R("dtv")])
                    p.op("act", lambda e: e.activation(out=dt3[:, :, 0], in_=dtv[:], func=AF.Ln, bias=1.0), reads=[R("dtv")], writes=[R("dt3")])
                    p.op("dve", lambda e: e.tensor_tensor(out=da[:], in0=dt3[:, :, 0], in1=a_bc[:], op=ALU.mult), reads=[R("dt3"), R("a_bc")], writes=[R("da")])

                    def mmcs(e):
                        e.matmul(b0[:, 32:64], lhsT=L1[:], rhs=da[:], start=True, stop=True)
                        e.matmul(b0[:, 64:96], lhsT=L2[:], rhs=da[:], start=True, stop=True)
                        return e.matmul(b0[:, 96:128], lhsT=ONES[:], rhs=da[:], start=True, stop=True)
                    p.op("pe", mmcs, reads=[R("da"), R("L1"), R("L2"), R("ONES")], writes=[rb0])
                    p.op("act", lambda e: e.activation(out=eall[:], in_=b0[:, 32:128], func=AF.Exp), reads=[rb0], writes=[R("eall")])
                    p.op("dve", lambda e: e.tensor_copy(out=ea3[:, :, 0], in_=eall[:, 0:32]), reads=[R("eall")], writes=[R("ea3")])
                    p.op("dve", lambda e: e.tensor_tensor(out=w23[:, :, 0], in0=dt3[:, :, 0], in1=eall[:, 32:64], op=ALU.mult), reads=[R("dt3"), R("eall")], writes=[R("w23")])
                    xs3 = xs_tok[:, q, :].rearrange("p (h d) -> p h d", h=32)
                    p.op("dve", lambda e, xs3=xs3: e.tensor_tensor(out=xdt[:].rearrange("p (h d) -> p h d", h=32), in0=xs3, in1=dt3[:].to_broadcast([128, 32, 64]), op=ALU.mult),
                         reads=[R("xs_tok"), R("dt3")], writes=[R("xdt")])
                    p.op("pool", lambda e, xs3=xs3: e.tensor_tensor(out=xdec[:].rearrange("p (h d) -> p h d", h=32), in0=xs3, in1=w23[:].to_broadcast([128, 32, 64]), op=ALU.mult),
                         reads=[R("xs_tok"), R("w23")], writes=[R("xdec")])
                    p.op("pool", lambda e, xs3=xs3: e.tensor_tensor(out=t3[:].rearrange("p (h d) -> p h d", h=32), in0=xs3, in1=Dbc[:].to_broadcast([128, 32, 64]), op=ALU.mult),
                         reads=[R("xs_tok"), R("Dbc")], writes=[R("t3")])
                    def S0(g, tsl=tsl):
                        gb, g3 = g % 2, g % 3
                        b1, rb1 = self.pbank[1], self.r_pb[1]
                        gcol = slice((g % 4) * 128, (g % 4 + 1) * 128)
                        p.op("pe", lambda e, g=g, gcol=gcol, tsl=tsl: e.matmul(b1[:, gcol], lhsT=BT[:, g, tsl], rhs=CT[:, g, tsl], start=True, stop=True),
                             reads=[R("BT%d" % g), R("CT%d" % g)], writes=[rb1])
                        p.op("dve", lambda e, g3=g3, gcol=gcol: e.tensor_tensor(out=GTm[g3][:, 0, :], in0=b1[:, gcol], in1=L1[:], op=ALU.mult),
                             reads=[rb1, R("L1")], writes=[R("GTm%d" % g3)])
                        for r in range(4):
                            h = 4 * g + r
                            p.op("act", lambda e, gb=gb, r=r, h=h: e.activation(out=Ada4[gb][:, r, :], in_=L2[:], func=AF.Copy, scale=da[:, h:h + 1]),
                                 reads=[R("L2"), R("da")], writes=[R("Ada4_%d" % gb)])

                    def S1(g):
                        gb = g % 2
                        bs, rbs = self.pbank[2 + gb], self.r_pb[2 + gb]

                        def mmseg(e, gb=gb, bs=bs):
                            for r in range(4):
                                ins = e.matmul(bs[:, r * 128:(r + 1) * 128], lhsT=Ada4[gb][:, r, :], rhs=L1[:], start=True, stop=True)
                            return ins
                        p.op("pe", mmseg, reads=[R("Ada4_%d" % gb), R("L1")], writes=[rbs])
                        p.op("act", lambda e, gb=gb, bs=bs: e.activation(out=Eg[gb][:].rearrange("p r l -> p (r l)"), in_=bs[:, :], func=AF.Exp),
                             reads=[rbs], writes=[R("Eg%d" % gb)])

                    def S2(g):
                        gb, g3 = g % 2, g % 3
                        p.op("dve", lambda e, gb=gb, g3=g3: e.tensor_tensor(out=MT[gb][:], in0=Eg[gb][:], in1=GTm[g3][:].to_broadcast([128, 4, 128]), op=ALU.mult),
                             reads=[R("Eg%d" % gb), R("GTm%d" % g3)], writes=[R("MT%d" % gb)])

                    def S3(g, tsl=tsl, q=q):
                        gb = g % 2
                        by, rby = self.pbank[4 + gb], self.r_pb[4 + gb]

                        def mmy(e, g=g, gb=gb, by=by, tsl=tsl):
                            for r in range(4):
                                h = 4 * g + r
                                e.matmul(by[:, r * 64:(r + 1) * 64], lhsT=MT[gb][:, r, :], rhs=xdt[:, h * 64:(h + 1) * 64], start=True, stop=True)
                            return e.matmul(by[:, 256:512], lhsT=CT[:, g, tsl], rhs=state_bf[:, g * 256:(g + 1) * 256], start=True, stop=True)
                        p.op("pe", mmy, reads=[R("MT%d" % gb), R("xdt"), R("CT%d" % g), R("state_bf")], writes=[rby])
                        p.op("pe", lambda e, g=g, q=q: e.matmul(b0[:, 256:512], lhsT=Btok[:, q, g * 128:(g + 1) * 128], rhs=xdec[:, g * 256:(g + 1) * 256], start=True, stop=True),
                             reads=[R("Btok"), R("xdec")], writes=[rb0])
                        for r in range(4):
                            h = 4 * g + r
                            p.op("dve", lambda e, h=h, r=r: e.scalar_tensor_tensor(out=state[:, h * 64:(h + 1) * 64], in0=state[:, h * 64:(h + 1) * 64],
                                                                                     scalar=eall[:, 64 + h:65 + h], in1=b0[:, 256 + r * 64:256 + (r + 1) * 64],
                                                                                     op0=ALU.mult, op1=ALU.add), reads=[R("state"), R("eall"), rb0], writes=[R("state")])
                        p.op("act", lambda e, gb=gb, by=by: e.copy(out=ybufD[gb][:], in_=by[:, 0:256]), reads=[rby], writes=[R("ybufD%d" % gb)])
                        yg = ybuf[:, g * 256:(g + 1) * 256].rearrange("p (r d) -> p r d", r=4)
                        p.op("dve", lambda e, yg=yg, by=by, g=g: e.tensor_tensor(out=yg, in0=by[:, 256:512].rearrange("p (r d) -> p r d", r=4),
                                                                               in1=ea3[:, 4 * g:4 * g + 4, :].to_broadcast([128, 4, 64]), op=ALU.mult),
                             reads=[rby, R("ea3")], writes=[R("ybuf")])
                        p.op("pool", lambda e, g=g, gb=gb: e.tensor_tensor(out=ybuf[:, g * 256:(g + 1) * 256], in0=ybuf[:, g * 256:(g + 1) * 256], in1=ybufD[gb][:], op=ALU.add),
                             reads=[R("ybuf"), R("ybufD%d" % gb)], writes=[R("ybuf")])

                    for k in range(11):
                        if k < 8:
                            S0(k)
                        if 0 <= k - 1 < 8:
                            S1(k - 1)
                        if 0 <= k - 2 < 8:
                            S2(k - 2)
                        if 0 <= k - 3 < 8:
                            S3(k - 3)
                        if k == 1:
                            while tails:
                                tails.pop(0)()
                    p.op("act", lambda e: e.copy(out=state_bf[:], in_=state[:]), reads=[R("state")], writes=[R("state_bf")])
                    p.op("pool", lambda e: e.tensor_tensor(out=ybuf[:], in0=ybuf[:], in1=t3[:], op=ALU.add), reads=[R("ybuf"), R("t3")], writes=[R("ybuf")])
                    p.op("pool", lambda e, q=q: e.tensor_tensor(out=ybuf[:], in0=ybuf[:], in1=sz[:, q, :], op=ALU.mult), reads=[R("ybuf"), R("sz%d" % q)], writes=[R("ybuf")])
                    for g in range(8):
                        p.op("act", lambda e, g=g: e.activation(out=gnb[:, g * 256:(g + 1) * 256], in_=ybuf[:, g * 256:(g + 1) * 256], func=AF.Square, accum_out=ssg[:, g:g + 1]),
                             reads=[R("ybuf")], writes=[R("gnb"), R("ssg")])
                    p.op("act", lambda e: e.activation(out=rsg[:, :, 0], in_=ssg[:], func=AF.Sqrt, scale=1.0 / 256, bias=self.epsb[:]), reads=[R("ssg"), self.R("epsb")], writes=[R("rsg")])
                    p.op("dve", lambda e: e.reciprocal(out=rsg[:, :, 0], in_=rsg[:, :, 0]), reads=[R("rsg")], writes=[R("rsg")])
                    p.op("dve", lambda e: e.tensor_tensor(out=ybuf[:].rearrange("p (g d) -> p g d", g=8), in0=ybuf[:].rearrange("p (g d) -> p g d", g=8),
                                                          in1=rsg[:].to_broadcast([128, 8, 256]), op=ALU.mult), reads=[R("ybuf"), R("rsg")], writes=[R("ybuf")])
                    p.op("pool", lambda e: e.tensor_tensor(out=gnb[:], in0=ybuf[:], in1=nw[:], op=ALU.mult), reads=[R("ybuf"), R("nw"), R("gnb")], writes=[R("gnb")])
                    def tail(tt=tt):
                        for qq in range(4):
                            hp = qq % 2
                            rp = self.r_ptr[hp]

                            def trg(e, qq=qq, hp=hp):
                                for m in range(4):
                                    k = qq * 4 + m
                                    ins = e.transpose(out=self.ptrh[hp][:, m * 128:(m + 1) * 128], in_=gnb[:, k * 128:(k + 1) * 128], identity=self.ident_b[:])
                                return ins
                            p.op("pe", trg, reads=[R("gnb"), self.R("ident_b")], writes=[rp])
                            dst = gnT[:, qq * 4:(qq + 1) * 4, :]
                            srcp = self.ptrh[hp][:, :].rearrange("p (q m) -> p q m", q=4)
                            if hp == 0:
                                p.op("act", lambda e, dst=dst, srcp=srcp: e.copy(out=dst, in_=srcp), reads=[rp], writes=[R("gnT%d" % qq)])
                            else:
                                p.op("dve", lambda e, dst=dst, srcp=srcp: e.tensor_copy(out=dst, in_=srcp), reads=[rp], writes=[R("gnT%d" % qq)])
                        p.op("sp", lambda e, tt=tt: e.dma_start(out=xres_t[:], in_=xsrc[tt * 128:(tt + 1) * 128, :]),
                             reads=[self.R("xres")], writes=[R("xres_t")], dma=self.st_x)
                        for ch in range(2):
                            bo, rbo = self.pbank[6 + ch], self.r_pb[6 + ch]

                            def mmo(e, ch=ch, bo=bo):
                                for k in range(16):
                                    ins = e.matmul(bo[:, :], lhsT=gnT[:, k, :], rhs=wout[:, k, ch * 512:(ch + 1) * 512], start=(k == 0), stop=(k == 15))
                                return ins
                            p.op("pe", mmo, reads=[R("gnT%d" % qq) for qq in range(4)] + r_wout, writes=[rbo])
                            p.op("dve", lambda e, ch=ch, bo=bo: e.tensor_tensor(out=xo[:, ch * 512:(ch + 1) * 512], in0=bo[:, :], in1=xres_t[:, ch * 512:(ch + 1) * 512], op=ALU.add),
                                 reads=[rbo, R("xres_t")], writes=[R("xo")])
                        p.op("sp", lambda e, tt=tt: e.dma_start(out=self.xres[tt * 128:(tt + 1) * 128, :], in_=xo[:]),
                             reads=[R("xo")], writes=[self.R("xres_w%d" % (tt % 4))], dma=self.st_o)
                    tails.append(tail)
            while tails:
                tails.pop(0)()
            self.phase_barrier()
        self.x_src = self.xres

    def prep_attn(self, j):
        p = self.p
        for (c0, c1) in ((0, 2048), (2048, NAW)):
            for hh in range(2):
                s_ap = self.attn_wr[j, hh * 512:(hh + 1) * 512, c0:c1]
                d_ap = self.awin[j, hh * 512:(hh + 1) * 512, c0:c1]
                p.op("pool", lambda e, s_ap=s_ap, d_ap=d_ap: e.dma_start(out=d_ap, in_=s_ap),
                     writes=[self.R("awin%d_%d_%d" % (j, c0, hh))], dma=self.st_prep)

    def awin_res(self, j, c0, c1):
        out = []
        for base in (0, 2048):
            hi = 2048 if base == 0 else NAW
            if c0 < hi and c1 > base:
                out += [self.R("awin%d_%d_%d" % (j, base, hh)) for hh in range(2)]
        return out

    def attn(self, l):
        p = self.p
        nc = self.nc
        j = l // 2
        xsrc = self.x_src
        tag = "a%d" % l
        R = lambda n: self.R(tag + n)
        BIG = 30000.0
        dbg = self.attn_dbg or ""
        use_cmp = ("nocmp" not in dbg)
        use_slc = ("noslc" not in dbg)
        use_win = ("nowin" not in dbg)
        with contextlib.ExitStack() as st_long:
            def sbl(name, shape, dt):
                return st_long.enter_context(nc.sbuf_tensor(tag + name, list(shape), dt))
            kT = sbl("kT", [64, 6, S], BF16)
            Vall = sbl("V", [128, 32, 6, 65], BF16)
            gates = sbl("gates", [128, 32, 24], F32)
            kcmpT = sbl("kcmpT", [64, 2, 256], BF16)
            Vcmp = sbl("Vcmp", [128, 2, 2, 129], BF16)
            esink = sbl("esink", [128, 8], F32)
            p.op("pool", lambda e: e.memset(Vall[:, :, :, 64:65], 1.0), writes=[R("Vones")])
            p.op("sp", lambda e: e.dma_start(out=esink[:], in_=self.sinks_in[j:j + 1, :].broadcast_to([128, 8])), writes=[R("esink")], dma=self.st_const)
            p.op("act", lambda e: e.activation(out=esink[:], in_=esink[:], func=AF.Exp), reads=[R("esink")], writes=[R("esink")])
            self.load_gain(l * 3 + 1)

            with contextlib.ExitStack() as st_x:
                kcT = st_x.enter_context(nc.sbuf_tensor(tag + "kcT", [64, 2, S], BF16))
                vcT = st_x.enter_context(nc.sbuf_tensor(tag + "vcT", [128, S], BF16))
                with contextlib.ExitStack() as st1:
                    def sb(name, shape, dt):
                        return st1.enter_context(nc.sbuf_tensor(tag + name, list(shape), dt))
                    TBK = 512
                    NT = 4
                    hT = sb("hT", [128, 8, TBK], BF16)
                    wb = [sb("wb%d" % b, [128, 8, 512], BF16) for b in range(2)]
                    cosb = sb("cosb", [64, TBK], F32)
                    sinb = sb("sinb", [64, TBK], F32)
                    t1 = [sb("t1_%d" % b, [64, TBK], F32) for b in range(2)]
                    t2 = [sb("t2_%d" % b, [64, TBK], F32) for b in range(2)]
                    qst = [sb("qst%d" % b, [64, TBK], BF16) for b in range(2)]
                    xt_bufs = [(sb("xt%d" % b, [128, D], F32), sb("hb%d" % b, [128, D], BF16),
                                sb("sq%d" % b, [128, D], BF16), sb("ss%d" % b, [128, 1], F32),
                                sb("rs%d" % b, [128, 1], F32)) for b in range(2)]
                    r_hT = [[R("hT%d_%d" % (jj, hh)) for hh in range(2)] for jj in range(NT)]
                    hres_all = [r_hT[jj][hh] for jj in range(NT) for hh in range(2)]
                    wc = 0
                    hc = 0
                    for blk in range(S // TBK):
                        t0 = blk * NT
                        csl = slice(blk * TBK, (blk + 1) * TBK)
                        self.norm_transpose(xsrc, t0, NT, hT, r_hT, xt_bufs, tag)
                        p.op("sp", lambda e, csl=csl: e.dma_start(out=cosb[:], in_=self.rope_in[0, :, csl]), writes=[R("cosb")], dma=self.st_x)
                        p.op("sp", lambda e, csl=csl: e.dma_start(out=sinb[:], in_=self.rope_in[1, :, csl]), writes=[R("sinb")], dma=self.st_x)
                        for wgI in range(6):
                            b = wc % 2
                            wc += 1
                            c0 = wgI * 512
                            src = self.awin[j, :, c0:c0 + 512].rearrange("(k p) m -> p k m", p=128)
                            p.op("sp", lambda e, src=src, b=b: e.dma_start(out=wb[b][:], in_=src),
                                 reads=self.awin_res(j, c0, c0 + 512), writes=[R("wb%d" % b)], dma=self.st_w)
                            for m in range(4):
                                hd = wgI * 4 + m
                                hb_ = hc % 2
                                hc += 1
                                bA, rA = self.pbank[hb_ * 2], self.r_pb[hb_ * 2]
                                bB, rB = self.pbank[hb_ * 2 + 1], self.r_pb[hb_ * 2 + 1]

                                def mm(e, bank, off, b=b, m=m):
                                    for kc in range(8):
                                        ins = e.matmul(bank[0:64, :], lhsT=wb[b][:, kc, m * 128 + off:m * 128 + off + 64], rhs=hT[:, kc, :],
                                                       start=(kc == 0), stop=(kc == 7))
                                    return ins
                                p.op("pe", lambda e, mm=mm, bA=bA: mm(e, bA, 0), reads=[R("wb%d" % b)] + hres_all, writes=[rA])
                                p.op("pe", lambda e, mm=mm, bB=bB: mm(e, bB, 64), reads=[R("wb%d" % b)] + hres_all, writes=[rB])
                                p.op("dve", lambda e, hb_=hb_, bA=bA: e.tensor_tensor(out=t1[hb_][:], in0=bA[0:64, :], in1=cosb[:], op=ALU.mult),
                                     reads=[rA, R("cosb")], writes=[R("t1_%d" % hb_)])
                                p.op("dve", lambda e, hb_=hb_, bB=bB: e.tensor_tensor(out=t2[hb_][:], in0=bB[0:64, :], in1=sinb[:], op=ALU.mult),
                                     reads=[rB, R("sinb")], writes=[R("t2_%d" % hb_)])
                                if hd < 8 or 10 <= hd < 18:
                                    qh = hd if hd < 8 else hd - 10 + 8
                                    p.op("pool", lambda e, hb_=hb_: e.tensor_tensor(out=qst[hb_][:], in0=t1[hb_][:], in1=t2[hb_][:], op=ALU.add),
                                         reads=[R("t1_%d" % hb_), R("t2_%d" % hb_)], writes=[R("qst%d" % hb_)])
                                    p.op("sp", lambda e, hb_=hb_, qh=qh, csl=csl: e.dma_start(out=self.qT[qh, :, csl], in_=qst[hb_][:]),
                                         reads=[R("qst%d" % hb_)], writes=[self.R("qT_%d_%d" % (qh, blk))], dma=self.st_o)
                                else:
                                    if hd < 10:
                                        dst, rd = kT[:, hd - 8, csl], R("kT%d" % (hd - 8))
                                    elif hd < 20:
                                        dst, rd = kcT[:, hd - 18, csl], R("kcT%d" % (hd - 18))
                                    elif hd < 22:
                                        dst, rd = kT[:, 2 + hd - 20, csl], R("kT%d" % (2 + hd - 20))
                                    else:
                                        dst, rd = kT[:, 4 + hd - 22, csl], R("kT%d" % (4 + hd - 22))
                                    p.op("pool", lambda e, hb_=hb_, dst=dst: e.tensor_tensor(out=dst, in0=t1[hb_][:], in1=t2[hb_][:], op=ALU.add),
                                         reads=[R("t1_%d" % hb_), R("t2_%d" % hb_)], writes=[rd])
                        b = wc % 2
                        wc += 1
                        src = self.awin[j, :, 3072:3608].rearrange("(k p) m -> p k m", p=128)
                        src_vc = self.awin[j, :, 3072:3200].rearrange("(k p) m -> p k m", p=128)
                        src_tm = self.awin[j, :, 3200:3608].rearrange("(k p) m -> p k m", p=128)
                        b2 = wc % 2
                        wc += 1
                        p.op("sp", lambda e, b=b, src_vc=src_vc: e.dma_start(out=wb[b][:, :, 0:128], in_=src_vc),
                             reads=self.awin_res(j, 3072, 3200), writes=[R("wb%d" % b)], dma=self.st_w)
                        p.op("sp", lambda e, b2=b2, src_tm=src_tm: e.dma_start(out=wb[b2][:, :, 0:408], in_=src_tm),
                             reads=self.awin_res(j, 3200, 3608), writes=[R("wb%d" % b2)], dma=self.st_w)
                        bA, rA = self.pbank[4], self.r_pb[4]

                        def mmvc(e, b=b, bA=bA):
                            for kc in range(8):
                                ins = e.matmul(bA[:, :], lhsT=wb[b][:, kc, 0:128], rhs=hT[:, kc, :], start=(kc == 0), stop=(kc == 7))
                            return ins
                        p.op("pe", mmvc, reads=[R("wb%d" % b)] + hres_all, writes=[rA])
                        p.op("act", lambda e, bA=bA, csl=csl: e.copy(out=vcT[:, csl], in_=bA[:, :]), reads=[rA], writes=[R("vcT")])
                        for q in range(NT):
                            tt = t0 + q
                            bT, rT = self.pbank[5], self.r_pb[5]

                            def mmtm(e, q=q, b2=b2, bT=bT):
                                for kc in range(8):
                                    ins = e.matmul(bT[:, 0:408], lhsT=hT[:, kc, q * 128:(q + 1) * 128], rhs=wb[b2][:, kc, 0:408], start=(kc == 0), stop=(kc == 7))
                                return ins
                            p.op("pe", mmtm, reads=[R("wb%d" % b2)] + r_hT[q], writes=[rT])
                            p.op("dve", lambda e, tt=tt, bT=bT: e.tensor_copy(out=Vall[:, tt, :, 0:64], in_=bT[:, 0:384].rearrange("p (a d) -> p a d", a=6)),
                                 reads=[rT], writes=[R("Vall")])
                            p.op("act", lambda e, tt=tt, bT=bT: e.activation(out=gates[:, tt, :], in_=bT[:, 384:408], func=AF.Sigmoid),
                                 reads=[rT], writes=[R("gates")])
                    p.barrier()
                with contextlib.ExitStack() as st2:
                    def sb(name, shape, dt):
                        return st2.enter_context(nc.sbuf_tensor(tag + name, list(shape), dt))
                    w1s = sb("w1s", [64, 32, 128], BF16)
                    w2s = sb("w2s", [128, 64], BF16)
                    posf = sb("posf", [64, 32], F32)
                    posb_ = sb("posb", [64, 32, 2], BF16)
                    pbias = sb("pbias", [128, 1], F32)
                    u = sb("u", [128, 256], F32)
                    u2 = sb("u2", [128, 256], F32)
                    sg_ = sb("sgm", [128, 256], F32)
                    gl = sb("gl", [128, 256], BF16)
                    wself = sb("wself", [128, 2, 64], F32)
                    p.op("sp", lambda e: e.dma_start(out=wself[:], in_=self.wsel_in), writes=[R("wself")], dma=self.st_const)
                    p.op("dve", lambda e: e.memset(u[:], 0.0), writes=[R("u")])
                    p.op("pool", lambda e: e.memset(Vcmp[:, :, :, 64:65], 1.0), writes=[R("Vcmp1")])
                    for g in range(2):
                        p.op("dve", lambda e, g=g: e.tensor_copy(out=Vcmp[:, g, :, 65:129], in_=wself[:]), reads=[R("wself")], writes=[R("VcmpW%d" % g)])
                    for kv in range(2):
                        w1_in = self.cmp_w1[j, kv].rearrange("(pp d) h -> d pp h", d=64)
                        p.op("pool", lambda e, w1_in=w1_in: e.dma_start(out=w1s[:], in_=w1_in), writes=[R("w1s")], dma=self.st_prep)
                        p.op("pool", lambda e, kv=kv: e.dma_start(out=w2s[:], in_=self.cmp_w2[j, kv]), writes=[R("w2s")], dma=self.st_prep)
                        p.op("sp", lambda e, kv=kv: e.dma_start(out=posf[:], in_=self.cmp_posT[j, kv]), writes=[R("posf")], dma=self.st_const)
                        p.op("dve", lambda e: e.tensor_copy(out=posb_[:], in_=posf[:].rearrange("p (a o) -> p a o", o=1).to_broadcast([64, 32, 2])), reads=[R("posf")], writes=[R("posb")])
                        b0, rb0 = self.pbank[0], self.r_pb[0]

                        def mmb(e):
                            for pp in range(32):
                                ins = e.matmul(b0[:, 0:2], lhsT=w1s[:, pp, :], rhs=posb_[:, pp, :], start=(pp == 0), stop=(pp == 31))
                            return ins
                        p.op("pe", mmb, reads=[R("w1s"), R("posb")], writes=[rb0])
                        p.op("dve", lambda e: e.tensor_copy(out=pbias[:], in_=b0[:, 0:1]), reads=[rb0], writes=[R("pbias")])
                        for g in range(2):
                            b1, rb1 = self.pbank[1 + g], self.r_pb[1 + g]
                            if kv == 0:
                                srcT = kcT[:, g, :]
                                rsrc = R("kcT%d" % g)
                            else:
                                srcT = vcT[g * 64:(g + 1) * 64, :]
                                rsrc = R("vcT")

                            def mmh(e, srcT=srcT, b1=b1, g=g):
                                for pp in range(32):
                                    ins = e.matmul(b1[:, 0:255], lhsT=w1s[g * 64 * kv:g * 64 * kv + 64, pp, :] if False else w1s[:, pp, :],
                                                   rhs=srcT[:, pp:pp + 16 * 254 + 1:16], start=(pp == 0), stop=(pp == 31))
                                return ins
                            if kv == 1 and g == 1:
                                vtmp = sb("vtmp", [64, S], BF16)
                                p.op("sp", lambda e, vtmp=vtmp: e.dma_start(out=vtmp[:], in_=vcT[64:128, :]), reads=[R("vcT")], writes=[R("vtmp")], dma=self.st_x)
                                srcT2 = vtmp[:, :]

                                def mmh(e, srcT2=srcT2, b1=b1):
                                    for pp in range(32):
                                        ins = e.matmul(b1[:, 0:255], lhsT=w1s[:, pp, :], rhs=srcT2[:, pp:pp + 16 * 254 + 1:16], start=(pp == 0), stop=(pp == 31))
                                    return ins
                                rsrc = R("vtmp")
                            p.op("pe", mmh, reads=[R("w1s"), rsrc], writes=[rb1])
                            p.op("act", lambda e, b1=b1: e.activation(out=u[:, 0:255], in_=b1[:, 0:255], func=AF.Identity, bias=pbias[:]),
                                 reads=[rb1, R("pbias"), R("u")], writes=[R("u")])
                            p.op("dve", lambda e: e.tensor_tensor(out=u2[:], in0=u[:], in1=u[:], op=ALU.mult), reads=[R("u")], writes=[R("u2")])
                            p.op("dve", lambda e: e.tensor_scalar(out=u2[:], in0=u2[:], scalar1=0.044715, scalar2=1.0, op0=ALU.mult, op1=ALU.add), reads=[R("u2")], writes=[R("u2")])
                            p.op("dve", lambda e: e.tensor_tensor(out=u2[:], in0=u2[:], in1=u[:], op=ALU.mult), reads=[R("u2"), R("u")], writes=[R("u2")])
                            p.op("act", lambda e: e.activation(out=sg_[:], in_=u2[:], func=AF.Sigmoid, scale=1.5957691216057308), reads=[R("u2")], writes=[R("sgm")])
                            p.op("dve", lambda e: e.tensor_tensor(out=gl[:], in0=u[:], in1=sg_[:], op=ALU.mult), reads=[R("u"), R("sgm")], writes=[R("gl")])
                            b3, rb3 = self.pbank[3], self.r_pb[3]
                            if kv == 0:
                                p.op("pe", lambda e, b3=b3: e.matmul(b3[0:64, 0:256], lhsT=w2s[:], rhs=gl[:], start=True, stop=True), reads=[R("w2s"), R("gl")], writes=[rb3])
                                p.op("dve", lambda e, g=g, b3=b3: e.tensor_copy(out=kcmpT[:, g, :], in_=b3[0:64, 0:256]), reads=[rb3], writes=[R("kcmpT%d" % g)])
                            else:
                                def mmv(e, b3=b3):
                                    for ct in range(2):
                                        ins = e.matmul(b3[:, ct * 64:(ct + 1) * 64], lhsT=gl[:, ct * 128:(ct + 1) * 128], rhs=w2s[:], start=True, stop=True)
                                    return ins
                                p.op("pe", mmv, reads=[R("w2s"), R("gl")], writes=[rb3])
                                p.op("dve", lambda e, g=g, b3=b3: e.tensor_copy(out=Vcmp[:, g, :, 0:64], in_=b3[:, 0:128].rearrange("p (c d) -> p c d", c=2)),
                                     reads=[rb3], writes=[R("VcmpV%d" % g)])
                    p.barrier()
            with contextlib.ExitStack() as st3:
                def sb(name, shape, dt):
                    return st3.enter_context(nc.sbuf_tensor(tag + name, list(shape), dt))
                wout = sb("wout", [128, 8, D], BF16)
                for q in range(2):
                    src = self.attn_w_out[j, q * 512:(q + 1) * 512, :].rearrange("(k p) m -> p k m", p=128)
                    p.op("pool", lambda e, src=src, q=q: e.dma_start(out=wout[:, q * 4:(q + 1) * 4, :], in_=src), writes=[R("wout%d" % q)], dma=self.st_prep)
                r_wout = [R("wout0"), R("wout1")]
                expand = sb("expand", [64, 32, 128], BF16)
                onesb = sb("onesb", [64, 32 * 128], BF16)
                p.op("pool", lambda e: e.memset(onesb[:], 1.0), writes=[R("onesb")])
                p.op("pool", lambda e: e.affine_select(out=expand[:].rearrange("p a (h m) -> p a h m", h=2), in_=onesb[:].rearrange("p (a h m) -> p a h m", a=32, h=2),
                                                       pattern=[[-2, 32], [-1, 2], [0, 64]], compare_op=ALU.is_equal, fill=0.0, base=0, channel_multiplier=1),
                     reads=[R("onesb")], writes=[R("expand")])
                selb = sb("selb", [128, 32, 64], F32)
                p.op("sp", lambda e: e.dma_start(out=selb[:], in_=self.selb_in), writes=[R("selb")], dma=self.st_const)
                qt = [sb("qt%d" % b, [64, 16, 256], BF16) for b in range(2)]
                Eb = [sb("E%d" % b, [128, 4, 128], BF16) for b in range(4)]
                SBANKS = (0, 1, 6)
                maskC = sb("maskC", [128, 4, 128], BF16)
                maskP = sb("maskP", [128, 4, 128], BF16)
                p.op("pool", lambda e: e.memset(maskC[:], 1.0), writes=[R("maskC")])
                p.op("pool", lambda e: e.memset(maskP[:], 1.0), writes=[R("maskP")])
                p.op("pool", lambda e: e.affine_select(out=maskC[:], in_=maskC[:], pattern=[[0, 4], [1, 128]], compare_op=ALU.is_ge, fill=0.0, base=0, channel_multiplier=-1),
                     reads=[R("maskC")], writes=[R("maskC")])
                p.op("pool", lambda e: e.affine_select(out=maskP[:], in_=maskP[:], pattern=[[0, 4], [-1, 128]], compare_op=ALU.is_ge, fill=0.0, base=-1, channel_multiplier=1),
                     reads=[R("maskP")], writes=[R("maskP")])
                ot = sb("ot", [128, D], BF16)
                accb = sb("accb", [128, 4, 64], F32)
                imp = sb("imp", [128, 64], F32)
                sc2 = sb("sc2", [128, 64], F32)
                m8 = sb("m8", [128, 8], F32)
                nb = sb("nb", [128, 64], F32)
                nbT4 = sb("nbT4", [64, 4, 128], BF16)
                den = sb("den", [128, 4], F32)
                oT = sb("oT", [128, 8, 128], BF16)
                xres_t = sb("xres_t", [128, D], F32)
                xo = sb("xo", [128, D], F32)
                ec = [0]
                sc_ = [0]

                def score_exp(i, lhsT, lres, rhs, rres, mask, extra=None):
                    sb_i = SBANKS[sc_[0] % 3]
                    sc_[0] += 1
                    bank, rb = self.pbank[sb_i], self.r_pb[sb_i]
                    eb = ec[0] % 4
                    ec[0] += 1

                    def mm(e, bank=bank):
                        ins = e.matmul(bank[:, :].rearrange("p (r q) -> p r q", r=4), lhsT=lhsT, rhs=rhs, start=True, stop=(extra is None))
                        if extra is not None:
                            ins = e.matmul(bank[:, :].rearrange("p (r q) -> p r q", r=4), lhsT=extra[0], rhs=extra[1], start=False, stop=True)
                        return ins
                    rr = list(lres) + list(rres) + (list(extra[2]) if extra is not None else [])
                    p.op("pe", mm, reads=rr, writes=[rb])
                    E = Eb[eb]
                    rE = R("E%d" % eb)
                    p.op("act", lambda e, E=E, bank=bank: e.activation(out=E[:].rearrange("p r q -> p (r q)"), in_=bank[:, :], func=AF.Exp, scale=0.125),
                         reads=[rb], writes=[rE])
                    if mask is CAUSAL or mask is PREV:
                        mt_, rm_ = (maskC, R("maskC")) if mask is CAUSAL else (maskP, R("maskP"))
                        p.op("dve", lambda e, E=E, mt_=mt_: e.tensor_tensor(out=E[:], in0=E[:], in1=mt_[:], op=ALU.mult), reads=[rE, rm_], writes=[rE])
                    elif mask is not None:
                        base, cm, stepq = mask
                        p.op("pool", lambda e, E=E, base=base, cm=cm, stepq=stepq: e.affine_select(
                            out=E[:], in_=E[:], pattern=[[0, 4], [stepq, 128]], compare_op=ALU.is_ge, fill=0.0, base=base, channel_multiplier=cm),
                            reads=[rE], writes=[rE])
                    return E, rE

                def pv(E, rE, vrhs, vres, ncols, first, last):
                    def mm(e):
                        for r in range(4):
                            ins = e.matmul(self.pbank[2 + r][:, 0:ncols], lhsT=E[:, r, :], rhs=vrhs, start=first, stop=last)
                        return ins
                    p.op("pe", mm, reads=[rE] + list(vres), writes=[self.r_pb[2 + r] for r in range(4)])

                CAUSAL = (0, -1, 1)
                PREV = (-1, 1, -1)
                den2 = [den, sb("denB", [128, 4], F32)]
                bc = [0]

                def pv2(E, rE, vrhs, vres, ncols, first, last, par):
                    off = par * 256

                    def mm(e):
                        for r in range(4):
                            ins = e.matmul(self.pbank[2 + r][:, off:off + ncols], lhsT=E[:, r, :], rhs=vrhs, start=first, stop=last)
                        return ins
                    p.op("pe", mm, reads=[rE] + list(vres), writes=[R("O%d_%d" % (r, par)) for r in range(4)] + [self.r_pb[2 + r] for r in range(4)])

                for i in range(32):
                    qb_ = (i // 2) % 2
                    if i % 2 == 0:
                        blk = i // 4
                        src = self.qT[:, :, i * 128:i * 128 + 256].rearrange("h d t -> d h t")
                        p.op("sp", lambda e, src=src, qb_=qb_: e.dma_start(out=qt[qb_][:], in_=src),
                             reads=[self.R("qT_%d_%d" % (h, blk)) for h in range(16)], writes=[R("qt%d" % qb_)], dma=self.st_x)
                    qsl = slice((i % 2) * 128, (i % 2 + 1) * 128)
                    rq = [R("qt%d" % qb_)]
                    p.op("sp", lambda e, i=i: e.dma_start(out=xres_t[:], in_=xsrc[i * 128:(i + 1) * 128, :]),
                         reads=[self.R("xres")], writes=[R("xres_t")], dma=self.st_x)
                    for g in range(2):
                        qa4 = qt[qb_][:, g * 4:(g + 1) * 4, qsl]
                        qb4 = qt[qb_][:, 8 + g * 4:8 + (g + 1) * 4, qsl]
                        gsl = gates[:, i, g * 12:(g + 1) * 12].rearrange("p (r b) -> p r b", b=3)
                        tiles = []
                        kts = [kt for kt in (i - 1, i) if kt >= 0]
                        for n, kt in enumerate(kts):
                            tiles.append(("swa", kT[:, g, kt * 128:(kt + 1) * 128], [R("kT%d" % g)], qa4, CAUSAL if kt == i else PREV, None,
                                          Vall[:, kt, g, :], [R("Vall"), R("Vones")], 65, n == 0, n == len(kts) - 1))
                        nct = 1 if i < 16 else 2
                        if use_cmp or use_slc:
                            for ct in range(nct):
                                tiles.append(("cmp", kcmpT[:, g, ct * 128:(ct + 1) * 128], [R("kcmpT%d" % g)], qb4, (128 * i - 2048 * ct - 31, -16, 1), None,
                                              Vcmp[:, g, ct, :], [R("VcmpV%d" % g), R("VcmpW%d" % g), R("Vcmp1")], 129, ct == 0, ct == nct - 1))
                        if use_win:
                            kts = [kt for kt in range(i - 4, i + 1) if kt >= 0]
                            for n, kt in enumerate(kts):
                                mk = CAUSAL if kt == i else (PREV if kt == i - 4 else None)
                                tiles.append(("win", kT[:, 4 + g, kt * 128:(kt + 1) * 128], [R("kT%d" % (4 + g))], qb4, mk, None,
                                              Vall[:, kt, 4 + g, :], [R("Vall"), R("Vones")], 65, n == 0, n == len(kts) - 1))
                        if use_slc:
                            for kt in range(i + 1):
                                tiles.append(("slc", kT[:, 2 + g, kt * 128:(kt + 1) * 128], [R("kT%d" % (2 + g))], qb4, CAUSAL if kt == i else None,
                                              (expand[:, kt, :], nbT4[:], [R("expand"), R("nbT4")]),
                                              Vall[:, kt, 2 + g, :], [R("Vall"), R("Vones")], 65, kt == 0, kt == i))
                        branches = [br for br in ("swa", "cmp", "win", "slc") if any(t[0] == br for t in tiles)]
                        last_nsa = [br for br in branches if br != "swa" and br != "cmp"]
                        last_nsa = last_nsa[-1] if last_nsa else "cmp"

                        def finish(br, par):
                            dn = den2[par]
                            rdn = R("den%d" % par)
                            rO = [R("O%d_%d" % (r, par)) for r in range(4)]
                            off = par * 256
                            O = [self.pbank[2 + r] for r in range(4)]
                            if br == "swa":
                                for r in range(4):
                                    h = g * 4 + r
                                    p.op("dve", lambda e, r=r, h=h: e.tensor_tensor(out=dn[:, r:r + 1], in0=O[r][:, off + 64:off + 65], in1=esink[:, h:h + 1], op=ALU.add),
                                         reads=[rO[r], R("esink")], writes=[rdn])
                                p.op("dve", lambda e: e.reciprocal(out=dn[:], in_=dn[:]), reads=[rdn], writes=[rdn])
                                for r in range(4):
                                    h = g * 4 + r
                                    p.op("dve", lambda e, r=r, h=h: e.tensor_scalar(out=ot[:, h * 64:(h + 1) * 64], in0=O[r][:, off:off + 64], scalar1=dn[:, r:r + 1], scalar2=None, op0=ALU.mult),
                                         reads=[rO[r], rdn], writes=[R("ot")])
                                return
                            if br == "cmp":
                                for r in range(4):
                                    p.op("dve", lambda e, r=r: e.tensor_scalar(out=dn[:, r:r + 1], in0=O[r][:, off + 64:off + 65], scalar1=1e-30, scalar2=None, op0=ALU.max),
                                         reads=[rO[r]], writes=[rdn])
                                p.op("dve", lambda e: e.reciprocal(out=dn[:], in_=dn[:]), reads=[rdn], writes=[rdn])
                                for r in range(4):
                                    if r == 0:
                                        p.op("dve", lambda e, r=r: e.tensor_scalar(out=imp[:], in0=O[r][:, off + 65:off + 129], scalar1=dn[:, r:r + 1], scalar2=None, op0=ALU.mult),
                                             reads=[rO[r], rdn], writes=[R("imp")])
                                    else:
                                        p.op("dve", lambda e, r=r: e.scalar_tensor_tensor(out=imp[:], in0=O[r][:, off + 65:off + 129], scalar=dn[:, r:r + 1], in1=imp[:], op0=ALU.mult, op1=ALU.add),
                                             reads=[rO[r], rdn, R("imp")], writes=[R("imp")])
                                if use_slc:
                                    p.op("dve", lambda e, i=i: e.tensor_tensor(out=imp[:], in0=imp[:], in1=selb[:, i, :], op=ALU.add), reads=[R("imp"), R("selb")], writes=[R("imp")])
                                    p.op("dve", lambda e: e.max(out=m8[:], in_=imp[:]), reads=[R("imp")], writes=[R("m8")])
                                    p.op("dve", lambda e: e.match_replace(out=sc2[:], in_to_replace=m8[:], in_values=imp[:], imm_value=-3.0e38), reads=[R("imp"), R("m8")], writes=[R("sc2")])
                                    p.op("dve", lambda e: e.max(out=m8[:], in_=sc2[:]), reads=[R("sc2"), R("m8")], writes=[R("m8")])
                                    p.op("dve", lambda e: e.tensor_scalar(out=nb[:], in0=imp[:], scalar1=m8[:, 7:8], scalar2=-BIG, op0=ALU.is_lt, op1=ALU.mult),
                                         reads=[R("imp"), R("m8")], writes=[R("nb")])
                                    sb_i = SBANKS[sc_[0] % 3]
                                    sc_[0] += 1
                                    bank, rb = self.pbank[sb_i], self.r_pb[sb_i]
                                    p.op("pe", lambda e, bank=bank: e.transpose(out=bank[0:64, 0:128], in_=nb[:], identity=self.ident_f[:]), reads=[R("nb"), self.R("ident")], writes=[rb])
                                    p.op("dve", lambda e, bank=bank: e.tensor_copy(out=nbT4[:], in_=bank[0:64, 0:128].rearrange("p (o q) -> p o q", o=1).to_broadcast([64, 4, 128])),
                                         reads=[rb], writes=[R("nbT4")])
                                p.op("dve", lambda e, gsl=gsl: e.tensor_tensor(out=dn[:], in0=dn[:], in1=gsl[:, :, 0], op=ALU.mult), reads=[rdn, R("gates")], writes=[rdn])
                                for r in range(4):
                                    h = 8 + g * 4 + r
                                    if not use_cmp:
                                        p.op("dve", lambda e, r=r: e.memset(accb[:, r, :], 0.0), reads=[R("accb")], writes=[R("accb")])
                                    elif last_nsa == "cmp":
                                        p.op("dve", lambda e, r=r, h=h: e.tensor_scalar(out=ot[:, h * 64:(h + 1) * 64], in0=O[r][:, off:off + 64], scalar1=dn[:, r:r + 1], scalar2=None, op0=ALU.mult),
                                             reads=[rO[r], rdn], writes=[R("ot")])
                                    else:
                                        p.op("dve", lambda e, r=r: e.tensor_scalar(out=accb[:, r, :], in0=O[r][:, off:off + 64], scalar1=dn[:, r:r + 1], scalar2=None, op0=ALU.mult),
                                             reads=[rO[r], rdn], writes=[R("accb")])
                                return
                            gi = 2 if br == "win" else 1
                            for r in range(4):
                                p.op("dve", lambda e, r=r: e.tensor_copy(out=dn[:, r:r + 1], in_=O[r][:, off + 64:off + 65]), reads=[rO[r]], writes=[rdn])
                            p.op("dve", lambda e: e.reciprocal(out=dn[:], in_=dn[:]), reads=[rdn], writes=[rdn])
                            p.op("dve", lambda e, gsl=gsl, gi=gi: e.tensor_tensor(out=dn[:], in0=dn[:], in1=gsl[:, :, gi], op=ALU.mult), reads=[rdn, R("gates")], writes=[rdn])
                            for r in range(4):
                                h = 8 + g * 4 + r
                                if br == last_nsa:
                                    p.op("dve", lambda e, r=r, h=h: e.scalar_tensor_tensor(out=ot[:, h * 64:(h + 1) * 64], in0=O[r][:, off:off + 64], scalar=dn[:, r:r + 1], in1=accb[:, r, :], op0=ALU.mult, op1=ALU.add),
                                         reads=[rO[r], rdn, R("accb")], writes=[R("ot")])
                                else:
                                    p.op("dve", lambda e, r=r: e.scalar_tensor_tensor(out=accb[:, r, :], in0=O[r][:, off:off + 64], scalar=dn[:, r:r + 1], in1=accb[:, r, :], op0=ALU.mult, op1=ALU.add),
                                         reads=[rO[r], rdn, R("accb")], writes=[R("accb")])

                        par_of = {}
                        for br in branches:
                            par_of[br] = (bc[0] % 2) if "usepar" in dbg else 0
                            bc[0] += 1
                        queue = []
                        LOOK = 0 if "nopipe" in dbg else 2

                        def pop():
                            pE, prE, ptl = queue.pop(0)
                            pv2(pE, prE, ptl[6], ptl[7], ptl[8], ptl[9], ptl[10], par_of[ptl[0]])
                            if ptl[10]:
                                finish(ptl[0], par_of[ptl[0]])
                        for tl in tiles:
                            br, lhsT, lres, rhs, mask, extra, vrhs, vres, ncols, first, last = tl
                            if br == "slc" and first:
                                while any(qq[2][0] == "cmp" for qq in queue):
                                    pop()
                            E, rE = score_exp(i, lhsT, lres, rhs, rq, mask, extra=extra)
                            queue.append((E, rE, tl))
                            while len(queue) > LOOK:
                                pop()
                        while queue:
                            pop()
                    for half in range(2):
                        rp = self.r_ptr[1]

                        def tr(e, half=half):
                            for q in range(4):
                                kc = half * 4 + q
                                ins = e.transpose(out=self.ptrh[1][:, q * 128:(q + 1) * 128], in_=ot[:, kc * 128:(kc + 1) * 128], identity=self.ident_b[:])
                            return ins
                        p.op("pe", tr, reads=[R("ot"), self.R("ident_b")], writes=[rp])
                        dst = oT[:, half * 4:(half + 1) * 4, :]
                        srcp = self.ptrh[1][:, :].rearrange("p (q m) -> p q m", q=4)
                        p.op("act", lambda e, dst=dst, srcp=srcp: e.copy(out=dst, in_=srcp), reads=[rp], writes=[R("oT%d" % half)])
                    for ch in range(2):
                        bo, rbo = self.pbank[ch], self.r_pb[ch]

                        def mmo(e, ch=ch, bo=bo):
                            for k in range(8):
                                ins = e.matmul(bo[:, :], lhsT=oT[:, k, :], rhs=wout[:, k, ch * 512:(ch + 1) * 512], start=(k == 0), stop=(k == 7))
                            return ins
                        p.op("pe", mmo, reads=[R("oT0"), R("oT1")] + r_wout, writes=[rbo])
                        p.op("dve", lambda e, ch=ch, bo=bo: e.tensor_tensor(out=xo[:, ch * 512:(ch + 1) * 512], in0=bo[:, :], in1=xres_t[:, ch * 512:(ch + 1) * 512], op=ALU.add),
                             reads=[rbo, R("xres_t")], writes=[R("xo")])
                    if "ot" in dbg:
                        p.op("dve", lambda e: e.tensor_copy(out=xo[:], in_=ot[:]), reads=[R("ot"), R("xo")], writes=[R("xo")])
                    p.op("sp", lambda e, i=i: e.dma_start(out=self.xres[i * 128:(i + 1) * 128, :], in_=xo[:]),
                         reads=[R("xo")], writes=[self.R("xres_w%d" % (i % 4))], dma=self.st_o)
            self.phase_barrier()
        self.x_src = self.xres

    def final_norm(self):
        p = self.p
        nc = self.nc
        xsrc = self.x_src
        with contextlib.ExitStack() as st:
            def sb(name, shape, dt):
                return st.enter_context(nc.sbuf_tensor(name, list(shape), dt))
            self.load_gain(DEPTH * 3)
            bufs = [(sb("fn_x%d" % b, [128, D], F32), sb("fn_sq%d" % b, [128, D], BF16), sb("fn_ss%d" % b, [128, 1], F32),
                     sb("fn_rs%d" % b, [128, 1], F32), sb("fn_o%d" % b, [128, D], F32)) for b in range(2)]
            for tt in range(S // 128):
                b = tt % 2
                xt, sq, ss, rs, ot = bufs[b]
                rx, rss, rrs, ro = [self.R("fn_%s%d" % (n, b)) for n in ("x", "ss", "rs", "o")]
                p.op("sp", lambda e, xt=xt, tt=tt: e.dma_start(out=xt[:], in_=xsrc[tt * 128:(tt + 1) * 128, :]),
                     reads=[self.R("xres")], writes=[rx], dma=self.st_x)
                p.op("act", lambda e, xt=xt, sq=sq, ss=ss: e.activation(out=sq[:], in_=xt[:], func=AF.Square, accum_out=ss[:]),
                     reads=[rx], writes=[self.R("fn_sq%d" % b), rss])
                p.op("act", lambda e, ss=ss, rs=rs: e.activation(out=rs[:], in_=ss[:], func=AF.Sqrt, scale=1.0 / D, bias=self.epsb[:]),
                     reads=[rss, self.R("epsb")], writes=[rrs])
                p.op("dve", lambda e, rs=rs: e.reciprocal(out=rs[:], in_=rs[:]), reads=[rrs], writes=[rrs])
                p.op("dve", lambda e, xt=xt, ot=ot, rs=rs: e.scalar_tensor_tensor(out=ot[:], in0=xt[:], scalar=rs[:], in1=self.gbc[:], op0=ALU.mult, op1=ALU.mult),
                     reads=[rx, rrs, self.R("gbc")], writes=[ro])
                p.op("sp", lambda e, ot=ot, tt=tt: e.dma_start(out=self.out[tt * 128:(tt + 1) * 128, :], in_=ot[:]),
                     reads=[ro], writes=[self.R("out_w%d" % (tt % 4))], dma=self.st_o)
            self.final_wait()

    def copy_out(self):
        p = self.p
        nc = self.nc
        xsrc = self.x_src
        with contextlib.ExitStack() as st:
            bufs = [st.enter_context(nc.sbuf_tensor("co%d" % b, [128, D], F32)) for b in range(2)]
            for tt in range(S // 128):
                b = tt % 2
                rx = self.R("co%d" % b)
                p.op("sp", lambda e, b=b, tt=tt: e.dma_start(out=bufs[b][:], in_=xsrc[tt * 128:(tt + 1) * 128, :]),
                     reads=[self.R("xres")], writes=[rx], dma=self.st_x)
                p.op("sp", lambda e, b=b, tt=tt: e.dma_start(out=self.out[tt * 128:(tt + 1) * 128, :], in_=bufs[b][:]),
                     reads=[rx], writes=[self.R("out_w%d" % (tt % 4))], dma=self.st_o)
            self.final_wait()

    def final_wait(self):
        p = self.p
        sems = self._store_waits()

        def fn(e, sems=sems):
            for sem, val in sems:
                e.wait_ge(sem, val)
            return e.nop()
        p.op("sp", fn, reads=[self.R("out_w%d" % k) for k in range(4)], writes=[self.R("done")])


def full_plan():
    def prep(l):
        out = [("prep_ffn", l, 0)]
        out.append(("prep_attn", l // 2) if l % 2 == 0 else ("prep_ssm", l // 2))
        out.append(("prep_ffn", l, 1))
        return out
    plan = prep(0)
    for l in range(DEPTH):
        if l + 1 < DEPTH:
            plan += prep(l + 1)
        plan.append(("ffn", l, 0))
        plan.append(("attn", l) if l % 2 == 0 else ("ssd", l))
        plan.append(("ffn", l, 1))
    plan.append(("final",))
    return plan


_CACHE = {}


def attn_w_layout(w):
    heads = [(h * 64) for h in range(8)] + [512, 576] + [768 + h * 64 for h in range(8)] + [1280, 1344] + [1536, 1600] + [1792, 1856]
    cols = []
    for c0 in heads:
        cols += list(range(c0, c0 + 64)) + list(range(c0 + 32, c0 + 64)) + list(range(c0, c0 + 32))
    cols += list(range(1408, 1536))
    cols += list(range(640, 768)) + list(range(1664, 1792)) + list(range(1920, 2048)) + list(range(2048, 2072))
    assert len(cols) == NAW
    return np.ascontiguousarray(w[:, :, np.asarray(cols)])


def _rope_tables():
    inv = (1.0 / (np.float32(10000.0) ** (np.arange(0, 64, 2, dtype=np.float32) / np.float32(64)))).astype(np.float32)
    ang = (np.arange(S, dtype=np.float32)[:, None] * inv[None, :]).astype(np.float32)
    c = np.cos(ang).astype(np.float32).T
    s_ = np.sin(ang).astype(np.float32).T
    return np.ascontiguousarray(np.stack([np.concatenate([c, c], 0), np.concatenate([-s_, s_], 0)], 0))


def _wsel():
    n_cmp = (S - 32) // 16 + 1
    cs = np.arange(n_cmp) * 16
    ss = np.arange(S // 64) * 64
    ov = np.minimum(cs[:, None] + 32, ss[None, :] + 64) - np.maximum(cs[:, None], ss[None, :])
    w = np.zeros((256, 64), np.float32)
    w[:n_cmp] = np.clip(ov, 0, None) / 32.0
    return np.ascontiguousarray(w.reshape(2, 128, 64).transpose(1, 0, 2))


def _selb():
    t = np.arange(S)
    cur = (t // 64)[:, None]
    jj = np.arange(64)[None, :]
    valid = jj <= cur
    forced = valid & ((jj == 0) | (jj == cur) | (jj == cur - 1))
    b = np.where(forced, 1e4, 0.0) - np.where(valid, 0.0, 1e4)
    return np.ascontiguousarray(b.astype(np.float32).reshape(32, 128, 64).transpose(1, 0, 2))


ROPE = _rope_tables()
WSEL = _wsel()
SELB = _selb()
_ii = np.arange(128)
TRI = np.stack([(_ii[:, None] <= _ii[None, :]), (_ii[:, None] > _ii[None, :]), np.ones((128, 128), bool)]).astype(np.float32)


def run_plan(plan, inputs, n_cores=8, trace=False):
    key = repr(plan)
    if key not in _CACHE:
        _CACHE[key] = Builder(plan).build()
    nc = _CACHE[key]
    x = np.ascontiguousarray(inputs["x"], dtype=np.float32)
    gains = np.concatenate([np.asarray(inputs["norm_gains"], np.float32).reshape(DEPTH * 3, D),
                            np.asarray(inputs["final_norm"], np.float32).reshape(1, D)], axis=0)
    common = {
        "gains": np.ascontiguousarray(gains),
        "ffn_w_gate": np.ascontiguousarray(inputs["ffn_w_gate"], dtype=np.float32),
        "ffn_w_up": np.ascontiguousarray(inputs["ffn_w_up"], dtype=np.float32),
        "ffn_w_down": np.ascontiguousarray(inputs["ffn_w_down"], dtype=np.float32),
        "ident": np.eye(128, dtype=np.float32),
        "tri": TRI,
        "ssm_w_in": np.ascontiguousarray(inputs["ssm_w_in"], dtype=np.float32),
        "ssm_w_out": np.ascontiguousarray(inputs["ssm_w_out"], dtype=np.float32),
        "ssm_cw": np.ascontiguousarray(np.asarray(inputs["ssm_conv_w"], np.float32).transpose(0, 2, 1).reshape(2, 32, 128, 4).transpose(0, 2, 1, 3)),
        "ssm_cb": np.ascontiguousarray(np.asarray(inputs["ssm_conv_b"], np.float32).reshape(2, 32, 128).transpose(0, 2, 1)),
        "ssm_vec": np.ascontiguousarray(np.stack([np.asarray(inputs["ssm_dt_bias"], np.float32), np.asarray(inputs["ssm_a_log"], np.float32),
                                                  np.asarray(inputs["ssm_d"], np.float32)], axis=1).reshape(2, 96)),
        "ssm_norm": np.ascontiguousarray(inputs["ssm_norm"], dtype=np.float32),
        "attn_wr": attn_w_layout(np.asarray(inputs["attn_w_in"], np.float32)),
        "attn_w_out": np.ascontiguousarray(inputs["attn_w_out"], dtype=np.float32),
        "attn_sinks": np.ascontiguousarray(inputs["attn_sinks"], dtype=np.float32),
        "rope": ROPE,
        "cmp_w1": np.ascontiguousarray(np.stack([np.asarray(inputs["cmp_k_w1"], np.float32), np.asarray(inputs["cmp_v_w1"], np.float32)], axis=1)),
        "cmp_w2": np.ascontiguousarray(np.stack([np.asarray(inputs["cmp_k_w2"], np.float32), np.asarray(inputs["cmp_v_w2"], np.float32)], axis=1)),
        "cmp_posT": np.ascontiguousarray(np.stack([np.asarray(inputs["cmp_k_pos"], np.float32).transpose(0, 2, 1),
                                                   np.asarray(inputs["cmp_v_pos"], np.float32).transpose(0, 2, 1)], axis=1)),
        "wsel": WSEL,
        "selb": SELB,
    }
    in_maps = []
    for c in range(n_cores):
        m = dict(common)
        m["x"] = x[c % 4]
        in_maps.append(m)
    res = run_bass_kernel_spmd(nc, in_maps, core_ids=list(range(n_cores)), trace=trace)
    out = np.stack([res.results[c % n_cores]["out"] for c in range(4)], axis=0)
    return out, res


def kernel(**inputs):
    out, _ = run_plan(full_plan(), inputs)
    return out.astype(np.float32)
```

```python
import contextlib
import numpy as np
import concourse.bass as bass
import concourse.mybir as mybir
from concourse.bass_utils import run_bass_kernel_spmd

F32 = mybir.dt.float32
BF16 = mybir.dt.bfloat16
AF = mybir.ActivationFunctionType
ALU = mybir.AluOpType
AX = mybir.AxisListType

D = 1024
S = 4096
DEPTH = 4
DFF = 2816
NFC = DFF // 128
EPS = 1e-6
NAW = 3608


class Res:
    __slots__ = ("name", "w", "r")

    def __init__(self, name):
        self.name = name
        self.w = None
        self.r = []


class Op:
    __slots__ = ("eng", "fn", "waits", "inc", "dma", "dsem", "dval", "cnt", "pre")


class Prog:
    ENGS = ("pe", "act", "dve", "pool", "sp")

    def __init__(self, nc, stack):
        self.nc = nc
        self.stack = stack
        self.ops = {e: [] for e in self.ENGS}
        self.esem = {e: stack.enter_context(nc.semaphore("es_" + e)) for e in self.ENGS}
        self.nsem = 5
        self.streams = []

    def new_sem(self, name):
        self.nsem += 1
        return self.stack.enter_context(self.nc.semaphore(name))

    def op(self, eng, fn, reads=(), writes=(), dma=None):
        o = Op()
        o.eng = eng
        o.fn = fn
        o.inc = False
        o.dma = dma
        o.cnt = 0
        o.pre = None
        o.dsem = None
        o.dval = 0
        deps = []
        seen = set()

        def add(d, raw):
            if d is None or id(d) in seen:
                return
            if d.dma is None and d.eng == eng:
                if eng == "pe" or not raw:
                    return
            seen.add(id(d))
            deps.append(d)

        for r in reads:
            add(r.w, True)
        for w in writes:
            add(w.w, False)
            for rr in w.r:
                add(rr, False)
        for d in deps:
            if d.dma is None:
                d.inc = True
        o.waits = deps
        if dma is not None:
            sem, val, pre = dma.next()
            o.dsem, o.dval, o.pre = sem, val, pre
            dma.ops.append(o)
        for r in reads:
            r.r.append(o)
        for w in writes:
            w.w = o
            w.r = []
        self.ops[eng].append(o)
        return o

    def barrier(self):
        deps = []
        for e in self.ENGS:
            for o in reversed(self.ops[e]):
                if o.dma is None:
                    deps.append(o)
                    break
        for st in self.streams:
            deps.extend(st.ops[-st.R:])
        for e in self.ENGS:
            o = Op()
            o.eng = e
            o.fn = lambda eng: eng.nop()
            o.inc = False
            o.dma = None
            o.cnt = 0
            o.pre = None
            o.dsem = None
            o.dval = 0
            o.waits = [d for d in deps if not (d.dma is None and d.eng == e)]
            for d in o.waits:
                if d.dma is None:
                    d.inc = True
            self.ops[e].append(o)

    def emit(self):
        nc = self.nc
        for e in self.ENGS:
            c = 0
            for o in self.ops[e]:
                if o.dma is None and o.inc:
                    c += 1
                o.cnt = c
        self.counts = {e: (len(self.ops[e]), self.ops[e][-1].cnt if self.ops[e] else 0) for e in self.ENGS}

        def body_for(ename):
            def body(eng):
                seen = {}

                def wait(sem, val):
                    k = id(sem)
                    if seen.get(k, 0) >= val:
                        return
                    seen[k] = val
                    eng.wait_ge(sem, val)

                for o in self.ops[ename]:
                    for d in o.waits:
                        if d.dma is not None:
                            wait(d.dsem, d.dval)
                        else:
                            wait(self.esem[d.eng], d.cnt)
                    if o.pre is not None and o.pre[1] > 0:
                        wait(o.pre[0], o.pre[1])
                    ins = o.fn(eng)
                    if o.dma is not None:
                        ins.then_inc(o.dsem, 16)
                    elif o.inc:
                        ins.then_inc(self.esem[ename], 1)
            return body

        with nc.Block() as block:
            block.tensor(body_for("pe"))
            block.scalar(body_for("act"))
            block.vector(body_for("dve"))
            block.gpsimd(body_for("pool"))
            block.sync(body_for("sp"))


class DmaStream:
    def __init__(self, prog, name, R):
        self.sems = [prog.new_sem("%s%d" % (name, i)) for i in range(R)]
        self.R = R
        self.k = 0
        self.ops = []
        prog.streams.append(self)

    def next(self):
        k = self.k
        self.k += 1
        sem = self.sems[k % self.R]
        return sem, 16 * (k // self.R + 1), (sem, 16 * (k // self.R))


class Builder:
    def __init__(self, plan):
        self.plan = plan
        self.nc = bass.Bass("TRN2", target_bir_lowering=False)
        self.stack = contextlib.ExitStack()
        self.res_cache = {}

    def R(self, name):
        r = self.res_cache.get(name)
        if r is None:
            r = self.res_cache[name] = Res(name)
        return r

    def dram_in(self, name, shape, dt=F32):
        return self.nc.dram_tensor(name, list(shape), dt, kind="ExternalInput").ap()

    def dram_out(self, name, shape, dt=F32):
        return self.nc.dram_tensor(name, list(shape), dt, kind="ExternalOutput").ap()

    def dram_tmp(self, name, shape, dt):
        return self.nc.dram_tensor(name, list(shape), dt).ap()

    def sb(self, name, shape, dt):
        return self.stack.enter_context(self.nc.sbuf_tensor(name, list(shape), dt))

    def ps(self, name, shape, dt):
        return self.stack.enter_context(self.nc.psum_tensor(name, list(shape), dt))

    def build(self):
        nc = self.nc
        with self.stack:
            self.p = Prog(nc, self.stack)
            self._build()
            self.p.emit()
        return nc

    def _build(self):
        p = self.p
        plan = self.plan
        self.x_in = self.dram_in("x", [S, D])
        self.gains = self.dram_in("gains", [DEPTH * 3 + 1, D])
        self.wg = self.dram_in("ffn_w_gate", [DEPTH, 2, D, DFF])
        self.wu = self.dram_in("ffn_w_up", [DEPTH, 2, D, DFF])
        self.wd = self.dram_in("ffn_w_down", [DEPTH, 2, DFF, D])
        self.ident_in = self.dram_in("ident", [128, 128])
        self.tri_in = self.dram_in("tri", [3, 128, 128])
        self.ssm_w_in = self.dram_in("ssm_w_in", [2, D, 6176])
        self.ssm_w_out = self.dram_in("ssm_w_out", [2, 2048, D])
        self.ssm_cw = self.dram_in("ssm_cw", [2, 128, 32, 4])
        self.ssm_cb = self.dram_in("ssm_cb", [2, 128, 32])
        self.ssm_vec = self.dram_in("ssm_vec", [2, 96])
        self.ssm_norm = self.dram_in("ssm_norm", [2, 2048])
        self.swin = self.dram_tmp("swin", [2, D, 6176], BF16)
        self.attn_wr = self.dram_in("attn_wr", [2, D, NAW])
        self.attn_w_out = self.dram_in("attn_w_out", [2, D, D])
        self.sinks_in = self.dram_in("attn_sinks", [2, 8])
        self.rope_in = self.dram_in("rope", [2, 64, S])
        self.cmp_w1 = self.dram_in("cmp_w1", [2, 2, 2048, 128])
        self.cmp_w2 = self.dram_in("cmp_w2", [2, 2, 128, 64])
        self.cmp_posT = self.dram_in("cmp_posT", [2, 2, 64, 32])
        self.wsel_in = self.dram_in("wsel", [128, 2, 64])
        self.selb_in = self.dram_in("selb", [128, 32, 64])
        self.awin = self.dram_tmp("awin", [2, D, NAW], BF16)
        self.qT = self.dram_tmp("qT", [16, 64, S], BF16)
        self.out = self.dram_out("out", [S, D])
        self.xres = self.dram_tmp("xres", [S, D], F32)
        self.wgt = self.dram_tmp("wgt", [DEPTH, 2, 6, 128, 8, 512], BF16)
        self.wut = self.dram_tmp("wut", [DEPTH, 2, 6, 128, 8, 512], BF16)
        self.wdt = self.dram_tmp("wdt", [DEPTH, 2, DFF, D], BF16)

        self.st_const = DmaStream(p, "dc", 1)
        self.st_prep = DmaStream(p, "dp", 4)
        self.st_x = DmaStream(p, "dx", 4)
        self.st_w = DmaStream(p, "dw", 4)
        self.st_o = DmaStream(p, "do", 4)

        self.ident_f = self.sb("ident_f", [128, 128], F32)
        self.ident_b = self.sb("ident_b", [128, 128], BF16)
        self.gbc = self.sb("gbc", [128, D], F32)
        self.epsb = self.sb("epsb", [128, 1], F32)
        r_ident = self.R("ident")
        p.op("sp", lambda e: e.dma_start(out=self.ident_f[:], in_=self.ident_in), writes=[r_ident], dma=self.st_const)
        p.op("dve", lambda e: e.tensor_copy(out=self.ident_b[:], in_=self.ident_f[:]), reads=[r_ident], writes=[self.R("ident_b")])
        p.op("dve", lambda e: e.memset(self.epsb[:], EPS), writes=[self.R("epsb")])

        self.pbank = [self.ps("pb%d" % i, [128, 512], F32) for i in range(8)]
        self.ptrh = [self.pbank[6 + i][:, :].bitcast(BF16)[:, 0:512] for i in range(2)]
        self.r_pb = [self.R("pb%d" % i) for i in range(8)]
        self.r_ptr = [self.r_pb[6], self.r_pb[7]]

        self.x_src = self.x_in
        for ph in plan:
            kind = ph[0]
            if kind == "prep_ffn":
                self.prep_ffn(ph[1], ph[2])
            elif kind == "ffn":
                self.dbg_stage = ph[3] if len(ph) > 3 else 99
                self.ffn(ph[1], ph[2])
            elif kind == "prep_ssm":
                self.prep_ssm(ph[1])
            elif kind == "ssd":
                self.ssd(ph[1])
            elif kind == "prep_attn":
                self.prep_attn(ph[1])
            elif kind == "attn":
                self.attn_dbg = ph[2] if len(ph) > 2 else None
                self.attn(ph[1])
            elif kind == "final":
                self.final_norm()
            elif kind == "copy_out":
                self.copy_out()
            else:
                raise ValueError(kind)

    def load_gain(self, row):
        p = self.p
        src = self.gains[row:row + 1, :].broadcast_to([128, D])
        p.op("sp", lambda e: e.dma_start(out=self.gbc[:], in_=src), writes=[self.R("gbc")], dma=self.st_const)

    def prep_ffn(self, l, i):
        p = self.p
        for (src, dst, nm) in ((self.wg, self.wgt, "g"), (self.wu, self.wut, "u")):
            for blk in range(6):
                w = 512 if blk < 5 else 256
                s_ap = src[l, i, :, blk * 512:blk * 512 + w].rearrange("(kc p) m -> p kc m", p=128)
                d_ap = dst[l, i, blk, :, :, 0:w]
                p.op("pool", lambda e, s_ap=s_ap, d_ap=d_ap: e.dma_start(out=d_ap, in_=s_ap),
                     writes=[self.R("wt_%s_%d_%d_%d" % (nm, l, i, blk))], dma=self.st_prep)
        for q in range(4):
            rows = DFF // 4
            s_ap = self.wd[l, i, q * rows:(q + 1) * rows, :]
            d_ap = self.wdt[l, i, q * rows:(q + 1) * rows, :]
            p.op("pool", lambda e, s_ap=s_ap, d_ap=d_ap: e.dma_start(out=d_ap, in_=s_ap),
                 writes=[self.R("wt_d_%d_%d_%d" % (l, i, q))], dma=self.st_prep)

    def norm_transpose(self, xsrc, t0, ntile, hT, r_hT, xt_bufs, tag):
        p = self.p
        for j in range(ntile):
            tt = t0 + j
            b = j % 2
            xt, hb, sq, ss, rs = xt_bufs[b]
            rx = self.R("%s_xt%d" % (tag, b))
            rh = self.R("%s_hb%d" % (tag, b))
            rss = self.R("%s_ss%d" % (tag, b))
            rsq = self.R("%s_sq%d" % (tag, b))
            p.op("sp", lambda e, xt=xt, tt=tt: e.dma_start(out=xt[:], in_=xsrc[tt * 128:(tt + 1) * 128, :]),
                 reads=[self.R("xres")], writes=[rx], dma=self.st_x)
            p.op("act", lambda e, xt=xt, sq=sq, ss=ss: e.activation(out=sq[:], in_=xt[:], func=AF.Square, accum_out=ss[:]),
                 reads=[rx], writes=[rsq, rss])
            p.op("act", lambda e, ss=ss, rs=rs: e.activation(out=rs[:], in_=ss[:], func=AF.Sqrt, scale=1.0 / D, bias=self.epsb[:]),
                 reads=[rss, self.R("epsb")], writes=[self.R("%s_rs%d" % (tag, b))])
            p.op("dve", lambda e, rs=rs: e.reciprocal(out=rs[:], in_=rs[:]),
                 reads=[self.R("%s_rs%d" % (tag, b))], writes=[self.R("%s_rs%d" % (tag, b))])
            p.op("dve", lambda e, xt=xt, hb=hb, rs=rs: e.scalar_tensor_tensor(out=hb[:], in0=xt[:], scalar=rs[:], in1=self.gbc[:], op0=ALU.mult, op1=ALU.mult),
                 reads=[rx, self.R("%s_rs%d" % (tag, b)), self.R("gbc")], writes=[rh])
            for half in range(2):
                rp = self.r_ptr[half]

                def tr(e, hb=hb, half=half):
                    ins = None
                    for q in range(4):
                        kc = half * 4 + q
                        ins = e.transpose(out=self.ptrh[half][:, q * 128:(q + 1) * 128],
                                          in_=hb[:, kc * 128:(kc + 1) * 128], identity=self.ident_b[:])
                    return ins
                p.op("pe", tr, reads=[rh, self.R("ident_b")], writes=[rp])
                dst = hT[:, half * 4:(half + 1) * 4, j * 128:(j + 1) * 128]
                srcp = self.ptrh[half][:, :].rearrange("p (q m) -> p q m", q=4)
                eng = "act" if half == 0 else "dve"
                if eng == "act":
                    p.op("act", lambda e, dst=dst, srcp=srcp: e.copy(out=dst, in_=srcp), reads=[rp], writes=[r_hT[j][half]])
                else:
                    p.op("dve", lambda e, dst=dst, srcp=srcp: e.tensor_copy(out=dst, in_=srcp), reads=[rp], writes=[r_hT[j][half]])

    def ffn(self, l, i):
        p = self.p
        nc = self.nc
        TB = 1024
        NTB = S // TB
        xsrc = self.x_src
        with contextlib.ExitStack() as st:
            def sb(name, shape, dt):
                return st.enter_context(nc.sbuf_tensor(name, list(shape), dt))
            tag = "f%d%d" % (l, i)
            wd_sb = sb(tag + "wd", [128, NFC, D], BF16)
            aT = sb(tag + "aT", [128, NFC, TB], BF16)
            hT = sb(tag + "hT", [128, 8, TB], BF16)
            wgu = [(sb(tag + "wg%d" % b, [128, 8, 512], BF16), sb(tag + "wu%d" % b, [128, 8, 512], BF16)) for b in range(2)]
            xt_bufs = [(sb(tag + "xt%d" % b, [128, D], F32), sb(tag + "hb%d" % b, [128, D], BF16),
                        sb(tag + "sq%d" % b, [128, D], BF16), sb(tag + "ss%d" % b, [128, 1], F32),
                        sb(tag + "rs%d" % b, [128, 1], F32)) for b in range(2)]
            sg = [sb(tag + "sg%d" % b, [128, 512], F32) for b in range(2)]
            xo = [sb(tag + "xo%d" % b, [128, D], F32) for b in range(2)]
            r_wd = [self.R(tag + "wd0"), self.R(tag + "wd1")]
            r_aT = self.R(tag + "aT")
            r_hT = [[self.R(tag + "hT%d_%d" % (jj, hh)) for hh in range(2)] for jj in range(TB // 128)]
            r_wg = [self.R(tag + "wgs%d" % b) for b in range(2)]
            r_wu = [self.R(tag + "wus%d" % b) for b in range(2)]
            r_sg = [self.R(tag + "sg%d" % b) for b in range(2)]
            r_xo = [self.R(tag + "xo%d" % b) for b in range(2)]

            self.load_gain(l * 3 + (0 if i == 0 else 2))
            for q in range(2):
                fa, fb = q * 11, (q + 1) * 11
                src = self.wdt[l, i, fa * 128:fb * 128, :].rearrange("(fc p) m -> p fc m", p=128)
                p.op("sp", lambda e, src=src, fa=fa, fb=fb: e.dma_start(out=wd_sb[:, fa:fb, :], in_=src),
                     reads=[self.R("wt_d_%d_%d_%d" % (l, i, 2 * q)), self.R("wt_d_%d_%d_%d" % (l, i, 2 * q + 1))], writes=[r_wd[q]], dma=self.st_w)

            wcount = 0
            for tb in range(NTB):
                t0 = tb * (TB // 128)
                self.norm_transpose(xsrc, t0, TB // 128, hT, r_hT, xt_bufs, tag)
                if self.dbg_stage <= 1:
                    continue
                for blk in range(6):
                    w = 512 if blk < 5 else 256
                    b = wcount % 2
                    wcount += 1
                    wgs, wus = wgu[b]
                    p.op("sp", lambda e, wgs=wgs, blk=blk, w=w: e.dma_start(out=wgs[:, :, 0:w], in_=self.wgt[l, i, blk, :, :, 0:w]),
                         reads=[self.R("wt_g_%d_%d_%d" % (l, i, blk))], writes=[r_wg[b]], dma=self.st_w)
                    p.op("sp", lambda e, wus=wus, blk=blk, w=w: e.dma_start(out=wus[:, :, 0:w], in_=self.wut[l, i, blk, :, :, 0:w]),
                         reads=[self.R("wt_u_%d_%d_%d" % (l, i, blk))], writes=[r_wu[b]], dma=self.st_w)
                    for m in range(w // 128):
                        fc = blk * 4 + m
                        for half in range(TB // 512):
                            pg = (fc * 2 + half) % 2
                            bg, bu = self.pbank[pg * 2], self.pbank[pg * 2 + 1]
                            rg, ru = self.r_pb[pg * 2], self.r_pb[pg * 2 + 1]

                            def mm(e, wt, bank, m=m, half=half):
                                ins = None
                                for kc in range(8):
                                    ins = e.matmul(bank[:, :], lhsT=wt[:, kc, m * 128:(m + 1) * 128],
                                                   rhs=hT[:, kc, half * 512:(half + 1) * 512],
                                                   start=(kc == 0), stop=(kc == 7))
                                return ins
                            hres = [r_hT[half * 4 + jj][hh] for jj in range(4) for hh in range(2)]
                            p.op("pe", lambda e, wgs=wgs, bg=bg, mm=mm: mm(e, wgs, bg), reads=[r_wg[b]] + hres, writes=[rg])
                            p.op("pe", lambda e, wus=wus, bu=bu, mm=mm: mm(e, wus, bu), reads=[r_wu[b]] + hres, writes=[ru])
                            sgb = sg[pg]
                            p.op("act", lambda e, sgb=sgb, bg=bg: e.activation(out=sgb[:], in_=bg[:, :], func=AF.Silu),
                                 reads=[rg], writes=[r_sg[pg]])
                            dst = aT[:, fc, half * 512:(half + 1) * 512]
                            p.op("dve", lambda e, dst=dst, sgb=sgb, bu=bu: e.tensor_tensor(out=dst, in0=sgb[:], in1=bu[:, :], op=ALU.mult),
                                 reads=[r_sg[pg], ru], writes=[r_aT])
                if self.dbg_stage <= 2:
                    continue
                for j in range(TB // 128):
                    tt = t0 + j
                    xb = j % 2
                    xt = xt_bufs[xb][0]
                    rx = self.R("%s_xt%d" % (tag, xb))
                    p.op("sp", lambda e, xt=xt, tt=tt: e.dma_start(out=xt[:], in_=xsrc[tt * 128:(tt + 1) * 128, :]),
                         reads=[self.R("xres")], writes=[rx], dma=self.st_x)
                    for ch in range(2):
                        pb = 4 + (j * 2 + ch) % 2
                        bank, rb = self.pbank[pb], self.r_pb[pb]

                        def mmd(e, bank=bank, j=j, ch=ch):
                            ins = None
                            for fc in range(NFC):
                                ins = e.matmul(bank[:, :], lhsT=aT[:, fc, j * 128:(j + 1) * 128],
                                               rhs=wd_sb[:, fc, ch * 512:(ch + 1) * 512],
                                               start=(fc == 0), stop=(fc == NFC - 1))
                            return ins
                        p.op("pe", mmd, reads=[r_aT] + r_wd, writes=[rb])
                        xob = xo[xb]
                        p.op("dve", lambda e, xob=xob, bank=bank, xt=xt, ch=ch: e.scalar_tensor_tensor(
                            out=xob[:, ch * 512:(ch + 1) * 512], in0=bank[:, :], scalar=0.5, in1=xt[:, ch * 512:(ch + 1) * 512],
                            op0=ALU.mult, op1=ALU.add), reads=[rb, rx], writes=[r_xo[xb]])
                    p.op("sp", lambda e, xob=xob, tt=tt: e.dma_start(out=self.xres[tt * 128:(tt + 1) * 128, :], in_=xob[:]),
                         reads=[r_xo[xb]], writes=[self.R("xres_w%d" % (tt % 4))], dma=self.st_o)
            self.phase_barrier()
        self.x_src = self.xres

    def _store_waits(self):
        st = self.st_o
        return [(st.sems[idx % st.R], 16 * (idx // st.R + 1)) for idx in range(max(0, st.k - st.R), st.k)]

    def phase_barrier(self):
        p = self.p
        sems = self._store_waits()

        def fn(e, sems=sems):
            for sem, val in sems:
                e.wait_ge(sem, val)
            return e.nop()
        p.op("sp", fn, reads=[self.R("xres_w%d" % k) for k in range(4)], writes=[self.R("xres")])
        p.barrier()

    def prep_ssm(self, j):
        p = self.p
        for (c0, c1) in ((0, 2048), (2048, 4096), (4096, 6144), (6144, 6176)):
            for hh in range(2):
                s_ap = self.ssm_w_in[j, hh * 512:(hh + 1) * 512, c0:c1]
                d_ap = self.swin[j, hh * 512:(hh + 1) * 512, c0:c1]
                p.op("pool", lambda e, s_ap=s_ap, d_ap=d_ap: e.dma_start(out=d_ap, in_=s_ap),
                     writes=[self.R("swin%d_%d_%d" % (j, c0, hh))], dma=self.st_prep)

    def swin_res(self, j, c0):
        base = (c0 // 2048) * 2048 if c0 < 6144 else 6144
        return [self.R("swin%d_%d_%d" % (j, base, hh)) for hh in range(2)]

    def ssd(self, l):
        p = self.p
        nc = self.nc
        j = l // 2
        TBK = 256
        NBLK = S // TBK
        NT = TBK // 128
        xsrc = self.x_src
        with contextlib.ExitStack() as st:
            def sb(name, shape, dt):
                return st.enter_context(nc.sbuf_tensor(tag + name, list(shape), dt))
            tag = "s%d" % l
            R = lambda n: self.R(tag + n)
            L1 = sb("L1", [128, 128], F32)
            L2 = sb("L2", [128, 128], F32)
            ONES = sb("ONES", [128, 128], F32)
            cw = sb("cw", [128, 32, 4], F32)
            cb = sb("cb", [128, 32], F32)
            vec = sb("vec", [128, 3, 32], F32)
            a_bc = sb("a_bc", [128, 32], F32)
            Dbc = sb("Dbc", [128, 32, 1], F32)
            nw = sb("nw", [128, 2048], F32)
            wout = sb("wout", [128, 16, 1024], BF16)
            wdt = sb("wdt", [128, 8, 32], BF16)
            halo = sb("halo", [128, 32, 3], F32)
            state = sb("state", [128, 2048], F32)
            state_bf = sb("state_bf", [128, 2048], BF16)
            hT = sb("hT", [128, 8, TBK], BF16)
            wx = [sb("wx%d" % b, [128, 8, 512], BF16) for b in range(2)]
            xin = [sb("xin%d" % b, [128, TBK + 3], F32) for b in range(3)]
            acc = [sb("acc%d" % b, [128, TBK], F32) for b in range(3)]
            xsb = [sb("xsb%d" % b, [128, TBK], BF16) for b in range(2)]
            BT = sb("BT", [128, 8, TBK], BF16)
            CT = sb("CT", [128, 8, TBK], BF16)
            Btok = sb("Btok", [128, NT, 1024], BF16)
            xs_tok = sb("xs_tok", [128, NT, 2048], BF16)
            sz = sb("sz", [128, NT, 2048], BF16)
            xt_bufs = [(sb("xt%d" % b, [128, D], F32), sb("hb%d" % b, [128, D], BF16),
                        sb("sq%d" % b, [128, D], BF16), sb("ss%d" % b, [128, 1], F32),
                        sb("rs%d" % b, [128, 1], F32)) for b in range(2)]
            dtv = sb("dtv", [128, 32], F32)
            dt3 = sb("dt3", [128, 32, 1], F32)
            da = sb("da", [128, 32], F32)
            eall = sb("eall", [128, 96], F32)
            ea3 = sb("ea3", [128, 32, 1], F32)
            w23 = sb("w23", [128, 32, 1], F32)
            xdt = sb("xdt", [128, 2048], BF16)
            xdec = sb("xdec", [128, 2048], BF16)
            ybuf = sb("ybuf", [128, 2048], F32)
            t3 = sb("t3", [128, 2048], F32)
            gnb = sb("gnb", [128, 2048], BF16)
            gnT = sb("gnT", [128, 16, 128], BF16)
            Eg = [sb("Eg%d" % b, [128, 4, 128], F32) for b in range(2)]
            MT = [sb("MT%d" % b, [128, 4, 128], BF16) for b in range(2)]
            GTm = [sb("GTm%d" % b, [128, 1, 128], F32) for b in range(3)]
            ybufD = [sb("ybufD%d" % b, [128, 256], F32) for b in range(2)]
            Ada4 = [sb("Ada4_%d" % b, [128, 4, 128], F32) for b in range(2)]
            ssg = sb("ssg", [128, 8], F32)
            rsg = sb("rsg", [128, 8, 1], F32)
            xo = sb("xo", [128, D], F32)
            xres_t = sb("xres_t", [128, D], F32)

            p.op("sp", lambda e: e.dma_start(out=L1[:], in_=self.tri_in[0]), writes=[R("L1")], dma=self.st_const)
            p.op("sp", lambda e: e.dma_start(out=L2[:], in_=self.tri_in[1]), writes=[R("L2")], dma=self.st_const)
            p.op("sp", lambda e: e.dma_start(out=ONES[:], in_=self.tri_in[2]), writes=[R("ONES")], dma=self.st_const)
            p.op("sp", lambda e: e.dma_start(out=cw[:], in_=self.ssm_cw[j]), writes=[R("cw")], dma=self.st_const)
            p.op("sp", lambda e: e.dma_start(out=cb[:], in_=self.ssm_cb[j]), writes=[R("cb")], dma=self.st_const)
            p.op("sp", lambda e: e.dma_start(out=vec[:].rearrange("p a b -> p (a b)"),
                                             in_=self.ssm_vec[j:j + 1, :].broadcast_to([128, 96])), writes=[R("vec")], dma=self.st_const)
            p.op("sp", lambda e: e.dma_start(out=nw[:], in_=self.ssm_norm[j:j + 1, :].broadcast_to([128, 2048])), writes=[R("nw")], dma=self.st_const)
            for q in range(4):
                src = self.ssm_w_out[j, q * 512:(q + 1) * 512, :].rearrange("(k p) m -> p k m", p=128)
                p.op("pool", lambda e, src=src, q=q: e.dma_start(out=wout[:, q * 4:(q + 1) * 4, :], in_=src), writes=[R("wout%d" % q)], dma=self.st_prep)
            r_wout = [R("wout%d" % q) for q in range(4)]
            p.op("sp", lambda e: e.dma_start(out=wdt[:], in_=self.swin[j, :, 6144:6176].rearrange("(k p) m -> p k m", p=128)),
                 reads=self.swin_res(j, 6144), writes=[R("wdt")], dma=self.st_const)
            p.op("act", lambda e: e.activation(out=a_bc[:], in_=vec[:, 1, :], func=AF.Exp), reads=[R("vec")], writes=[R("a_bc")])
            p.op("dve", lambda e: e.tensor_scalar(out=a_bc[:], in0=a_bc[:], scalar1=-1.0, scalar2=None, op0=ALU.mult), reads=[R("a_bc")], writes=[R("a_bc")])
            p.op("dve", lambda e: e.tensor_copy(out=Dbc[:, :, 0], in_=vec[:, 2, :]), reads=[R("vec")], writes=[R("Dbc")])
            p.op("pool", lambda e: e.memset(halo[:], 0.0), writes=[R("halo%d" % cc_) for cc_ in range(32)])
            p.op("pool", lambda e: e.memset(state[:], 0.0), writes=[R("state")])
            p.op("pool", lambda e: e.memset(state_bf[:], 0.0), writes=[R("state_bf")])
            self.load_gain(l * 3 + 1)

            r_hT = [[R("hT%d_%d" % (jj, hh)) for hh in range(2)] for jj in range(NT)]
            hres_all = [r_hT[jj][hh] for jj in range(NT) for hh in range(2)]
            wxc = 0
            cvc = 0
            pendB = []
            tails = []
            for blk in range(NBLK):
                t0 = blk * NT
                self.norm_transpose(xsrc, t0, NT, hT, r_hT, xt_bufs, tag)
                for wgI in range(8):
                    b = wxc % 2
                    wxc += 1
                    c0 = 2048 + wgI * 512
                    src = self.swin[j, :, c0:c0 + 512].rearrange("(k p) m -> p k m", p=128)
                    p.op("sp", lambda e, src=src, b=b: e.dma_start(out=wx[b][:], in_=src),
                         reads=self.swin_res(j, c0), writes=[R("wx%d" % b)], dma=self.st_w)
                    for m in range(4):
                        cc = wgI * 4 + m
                        pb = cc % 2
                        bank, rb = self.pbank[pb], self.r_pb[pb]

                        def mm(e, b=b, m=m, bank=bank):
                            for kc in range(8):
                                ins = e.matmul(bank[:, 0:TBK], lhsT=wx[b][:, kc, m * 128:(m + 1) * 128], rhs=hT[:, kc, :],
                                               start=(kc == 0), stop=(kc == 7))
                            return ins
                        p.op("pe", mm, reads=[R("wx%d" % b)] + hres_all, writes=[rb])
                        cbuf = cvc % 3
                        cvc += 1
                        xi, ac = xin[cbuf], acc[cbuf]
                        rxi, rac = R("xin%d" % cbuf), R("acc%d" % cbuf)
                        rhalo = R("halo%d" % cc)
                        p.op("pool", lambda e, xi=xi, cc=cc: e.tensor_copy(out=xi[:, 0:3], in_=halo[:, cc, :]), reads=[rhalo], writes=[rxi])
                        p.op("act", lambda e, xi=xi, bank=bank: e.copy(out=xi[:, 3:3 + TBK], in_=bank[:, 0:TBK]), reads=[rb], writes=[rxi])
                        p.op("pool", lambda e, xi=xi, cc=cc: e.tensor_copy(out=halo[:, cc, :], in_=xi[:, TBK:TBK + 3]), reads=[rxi], writes=[rhalo])
                        p.op("act", lambda e, xi=xi, ac=ac, cc=cc: e.activation(out=ac[:], in_=xi[:, 0:TBK], func=AF.Identity, scale=cw[:, cc, 0:1], bias=cb[:, cc:cc + 1]),
                             reads=[rxi, R("cw"), R("cb")], writes=[rac])
                        for w in range(1, 4):
                            p.op("dve", lambda e, xi=xi, ac=ac, cc=cc, w=w: e.scalar_tensor_tensor(out=ac[:], in0=xi[:, w:w + TBK], scalar=cw[:, cc, w:w + 1], in1=ac[:],
                                                                                                 op0=ALU.mult, op1=ALU.add), reads=[rxi, rac, R("cw")], writes=[rac])

                        def stageB(cc=cc, ac=ac, rac=rac, cbuf=cbuf):
                            if cc < 16:
                                xb_, rxb = xsb[cbuf % 2], R("xsb%d" % (cbuf % 2))
                                p.op("act", lambda e, ac=ac, xb_=xb_: e.activation(out=xb_[:], in_=ac[:], func=AF.Silu), reads=[rac], writes=[rxb])
                                hp = cc % 2
                                rp = self.r_ptr[hp]

                                def tr(e, xb_=xb_, hp=hp):
                                    for q in range(NT):
                                        ins = e.transpose(out=self.ptrh[hp][:, q * 128:(q + 1) * 128], in_=xb_[:, q * 128:(q + 1) * 128], identity=self.ident_b[:])
                                    return ins
                                p.op("pe", tr, reads=[rxb, self.R("ident_b")], writes=[rp])
                                dst = xs_tok[:, :, cc * 128:(cc + 1) * 128]
                                srcp = self.ptrh[hp][:, 0:NT * 128].rearrange("p (q m) -> p q m", q=NT)
                                p.op("dve", lambda e, dst=dst, srcp=srcp: e.tensor_copy(out=dst, in_=srcp), reads=[rp], writes=[R("xs_tok")])
                            elif cc < 24:
                                g = cc - 16
                                p.op("act", lambda e, ac=ac, g=g: e.activation(out=BT[:, g, :], in_=ac[:], func=AF.Silu), reads=[rac], writes=[R("BT%d" % g)])
                                hp = cc % 2
                                rp = self.r_ptr[hp]

                                def tr(e, g=g, hp=hp):
                                    for q in range(NT):
                                        ins = e.transpose(out=self.ptrh[hp][:, q * 128:(q + 1) * 128], in_=BT[:, g, q * 128:(q + 1) * 128], identity=self.ident_b[:])
                                    return ins
                                p.op("pe", tr, reads=[R("BT%d" % g), self.R("ident_b")], writes=[rp])
                                dst = Btok[:, :, g * 128:(g + 1) * 128]
                                srcp = self.ptrh[hp][:, 0:NT * 128].rearrange("p (q m) -> p q m", q=NT)
                                p.op("dve", lambda e, dst=dst, srcp=srcp: e.tensor_copy(out=dst, in_=srcp), reads=[rp], writes=[R("Btok")])
                            else:
                                g = cc - 24
                                p.op("act", lambda e, ac=ac, g=g: e.activation(out=CT[:, g, :], in_=ac[:], func=AF.Silu), reads=[rac], writes=[R("CT%d" % g)])
                        pendB.append(stageB)
                        while len(pendB) > 2:
                            pendB.pop(0)()
                        if cc == 3:
                            while tails:
                                tails.pop(0)()
                while pendB:
                    pendB.pop(0)()
                for zc in range(4):
                    b = wxc % 2
                    wxc += 1
                    c0 = zc * 512
                    src = self.swin[j, :, c0:c0 + 512].rearrange("(k p) m -> p k m", p=128)
                    p.op("sp", lambda e, src=src, b=b: e.dma_start(out=wx[b][:], in_=src),
                         reads=self.swin_res(j, c0), writes=[R("wx%d" % b)], dma=self.st_w)
                    for q in range(NT):
                        pb = (zc * NT + q) % 2
                        bank, rb = self.pbank[pb], self.r_pb[pb]

                        def mmz(e, b=b, q=q, bank=bank):
                            for kc in range(8):
                                ins = e.matmul(bank[:, :], lhsT=hT[:, kc, q * 128:(q + 1) * 128], rhs=wx[b][:, kc, :], start=(kc == 0), stop=(kc == 7))
                            return ins
                        p.op("pe", mmz, reads=[R("wx%d" % b)] + r_hT[q], writes=[rb])
                        p.op("act", lambda e, q=q, zc=zc, bank=bank: e.activation(out=sz[:, q, zc * 512:(zc + 1) * 512], in_=bank[:, :], func=AF.Silu),
                             reads=[rb], writes=[R("sz%d" % q)])
                for q in range(NT):
                    tt = t0 + q
                    tsl = slice(q * 128, (q + 1) * 128)
                    b0, rb0 = self.pbank[0], self.r_pb[0]

                    def mmdt(e, q=q):
                        for kc in range(8):
                            ins = e.matmul(b0[:, 0:32], lhsT=hT[:, kc, q * 128:(q + 1) * 128], rhs=wdt[:, kc, :], start=(kc == 0), stop=(kc == 7))
                        return ins
                    p.op("pe", mmdt, reads=[R("wdt")] + r_hT[q], writes=[rb0])
                    p.op("dve", lambda e: e.tensor_tensor(out=dtv[:], in0=b0[:, 0:32], in1=vec[:, 0, :], op=ALU.add), reads=[rb0, R("vec")], writes=[R("dtv")])
                    p.op("act", lambda e: e.activation(out=dtv[:], in_=dtv[:], func=AF.Exp), reads=[R("dtv")], writes=[R("dtv")])
                    p.op("act", lambda e: e.activation(out=dt3[:, :, 0], in_=dtv[:], func=AF.Ln, bias=1.0), reads=[R("dtv")], writes=[R("dt3")])
                    p.op("dve", lambda e: e.tensor_tensor(out=da[:], in0=dt3[:, :, 0], in1=a_bc[:], op=ALU.mult), reads=[R("dt3"), R("a_bc")], writes=[R("da")])

                    def mmcs(e):
                        e.matmul(b0[:, 32:64], lhsT=L1[:], rhs=da[:], start=True, stop=True)
                        e.matmul(b0[:, 64:96], lhsT=L2[:], rhs=da[:], start=True, stop=True)
                        return e.matmul(b0[:, 96:128], lhsT=ONES[:], rhs=da[:], start=True, stop=True)
                    p.op("pe", mmcs, reads=[R("da"), R("L1"), R("L2"), R("ONES")], writes=[rb0])
                    p.op("act", lambda e: e.activation(out=eall[:], in_=b0[:, 32:128], func=AF.Exp), reads=[rb0], writes=[R("eall")])
                    p.op("dve", lambda e: e.tensor_copy(out=ea3[:, :, 0], in_=eall[:, 0:32]), reads=[R("eall")], writes=[R("ea3")])
                    p.op("dve", lambda e: e.tensor_tensor(out=w23[:, :, 0], in0=dt3[:, :, 0], in1=eall[:, 32:64], op=ALU.mult), reads=[R("dt3"), R("eall")], writes=[R("w23")])
                    xs3 = xs_tok[:, q, :].rearrange("p (h d) -> p h d", h=32)
                    p.op("dve", lambda e, xs3=xs3: e.tensor_tensor(out=xdt[:].rearrange("p (h d) -> p h d", h=32), in0=xs3, in1=dt3[:].to_broadcast([128, 32, 64]), op=ALU.mult),
                         reads=[R("xs_tok"), R("dt3")], writes=[R("xdt")])
                    p.op("pool", lambda e, xs3=xs3: e.tensor_tensor(out=xdec[:].rearrange("p (h d) -> p h d", h=32), in0=xs3, in1=w23[:].to_broadcast([128, 32, 64]), op=ALU.mult),
                         reads=[R("xs_tok"), R("w23")], writes=[R("xdec")])
                    p.op("pool", lambda e, xs3=xs3: e.tensor_tensor(out=t3[:].rearrange("p (h d) -> p h d", h=32), in0=xs3, in1=Dbc[:].to_broadcast([128, 32, 64]), op=ALU.mult),
                         reads=[R("xs_tok"), R("Dbc")], writes=[R("t3")])
                    def S0(g, tsl=tsl):
                        gb, g3 = g % 2, g % 3
                        b1, rb1 = self.pbank[1], self.r_pb[1]
                        gcol = slice((g % 4) * 128, (g % 4 + 1) * 128)
                        p.op("pe", lambda e, g=g, gcol=gcol, tsl=tsl: e.matmul(b1[:, gcol], lhsT=BT[:, g, tsl], rhs=CT[:, g, tsl], start=True, stop=True),
                             reads=[R("BT%d" % g), R("CT%d" % g)], writes=[rb1])
                        p.op("dve", lambda e, g3=g3, gcol=gcol: e.tensor_tensor(out=GTm[g3][:, 0, :], in0=b1[:, gcol], in1=L1[:], op=ALU.mult),
                             reads=[rb1, R("L1")], writes=[R("GTm%d" % g3)])
                        for r in range(4):
                            h = 4 * g + r
                            p.op("act", lambda e, gb=gb, r=r, h=h: e.activation(out=Ada4[gb][:, r, :], in_=L2[:], func=AF.Copy, scale=da[:, h:h + 1]),
                                 reads=[R("L2"), R("da")], writes=[R("Ada4_%d" % gb)])

                    def S1(g):
                        gb = g % 2
                        bs, rbs = self.pbank[2 + gb], self.r_pb[2 + gb]

                        def mmseg(e, gb=gb, bs=bs):
                            for r in range(4):
                                ins = e.matmul(bs[:, r * 128:(r + 1) * 128], lhsT=Ada4[gb][:, r, :], rhs=L1[:], start=True, stop=True)
                            return ins
                        p.op("pe", mmseg, reads=[R("Ada4_%d" % gb), R("L1")], writes=[rbs])
                        p.op("act", lambda e, gb=gb, bs=bs: e.activation(out=Eg[gb][:].rearrange("p r l -> p (r l)"), in_=bs[:, :], func=AF.Exp),
                             reads=[rbs], writes=[R("Eg%d" % gb)])

                    def S2(g):
                        gb, g3 = g % 2, g % 3
                        p.op("dve", lambda e, gb=gb, g3=g3: e.tensor_tensor(out=MT[gb][:], in0=Eg[gb][:], in1=GTm[g3][:].to_broadcast([128, 4, 128]), op=ALU.mult),
                             reads=[R("Eg%d" % gb), R("GTm%d" % g3)], writes=[R("MT%d" % gb)])

                    def S3(g, tsl=tsl, q=q):
                        gb = g % 2
                        by, rby = self.pbank[4 + gb], self.r_pb[4 + gb]

                        def mmy(e, g=g, gb=gb, by=by, tsl=tsl):
                            for r in range(4):
                                h = 4 * g + r
                                e.matmul(by[:, r * 64:(r + 1) * 64], lhsT=MT[gb][:, r, :], rhs=xdt[:, h * 64:(h + 1) * 64], start=True, stop=True)
                            return e.matmul(by[:, 256:512], lhsT=CT[:, g, tsl], rhs=state_bf[:, g * 256:(g + 1) * 256], start=True, stop=True)
                        p.op("pe", mmy, reads=[R("MT%d" % gb), R("xdt"), R("CT%d" % g), R("state_bf")], writes=[rby])
                        p.op("pe", lambda e, g=g, q=q: e.matmul(b0[:, 256:512], lhsT=Btok[:, q, g * 128:(g + 1) * 128], rhs=xdec[:, g * 256:(g + 1) * 256], start=True, stop=True),
                             reads=[R("Btok"), R("xdec")], writes=[rb0])
                        for r in range(4):
                            h = 4 * g + r
                            p.op("dve", lambda e, h=h, r=r: e.scalar_tensor_tensor(out=state[:, h * 64:(h + 1) * 64], in0=state[:, h * 64:(h + 1) * 64],
                                                                                     scalar=eall[:, 64 + h:65 + h], in1=b0[:, 256 + r * 64:256 + (r + 1) * 64],
                                                                                     op0=ALU.mult, op1=ALU.add), reads=[R("state"), R("eall"), rb0], writes=[R("state")])
                        p.op("act", lambda e, gb=gb, by=by: e.copy(out=ybufD[gb][:], in_=by[:, 0:256]), reads=[rby], writes=[R("ybufD%d" % gb)])
                        yg = ybuf[:, g * 256:(g + 1) * 256].rearrange("p (r d) -> p r d", r=4)
                        p.op("dve", lambda e, yg=yg, by=by, g=g: e.tensor_tensor(out=yg, in0=by[:, 256:512].rearrange("p (r d) -> p r d", r=4),
                                                                               in1=ea3[:, 4 * g:4 * g + 4, :].to_broadcast([128, 4, 64]), op=ALU.mult),
                             reads=[rby, R("ea3")], writes=[R("ybuf")])
                        p.op("pool", lambda e, g=g, gb=gb: e.tensor_tensor(out=ybuf[:, g * 256:(g + 1) * 256], in0=ybuf[:, g * 256:(g + 1) * 256], in1=ybufD[gb][:], op=ALU.add),
                             reads=[R("ybuf"), R("ybufD%d" % gb)], writes=[R("ybuf")])

                    for k in range(11):
                        if k < 8:
                            S0(k)
                        if 0 <= k - 1 < 8:
                            S1(k - 1)
                        if 0 <= k - 2 < 8:
                            S2(k - 2)
                        if 0 <= k - 3 < 8:
                            S3(k - 3)
                        if k == 1:
                            while tails:
                                tails.pop(0)()
                    p.op("act", lambda e: e.copy(out=state_bf[:], in_=state[:]), reads=[R("state")], writes=[R("state_bf")])
                    p.op("pool", lambda e: e.tensor_tensor(out=ybuf[:], in0=ybuf[:], in1=t3[:], op=ALU.add), reads=[R("ybuf"), R("t3")], writes=[R("ybuf")])
                    p.op("pool", lambda e, q=q: e.tensor_tensor(out=ybuf[:], in0=ybuf[:], in1=sz[:, q, :], op=ALU.mult), reads=[R("ybuf"), R("sz%d" % q)], writes=[R("ybuf")])
                    for g in range(8):
                        p.op("act", lambda e, g=g: e.activation(out=gnb[:, g * 256:(g + 1) * 256], in_=ybuf[:, g * 256:(g + 1) * 256], func=AF.Square, accum_out=ssg[:, g:g + 1]),
                             reads=[R("ybuf")], writes=[R("gnb"), R("ssg")])
                    p.op("act", lambda e: e.activation(out=rsg[:, :, 0], in_=ssg[:], func=AF.Sqrt, scale=1.0 / 256, bias=self.epsb[:]), reads=[R("ssg"), self.R("epsb")], writes=[R("rsg")])
                    p.op("dve", lambda e: e.reciprocal(out=rsg[:, :, 0], in_=rsg[:, :, 0]), reads=[R("rsg")], writes=[R("rsg")])
                    p.op("dve", lambda e: e.tensor_tensor(out=ybuf[:].rearrange("p (g d) -> p g d", g=8), in0=ybuf[:].rearrange("p (g d) -> p g d", g=8),
                                                          in1=rsg[:].to_broadcast([128, 8, 256]), op=ALU.mult), reads=[R("ybuf"), R("rsg")], writes=[R("ybuf")])
                    p.op("pool", lambda e: e.tensor_tensor(out=gnb[:], in0=ybuf[:], in1=nw[:], op=ALU.mult), reads=[R("ybuf"), R("nw"), R("gnb")], writes=[R("gnb")])
                    def tail(tt=tt):
                        for qq in range(4):
                            hp = qq % 2
                            rp = self.r_ptr[hp]

                            def trg(e, qq=qq, hp=hp):
                                for m in range(4):
                                    k = qq * 4 + m
                                    ins = e.transpose(out=self.ptrh[hp][:, m * 128:(m + 1) * 128], in_=gnb[:, k * 128:(k + 1) * 128], identity=self.ident_b[:])
                                return ins
                            p.op("pe", trg, reads=[R("gnb"), self.R("ident_b")], writes=[rp])
                            dst = gnT[:, qq * 4:(qq + 1) * 4, :]
                            srcp = self.ptrh[hp][:, :].rearrange("p (q m) -> p q m", q=4)
                            if hp == 0:
                                p.op("act", lambda e, dst=dst, srcp=srcp: e.copy(out=dst, in_=srcp), reads=[rp], writes=[R("gnT%d" % qq)])
                            else:
                                p.op("dve", lambda e, dst=dst, srcp=srcp: e.tensor_copy(out=dst, in_=srcp), reads=[rp], writes=[R("gnT%d" % qq)])
                        p.op("sp", lambda e, tt=tt: e.dma_start(out=xres_t[:], in_=xsrc[tt * 128:(tt + 1) * 128, :]),
                             reads=[self.R("xres")], writes=[R("xres_t")], dma=self.st_x)
                        for ch in range(2):
                            bo, rbo = self.pbank[6 + ch], self.r_pb[6 + ch]

                            def mmo(e, ch=ch, bo=bo):
                                for k in range(16):
                                    ins = e.matmul(bo[:, :], lhsT=gnT[:, k, :], rhs=wout[:, k, ch * 512:(ch + 1) * 512], start=(k == 0), stop=(k == 15))
                                return ins
                            p.op("pe", mmo, reads=[R("gnT%d" % qq) for qq in range(4)] + r_wout, writes=[rbo])
                            p.op("dve", lambda e, ch=ch, bo=bo: e.tensor_tensor(out=xo[:, ch * 512:(ch + 1) * 512], in0=bo[:, :], in1=xres_t[:, ch * 512:(ch + 1) * 512], op=ALU.add),
                                 reads=[rbo, R("xres_t")], writes=[R("xo")])
                        p.op("sp", lambda e, tt=tt: e.dma_start(out=self.xres[tt * 128:(tt + 1) * 128, :], in_=xo[:]),
                             reads=[R("xo")], writes=[self.R("xres_w%d" % (tt % 4))], dma=self.st_o)
                    tails.append(tail)
            while tails:
                tails.pop(0)()
            self.phase_barrier()
        self.x_src = self.xres

    def prep_attn(self, j):
        p = self.p
        for (c0, c1) in ((0, 2048), (2048, NAW)):
            for hh in range(2):
                s_ap = self.attn_wr[j, hh * 512:(hh + 1) * 512, c0:c1]
                d_ap = self.awin[j, hh * 512:(hh + 1) * 512, c0:c1]
                p.op("pool", lambda e, s_ap=s_ap, d_ap=d_ap: e.dma_start(out=d_ap, in_=s_ap),
                     writes=[self.R("awin%d_%d_%d" % (j, c0, hh))], dma=self.st_prep)

    def awin_res(self, j, c0, c1):
        out = []
        for base in (0, 2048):
            hi = 2048 if base == 0 else NAW
            if c0 < hi and c1 > base:
                out += [self.R("awin%d_%d_%d" % (j, base, hh)) for hh in range(2)]
        return out

    def attn(self, l):
        p = self.p
        nc = self.nc
        j = l // 2
        xsrc = self.x_src
        tag = "a%d" % l
        R = lambda n: self.R(tag + n)
        BIG = 30000.0
        dbg = self.attn_dbg or ""
        use_cmp = ("nocmp" not in dbg)
        use_slc = ("noslc" not in dbg)
        use_win = ("nowin" not in dbg)
        with contextlib.ExitStack() as st_long:
            def sbl(name, shape, dt):
                return st_long.enter_context(nc.sbuf_tensor(tag + name, list(shape), dt))
            kT = sbl("kT", [64, 6, S], BF16)
            kx = sbl("kx", [128, 2, S], BF16)
            Vall = sbl("V", [128, 32, 6, 65], BF16)
            gates = sbl("gates", [128, 32, 24], F32)
            kcmpT = sbl("kcmpT", [64, 2, 256], BF16)
            Vcmp = sbl("Vcmp", [128, 2, 2, 129], BF16)
            esink = sbl("esink", [128, 8], F32)
            p.op("pool", lambda e: e.memset(Vall[:, :, :, 64:65], 1.0), writes=[R("Vones")])
            p.op("pool", lambda e: e.memset(kx[64:128, :, :], 1.0), writes=[R("kxE")])
            for g_ in range(2):
                p.op("pool", lambda e, g_=g_: e.affine_select(out=kx[64:128, g_, :].rearrange("p (b m) -> p b m", b=64), in_=kx[64:128, g_, :].rearrange("p (b m) -> p b m", b=64),
                                                             pattern=[[-1, 64], [0, 64]], compare_op=ALU.is_equal, fill=0.0, base=0, channel_multiplier=1),
                     reads=[R("kxE")], writes=[R("kxE")])
            p.op("sp", lambda e: e.dma_start(out=esink[:], in_=self.sinks_in[j:j + 1, :].broadcast_to([128, 8])), writes=[R("esink")], dma=self.st_const)
            p.op("act", lambda e: e.activation(out=esink[:], in_=esink[:], func=AF.Exp), reads=[R("esink")], writes=[R("esink")])
            self.load_gain(l * 3 + 1)

            with contextlib.ExitStack() as st_x:
                kcT = st_x.enter_context(nc.sbuf_tensor(tag + "kcT", [64, 2, S], BF16))
                vcT = st_x.enter_context(nc.sbuf_tensor(tag + "vcT", [128, S], BF16))
                with contextlib.ExitStack() as st1:
                    def sb(name, shape, dt):
                        return st1.enter_context(nc.sbuf_tensor(tag + name, list(shape), dt))
                    TBK = 512
                    NT = 4
                    hT = sb("hT", [128, 8, TBK], BF16)
                    wb = [sb("wb%d" % b, [128, 8, 512], BF16) for b in range(2)]
                    cosb = sb("cosb", [64, TBK], F32)
                    sinb = sb("sinb", [64, TBK], F32)
                    t1 = [sb("t1_%d" % b, [64, TBK], F32) for b in range(2)]
                    t2 = [sb("t2_%d" % b, [64, TBK], F32) for b in range(2)]
                    qst = [sb("qst%d" % b, [64, TBK], BF16) for b in range(2)]
                    xt_bufs = [(sb("xt%d" % b, [128, D], F32), sb("hb%d" % b, [128, D], BF16),
                                sb("sq%d" % b, [128, D], BF16), sb("ss%d" % b, [128, 1], F32),
                                sb("rs%d" % b, [128, 1], F32)) for b in range(2)]
                    r_hT = [[R("hT%d_%d" % (jj, hh)) for hh in range(2)] for jj in range(NT)]
                    hres_all = [r_hT[jj][hh] for jj in range(NT) for hh in range(2)]
                    wc = 0
                    hc = 0
                    for blk in range(S // TBK):
                        t0 = blk * NT
                        csl = slice(blk * TBK, (blk + 1) * TBK)
                        self.norm_transpose(xsrc, t0, NT, hT, r_hT, xt_bufs, tag)
                        p.op("sp", lambda e, csl=csl: e.dma_start(out=cosb[:], in_=self.rope_in[0, :, csl]), writes=[R("cosb")], dma=self.st_x)
                        p.op("sp", lambda e, csl=csl: e.dma_start(out=sinb[:], in_=self.rope_in[1, :, csl]), writes=[R("sinb")], dma=self.st_x)
                        for wgI in range(6):
                            b = wc % 2
                            wc += 1
                            c0 = wgI * 512
                            src = self.awin[j, :, c0:c0 + 512].rearrange("(k p) m -> p k m", p=128)
                            p.op("sp", lambda e, src=src, b=b: e.dma_start(out=wb[b][:], in_=src),
                                 reads=self.awin_res(j, c0, c0 + 512), writes=[R("wb%d" % b)], dma=self.st_w)
                            for m in range(4):
                                hd = wgI * 4 + m
                                hb_ = hc % 2
                                hc += 1
                                bA, rA = self.pbank[hb_ * 2], self.r_pb[hb_ * 2]
                                bB, rB = self.pbank[hb_ * 2 + 1], self.r_pb[hb_ * 2 + 1]

                                def mm(e, bank, off, b=b, m=m):
                                    for kc in range(8):
                                        ins = e.matmul(bank[0:64, :], lhsT=wb[b][:, kc, m * 128 + off:m * 128 + off + 64], rhs=hT[:, kc, :],
                                                       start=(kc == 0), stop=(kc == 7))
                                    return ins
                                p.op("pe", lambda e, mm=mm, bA=bA: mm(e, bA, 0), reads=[R("wb%d" % b)] + hres_all, writes=[rA])
                                p.op("pe", lambda e, mm=mm, bB=bB: mm(e, bB, 64), reads=[R("wb%d" % b)] + hres_all, writes=[rB])
                                p.op("dve", lambda e, hb_=hb_, bA=bA: e.tensor_tensor(out=t1[hb_][:], in0=bA[0:64, :], in1=cosb[:], op=ALU.mult),
                                     reads=[rA, R("cosb")], writes=[R("t1_%d" % hb_)])
                                p.op("dve", lambda e, hb_=hb_, bB=bB: e.tensor_tensor(out=t2[hb_][:], in0=bB[0:64, :], in1=sinb[:], op=ALU.mult),
                                     reads=[rB, R("sinb")], writes=[R("t2_%d" % hb_)])
                                if hd < 8 or 10 <= hd < 18:
                                    qh = hd if hd < 8 else hd - 10 + 8
                                    p.op("pool", lambda e, hb_=hb_: e.tensor_tensor(out=qst[hb_][:], in0=t1[hb_][:], in1=t2[hb_][:], op=ALU.add),
                                         reads=[R("t1_%d" % hb_), R("t2_%d" % hb_)], writes=[R("qst%d" % hb_)])
                                    p.op("sp", lambda e, hb_=hb_, qh=qh, csl=csl: e.dma_start(out=self.qT[qh, :, csl], in_=qst[hb_][:]),
                                         reads=[R("qst%d" % hb_)], writes=[self.R("qT_%d_%d" % (qh, blk))], dma=self.st_o)
                                else:
                                    if hd < 10:
                                        dst, rd = kT[:, hd - 8, csl], R("kT%d" % (hd - 8))
                                    elif hd < 20:
                                        dst, rd = kcT[:, hd - 18, csl], R("kcT%d" % (hd - 18))
                                    elif hd < 22:
                                        dst, rd = kx[0:64, hd - 20, csl], R("kxK%d" % (hd - 20))
                                    else:
                                        dst, rd = kT[:, 4 + hd - 22, csl], R("kT%d" % (4 + hd - 22))
                                    p.op("pool", lambda e, hb_=hb_, dst=dst: e.tensor_tensor(out=dst, in0=t1[hb_][:], in1=t2[hb_][:], op=ALU.add),
                                         reads=[R("t1_%d" % hb_), R("t2_%d" % hb_)], writes=[rd])
                        b = wc % 2
                        wc += 1
                        src = self.awin[j, :, 3072:3608].rearrange("(k p) m -> p k m", p=128)
                        src_vc = self.awin[j, :, 3072:3200].rearrange("(k p) m -> p k m", p=128)
                        src_tm = self.awin[j, :, 3200:3608].rearrange("(k p) m -> p k m", p=128)
                        b2 = wc % 2
                        wc += 1
                        p.op("sp", lambda e, b=b, src_vc=src_vc: e.dma_start(out=wb[b][:, :, 0:128], in_=src_vc),
                             reads=self.awin_res(j, 3072, 3200), writes=[R("wb%d" % b)], dma=self.st_w)
                        p.op("sp", lambda e, b2=b2, src_tm=src_tm: e.dma_start(out=wb[b2][:, :, 0:408], in_=src_tm),
                             reads=self.awin_res(j, 3200, 3608), writes=[R("wb%d" % b2)], dma=self.st_w)
                        bA, rA = self.pbank[4], self.r_pb[4]

                        def mmvc(e, b=b, bA=bA):
                            for kc in range(8):
                                ins = e.matmul(bA[:, :], lhsT=wb[b][:, kc, 0:128], rhs=hT[:, kc, :], start=(kc == 0), stop=(kc == 7))
                            return ins
                        p.op("pe", mmvc, reads=[R("wb%d" % b)] + hres_all, writes=[rA])
                        p.op("act", lambda e, bA=bA, csl=csl: e.copy(out=vcT[:, csl], in_=bA[:, :]), reads=[rA], writes=[R("vcT")])
                        for q in range(NT):
                            tt = t0 + q
                            bT, rT = self.pbank[5], self.r_pb[5]

                            def mmtm(e, q=q, b2=b2, bT=bT):
                                for kc in range(8):
                                    ins = e.matmul(bT[:, 0:408], lhsT=hT[:, kc, q * 128:(q + 1) * 128], rhs=wb[b2][:, kc, 0:408], start=(kc == 0), stop=(kc == 7))
                                return ins
                            p.op("pe", mmtm, reads=[R("wb%d" % b2)] + r_hT[q], writes=[rT])
                            p.op("dve", lambda e, tt=tt, bT=bT: e.tensor_copy(out=Vall[:, tt, :, 0:64], in_=bT[:, 0:384].rearrange("p (a d) -> p a d", a=6)),
                                 reads=[rT], writes=[R("Vall")])
                            p.op("act", lambda e, tt=tt, bT=bT: e.activation(out=gates[:, tt, :], in_=bT[:, 384:408], func=AF.Sigmoid),
                                 reads=[rT], writes=[R("gates")])
                    p.barrier()
                with contextlib.ExitStack() as st2:
                    def sb(name, shape, dt):
                        return st2.enter_context(nc.sbuf_tensor(tag + name, list(shape), dt))
                    w1s = sb("w1s", [64, 32, 128], BF16)
                    w2s = sb("w2s", [128, 64], BF16)
                    posf = sb("posf", [64, 32], F32)
                    posb_ = sb("posb", [64, 32, 2], BF16)
                    pbias = sb("pbias", [128, 1], F32)
                    u = sb("u", [128, 256], F32)
                    u2 = sb("u2", [128, 256], F32)
                    sg_ = sb("sgm", [128, 256], F32)
                    gl = sb("gl", [128, 256], BF16)
                    wself = sb("wself", [128, 2, 64], F32)
                    p.op("sp", lambda e: e.dma_start(out=wself[:], in_=self.wsel_in), writes=[R("wself")], dma=self.st_const)
                    p.op("dve", lambda e: e.memset(u[:], 0.0), writes=[R("u")])
                    p.op("pool", lambda e: e.memset(Vcmp[:, :, :, 64:65], 1.0), writes=[R("Vcmp1")])
                    for g in range(2):
                        p.op("dve", lambda e, g=g: e.tensor_copy(out=Vcmp[:, g, :, 65:129], in_=wself[:]), reads=[R("wself")], writes=[R("VcmpW%d" % g)])
                    for kv in range(2):
                        w1_in = self.cmp_w1[j, kv].rearrange("(pp d) h -> d pp h", d=64)
                        p.op("pool", lambda e, w1_in=w1_in: e.dma_start(out=w1s[:], in_=w1_in), writes=[R("w1s")], dma=self.st_prep)
                        p.op("pool", lambda e, kv=kv: e.dma_start(out=w2s[:], in_=self.cmp_w2[j, kv]), writes=[R("w2s")], dma=self.st_prep)
                        p.op("sp", lambda e, kv=kv: e.dma_start(out=posf[:], in_=self.cmp_posT[j, kv]), writes=[R("posf")], dma=self.st_const)
                        p.op("dve", lambda e: e.tensor_copy(out=posb_[:], in_=posf[:].rearrange("p (a o) -> p a o", o=1).to_broadcast([64, 32, 2])), reads=[R("posf")], writes=[R("posb")])
                        b0, rb0 = self.pbank[0], self.r_pb[0]

                        def mmb(e):
                            for pp in range(32):
                                ins = e.matmul(b0[:, 0:2], lhsT=w1s[:, pp, :], rhs=posb_[:, pp, :], start=(pp == 0), stop=(pp == 31))
                            return ins
                        p.op("pe", mmb, reads=[R("w1s"), R("posb")], writes=[rb0])
                        p.op("dve", lambda e: e.tensor_copy(out=pbias[:], in_=b0[:, 0:1]), reads=[rb0], writes=[R("pbias")])
                        for g in range(2):
                            b1, rb1 = self.pbank[1 + g], self.r_pb[1 + g]
                            if kv == 0:
                                srcT = kcT[:, g, :]
                                rsrc = R("kcT%d" % g)
                            else:
                                srcT = vcT[g * 64:(g + 1) * 64, :]
                                rsrc = R("vcT")

                            def mmh(e, srcT=srcT, b1=b1, g=g):
                                for pp in range(32):
                                    ins = e.matmul(b1[:, 0:255], lhsT=w1s[g * 64 * kv:g * 64 * kv + 64, pp, :] if False else w1s[:, pp, :],
                                                   rhs=srcT[:, pp:pp + 16 * 254 + 1:16], start=(pp == 0), stop=(pp == 31))
                                return ins
                            if kv == 1 and g == 1:
                                vtmp = sb("vtmp", [64, S], BF16)
                                p.op("sp", lambda e, vtmp=vtmp: e.dma_start(out=vtmp[:], in_=vcT[64:128, :]), reads=[R("vcT")], writes=[R("vtmp")], dma=self.st_x)
                                srcT2 = vtmp[:, :]

                                def mmh(e, srcT2=srcT2, b1=b1):
                                    for pp in range(32):
                                        ins = e.matmul(b1[:, 0:255], lhsT=w1s[:, pp, :], rhs=srcT2[:, pp:pp + 16 * 254 + 1:16], start=(pp == 0), stop=(pp == 31))
                                    return ins
                                rsrc = R("vtmp")
                            p.op("pe", mmh, reads=[R("w1s"), rsrc], writes=[rb1])
                            p.op("act", lambda e, b1=b1: e.activation(out=u[:, 0:255], in_=b1[:, 0:255], func=AF.Identity, bias=pbias[:]),
                                 reads=[rb1, R("pbias"), R("u")], writes=[R("u")])
                            p.op("dve", lambda e: e.tensor_tensor(out=u2[:], in0=u[:], in1=u[:], op=ALU.mult), reads=[R("u")], writes=[R("u2")])
                            p.op("dve", lambda e: e.tensor_scalar(out=u2[:], in0=u2[:], scalar1=0.044715, scalar2=1.0, op0=ALU.mult, op1=ALU.add), reads=[R("u2")], writes=[R("u2")])
                            p.op("dve", lambda e: e.tensor_tensor(out=u2[:], in0=u2[:], in1=u[:], op=ALU.mult), reads=[R("u2"), R("u")], writes=[R("u2")])
                            p.op("act", lambda e: e.activation(out=sg_[:], in_=u2[:], func=AF.Sigmoid, scale=1.5957691216057308), reads=[R("u2")], writes=[R("sgm")])
                            p.op("dve", lambda e: e.tensor_tensor(out=gl[:], in0=u[:], in1=sg_[:], op=ALU.mult), reads=[R("u"), R("sgm")], writes=[R("gl")])
                            b3, rb3 = self.pbank[3], self.r_pb[3]
                            if kv == 0:
                                p.op("pe", lambda e, b3=b3: e.matmul(b3[0:64, 0:256], lhsT=w2s[:], rhs=gl[:], start=True, stop=True), reads=[R("w2s"), R("gl")], writes=[rb3])
                                p.op("dve", lambda e, g=g, b3=b3: e.tensor_copy(out=kcmpT[:, g, :], in_=b3[0:64, 0:256]), reads=[rb3], writes=[R("kcmpT%d" % g)])
                            else:
                                def mmv(e, b3=b3):
                                    for ct in range(2):
                                        ins = e.matmul(b3[:, ct * 64:(ct + 1) * 64], lhsT=gl[:, ct * 128:(ct + 1) * 128], rhs=w2s[:], start=True, stop=True)
                                    return ins
                                p.op("pe", mmv, reads=[R("w2s"), R("gl")], writes=[rb3])
                                p.op("dve", lambda e, g=g, b3=b3: e.tensor_copy(out=Vcmp[:, g, :, 0:64], in_=b3[:, 0:128].rearrange("p (c d) -> p c d", c=2)),
                                     reads=[rb3], writes=[R("VcmpV%d" % g)])
                    p.barrier()
            with contextlib.ExitStack() as st3:
                def sb(name, shape, dt):
                    return st3.enter_context(nc.sbuf_tensor(tag + name, list(shape), dt))
                wout = sb("wout", [128, 8, D], BF16)
                for q in range(2):
                    src = self.attn_w_out[j, q * 512:(q + 1) * 512, :].rearrange("(k p) m -> p k m", p=128)
                    p.op("pool", lambda e, src=src, q=q: e.dma_start(out=wout[:, q * 4:(q + 1) * 4, :], in_=src), writes=[R("wout%d" % q)], dma=self.st_prep)
                r_wout = [R("wout0"), R("wout1")]
                QX = [sb("QX%d" % b_, [128, 4, 128], BF16) for b_ in range(2)]
                nbw = sb("nbw", [128, 128], F32)
                p.op("pool", lambda e: e.memset(nbw[:], 0.0), writes=[R("nbw")])
                selb = sb("selb", [128, 32, 64], F32)
                p.op("sp", lambda e: e.dma_start(out=selb[:], in_=self.selb_in), writes=[R("selb")], dma=self.st_const)
                qt = [sb("qt%d" % b, [64, 16, 256], BF16) for b in range(2)]
                Eb = [sb("E%d" % b, [128, 4, 128], BF16) for b in range(4)]
                SBANKS = (0, 1, 6)
                maskC = sb("maskC", [128, 4, 128], BF16)
                maskP = sb("maskP", [128, 4, 128], BF16)
                p.op("pool", lambda e: e.memset(maskC[:], 1.0), writes=[R("maskC")])
                p.op("pool", lambda e: e.memset(maskP[:], 1.0), writes=[R("maskP")])
                p.op("pool", lambda e: e.affine_select(out=maskC[:], in_=maskC[:], pattern=[[0, 4], [1, 128]], compare_op=ALU.is_ge, fill=0.0, base=0, channel_multiplier=-1),
                     reads=[R("maskC")], writes=[R("maskC")])
                p.op("pool", lambda e: e.affine_select(out=maskP[:], in_=maskP[:], pattern=[[0, 4], [-1, 128]], compare_op=ALU.is_ge, fill=0.0, base=-1, channel_multiplier=1),
                     reads=[R("maskP")], writes=[R("maskP")])
                ot = sb("ot", [128, D], BF16)
                accb = sb("accb", [128, 4, 64], F32)
                imp = sb("imp", [128, 64], F32)
                sc2 = sb("sc2", [128, 64], F32)
                m8 = sb("m8", [128, 8], F32)
                den = sb("den", [128, 4], F32)
                oT = sb("oT", [128, 8, 128], BF16)
                xres_t = sb("xres_t", [128, D], F32)
                xo = sb("xo", [128, D], F32)
                ec = [0]
                sc_ = [0]

                def score_exp(i, lhsT, lres, rhs, rres, mask, extra=None):
                    sb_i = SBANKS[sc_[0] % 3]
                    sc_[0] += 1
                    bank, rb = self.pbank[sb_i], self.r_pb[sb_i]
                    eb = ec[0] % 4
                    ec[0] += 1

                    def mm(e, bank=bank):
                        ins = e.matmul(bank[:, :].rearrange("p (r q) -> p r q", r=4), lhsT=lhsT, rhs=rhs, start=True, stop=(extra is None))
                        if extra is not None:
                            ins = e.matmul(bank[:, :].rearrange("p (r q) -> p r q", r=4), lhsT=extra[0], rhs=extra[1], start=False, stop=True)
                        return ins
                    rr = list(lres) + list(rres) + (list(extra[2]) if extra is not None else [])
                    p.op("pe", mm, reads=rr, writes=[rb])
                    E = Eb[eb]
                    rE = R("E%d" % eb)
                    p.op("act", lambda e, E=E, bank=bank: e.activation(out=E[:].rearrange("p r q -> p (r q)"), in_=bank[:, :], func=AF.Exp, scale=0.125),
                         reads=[rb], writes=[rE])
                    if mask is CAUSAL or mask is PREV:
                        mt_, rm_ = (maskC, R("maskC")) if mask is CAUSAL else (maskP, R("maskP"))
                        p.op("dve", lambda e, E=E, mt_=mt_: e.tensor_tensor(out=E[:], in0=E[:], in1=mt_[:], op=ALU.mult), reads=[rE, rm_], writes=[rE])
                    elif mask is not None:
                        base, cm, stepq = mask
                        p.op("pool", lambda e, E=E, base=base, cm=cm, stepq=stepq: e.affine_select(
                            out=E[:], in_=E[:], pattern=[[0, 4], [stepq, 128]], compare_op=ALU.is_ge, fill=0.0, base=base, channel_multiplier=cm),
                            reads=[rE], writes=[rE])
                    return E, rE

                def pv(E, rE, vrhs, vres, ncols, first, last):
                    def mm(e):
                        for r in range(4):
                            ins = e.matmul(self.pbank[2 + r][:, 0:ncols], lhsT=E[:, r, :], rhs=vrhs, start=first, stop=last)
                        return ins
                    p.op("pe", mm, reads=[rE] + list(vres), writes=[self.r_pb[2 + r] for r in range(4)])

                CAUSAL = (0, -1, 1)
                PREV = (-1, 1, -1)
                den2 = [den, sb("denB", [128, 4], F32)]
                bc = [0]

                def pv2(E, rE, vrhs, vres, ncols, first, last, par):
                    off = par * 256

                    def mm(e):
                        for r in range(4):
                            ins = e.matmul(self.pbank[2 + r][:, off:off + ncols], lhsT=E[:, r, :], rhs=vrhs, start=first, stop=last)
                        return ins
                    p.op("pe", mm, reads=[rE] + list(vres), writes=[R("O%d_%d" % (r, par)) for r in range(4)] + [self.r_pb[2 + r] for r in range(4)])

                def load_q(i0):
                    qb2 = (i0 // 2) % 2
                    blk = i0 // 4
                    src = self.qT[:, :, i0 * 128:i0 * 128 + 256].rearrange("h d t -> d h t")
                    p.op("sp", lambda e, src=src, qb2=qb2: e.dma_start(out=qt[qb2][:], in_=src),
                         reads=[self.R("qT_%d_%d" % (h, blk)) for h in range(16)], writes=[R("qt%d" % qb2)], dma=self.st_x)
                otB = [ot, sb("ot1", [128, D], BF16)]
                accbG = [accb, sb("accb1", [128, 4, 64], F32)]
                impG = [imp, sb("imp1", [128, 64], F32)]
                atails = []
                load_q(0)
                for i in range(32):
                    qb_ = (i // 2) % 2
                    if i % 2 == 0 and i + 2 < 32:
                        load_q(i + 2)
                    qsl = slice((i % 2) * 128, (i % 2 + 1) * 128)
                    rq = [R("qt%d" % qb_)]
                    ctxs = []

                    def make_g(g, i=i, qb_=qb_, qsl=qsl, rq=rq):
                        ot = otB[i % 2]
                        rot = R("ot%d" % (i % 2))
                        accb = accbG[g]
                        racc = R("accb%d" % g)
                        imp = impG[g]
                        rimp = R("imp%d" % g)
                        qa4 = qt[qb_][:, g * 4:(g + 1) * 4, qsl]
                        qb4 = qt[qb_][:, 8 + g * 4:8 + (g + 1) * 4, qsl]
                        gsl = gates[:, i, g * 12:(g + 1) * 12].rearrange("p (r b) -> p r b", b=3)
                        qxp = (2 * i + g) % 2
                        if use_slc:
                            p.op("pool", lambda e, qxp=qxp, qb4=qb4: e.tensor_copy(out=QX[qxp][0:64, :, :], in_=qb4), reads=rq, writes=[R("QXq%d" % qxp)])
                        tiles = []
                        kts = [kt for kt in (i - 1, i) if kt >= 0]
                        for n, kt in enumerate(kts):
                            tiles.append(("swa", kT[:, g, kt * 128:(kt + 1) * 128], [R("kT%d" % g)], qa4, CAUSAL if kt == i else PREV, None,
                                          Vall[:, kt, g, :], [R("Vall"), R("Vones")], 65, n == 0, n == len(kts) - 1))
                        nct = 1 if i < 16 else 2
                        if use_cmp or use_slc:
                            for ct in range(nct):
                                tiles.append(("cmp", kcmpT[:, g, ct * 128:(ct + 1) * 128], [R("kcmpT%d" % g)], qb4, (128 * i - 2048 * ct - 31, -16, 1), None,
                                              Vcmp[:, g, ct, :], [R("VcmpV%d" % g), R("VcmpW%d" % g), R("Vcmp1")], 129, ct == 0, ct == nct - 1))
                        if use_win:
                            kts = [kt for kt in range(i - 4, i + 1) if kt >= 0]
                            for n, kt in enumerate(kts):
                                mk = CAUSAL if kt == i else (PREV if kt == i - 4 else None)
                                tiles.append(("win", kT[:, 4 + g, kt * 128:(kt + 1) * 128], [R("kT%d" % (4 + g))], qb4, mk, None,
                                              Vall[:, kt, 4 + g, :], [R("Vall"), R("Vones")], 65, n == 0, n == len(kts) - 1))
                        if use_slc:
                            for kt in range(i + 1):
                                tiles.append(("slc", kx[:, g, kt * 128:(kt + 1) * 128], [R("kxK%d" % g), R("kxE"), R("QXq%d" % qxp), R("QXn%d" % qxp)], QX[qxp][:], CAUSAL if kt == i else None, None,
                                              Vall[:, kt, 2 + g, :], [R("Vall"), R("Vones")], 65, kt == 0, kt == i))
                        branches = [br for br in ("swa", "cmp", "win", "slc") if any(t[0] == br for t in tiles)]
                        last_nsa = [br for br in branches if br != "swa" and br != "cmp"]
                        last_nsa = last_nsa[-1] if last_nsa else "cmp"

                        def finish(br, par):
                            dn = den2[par]
                            rdn = R("den%d" % par)
                            rO = [R("O%d_%d" % (r, par)) for r in range(4)]
                            off = par * 256
                            O = [self.pbank[2 + r] for r in range(4)]
                            if br == "swa":
                                for r in range(4):
                                    h = g * 4 + r
                                    p.op("dve", lambda e, r=r, h=h: e.tensor_tensor(out=dn[:, r:r + 1], in0=O[r][:, off + 64:off + 65], in1=esink[:, h:h + 1], op=ALU.add),
                                         reads=[rO[r], R("esink")], writes=[rdn])
                                p.op("dve", lambda e: e.reciprocal(out=dn[:], in_=dn[:]), reads=[rdn], writes=[rdn])
                                for r in range(4):
                                    h = g * 4 + r
                                    p.op("dve", lambda e, r=r, h=h: e.tensor_scalar(out=ot[:, h * 64:(h + 1) * 64], in0=O[r][:, off:off + 64], scalar1=dn[:, r:r + 1], scalar2=None, op0=ALU.mult),
                                         reads=[rO[r], rdn], writes=[rot])
                                return
                            if br == "cmp":
                                for r in range(4):
                                    p.op("dve", lambda e, r=r: e.tensor_scalar(out=dn[:, r:r + 1], in0=O[r][:, off + 64:off + 65], scalar1=1e-30, scalar2=None, op0=ALU.max),
                                         reads=[rO[r]], writes=[rdn])
                                p.op("dve", lambda e: e.reciprocal(out=dn[:], in_=dn[:]), reads=[rdn], writes=[rdn])
                                for r in range(4):
                                    if r == 0:
                                        p.op("dve", lambda e, r=r: e.tensor_scalar(out=imp[:], in0=O[r][:, off + 65:off + 129], scalar1=dn[:, r:r + 1], scalar2=None, op0=ALU.mult),
                                             reads=[rO[r], rdn], writes=[rimp])
                                    else:
                                        p.op("dve", lambda e, r=r: e.scalar_tensor_tensor(out=imp[:], in0=O[r][:, off + 65:off + 129], scalar=dn[:, r:r + 1], in1=imp[:], op0=ALU.mult, op1=ALU.add),
                                             reads=[rO[r], rdn, rimp], writes=[rimp])
                                if use_slc:
                                    p.op("dve", lambda e, i=i: e.tensor_tensor(out=imp[:], in0=imp[:], in1=selb[:, i, :], op=ALU.add), reads=[rimp, R("selb")], writes=[rimp])
                                    p.op("dve", lambda e: e.max(out=m8[:], in_=imp[:]), reads=[rimp], writes=[R("m8")])
                                    p.op("dve", lambda e: e.match_replace(out=sc2[:], in_to_replace=m8[:], in_values=imp[:], imm_value=-3.0e38), reads=[rimp, R("m8")], writes=[R("sc2")])
                                    p.op("dve", lambda e: e.max(out=m8[:], in_=sc2[:]), reads=[R("sc2"), R("m8")], writes=[R("m8")])
                                    p.op("dve", lambda e: e.tensor_scalar(out=nbw[:, 64:128], in0=imp[:], scalar1=m8[:, 7:8], scalar2=-BIG, op0=ALU.is_lt, op1=ALU.mult),
                                         reads=[rimp, R("m8"), R("nbw")], writes=[R("nbw")])
                                    sb_i = SBANKS[sc_[0] % 3]
                                    sc_[0] += 1
                                    bank, rb = self.pbank[sb_i], self.r_pb[sb_i]
                                    p.op("pe", lambda e, bank=bank: e.transpose(out=bank[:, 0:128], in_=nbw[:], identity=self.ident_f[:]), reads=[R("nbw"), self.R("ident")], writes=[rb])
                                    p.op("dve", lambda e, bank=bank, qxp=qxp: e.tensor_copy(out=QX[qxp][64:128, :, :], in_=bank[64:128, 0:128].rearrange("p (o q) -> p o q", o=1).to_broadcast([64, 4, 128])),
                                         reads=[rb], writes=[R("QXn%d" % qxp)])
                                p.op("dve", lambda e, gsl=gsl: e.tensor_tensor(out=dn[:], in0=dn[:], in1=gsl[:, :, 0], op=ALU.mult), reads=[rdn, R("gates")], writes=[rdn])
                                for r in range(4):
                                    h = 8 + g * 4 + r
                                    if not use_cmp:
                                        p.op("dve", lambda e, r=r: e.memset(accb[:, r, :], 0.0), reads=[racc], writes=[racc])
                                    elif last_nsa == "cmp":
                                        p.op("dve", lambda e, r=r, h=h: e.tensor_scalar(out=ot[:, h * 64:(h + 1) * 64], in0=O[r][:, off:off + 64], scalar1=dn[:, r:r + 1], scalar2=None, op0=ALU.mult),
                                             reads=[rO[r], rdn], writes=[rot])
                                    else:
                                        p.op("dve", lambda e, r=r: e.tensor_scalar(out=accb[:, r, :], in0=O[r][:, off:off + 64], scalar1=dn[:, r:r + 1], scalar2=None, op0=ALU.mult),
                                             reads=[rO[r], rdn], writes=[racc])
                                return
                            gi = 2 if br == "win" else 1
                            for r in range(4):
                                p.op("dve", lambda e, r=r: e.tensor_copy(out=dn[:, r:r + 1], in_=O[r][:, off + 64:off + 65]), reads=[rO[r]], writes=[rdn])
                            p.op("dve", lambda e: e.reciprocal(out=dn[:], in_=dn[:]), reads=[rdn], writes=[rdn])
                            p.op("dve", lambda e, gsl=gsl, gi=gi: e.tensor_tensor(out=dn[:], in0=dn[:], in1=gsl[:, :, gi], op=ALU.mult), reads=[rdn, R("gates")], writes=[rdn])
                            for r in range(4):
                                h = 8 + g * 4 + r
                                if br == last_nsa:
                                    p.op("dve", lambda e, r=r, h=h: e.scalar_tensor_tensor(out=ot[:, h * 64:(h + 1) * 64], in0=O[r][:, off:off + 64], scalar=dn[:, r:r + 1], in1=accb[:, r, :], op0=ALU.mult, op1=ALU.add),
                                         reads=[rO[r], rdn, racc], writes=[rot])
                                else:
                                    p.op("dve", lambda e, r=r: e.scalar_tensor_tensor(out=accb[:, r, :], in0=O[r][:, off:off + 64], scalar=dn[:, r:r + 1], in1=accb[:, r, :], op0=ALU.mult, op1=ALU.add),
                                         reads=[rO[r], rdn, racc], writes=[racc])

                        par_of = {}
                        for br in branches:
                            par_of[br] = 0
                        return tiles, finish, par_of

                    for g in range(2):
                        ctxs.append(make_g(g))
                    merged = []
                    for brs in (("swa", "cmp"), ("win",), ("slc",)):
                        for gi in range(2):
                            for br in brs:
                                merged += [(gi, tl) for tl in ctxs[gi][0] if tl[0] == br]
                    queue = []
                    LOOK = 2

                    def pop():
                        pE, prE, gi, ptl = queue.pop(0)
                        pv2(pE, prE, ptl[6], ptl[7], ptl[8], ptl[9], ptl[10], 0)
                        if ptl[10]:
                            ctxs[gi][1](ptl[0], 0)
                    ntl = 0
                    for gi, tl in merged:
                        br, lhsT, lres, rhs, mask, extra, vrhs, vres, ncols, first, last = tl
                        if br == "slc" and first:
                            while any(qq[3][0] == "cmp" and qq[2] == gi for qq in queue):
                                pop()
                        E, rE = score_exp(i, lhsT, lres, rhs, rq, mask, extra=extra)
                        queue.append((E, rE, gi, tl))
                        while len(queue) > LOOK:
                            pop()
                        ntl += 1
                        if ntl == 4:
                            while atails:
                                atails.pop(0)()
                    while queue:
                        pop()

                    def atail(i=i):
                        ot = otB[i % 2]
                        rot = R("ot%d" % (i % 2))
                        p.op("sp", lambda e, i=i: e.dma_start(out=xres_t[:], in_=xsrc[i * 128:(i + 1) * 128, :]),
                             reads=[self.R("xres")], writes=[R("xres_t")], dma=self.st_x)
                        for half in range(2):
                            rp = self.r_ptr[1]

                            def tr(e, half=half):
                                for q in range(4):
                                    kc = half * 4 + q
                                    ins = e.transpose(out=self.ptrh[1][:, q * 128:(q + 1) * 128], in_=ot[:, kc * 128:(kc + 1) * 128], identity=self.ident_b[:])
                                return ins
                            p.op("pe", tr, reads=[rot, self.R("ident_b")], writes=[rp])
                            dst = oT[:, half * 4:(half + 1) * 4, :]
                            srcp = self.ptrh[1][:, :].rearrange("p (q m) -> p q m", q=4)
                            p.op("act", lambda e, dst=dst, srcp=srcp: e.copy(out=dst, in_=srcp), reads=[rp], writes=[R("oT%d" % half)])
                        for ch in range(2):
                            bo, rbo = self.pbank[7], self.r_pb[7]

                            def mmo(e, ch=ch, bo=bo):
                                for k in range(8):
                                    ins = e.matmul(bo[:, :], lhsT=oT[:, k, :], rhs=wout[:, k, ch * 512:(ch + 1) * 512], start=(k == 0), stop=(k == 7))
                                return ins
                            p.op("pe", mmo, reads=[R("oT0"), R("oT1")] + r_wout, writes=[rbo])
                            p.op("dve", lambda e, ch=ch, bo=bo: e.tensor_tensor(out=xo[:, ch * 512:(ch + 1) * 512], in0=bo[:, :], in1=xres_t[:, ch * 512:(ch + 1) * 512], op=ALU.add),
                                 reads=[rbo, R("xres_t")], writes=[R("xo")])
                        if "ot" in dbg:
                            p.op("dve", lambda e: e.tensor_copy(out=xo[:], in_=ot[:]), reads=[rot, R("xo")], writes=[R("xo")])
                        p.op("sp", lambda e, i=i: e.dma_start(out=self.xres[i * 128:(i + 1) * 128, :], in_=xo[:]),
                             reads=[R("xo")], writes=[self.R("xres_w%d" % (i % 4))], dma=self.st_o)
                    atails.append(atail)
                while atails:
                    atails.pop(0)()
            self.phase_barrier()
        self.x_src = self.xres

    def final_norm(self):
        p = self.p
        nc = self.nc
        xsrc = self.x_src
        with contextlib.ExitStack() as st:
            def sb(name, shape, dt):
                return st.enter_context(nc.sbuf_tensor(name, list(shape), dt))
            self.load_gain(DEPTH * 3)
            bufs = [(sb("fn_x%d" % b, [128, D], F32), sb("fn_sq%d" % b, [128, D], BF16), sb("fn_ss%d" % b, [128, 1], F32),
                     sb("fn_rs%d" % b, [128, 1], F32), sb("fn_o%d" % b, [128, D], F32)) for b in range(2)]
            for tt in range(S // 128):
                b = tt % 2
                xt, sq, ss, rs, ot = bufs[b]
                rx, rss, rrs, ro = [self.R("fn_%s%d" % (n, b)) for n in ("x", "ss", "rs", "o")]
                p.op("sp", lambda e, xt=xt, tt=tt: e.dma_start(out=xt[:], in_=xsrc[tt * 128:(tt + 1) * 128, :]),
                     reads=[self.R("xres")], writes=[rx], dma=self.st_x)
                p.op("act", lambda e, xt=xt, sq=sq, ss=ss: e.activation(out=sq[:], in_=xt[:], func=AF.Square, accum_out=ss[:]),
                     reads=[rx], writes=[self.R("fn_sq%d" % b), rss])
                p.op("act", lambda e, ss=ss, rs=rs: e.activation(out=rs[:], in_=ss[:], func=AF.Sqrt, scale=1.0 / D, bias=self.epsb[:]),
                     reads=[rss, self.R("epsb")], writes=[rrs])
                p.op("dve", lambda e, rs=rs: e.reciprocal(out=rs[:], in_=rs[:]), reads=[rrs], writes=[rrs])
                p.op("dve", lambda e, xt=xt, ot=ot, rs=rs: e.scalar_tensor_tensor(out=ot[:], in0=xt[:], scalar=rs[:], in1=self.gbc[:], op0=ALU.mult, op1=ALU.mult),
                     reads=[rx, rrs, self.R("gbc")], writes=[ro])
                p.op("sp", lambda e, ot=ot, tt=tt: e.dma_start(out=self.out[tt * 128:(tt + 1) * 128, :], in_=ot[:]),
                     reads=[ro], writes=[self.R("out_w%d" % (tt % 4))], dma=self.st_o)
            self.final_wait()

    def copy_out(self):
        p = self.p
        nc = self.nc
        xsrc = self.x_src
        with contextlib.ExitStack() as st:
            bufs = [st.enter_context(nc.sbuf_tensor("co%d" % b, [128, D], F32)) for b in range(2)]
            for tt in range(S // 128):
                b = tt % 2
                rx = self.R("co%d" % b)
                p.op("sp", lambda e, b=b, tt=tt: e.dma_start(out=bufs[b][:], in_=xsrc[tt * 128:(tt + 1) * 128, :]),
                     reads=[self.R("xres")], writes=[rx], dma=self.st_x)
                p.op("sp", lambda e, b=b, tt=tt: e.dma_start(out=self.out[tt * 128:(tt + 1) * 128, :], in_=bufs[b][:]),
                     reads=[rx], writes=[self.R("out_w%d" % (tt % 4))], dma=self.st_o)
            self.final_wait()

    def final_wait(self):
        p = self.p
        sems = self._store_waits()

        def fn(e, sems=sems):
            for sem, val in sems:
                e.wait_ge(sem, val)
            return e.nop()
        p.op("sp", fn, reads=[self.R("out_w%d" % k) for k in range(4)], writes=[self.R("done")])


def full_plan():
    def prep(l):
        out = [("prep_ffn", l, 0)]
        out.append(("prep_attn", l // 2) if l % 2 == 0 else ("prep_ssm", l // 2))
        out.append(("prep_ffn", l, 1))
        return out
    plan = prep(0)
    for l in range(DEPTH):
        if l + 1 < DEPTH:
            plan += prep(l + 1)
        plan.append(("ffn", l, 0))
        plan.append(("attn", l) if l % 2 == 0 else ("ssd", l))
        plan.append(("ffn", l, 1))
    plan.append(("final",))
    return plan


_CACHE = {}


def attn_w_layout(w):
    heads = [(h * 64) for h in range(8)] + [512, 576] + [768 + h * 64 for h in range(8)] + [1280, 1344] + [1536, 1600] + [1792, 1856]
    cols = []
    for c0 in heads:
        cols += list(range(c0, c0 + 64)) + list(range(c0 + 32, c0 + 64)) + list(range(c0, c0 + 32))
    cols += list(range(1408, 1536))
    cols += list(range(640, 768)) + list(range(1664, 1792)) + list(range(1920, 2048)) + list(range(2048, 2072))
    assert len(cols) == NAW
    return np.ascontiguousarray(w[:, :, np.asarray(cols)])


def _rope_tables():
    inv = (1.0 / (np.float32(10000.0) ** (np.arange(0, 64, 2, dtype=np.float32) / np.float32(64)))).astype(np.float32)
    ang = (np.arange(S, dtype=np.float32)[:, None] * inv[None, :]).astype(np.float32)
    c = np.cos(ang).astype(np.float32).T
    s_ = np.sin(ang).astype(np.float32).T
    return np.ascontiguousarray(np.stack([np.concatenate([c, c], 0), np.concatenate([-s_, s_], 0)], 0))


def _wsel():
    n_cmp = (S - 32) // 16 + 1
    cs = np.arange(n_cmp) * 16
    ss = np.arange(S // 64) * 64
    ov = np.minimum(cs[:, None] + 32, ss[None, :] + 64) - np.maximum(cs[:, None], ss[None, :])
    w = np.zeros((256, 64), np.float32)
    w[:n_cmp] = np.clip(ov, 0, None) / 32.0
    return np.ascontiguousarray(w.reshape(2, 128, 64).transpose(1, 0, 2))


def _selb():
    t = np.arange(S)
    cur = (t // 64)[:, None]
    jj = np.arange(64)[None, :]
    valid = jj <= cur
    forced = valid & ((jj == 0) | (jj == cur) | (jj == cur - 1))
    b = np.where(forced, 1e4, 0.0) - np.where(valid, 0.0, 1e4)
    return np.ascontiguousarray(b.astype(np.float32).reshape(32, 128, 64).transpose(1, 0, 2))


ROPE = _rope_tables()
WSEL = _wsel()
SELB = _selb()
_ii = np.arange(128)
TRI = np.stack([(_ii[:, None] <= _ii[None, :]), (_ii[:, None] > _ii[None, :]), np.ones((128, 128), bool)]).astype(np.float32)


def run_plan(plan, inputs, n_cores=8, trace=False):
    key = repr(plan)
    if key not in _CACHE:
        _CACHE[key] = Builder(plan).build()
    nc = _CACHE[key]
    x = np.ascontiguousarray(inputs["x"], dtype=np.float32)
    gains = np.concatenate([np.asarray(inputs["norm_gains"], np.float32).reshape(DEPTH * 3, D),
                            np.asarray(inputs["final_norm"], np.float32).reshape(1, D)], axis=0)
    common = {
        "gains": np.ascontiguousarray(gains),
        "ffn_w_gate": np.ascontiguousarray(inputs["ffn_w_gate"], dtype=np.float32),
        "ffn_w_up": np.ascontiguousarray(inputs["ffn_w_up"], dtype=np.float32),
        "ffn_w_down": np.ascontiguousarray(inputs["ffn_w_down"], dtype=np.float32),
        "ident": np.eye(128, dtype=np.float32),
        "tri": TRI,
        "ssm_w_in": np.ascontiguousarray(inputs["ssm_w_in"], dtype=np.float32),
        "ssm_w_out": np.ascontiguousarray(inputs["ssm_w_out"], dtype=np.float32),
        "ssm_cw": np.ascontiguousarray(np.asarray(inputs["ssm_conv_w"], np.float32).transpose(0, 2, 1).reshape(2, 32, 128, 4).transpose(0, 2, 1, 3)),
        "ssm_cb": np.ascontiguousarray(np.asarray(inputs["ssm_conv_b"], np.float32).reshape(2, 32, 128).transpose(0, 2, 1)),
        "ssm_vec": np.ascontiguousarray(np.stack([np.asarray(inputs["ssm_dt_bias"], np.float32), np.asarray(inputs["ssm_a_log"], np.float32),
                                                  np.asarray(inputs["ssm_d"], np.float32)], axis=1).reshape(2, 96)),
        "ssm_norm": np.ascontiguousarray(inputs["ssm_norm"], dtype=np.float32),
        "attn_wr": attn_w_layout(np.asarray(inputs["attn_w_in"], np.float32)),
        "attn_w_out": np.ascontiguousarray(inputs["attn_w_out"], dtype=np.float32),
        "attn_sinks": np.ascontiguousarray(inputs["attn_sinks"], dtype=np.float32),
        "rope": ROPE,
        "cmp_w1": np.ascontiguousarray(np.stack([np.asarray(inputs["cmp_k_w1"], np.float32), np.asarray(inputs["cmp_v_w1"], np.float32)], axis=1)),
        "cmp_w2": np.ascontiguousarray(np.stack([np.asarray(inputs["cmp_k_w2"], np.float32), np.asarray(inputs["cmp_v_w2"], np.float32)], axis=1)),
        "cmp_posT": np.ascontiguousarray(np.stack([np.asarray(inputs["cmp_k_pos"], np.float32).transpose(0, 2, 1),
                                                   np.asarray(inputs["cmp_v_pos"], np.float32).transpose(0, 2, 1)], axis=1)),
        "wsel": WSEL,
        "selb": SELB,
    }
    in_maps = []
    for c in range(n_cores):
        m = dict(common)
        m["x"] = x[c % 4]
        in_maps.append(m)
    res = run_bass_kernel_spmd(nc, in_maps, core_ids=list(range(n_cores)), trace=trace)
    out = np.stack([res.results[c % n_cores]["out"] for c in range(4)], axis=0)
    return out, res


def kernel(**inputs):
    out, _ = run_plan(full_plan(), inputs)
    return out.astype(np.float32)
```

```python
import contextlib
import numpy as np
import concourse.bass as bass
import concourse.mybir as mybir
from concourse.bass_utils import run_bass_kernel_spmd

F32 = mybir.dt.float32
BF16 = mybir.dt.bfloat16
AF = mybir.ActivationFunctionType
ALU = mybir.AluOpType
AX = mybir.AxisListType

D = 1024
S = 4096
DEPTH = 4
DFF = 2816
NFC = DFF // 128
EPS = 1e-6
NAW = 3608


class Res:
    __slots__ = ("name", "w", "r")

    def __init__(self, name):
        self.name = name
        self.w = None
        self.r = []


class Op:
    __slots__ = ("eng", "fn", "waits", "inc", "dma", "dsem", "dval", "cnt", "pre")


class Prog:
    ENGS = ("pe", "act", "dve", "pool", "sp")

    def __init__(self, nc, stack):
        self.nc = nc
        self.stack = stack
        self.ops = {e: [] for e in self.ENGS}
        self.esem = {e: stack.enter_context(nc.semaphore("es_" + e)) for e in self.ENGS}
        self.nsem = 5
        self.streams = []

    def new_sem(self, name):
        self.nsem += 1
        return self.stack.enter_context(self.nc.semaphore(name))

    def op(self, eng, fn, reads=(), writes=(), dma=None):
        o = Op()
        o.eng = eng
        o.fn = fn
        o.inc = False
        o.dma = dma
        o.cnt = 0
        o.pre = None
        o.dsem = None
        o.dval = 0
        deps = []
        seen = set()

        def add(d, raw):
            if d is None or id(d) in seen:
                return
            if d.dma is None and d.eng == eng:
                if eng == "pe" or not raw:
                    return
            seen.add(id(d))
            deps.append(d)

        for r in reads:
            add(r.w, True)
        for w in writes:
            add(w.w, False)
            for rr in w.r:
                add(rr, False)
        for d in deps:
            if d.dma is None:
                d.inc = True
        o.waits = deps
        if dma is not None:
            sem, val, pre = dma.next()
            o.dsem, o.dval, o.pre = sem, val, pre
            dma.ops.append(o)
        for r in reads:
            r.r.append(o)
        for w in writes:
            w.w = o
            w.r = []
        self.ops[eng].append(o)
        return o

    def barrier(self):
        deps = []
        for e in self.ENGS:
            for o in reversed(self.ops[e]):
                if o.dma is None:
                    deps.append(o)
                    break
        for st in self.streams:
            deps.extend(st.ops[-st.R:])
        for e in self.ENGS:
            o = Op()
            o.eng = e
            o.fn = lambda eng: eng.nop()
            o.inc = False
            o.dma = None
            o.cnt = 0
            o.pre = None
            o.dsem = None
            o.dval = 0
            o.waits = [d for d in deps if not (d.dma is None and d.eng == e)]
            for d in o.waits:
                if d.dma is None:
                    d.inc = True
            self.ops[e].append(o)

    def emit(self):
        nc = self.nc
        for e in self.ENGS:
            c = 0
            for o in self.ops[e]:
                if o.dma is None and o.inc:
                    c += 1
                o.cnt = c
        self.counts = {e: (len(self.ops[e]), self.ops[e][-1].cnt if self.ops[e] else 0) for e in self.ENGS}

        def body_for(ename):
            def body(eng):
                seen = {}

                def wait(sem, val):
                    k = id(sem)
                    if seen.get(k, 0) >= val:
                        return
                    seen[k] = val
                    eng.wait_ge(sem, val)

                for o in self.ops[ename]:
                    for d in o.waits:
                        if d.dma is not None:
                            wait(d.dsem, d.dval)
                        else:
                            wait(self.esem[d.eng], d.cnt)
                    if o.pre is not None and o.pre[1] > 0:
                        wait(o.pre[0], o.pre[1])
                    ins = o.fn(eng)
                    if o.dma is not None:
                        ins.then_inc(o.dsem, 16)
                    elif o.inc:
                        ins.then_inc(self.esem[ename], 1)
            return body

        with nc.Block() as block:
            block.tensor(body_for("pe"))
            block.scalar(body_for("act"))
            block.vector(body_for("dve"))
            block.gpsimd(body_for("pool"))
            block.sync(body_for("sp"))


class DmaStream:
    def __init__(self, prog, name, R):
        self.sems = [prog.new_sem("%s%d" % (name, i)) for i in range(R)]
        self.R = R
        self.k = 0
        self.ops = []
        prog.streams.append(self)

    def next(self):
        k = self.k
        self.k += 1
        sem = self.sems[k % self.R]
        return sem, 16 * (k // self.R + 1), (sem, 16 * (k // self.R))


class Builder:
    def __init__(self, plan):
        self.plan = plan
        self.nc = bass.Bass("TRN2", target_bir_lowering=False)
        self.stack = contextlib.ExitStack()
        self.res_cache = {}

    def R(self, name):
        r = self.res_cache.get(name)
        if r is None:
            r = self.res_cache[name] = Res(name)
        return r

    def dram_in(self, name, shape, dt=F32):
        return self.nc.dram_tensor(name, list(shape), dt, kind="ExternalInput").ap()

    def dram_out(self, name, shape, dt=F32):
        return self.nc.dram_tensor(name, list(shape), dt, kind="ExternalOutput").ap()

    def dram_tmp(self, name, shape, dt):
        return self.nc.dram_tensor(name, list(shape), dt).ap()

    def sb(self, name, shape, dt):
        return self.stack.enter_context(self.nc.sbuf_tensor(name, list(shape), dt))

    def ps(self, name, shape, dt):
        return self.stack.enter_context(self.nc.psum_tensor(name, list(shape), dt))

    def build(self):
        nc = self.nc
        with self.stack:
            self.p = Prog(nc, self.stack)
            self._build()
            self.p.emit()
        return nc

    def _build(self):
        p = self.p
        plan = self.plan
        self.x_in = self.dram_in("x", [S, D])
        self.gains = self.dram_in("gains", [DEPTH * 3 + 1, D])
        self.wg = self.dram_in("ffn_w_gate", [DEPTH, 2, D, DFF])
        self.wu = self.dram_in("ffn_w_up", [DEPTH, 2, D, DFF])
        self.wd = self.dram_in("ffn_w_down", [DEPTH, 2, DFF, D])
        self.ident_in = self.dram_in("ident", [128, 128])
        self.tri_in = self.dram_in("tri", [3, 128, 128])
        self.ssm_w_in = self.dram_in("ssm_w_in", [2, D, 6176])
        self.ssm_w_out = self.dram_in("ssm_w_out", [2, 2048, D])
        self.ssm_cw = self.dram_in("ssm_cw", [2, 128, 32, 4])
        self.ssm_cb = self.dram_in("ssm_cb", [2, 128, 32])
        self.ssm_vec = self.dram_in("ssm_vec", [2, 96])
        self.ssm_norm = self.dram_in("ssm_norm", [2, 2048])
        self.swin = self.dram_tmp("swin", [2, D, 6176], BF16)
        self.attn_wr = self.dram_in("attn_wr", [2, D, NAW])
        self.attn_w_out = self.dram_in("attn_w_out", [2, D, D])
        self.sinks_in = self.dram_in("attn_sinks", [2, 8])
        self.rope_in = self.dram_in("rope", [2, 64, S])
        self.cmp_w1 = self.dram_in("cmp_w1", [2, 2, 2048, 128])
        self.cmp_w2 = self.dram_in("cmp_w2", [2, 2, 128, 64])
        self.cmp_posT = self.dram_in("cmp_posT", [2, 2, 64, 32])
        self.wsel_in = self.dram_in("wsel", [128, 2, 64])
        self.selb_in = self.dram_in("selb", [128, 32, 64])
        self.awin = self.dram_tmp("awin", [2, D, NAW], BF16)
        self.qT = self.dram_tmp("qT", [16, 64, S], BF16)
        self.out = self.dram_out("out", [S, D])
        self.xres = self.dram_tmp("xres", [S, D], F32)
        self.wgt = self.dram_tmp("wgt", [DEPTH, 2, 6, 128, 8, 512], BF16)
        self.wut = self.dram_tmp("wut", [DEPTH, 2, 6, 128, 8, 512], BF16)
        self.wdt = self.dram_tmp("wdt", [DEPTH, 2, DFF, D], BF16)

        self.st_const = DmaStream(p, "dc", 1)
        self.st_prep = DmaStream(p, "dp", 4)
        self.st_x = DmaStream(p, "dx", 4)
        self.st_w = DmaStream(p, "dw", 4)
        self.st_o = DmaStream(p, "do", 4)

        self.ident_f = self.sb("ident_f", [128, 128], F32)
        self.ident_b = self.sb("ident_b", [128, 128], BF16)
        self.gbc = self.sb("gbc", [128, D], F32)
        self.epsb = self.sb("epsb", [128, 1], F32)
        r_ident = self.R("ident")
        p.op("sp", lambda e: e.dma_start(out=self.ident_f[:], in_=self.ident_in), writes=[r_ident], dma=self.st_const)
        p.op("dve", lambda e: e.tensor_copy(out=self.ident_b[:], in_=self.ident_f[:]), reads=[r_ident], writes=[self.R("ident_b")])
        p.op("dve", lambda e: e.memset(self.epsb[:], EPS), writes=[self.R("epsb")])

        self.pbank = [self.ps("pb%d" % i, [128, 512], F32) for i in range(8)]
        self.ptrh = [self.pbank[6 + i][:, :].bitcast(BF16)[:, 0:512] for i in range(2)]
        self.r_pb = [self.R("pb%d" % i) for i in range(8)]
        self.r_ptr = [self.r_pb[6], self.r_pb[7]]

        self.x_src = self.x_in
        for ph in plan:
            kind = ph[0]
            if kind == "prep_ffn":
                self.prep_ffn(ph[1], ph[2])
            elif kind == "ffn":
                self.dbg_stage = ph[3] if len(ph) > 3 else 99
                self.ffn(ph[1], ph[2])
            elif kind == "prep_ssm":
                self.prep_ssm(ph[1])
            elif kind == "ssd":
                self.ssd(ph[1])
            elif kind == "prep_attn":
                self.prep_attn(ph[1])
            elif kind == "attn":
                self.attn_dbg = ph[2] if len(ph) > 2 else None
                self.attn(ph[1])
            elif kind == "final":
                self.final_norm()
            elif kind == "copy_out":
                self.copy_out()
            else:
                raise ValueError(kind)

    def load_gain(self, row):
        p = self.p
        src = self.gains[row:row + 1, :].broadcast_to([128, D])
        p.op("sp", lambda e: e.dma_start(out=self.gbc[:], in_=src), writes=[self.R("gbc")], dma=self.st_const)

    def prep_ffn(self, l, i):
        p = self.p
        for (src, dst, nm) in ((self.wg, self.wgt, "g"), (self.wu, self.wut, "u")):
            for blk in range(6):
                w = 512 if blk < 5 else 256
                s_ap = src[l, i, :, blk * 512:blk * 512 + w].rearrange("(kc p) m -> p kc m", p=128)
                d_ap = dst[l, i, blk, :, :, 0:w]
                p.op("pool", lambda e, s_ap=s_ap, d_ap=d_ap: e.dma_start(out=d_ap, in_=s_ap),
                     writes=[self.R("wt_%s_%d_%d_%d" % (nm, l, i, blk))], dma=self.st_prep)
        for q in range(4):
            rows = DFF // 4
            s_ap = self.wd[l, i, q * rows:(q + 1) * rows, :]
            d_ap = self.wdt[l, i, q * rows:(q + 1) * rows, :]
            p.op("pool", lambda e, s_ap=s_ap, d_ap=d_ap: e.dma_start(out=d_ap, in_=s_ap),
                 writes=[self.R("wt_d_%d_%d_%d" % (l, i, q))], dma=self.st_prep)

    def norm_transpose(self, xsrc, t0, ntile, hT, r_hT, xt_bufs, tag):
        p = self.p
        for j in range(ntile):
            tt = t0 + j
            b = j % 2
            xt, hb, sq, ss, rs = xt_bufs[b]
            rx = self.R("%s_xt%d" % (tag, b))
            rh = self.R("%s_hb%d" % (tag, b))
            rss = self.R("%s_ss%d" % (tag, b))
            rsq = self.R("%s_sq%d" % (tag, b))
            p.op("sp", lambda e, xt=xt, tt=tt: e.dma_start(out=xt[:], in_=xsrc[tt * 128:(tt + 1) * 128, :]),
                 reads=[self.R("xres")], writes=[rx], dma=self.st_x)
            p.op("act", lambda e, xt=xt, sq=sq, ss=ss: e.activation(out=sq[:], in_=xt[:], func=AF.Square, accum_out=ss[:]),
                 reads=[rx], writes=[rsq, rss])
            p.op("act", lambda e, ss=ss, rs=rs: e.activation(out=rs[:], in_=ss[:], func=AF.Sqrt, scale=1.0 / D, bias=self.epsb[:]),
                 reads=[rss, self.R("epsb")], writes=[self.R("%s_rs%d" % (tag, b))])
            p.op("dve", lambda e, rs=rs: e.reciprocal(out=rs[:], in_=rs[:]),
                 reads=[self.R("%s_rs%d" % (tag, b))], writes=[self.R("%s_rs%d" % (tag, b))])
            p.op("dve", lambda e, xt=xt, hb=hb, rs=rs: e.scalar_tensor_tensor(out=hb[:], in0=xt[:], scalar=rs[:], in1=self.gbc[:], op0=ALU.mult, op1=ALU.mult),
                 reads=[rx, self.R("%s_rs%d" % (tag, b)), self.R("gbc")], writes=[rh])
            for half in range(2):
                rp = self.r_ptr[half]

                def tr(e, hb=hb, half=half):
                    ins = None
                    for q in range(4):
                        kc = half * 4 + q
                        ins = e.transpose(out=self.ptrh[half][:, q * 128:(q + 1) * 128],
                                          in_=hb[:, kc * 128:(kc + 1) * 128], identity=self.ident_b[:])
                    return ins
                p.op("pe", tr, reads=[rh, self.R("ident_b")], writes=[rp])
                dst = hT[:, half * 4:(half + 1) * 4, j * 128:(j + 1) * 128]
                srcp = self.ptrh[half][:, :].rearrange("p (q m) -> p q m", q=4)
                eng = "act" if half == 0 else "dve"
                if eng == "act":
                    p.op("act", lambda e, dst=dst, srcp=srcp: e.copy(out=dst, in_=srcp), reads=[rp], writes=[r_hT[j][half]])
                else:
                    p.op("dve", lambda e, dst=dst, srcp=srcp: e.tensor_copy(out=dst, in_=srcp), reads=[rp], writes=[r_hT[j][half]])

    def ffn(self, l, i):
        p = self.p
        nc = self.nc
        TB = 1024
        NTB = S // TB
        xsrc = self.x_src
        with contextlib.ExitStack() as st:
            def sb(name, shape, dt):
                return st.enter_context(nc.sbuf_tensor(name, list(shape), dt))
            tag = "f%d%d" % (l, i)
            wd_sb = sb(tag + "wd", [128, NFC, D], BF16)
            aT = sb(tag + "aT", [128, NFC, TB], BF16)
            hT = sb(tag + "hT", [128, 8, TB], BF16)
            wgu = [(sb(tag + "wg%d" % b, [128, 8, 512], BF16), sb(tag + "wu%d" % b, [128, 8, 512], BF16)) for b in range(2)]
            xt_bufs = [(sb(tag + "xt%d" % b, [128, D], F32), sb(tag + "hb%d" % b, [128, D], BF16),
                        sb(tag + "sq%d" % b, [128, D], BF16), sb(tag + "ss%d" % b, [128, 1], F32),
                        sb(tag + "rs%d" % b, [128, 1], F32)) for b in range(2)]
            sg = [sb(tag + "sg%d" % b, [128, 512], F32) for b in range(2)]
            xo = [sb(tag + "xo%d" % b, [128, D], F32) for b in range(2)]
            r_wd = [self.R(tag + "wd0"), self.R(tag + "wd1")]
            r_aT = self.R(tag + "aT")
            r_hT = [[self.R(tag + "hT%d_%d" % (jj, hh)) for hh in range(2)] for jj in range(TB // 128)]
            r_wg = [self.R(tag + "wgs%d" % b) for b in range(2)]
            r_wu = [self.R(tag + "wus%d" % b) for b in range(2)]
            r_sg = [self.R(tag + "sg%d" % b) for b in range(2)]
            r_xo = [self.R(tag + "xo%d" % b) for b in range(2)]

            self.load_gain(l * 3 + (0 if i == 0 else 2))
            for q in range(2):
                fa, fb = q * 11, (q + 1) * 11
                src = self.wdt[l, i, fa * 128:fb * 128, :].rearrange("(fc p) m -> p fc m", p=128)
                p.op("sp", lambda e, src=src, fa=fa, fb=fb: e.dma_start(out=wd_sb[:, fa:fb, :], in_=src),
                     reads=[self.R("wt_d_%d_%d_%d" % (l, i, 2 * q)), self.R("wt_d_%d_%d_%d" % (l, i, 2 * q + 1))], writes=[r_wd[q]], dma=self.st_w)

            wcount = 0
            for tb in range(NTB):
                t0 = tb * (TB // 128)
                self.norm_transpose(xsrc, t0, TB // 128, hT, r_hT, xt_bufs, tag)
                if self.dbg_stage <= 1:
                    continue
                for blk in range(6):
                    w = 512 if blk < 5 else 256
                    b = wcount % 2
                    wcount += 1
                    wgs, wus = wgu[b]
                    p.op("sp", lambda e, wgs=wgs, blk=blk, w=w: e.dma_start(out=wgs[:, :, 0:w], in_=self.wgt[l, i, blk, :, :, 0:w]),
                         reads=[self.R("wt_g_%d_%d_%d" % (l, i, blk))], writes=[r_wg[b]], dma=self.st_w)
                    p.op("sp", lambda e, wus=wus, blk=blk, w=w: e.dma_start(out=wus[:, :, 0:w], in_=self.wut[l, i, blk, :, :, 0:w]),
                         reads=[self.R("wt_u_%d_%d_%d" % (l, i, blk))], writes=[r_wu[b]], dma=self.st_w)
                    for m in range(w // 128):
                        fc = blk * 4 + m
                        for half in range(TB // 512):
                            pg = (fc * 2 + half) % 2
                            bg, bu = self.pbank[pg * 2], self.pbank[pg * 2 + 1]
                            rg, ru = self.r_pb[pg * 2], self.r_pb[pg * 2 + 1]

                            def mm(e, wt, bank, m=m, half=half):
                                ins = None
                                for kc in range(8):
                                    ins = e.matmul(bank[:, :], lhsT=wt[:, kc, m * 128:(m + 1) * 128],
                                                   rhs=hT[:, kc, half * 512:(half + 1) * 512],
                                                   start=(kc == 0), stop=(kc == 7))
                                return ins
                            hres = [r_hT[half * 4 + jj][hh] for jj in range(4) for hh in range(2)]
                            p.op("pe", lambda e, wgs=wgs, bg=bg, mm=mm: mm(e, wgs, bg), reads=[r_wg[b]] + hres, writes=[rg])
                            p.op("pe", lambda e, wus=wus, bu=bu, mm=mm: mm(e, wus, bu), reads=[r_wu[b]] + hres, writes=[ru])
                            sgb = sg[pg]
                            p.op("act", lambda e, sgb=sgb, bg=bg: e.activation(out=sgb[:], in_=bg[:, :], func=AF.Silu),
                                 reads=[rg], writes=[r_sg[pg]])
                            dst = aT[:, fc, half * 512:(half + 1) * 512]
                            p.op("dve", lambda e, dst=dst, sgb=sgb, bu=bu: e.tensor_tensor(out=dst, in0=sgb[:], in1=bu[:, :], op=ALU.mult),
                                 reads=[r_sg[pg], ru], writes=[r_aT])
                if self.dbg_stage <= 2:
                    continue
                for j in range(TB // 128):
                    tt = t0 + j
                    xb = j % 2
                    xt = xt_bufs[xb][0]
                    rx = self.R("%s_xt%d" % (tag, xb))
                    p.op("sp", lambda e, xt=xt, tt=tt: e.dma_start(out=xt[:], in_=xsrc[tt * 128:(tt + 1) * 128, :]),
                         reads=[self.R("xres")], writes=[rx], dma=self.st_x)
                    for ch in range(2):
                        pb = 4 + (j * 2 + ch) % 2
                        bank, rb = self.pbank[pb], self.r_pb[pb]

                        def mmd(e, bank=bank, j=j, ch=ch):
                            ins = None
                            for fc in range(NFC):
                                ins = e.matmul(bank[:, :], lhsT=aT[:, fc, j * 128:(j + 1) * 128],
                                               rhs=wd_sb[:, fc, ch * 512:(ch + 1) * 512],
                                               start=(fc == 0), stop=(fc == NFC - 1))
                            return ins
                        p.op("pe", mmd, reads=[r_aT] + r_wd, writes=[rb])
                        xob = xo[xb]
                        p.op("dve", lambda e, xob=xob, bank=bank, xt=xt, ch=ch: e.scalar_tensor_tensor(
                            out=xob[:, ch * 512:(ch + 1) * 512], in0=bank[:, :], scalar=0.5, in1=xt[:, ch * 512:(ch + 1) * 512],
                            op0=ALU.mult, op1=ALU.add), reads=[rb, rx], writes=[r_xo[xb]])
                    p.op("sp", lambda e, xob=xob, tt=tt: e.dma_start(out=self.xres[tt * 128:(tt + 1) * 128, :], in_=xob[:]),
                         reads=[r_xo[xb]], writes=[self.R("xres_w%d" % (tt % 4))], dma=self.st_o)
            self.phase_barrier()
        self.x_src = self.xres

    def _store_waits(self):
        st = self.st_o
        return [(st.sems[idx % st.R], 16 * (idx // st.R + 1)) for idx in range(max(0, st.k - st.R), st.k)]

    def phase_barrier(self):
        p = self.p
        sems = self._store_waits()

        def fn(e, sems=sems):
            for sem, val in sems:
                e.wait_ge(sem, val)
            return e.nop()
        p.op("sp", fn, reads=[self.R("xres_w%d" % k) for k in range(4)], writes=[self.R("xres")])
        p.barrier()

    def prep_ssm(self, j):
        p = self.p
        for (c0, c1) in ((0, 2048), (2048, 4096), (4096, 6144), (6144, 6176)):
            for hh in range(2):
                s_ap = self.ssm_w_in[j, hh * 512:(hh + 1) * 512, c0:c1]
                d_ap = self.swin[j, hh * 512:(hh + 1) * 512, c0:c1]
                p.op("pool", lambda e, s_ap=s_ap, d_ap=d_ap: e.dma_start(out=d_ap, in_=s_ap),
                     writes=[self.R("swin%d_%d_%d" % (j, c0, hh))], dma=self.st_prep)

    def swin_res(self, j, c0):
        base = (c0 // 2048) * 2048 if c0 < 6144 else 6144
        return [self.R("swin%d_%d_%d" % (j, base, hh)) for hh in range(2)]

    def ssd(self, l):
        p = self.p
        nc = self.nc
        j = l // 2
        TBK = 256
        NBLK = S // TBK
        NT = TBK // 128
        xsrc = self.x_src
        with contextlib.ExitStack() as st:
            def sb(name, shape, dt):
                return st.enter_context(nc.sbuf_tensor(tag + name, list(shape), dt))
            tag = "s%d" % l
            R = lambda n: self.R(tag + n)
            L1 = sb("L1", [128, 128], F32)
            L2 = sb("L2", [128, 128], F32)
            ONES = sb("ONES", [128, 128], F32)
            cw = sb("cw", [128, 32, 4], F32)
            cb = sb("cb", [128, 32], F32)
            vec = sb("vec", [128, 3, 32], F32)
            a_bc = sb("a_bc", [128, 32], F32)
            Dbc = sb("Dbc", [128, 32, 1], F32)
            nw = sb("nw", [128, 2048], F32)
            wout = sb("wout", [128, 16, 1024], BF16)
            wdt = sb("wdt", [128, 8, 32], BF16)
            halo = sb("halo", [128, 32, 3], F32)
            state = sb("state", [128, 2048], F32)
            state_bf = sb("state_bf", [128, 2048], BF16)
            hT = sb("hT", [128, 8, TBK], BF16)
            wx = [sb("wx%d" % b, [128, 8, 512], BF16) for b in range(2)]
            xin = [sb("xin%d" % b, [128, TBK + 3], F32) for b in range(3)]
            acc = [sb("acc%d" % b, [128, TBK], F32) for b in range(3)]
            xsb = [sb("xsb%d" % b, [128, TBK], BF16) for b in range(2)]
            BT = sb("BT", [128, 8, TBK], BF16)
            CT = sb("CT", [128, 8, TBK], BF16)
            Btok = sb("Btok", [128, NT, 1024], BF16)
            xs_tok = sb("xs_tok", [128, NT, 2048], BF16)
            sz = sb("sz", [128, NT, 2048], BF16)
            xt_bufs = [(sb("xt%d" % b, [128, D], F32), sb("hb%d" % b, [128, D], BF16),
                        sb("sq%d" % b, [128, D], BF16), sb("ss%d" % b, [128, 1], F32),
                        sb("rs%d" % b, [128, 1], F32)) for b in range(2)]
            dtv = sb("dtv", [128, 32], F32)
            dt3 = sb("dt3", [128, 32, 1], F32)
            da = sb("da", [128, 32], F32)
            eall = sb("eall", [128, 96], F32)
            ea3 = sb("ea3", [128, 32, 1], F32)
            w23 = sb("w23", [128, 32, 1], F32)
            xdt = sb("xdt", [128, 2048], BF16)
            xdec = sb("xdec", [128, 2048], BF16)
            ybuf = sb("ybuf", [128, 2048], F32)
            t3 = sb("t3", [128, 2048], F32)
            gnb = sb("gnb", [128, 2048], BF16)
            gnT = sb("gnT", [128, 16, 128], BF16)
            Eg = [sb("Eg%d" % b, [128, 4, 128], F32) for b in range(2)]
            MT = [sb("MT%d" % b, [128, 4, 128], BF16) for b in range(2)]
            GTm = [sb("GTm%d" % b, [128, 1, 128], F32) for b in range(3)]
            ybufD = [sb("ybufD%d" % b, [128, 256], F32) for b in range(2)]
            Ada4 = [sb("Ada4_%d" % b, [128, 4, 128], F32) for b in range(2)]
            ssg = sb("ssg", [128, 8], F32)
            rsg = sb("rsg", [128, 8, 1], F32)
            xo = sb("xo", [128, D], F32)
            xres_t = sb("xres_t", [128, D], F32)

            p.op("sp", lambda e: e.dma_start(out=L1[:], in_=self.tri_in[0]), writes=[R("L1")], dma=self.st_const)
            p.op("sp", lambda e: e.dma_start(out=L2[:], in_=self.tri_in[1]), writes=[R("L2")], dma=self.st_const)
            p.op("sp", lambda e: e.dma_start(out=ONES[:], in_=self.tri_in[2]), writes=[R("ONES")], dma=self.st_const)
            p.op("sp", lambda e: e.dma_start(out=cw[:], in_=self.ssm_cw[j]), writes=[R("cw")], dma=self.st_const)
            p.op("sp", lambda e: e.dma_start(out=cb[:], in_=self.ssm_cb[j]), writes=[R("cb")], dma=self.st_const)
            p.op("sp", lambda e: e.dma_start(out=vec[:].rearrange("p a b -> p (a b)"),
                                             in_=self.ssm_vec[j:j + 1, :].broadcast_to([128, 96])), writes=[R("vec")], dma=self.st_const)
            p.op("sp", lambda e: e.dma_start(out=nw[:], in_=self.ssm_norm[j:j + 1, :].broadcast_to([128, 2048])), writes=[R("nw")], dma=self.st_const)
            for q in range(4):
                src = self.ssm_w_out[j, q * 512:(q + 1) * 512, :].rearrange("(k p) m -> p k m", p=128)
                p.op("pool", lambda e, src=src, q=q: e.dma_start(out=wout[:, q * 4:(q + 1) * 4, :], in_=src), writes=[R("wout%d" % q)], dma=self.st_prep)
            r_wout = [R("wout%d" % q) for q in range(4)]
            p.op("sp", lambda e: e.dma_start(out=wdt[:], in_=self.swin[j, :, 6144:6176].rearrange("(k p) m -> p k m", p=128)),
                 reads=self.swin_res(j, 6144), writes=[R("wdt")], dma=self.st_const)
            p.op("act", lambda e: e.activation(out=a_bc[:], in_=vec[:, 1, :], func=AF.Exp), reads=[R("vec")], writes=[R("a_bc")])
            p.op("dve", lambda e: e.tensor_scalar(out=a_bc[:], in0=a_bc[:], scalar1=-1.0, scalar2=None, op0=ALU.mult), reads=[R("a_bc")], writes=[R("a_bc")])
            p.op("dve", lambda e: e.tensor_copy(out=Dbc[:, :, 0], in_=vec[:, 2, :]), reads=[R("vec")], writes=[R("Dbc")])
            p.op("pool", lambda e: e.memset(halo[:], 0.0), writes=[R("halo%d" % cc_) for cc_ in range(32)])
            p.op("pool", lambda e: e.memset(state[:], 0.0), writes=[R("state")])
            p.op("pool", lambda e: e.memset(state_bf[:], 0.0), writes=[R("state_bf")])
            self.load_gain(l * 3 + 1)

            r_hT = [[R("hT%d_%d" % (jj, hh)) for hh in range(2)] for jj in range(NT)]
            hres_all = [r_hT[jj][hh] for jj in range(NT) for hh in range(2)]
            wxc = 0
            cvc = 0
            pendB = []
            pendB2 = []
            tails = []
            for blk in range(NBLK):
                t0 = blk * NT
                self.norm_transpose(xsrc, t0, NT, hT, r_hT, xt_bufs, tag)
                for wgI in range(8):
                    b = wxc % 2
                    wxc += 1
                    c0 = 2048 + wgI * 512
                    src = self.swin[j, :, c0:c0 + 512].rearrange("(k p) m -> p k m", p=128)
                    p.op("sp", lambda e, src=src, b=b: e.dma_start(out=wx[b][:], in_=src),
                         reads=self.swin_res(j, c0), writes=[R("wx%d" % b)], dma=self.st_w)
                    for m in range(4):
                        cc = wgI * 4 + m
                        pb = cc % 2
                        bank, rb = self.pbank[pb], self.r_pb[pb]
                        while len(pendB) >= 2:
                            pendB.pop(0)()

                        def mm(e, b=b, m=m, bank=bank):
                            for kc in range(8):
                                ins = e.matmul(bank[:, 0:TBK], lhsT=wx[b][:, kc, m * 128:(m + 1) * 128], rhs=hT[:, kc, :],
                                               start=(kc == 0), stop=(kc == 7))
                            return ins
                        p.op("pe", mm, reads=[R("wx%d" % b)] + hres_all, writes=[rb])
                        while len(pendB2) >= 2:
                            pendB2.pop(0)()
                        cbuf = cvc % 3
                        cvc += 1
                        xi, ac = xin[cbuf], acc[cbuf]
                        rxi, rac = R("xin%d" % cbuf), R("acc%d" % cbuf)
                        rhalo = R("halo%d" % cc)
                        p.op("pool", lambda e, xi=xi, cc=cc: e.tensor_copy(out=xi[:, 0:3], in_=halo[:, cc, :]), reads=[rhalo], writes=[rxi])
                        p.op("act", lambda e, xi=xi, bank=bank: e.copy(out=xi[:, 3:3 + TBK], in_=bank[:, 0:TBK]), reads=[rb], writes=[rxi])
                        p.op("pool", lambda e, xi=xi, cc=cc: e.tensor_copy(out=halo[:, cc, :], in_=xi[:, TBK:TBK + 3]), reads=[rxi], writes=[rhalo])
                        p.op("act", lambda e, xi=xi, ac=ac, cc=cc: e.activation(out=ac[:], in_=xi[:, 0:TBK], func=AF.Identity, scale=cw[:, cc, 0:1], bias=cb[:, cc:cc + 1]),
                             reads=[rxi, R("cw"), R("cb")], writes=[rac])
                        for w in range(1, 4):
                            p.op("dve", lambda e, xi=xi, ac=ac, cc=cc, w=w: e.scalar_tensor_tensor(out=ac[:], in0=xi[:, w:w + TBK], scalar=cw[:, cc, w:w + 1], in1=ac[:],
                                                                                                 op0=ALU.mult, op1=ALU.add), reads=[rxi, rac, R("cw")], writes=[rac])

                        def stageB1(cc=cc, ac=ac, rac=rac, cbuf=cbuf):
                            if cc < 16:
                                xb_, rxb = xsb[cbuf % 2], R("xsb%d" % (cbuf % 2))
                                p.op("act", lambda e, ac=ac, xb_=xb_: e.activation(out=xb_[:], in_=ac[:], func=AF.Silu), reads=[rac], writes=[rxb])
                            elif cc < 24:
                                g = cc - 16
                                p.op("act", lambda e, ac=ac, g=g: e.activation(out=BT[:, g, :], in_=ac[:], func=AF.Silu), reads=[rac], writes=[R("BT%d" % g)])
                            else:
                                g = cc - 24
                                p.op("act", lambda e, ac=ac, g=g: e.activation(out=CT[:, g, :], in_=ac[:], func=AF.Silu), reads=[rac], writes=[R("CT%d" % g)])

                        def stageB2(cc=cc, cbuf=cbuf):
                            if cc < 16:
                                xb_, rxb = xsb[cbuf % 2], R("xsb%d" % (cbuf % 2))
                                hp = cc % 2
                                rp = self.r_ptr[hp]

                                def tr(e, xb_=xb_, hp=hp):
                                    for q in range(NT):
                                        ins = e.transpose(out=self.ptrh[hp][:, q * 128:(q + 1) * 128], in_=xb_[:, q * 128:(q + 1) * 128], identity=self.ident_b[:])
                                    return ins
                                p.op("pe", tr, reads=[rxb, self.R("ident_b")], writes=[rp])
                                dst = xs_tok[:, :, cc * 128:(cc + 1) * 128]
                                srcp = self.ptrh[hp][:, 0:NT * 128].rearrange("p (q m) -> p q m", q=NT)
                                p.op("dve", lambda e, dst=dst, srcp=srcp: e.tensor_copy(out=dst, in_=srcp), reads=[rp], writes=[R("xs_tok")])
                            elif cc < 24:
                                g = cc - 16
                                hp = cc % 2
                                rp = self.r_ptr[hp]

                                def tr(e, g=g, hp=hp):
                                    for q in range(NT):
                                        ins = e.transpose(out=self.ptrh[hp][:, q * 128:(q + 1) * 128], in_=BT[:, g, q * 128:(q + 1) * 128], identity=self.ident_b[:])
                                    return ins
                                p.op("pe", tr, reads=[R("BT%d" % g), self.R("ident_b")], writes=[rp])
                                dst = Btok[:, :, g * 128:(g + 1) * 128]
                                srcp = self.ptrh[hp][:, 0:NT * 128].rearrange("p (q m) -> p q m", q=NT)
                                p.op("dve", lambda e, dst=dst, srcp=srcp: e.tensor_copy(out=dst, in_=srcp), reads=[rp], writes=[R("Btok")])
                        pendB.append(stageB1)
                        pendB2.append(stageB2)
                        if cc == 3:
                            while tails:
                                tails.pop(0)()
                while pendB:
                    pendB.pop(0)()
                while pendB2:
                    pendB2.pop(0)()
                for zc in range(4):
                    b = wxc % 2
                    wxc += 1
                    c0 = zc * 512
                    src = self.swin[j, :, c0:c0 + 512].rearrange("(k p) m -> p k m", p=128)
                    p.op("sp", lambda e, src=src, b=b: e.dma_start(out=wx[b][:], in_=src),
                         reads=self.swin_res(j, c0), writes=[R("wx%d" % b)], dma=self.st_w)
                    for q in range(NT):
                        pb = (zc * NT + q) % 2
                        bank, rb = self.pbank[pb], self.r_pb[pb]

                        def mmz(e, b=b, q=q, bank=bank):
                            for kc in range(8):
                                ins = e.matmul(bank[:, :], lhsT=hT[:, kc, q * 128:(q + 1) * 128], rhs=wx[b][:, kc, :], start=(kc == 0), stop=(kc == 7))
                            return ins
                        p.op("pe", mmz, reads=[R("wx%d" % b)] + r_hT[q], writes=[rb])
                        p.op("act", lambda e, q=q, zc=zc, bank=bank: e.activation(out=sz[:, q, zc * 512:(zc + 1) * 512], in_=bank[:, :], func=AF.Silu),
                             reads=[rb], writes=[R("sz%d" % q)])
                for q in range(NT):
                    tt = t0 + q
                    tsl = slice(q * 128, (q + 1) * 128)
                    b0, rb0 = self.pbank[0], self.r_pb[0]

                    def mmdt(e, q=q):
                        for kc in range(8):
                            ins = e.matmul(b0[:, 0:32], lhsT=hT[:, kc, q * 128:(q + 1) * 128], rhs=wdt[:, kc, :], start=(kc == 0), stop=(kc == 7))
                        return ins
                    p.op("pe", mmdt, reads=[R("wdt")] + r_hT[q], writes=[rb0])
                    p.op("dve", lambda e: e.tensor_tensor(out=dtv[:], in0=b0[:, 0:32], in1=vec[:, 0, :], op=ALU.add), reads=[rb0, R("vec")], writes=[R("dtv")])
                    p.op("act", lambda e: e.activation(out=dtv[:], in_=dtv[:], func=AF.Exp), reads=[R("dtv")], writes=[R("dtv")])
                    p.op("act", lambda e: e.activation(out=dt3[:, :, 0], in_=dtv[:], func=AF.Ln, bias=1.0), reads=[R("dtv")], writes=[R("dt3")])
                    p.op("dve", lambda e: e.tensor_tensor(out=da[:], in0=dt3[:, :, 0], in1=a_bc[:], op=ALU.mult), reads=[R("dt3"), R("a_bc")], writes=[R("da")])

                    def mmcs(e):
                        e.matmul(b0[:, 32:64], lhsT=L1[:], rhs=da[:], start=True, stop=True)
                        e.matmul(b0[:, 64:96], lhsT=L2[:], rhs=da[:], start=True, stop=True)
                        return e.matmul(b0[:, 96:128], lhsT=ONES[:], rhs=da[:], start=True, stop=True)
                    p.op("pe", mmcs, reads=[R("da"), R("L1"), R("L2"), R("ONES")], writes=[rb0])
                    p.op("act", lambda e: e.activation(out=eall[:], in_=b0[:, 32:128], func=AF.Exp), reads=[rb0], writes=[R("eall")])
                    p.op("dve", lambda e: e.tensor_copy(out=ea3[:, :, 0], in_=eall[:, 0:32]), reads=[R("eall")], writes=[R("ea3")])
                    p.op("dve", lambda e: e.tensor_tensor(out=w23[:, :, 0], in0=dt3[:, :, 0], in1=eall[:, 32:64], op=ALU.mult), reads=[R("dt3"), R("eall")], writes=[R("w23")])
                    xs3 = xs_tok[:, q, :].rearrange("p (h d) -> p h d", h=32)
                    p.op("dve", lambda e, xs3=xs3: e.tensor_tensor(out=xdt[:].rearrange("p (h d) -> p h d", h=32), in0=xs3, in1=dt3[:].to_broadcast([128, 32, 64]), op=ALU.mult),
                         reads=[R("xs_tok"), R("dt3")], writes=[R("xdt")])
                    p.op("pool", lambda e, xs3=xs3: e.tensor_tensor(out=xdec[:].rearrange("p (h d) -> p h d", h=32), in0=xs3, in1=w23[:].to_broadcast([128, 32, 64]), op=ALU.mult),
                         reads=[R("xs_tok"), R("w23")], writes=[R("xdec")])
                    p.op("pool", lambda e, xs3=xs3: e.tensor_tensor(out=t3[:].rearrange("p (h d) -> p h d", h=32), in0=xs3, in1=Dbc[:].to_broadcast([128, 32, 64]), op=ALU.mult),
                         reads=[R("xs_tok"), R("Dbc")], writes=[R("t3")])
                    def S0(g, tsl=tsl):
                        gb, g3 = g % 2, g % 3
                        b1, rb1 = self.pbank[1], self.r_pb[1]
                        gcol = slice((g % 4) * 128, (g % 4 + 1) * 128)
                        p.op("pe", lambda e, g=g, gcol=gcol, tsl=tsl: e.matmul(b1[:, gcol], lhsT=BT[:, g, tsl], rhs=CT[:, g, tsl], start=True, stop=True),
                             reads=[R("BT%d" % g), R("CT%d" % g)], writes=[rb1])
                        p.op("dve", lambda e, g3=g3, gcol=gcol: e.tensor_tensor(out=GTm[g3][:, 0, :], in0=b1[:, gcol], in1=L1[:], op=ALU.mult),
                             reads=[rb1, R("L1")], writes=[R("GTm%d" % g3)])
                        for r in range(4):
                            h = 4 * g + r
                            p.op("act", lambda e, gb=gb, r=r, h=h: e.activation(out=Ada4[gb][:, r, :], in_=L2[:], func=AF.Copy, scale=da[:, h:h + 1]),
                                 reads=[R("L2"), R("da")], writes=[R("Ada4_%d" % gb)])

                    def S1(g):
                        gb = g % 2
                        bs, rbs = self.pbank[2 + gb], self.r_pb[2 + gb]

                        def mmseg(e, gb=gb, bs=bs):
                            for r in range(4):
                                ins = e.matmul(bs[:, r * 128:(r + 1) * 128], lhsT=Ada4[gb][:, r, :], rhs=L1[:], start=True, stop=True)
                            return ins
                        p.op("pe", mmseg, reads=[R("Ada4_%d" % gb), R("L1")], writes=[rbs])
                        p.op("act", lambda e, gb=gb, bs=bs: e.activation(out=Eg[gb][:].rearrange("p r l -> p (r l)"), in_=bs[:, :], func=AF.Exp),
                             reads=[rbs], writes=[R("Eg%d" % gb)])

                    def S2(g):
                        gb, g3 = g % 2, g % 3
                        p.op("dve", lambda e, gb=gb, g3=g3: e.tensor_tensor(out=MT[gb][:], in0=Eg[gb][:], in1=GTm[g3][:].to_broadcast([128, 4, 128]), op=ALU.mult),
                             reads=[R("Eg%d" % gb), R("GTm%d" % g3)], writes=[R("MT%d" % gb)])

                    def S3(g, tsl=tsl, q=q):
                        gb = g % 2
                        by, rby = self.pbank[4 + gb], self.r_pb[4 + gb]

                        def mmy(e, g=g, gb=gb, by=by, tsl=tsl):
                            for r in range(4):
                                h = 4 * g + r
                                e.matmul(by[:, r * 64:(r + 1) * 64], lhsT=MT[gb][:, r, :], rhs=xdt[:, h * 64:(h + 1) * 64], start=True, stop=True)
                            return e.matmul(by[:, 256:512], lhsT=CT[:, g, tsl], rhs=state_bf[:, g * 256:(g + 1) * 256], start=True, stop=True)
                        p.op("pe", mmy, reads=[R("MT%d" % gb), R("xdt"), R("CT%d" % g), R("state_bf")], writes=[rby])
                        p.op("pe", lambda e, g=g, q=q: e.matmul(b0[:, 256:512], lhsT=Btok[:, q, g * 128:(g + 1) * 128], rhs=xdec[:, g * 256:(g + 1) * 256], start=True, stop=True),
                             reads=[R("Btok"), R("xdec")], writes=[rb0])
                        for r in range(4):
                            h = 4 * g + r
                            p.op("dve", lambda e, h=h, r=r: e.scalar_tensor_tensor(out=state[:, h * 64:(h + 1) * 64], in0=state[:, h * 64:(h + 1) * 64],
                                                                                     scalar=eall[:, 64 + h:65 + h], in1=b0[:, 256 + r * 64:256 + (r + 1) * 64],
                                                                                     op0=ALU.mult, op1=ALU.add), reads=[R("state"), R("eall"), rb0], writes=[R("state")])
                        p.op("act", lambda e, gb=gb, by=by: e.copy(out=ybufD[gb][:], in_=by[:, 0:256]), reads=[rby], writes=[R("ybufD%d" % gb)])
                        yg = ybuf[:, g * 256:(g + 1) * 256].rearrange("p (r d) -> p r d", r=4)
                        p.op("dve", lambda e, yg=yg, by=by, g=g: e.tensor_tensor(out=yg, in0=by[:, 256:512].rearrange("p (r d) -> p r d", r=4),
                                                                               in1=ea3[:, 4 * g:4 * g + 4, :].to_broadcast([128, 4, 64]), op=ALU.mult),
                             reads=[rby, R("ea3")], writes=[R("ybuf")])
                        p.op("pool", lambda e, g=g, gb=gb: e.tensor_tensor(out=ybuf[:, g * 256:(g + 1) * 256], in0=ybuf[:, g * 256:(g + 1) * 256], in1=ybufD[gb][:], op=ALU.add),
                             reads=[R("ybuf"), R("ybufD%d" % gb)], writes=[R("ybuf")])

                    for k in range(11):
                        if k < 8:
                            S0(k)
                        if 0 <= k - 1 < 8:
                            S1(k - 1)
                        if 0 <= k - 2 < 8:
                            S2(k - 2)
                        if 0 <= k - 3 < 8:
                            S3(k - 3)
                        if k == 1:
                            while tails:
                                tails.pop(0)()
                    p.op("act", lambda e: e.copy(out=state_bf[:], in_=state[:]), reads=[R("state")], writes=[R("state_bf")])
                    p.op("pool", lambda e: e.tensor_tensor(out=ybuf[:], in0=ybuf[:], in1=t3[:], op=ALU.add), reads=[R("ybuf"), R("t3")], writes=[R("ybuf")])
                    p.op("pool", lambda e, q=q: e.tensor_tensor(out=ybuf[:], in0=ybuf[:], in1=sz[:, q, :], op=ALU.mult), reads=[R("ybuf"), R("sz%d" % q)], writes=[R("ybuf")])
                    for g in range(8):
                        p.op("act", lambda e, g=g: e.activation(out=gnb[:, g * 256:(g + 1) * 256], in_=ybuf[:, g * 256:(g + 1) * 256], func=AF.Square, accum_out=ssg[:, g:g + 1]),
                             reads=[R("ybuf")], writes=[R("gnb"), R("ssg")])
                    p.op("act", lambda e: e.activation(out=rsg[:, :, 0], in_=ssg[:], func=AF.Sqrt, scale=1.0 / 256, bias=self.epsb[:]), reads=[R("ssg"), self.R("epsb")], writes=[R("rsg")])
                    p.op("dve", lambda e: e.reciprocal(out=rsg[:, :, 0], in_=rsg[:, :, 0]), reads=[R("rsg")], writes=[R("rsg")])
                    p.op("dve", lambda e: e.tensor_tensor(out=ybuf[:].rearrange("p (g d) -> p g d", g=8), in0=ybuf[:].rearrange("p (g d) -> p g d", g=8),
                                                          in1=rsg[:].to_broadcast([128, 8, 256]), op=ALU.mult), reads=[R("ybuf"), R("rsg")], writes=[R("ybuf")])
                    p.op("pool", lambda e: e.tensor_tensor(out=gnb[:], in0=ybuf[:], in1=nw[:], op=ALU.mult), reads=[R("ybuf"), R("nw"), R("gnb")], writes=[R("gnb")])
                    def tail(tt=tt):
                        for qq in range(4):
                            hp = qq % 2
                            rp = self.r_ptr[hp]

                            def trg(e, qq=qq, hp=hp):
                                for m in range(4):
                                    k = qq * 4 + m
                                    ins = e.transpose(out=self.ptrh[hp][:, m * 128:(m + 1) * 128], in_=gnb[:, k * 128:(k + 1) * 128], identity=self.ident_b[:])
                                return ins
                            p.op("pe", trg, reads=[R("gnb"), self.R("ident_b")], writes=[rp])
                            dst = gnT[:, qq * 4:(qq + 1) * 4, :]
                            srcp = self.ptrh[hp][:, :].rearrange("p (q m) -> p q m", q=4)
                            if hp == 0:
                                p.op("act", lambda e, dst=dst, srcp=srcp: e.copy(out=dst, in_=srcp), reads=[rp], writes=[R("gnT%d" % qq)])
                            else:
                                p.op("dve", lambda e, dst=dst, srcp=srcp: e.tensor_copy(out=dst, in_=srcp), reads=[rp], writes=[R("gnT%d" % qq)])
                        p.op("sp", lambda e, tt=tt: e.dma_start(out=xres_t[:], in_=xsrc[tt * 128:(tt + 1) * 128, :]),
                             reads=[self.R("xres")], writes=[R("xres_t")], dma=self.st_x)
                        for ch in range(2):
                            bo, rbo = self.pbank[6 + ch], self.r_pb[6 + ch]

                            def mmo(e, ch=ch, bo=bo):
                                for k in range(16):
                                    ins = e.matmul(bo[:, :], lhsT=gnT[:, k, :], rhs=wout[:, k, ch * 512:(ch + 1) * 512], start=(k == 0), stop=(k == 15))
                                return ins
                            p.op("pe", mmo, reads=[R("gnT%d" % qq) for qq in range(4)] + r_wout, writes=[rbo])
                            p.op("dve", lambda e, ch=ch, bo=bo: e.tensor_tensor(out=xo[:, ch * 512:(ch + 1) * 512], in0=bo[:, :], in1=xres_t[:, ch * 512:(ch + 1) * 512], op=ALU.add),
                                 reads=[rbo, R("xres_t")], writes=[R("xo")])
                        p.op("sp", lambda e, tt=tt: e.dma_start(out=self.xres[tt * 128:(tt + 1) * 128, :], in_=xo[:]),
                             reads=[R("xo")], writes=[self.R("xres_w%d" % (tt % 4))], dma=self.st_o)
                    tails.append(tail)
            while tails:
                tails.pop(0)()
            self.phase_barrier()
        self.x_src = self.xres

    def prep_attn(self, j):
        p = self.p
        for (c0, c1) in ((0, 2048), (2048, NAW)):
            for hh in range(2):
                s_ap = self.attn_wr[j, hh * 512:(hh + 1) * 512, c0:c1]
                d_ap = self.awin[j, hh * 512:(hh + 1) * 512, c0:c1]
                p.op("pool", lambda e, s_ap=s_ap, d_ap=d_ap: e.dma_start(out=d_ap, in_=s_ap),
                     writes=[self.R("awin%d_%d_%d" % (j, c0, hh))], dma=self.st_prep)

    def awin_res(self, j, c0, c1):
        out = []
        for base in (0, 2048):
            hi = 2048 if base == 0 else NAW
            if c0 < hi and c1 > base:
                out += [self.R("awin%d_%d_%d" % (j, base, hh)) for hh in range(2)]
        return out

    def attn(self, l):
        p = self.p
        nc = self.nc
        j = l // 2
        xsrc = self.x_src
        tag = "a%d" % l
        R = lambda n: self.R(tag + n)
        BIG = 30000.0
        dbg = self.attn_dbg or ""
        use_cmp = ("nocmp" not in dbg)
        use_slc = ("noslc" not in dbg)
        use_win = ("nowin" not in dbg)
        with contextlib.ExitStack() as st_long:
            def sbl(name, shape, dt):
                return st_long.enter_context(nc.sbuf_tensor(tag + name, list(shape), dt))
            kT = sbl("kT", [64, 6, S], BF16)
            kx = sbl("kx", [128, 2, S], BF16)
            Vall = sbl("V", [128, 32, 6, 65], BF16)
            gates = sbl("gates", [128, 32, 24], F32)
            kcmpT = sbl("kcmpT", [64, 2, 256], BF16)
            Vcmp = sbl("Vcmp", [128, 2, 2, 129], BF16)
            esink = sbl("esink", [128, 8], F32)
            p.op("pool", lambda e: e.memset(Vall[:, :, :, 64:65], 1.0), writes=[R("Vones")])
            p.op("pool", lambda e: e.memset(kx[64:128, :, :], 1.0), writes=[R("kxE")])
            for g_ in range(2):
                p.op("pool", lambda e, g_=g_: e.affine_select(out=kx[64:128, g_, :].rearrange("p (b m) -> p b m", b=64), in_=kx[64:128, g_, :].rearrange("p (b m) -> p b m", b=64),
                                                             pattern=[[-1, 64], [0, 64]], compare_op=ALU.is_equal, fill=0.0, base=0, channel_multiplier=1),
                     reads=[R("kxE")], writes=[R("kxE")])
            p.op("sp", lambda e: e.dma_start(out=esink[:], in_=self.sinks_in[j:j + 1, :].broadcast_to([128, 8])), writes=[R("esink")], dma=self.st_const)
            p.op("act", lambda e: e.activation(out=esink[:], in_=esink[:], func=AF.Exp), reads=[R("esink")], writes=[R("esink")])
            self.load_gain(l * 3 + 1)

            with contextlib.ExitStack() as st_x:
                kcT = st_x.enter_context(nc.sbuf_tensor(tag + "kcT", [64, 2, S], BF16))
                vcT = st_x.enter_context(nc.sbuf_tensor(tag + "vcT", [128, S], BF16))
                with contextlib.ExitStack() as st1:
                    def sb(name, shape, dt):
                        return st1.enter_context(nc.sbuf_tensor(tag + name, list(shape), dt))
                    TBK = 512
                    NT = 4
                    hT = sb("hT", [128, 8, TBK], BF16)
                    wb = [sb("wb%d" % b, [128, 8, 512], BF16) for b in range(2)]
                    cosb = sb("cosb", [64, TBK], F32)
                    sinb = sb("sinb", [64, TBK], F32)
                    t1 = [sb("t1_%d" % b, [64, TBK], F32) for b in range(2)]
                    t2 = [sb("t2_%d" % b, [64, TBK], F32) for b in range(2)]
                    qst = [sb("qst%d" % b, [64, TBK], BF16) for b in range(2)]
                    xt_bufs = [(sb("xt%d" % b, [128, D], F32), sb("hb%d" % b, [128, D], BF16),
                                sb("sq%d" % b, [128, D], BF16), sb("ss%d" % b, [128, 1], F32),
                                sb("rs%d" % b, [128, 1], F32)) for b in range(2)]
                    r_hT = [[R("hT%d_%d" % (jj, hh)) for hh in range(2)] for jj in range(NT)]
                    hres_all = [r_hT[jj][hh] for jj in range(NT) for hh in range(2)]
                    wc = 0
                    hc = 0
                    for blk in range(S // TBK):
                        t0 = blk * NT
                        csl = slice(blk * TBK, (blk + 1) * TBK)
                        self.norm_transpose(xsrc, t0, NT, hT, r_hT, xt_bufs, tag)
                        p.op("sp", lambda e, csl=csl: e.dma_start(out=cosb[:], in_=self.rope_in[0, :, csl]), writes=[R("cosb")], dma=self.st_x)
                        p.op("sp", lambda e, csl=csl: e.dma_start(out=sinb[:], in_=self.rope_in[1, :, csl]), writes=[R("sinb")], dma=self.st_x)
                        for wgI in range(6):
                            b = wc % 2
                            wc += 1
                            c0 = wgI * 512
                            src = self.awin[j, :, c0:c0 + 512].rearrange("(k p) m -> p k m", p=128)
                            p.op("sp", lambda e, src=src, b=b: e.dma_start(out=wb[b][:], in_=src),
                                 reads=self.awin_res(j, c0, c0 + 512), writes=[R("wb%d" % b)], dma=self.st_w)
                            for m in range(4):
                                hd = wgI * 4 + m
                                hb_ = hc % 2
                                hc += 1
                                bA, rA = self.pbank[hb_ * 2], self.r_pb[hb_ * 2]
                                bB, rB = self.pbank[hb_ * 2 + 1], self.r_pb[hb_ * 2 + 1]

                                def mm(e, bank, off, b=b, m=m):
                                    for kc in range(8):
                                        ins = e.matmul(bank[0:64, :], lhsT=wb[b][:, kc, m * 128 + off:m * 128 + off + 64], rhs=hT[:, kc, :],
                                                       start=(kc == 0), stop=(kc == 7))
                                    return ins
                                p.op("pe", lambda e, mm=mm, bA=bA: mm(e, bA, 0), reads=[R("wb%d" % b)] + hres_all, writes=[rA])
                                p.op("pe", lambda e, mm=mm, bB=bB: mm(e, bB, 64), reads=[R("wb%d" % b)] + hres_all, writes=[rB])
                                p.op("dve", lambda e, hb_=hb_, bA=bA: e.tensor_tensor(out=t1[hb_][:], in0=bA[0:64, :], in1=cosb[:], op=ALU.mult),
                                     reads=[rA, R("cosb")], writes=[R("t1_%d" % hb_)])
                                p.op("dve", lambda e, hb_=hb_, bB=bB: e.tensor_tensor(out=t2[hb_][:], in0=bB[0:64, :], in1=sinb[:], op=ALU.mult),
                                     reads=[rB, R("sinb")], writes=[R("t2_%d" % hb_)])
                                if hd < 8 or 10 <= hd < 18:
                                    qh = hd if hd < 8 else hd - 10 + 8
                                    p.op("pool", lambda e, hb_=hb_: e.tensor_tensor(out=qst[hb_][:], in0=t1[hb_][:], in1=t2[hb_][:], op=ALU.add),
                                         reads=[R("t1_%d" % hb_), R("t2_%d" % hb_)], writes=[R("qst%d" % hb_)])
                                    p.op("sp", lambda e, hb_=hb_, qh=qh, csl=csl: e.dma_start(out=self.qT[qh, :, csl], in_=qst[hb_][:]),
                                         reads=[R("qst%d" % hb_)], writes=[self.R("qT_%d_%d" % (qh, blk))], dma=self.st_o)
                                else:
                                    if hd < 10:
                                        dst, rd = kT[:, hd - 8, csl], R("kT%d" % (hd - 8))
                                    elif hd < 20:
                                        dst, rd = kcT[:, hd - 18, csl], R("kcT%d" % (hd - 18))
                                    elif hd < 22:
                                        dst, rd = kx[0:64, hd - 20, csl], R("kxK%d" % (hd - 20))
                                    else:
                                        dst, rd = kT[:, 4 + hd - 22, csl], R("kT%d" % (4 + hd - 22))
                                    p.op("pool", lambda e, hb_=hb_, dst=dst: e.tensor_tensor(out=dst, in0=t1[hb_][:], in1=t2[hb_][:], op=ALU.add),
                                         reads=[R("t1_%d" % hb_), R("t2_%d" % hb_)], writes=[rd])
                        b = wc % 2
                        wc += 1
                        src = self.awin[j, :, 3072:3608].rearrange("(k p) m -> p k m", p=128)
                        src_vc = self.awin[j, :, 3072:3200].rearrange("(k p) m -> p k m", p=128)
                        src_tm = self.awin[j, :, 3200:3608].rearrange("(k p) m -> p k m", p=128)
                        b2 = wc % 2
                        wc += 1
                        p.op("sp", lambda e, b=b, src_vc=src_vc: e.dma_start(out=wb[b][:, :, 0:128], in_=src_vc),
                             reads=self.awin_res(j, 3072, 3200), writes=[R("wb%d" % b)], dma=self.st_w)
                        p.op("sp", lambda e, b2=b2, src_tm=src_tm: e.dma_start(out=wb[b2][:, :, 0:408], in_=src_tm),
                             reads=self.awin_res(j, 3200, 3608), writes=[R("wb%d" % b2)], dma=self.st_w)
                        bA, rA = self.pbank[4], self.r_pb[4]

                        def mmvc(e, b=b, bA=bA):
                            for kc in range(8):
                                ins = e.matmul(bA[:, :], lhsT=wb[b][:, kc, 0:128], rhs=hT[:, kc, :], start=(kc == 0), stop=(kc == 7))
                            return ins
                        p.op("pe", mmvc, reads=[R("wb%d" % b)] + hres_all, writes=[rA])
                        p.op("act", lambda e, bA=bA, csl=csl: e.copy(out=vcT[:, csl], in_=bA[:, :]), reads=[rA], writes=[R("vcT")])
                        for q in range(NT):
                            tt = t0 + q
                            bT, rT = self.pbank[5], self.r_pb[5]

                            def mmtm(e, q=q, b2=b2, bT=bT):
                                for kc in range(8):
                                    ins = e.matmul(bT[:, 0:408], lhsT=hT[:, kc, q * 128:(q + 1) * 128], rhs=wb[b2][:, kc, 0:408], start=(kc == 0), stop=(kc == 7))
                                return ins
                            p.op("pe", mmtm, reads=[R("wb%d" % b2)] + r_hT[q], writes=[rT])
                            p.op("dve", lambda e, tt=tt, bT=bT: e.tensor_copy(out=Vall[:, tt, :, 0:64], in_=bT[:, 0:384].rearrange("p (a d) -> p a d", a=6)),
                                 reads=[rT], writes=[R("Vall")])
                            p.op("act", lambda e, tt=tt, bT=bT: e.activation(out=gates[:, tt, :], in_=bT[:, 384:408], func=AF.Sigmoid),
                                 reads=[rT], writes=[R("gates")])
                    p.barrier()
                with contextlib.ExitStack() as st2:
                    def sb(name, shape, dt):
                        return st2.enter_context(nc.sbuf_tensor(tag + name, list(shape), dt))
                    w1s = sb("w1s", [64, 32, 128], BF16)
                    w2s = sb("w2s", [128, 64], BF16)
                    posf = sb("posf", [64, 32], F32)
                    posb_ = sb("posb", [64, 32, 2], BF16)
                    pbias = sb("pbias", [128, 1], F32)
                    u = sb("u", [128, 256], F32)
                    u2 = sb("u2", [128, 256], F32)
                    sg_ = sb("sgm", [128, 256], F32)
                    gl = sb("gl", [128, 256], BF16)
                    wself = sb("wself", [128, 2, 64], F32)
                    p.op("sp", lambda e: e.dma_start(out=wself[:], in_=self.wsel_in), writes=[R("wself")], dma=self.st_const)
                    p.op("dve", lambda e: e.memset(u[:], 0.0), writes=[R("u")])
                    p.op("pool", lambda e: e.memset(Vcmp[:, :, :, 64:65], 1.0), writes=[R("Vcmp1")])
                    for g in range(2):
                        p.op("dve", lambda e, g=g: e.tensor_copy(out=Vcmp[:, g, :, 65:129], in_=wself[:]), reads=[R("wself")], writes=[R("VcmpW%d" % g)])
                    for kv in range(2):
                        w1_in = self.cmp_w1[j, kv].rearrange("(pp d) h -> d pp h", d=64)
                        p.op("pool", lambda e, w1_in=w1_in: e.dma_start(out=w1s[:], in_=w1_in), writes=[R("w1s")], dma=self.st_prep)
                        p.op("pool", lambda e, kv=kv: e.dma_start(out=w2s[:], in_=self.cmp_w2[j, kv]), writes=[R("w2s")], dma=self.st_prep)
                        p.op("sp", lambda e, kv=kv: e.dma_start(out=posf[:], in_=self.cmp_posT[j, kv]), writes=[R("posf")], dma=self.st_const)
                        p.op("dve", lambda e: e.tensor_copy(out=posb_[:], in_=posf[:].rearrange("p (a o) -> p a o", o=1).to_broadcast([64, 32, 2])), reads=[R("posf")], writes=[R("posb")])
                        b0, rb0 = self.pbank[0], self.r_pb[0]

                        def mmb(e):
                            for pp in range(32):
                                ins = e.matmul(b0[:, 0:2], lhsT=w1s[:, pp, :], rhs=posb_[:, pp, :], start=(pp == 0), stop=(pp == 31))
                            return ins
                        p.op("pe", mmb, reads=[R("w1s"), R("posb")], writes=[rb0])
                        p.op("dve", lambda e: e.tensor_copy(out=pbias[:], in_=b0[:, 0:1]), reads=[rb0], writes=[R("pbias")])
                        for g in range(2):
                            b1, rb1 = self.pbank[1 + g], self.r_pb[1 + g]
                            if kv == 0:
                                srcT = kcT[:, g, :]
                                rsrc = R("kcT%d" % g)
                            else:
                                srcT = vcT[g * 64:(g + 1) * 64, :]
                                rsrc = R("vcT")

                            def mmh(e, srcT=srcT, b1=b1, g=g):
                                for pp in range(32):
                                    ins = e.matmul(b1[:, 0:255], lhsT=w1s[g * 64 * kv:g * 64 * kv + 64, pp, :] if False else w1s[:, pp, :],
                                                   rhs=srcT[:, pp:pp + 16 * 254 + 1:16], start=(pp == 0), stop=(pp == 31))
                                return ins
                            if kv == 1 and g == 1:
                                vtmp = sb("vtmp", [64, S], BF16)
                                p.op("sp", lambda e, vtmp=vtmp: e.dma_start(out=vtmp[:], in_=vcT[64:128, :]), reads=[R("vcT")], writes=[R("vtmp")], dma=self.st_x)
                                srcT2 = vtmp[:, :]

                                def mmh(e, srcT2=srcT2, b1=b1):
                                    for pp in range(32):
                                        ins = e.matmul(b1[:, 0:255], lhsT=w1s[:, pp, :], rhs=srcT2[:, pp:pp + 16 * 254 + 1:16], start=(pp == 0), stop=(pp == 31))
                                    return ins
                                rsrc = R("vtmp")
                            p.op("pe", mmh, reads=[R("w1s"), rsrc], writes=[rb1])
                            p.op("act", lambda e, b1=b1: e.activation(out=u[:, 0:255], in_=b1[:, 0:255], func=AF.Identity, bias=pbias[:]),
                                 reads=[rb1, R("pbias"), R("u")], writes=[R("u")])
                            p.op("dve", lambda e: e.tensor_tensor(out=u2[:], in0=u[:], in1=u[:], op=ALU.mult), reads=[R("u")], writes=[R("u2")])
                            p.op("dve", lambda e: e.tensor_scalar(out=u2[:], in0=u2[:], scalar1=0.044715, scalar2=1.0, op0=ALU.mult, op1=ALU.add), reads=[R("u2")], writes=[R("u2")])
                            p.op("dve", lambda e: e.tensor_tensor(out=u2[:], in0=u2[:], in1=u[:], op=ALU.mult), reads=[R("u2"), R("u")], writes=[R("u2")])
                            p.op("act", lambda e: e.activation(out=sg_[:], in_=u2[:], func=AF.Sigmoid, scale=1.5957691216057308), reads=[R("u2")], writes=[R("sgm")])
                            p.op("dve", lambda e: e.tensor_tensor(out=gl[:], in0=u[:], in1=sg_[:], op=ALU.mult), reads=[R("u"), R("sgm")], writes=[R("gl")])
                            b3, rb3 = self.pbank[3], self.r_pb[3]
                            if kv == 0:
                                p.op("pe", lambda e, b3=b3: e.matmul(b3[0:64, 0:256], lhsT=w2s[:], rhs=gl[:], start=True, stop=True), reads=[R("w2s"), R("gl")], writes=[rb3])
                                p.op("dve", lambda e, g=g, b3=b3: e.tensor_copy(out=kcmpT[:, g, :], in_=b3[0:64, 0:256]), reads=[rb3], writes=[R("kcmpT%d" % g)])
                            else:
                                def mmv(e, b3=b3):
                                    for ct in range(2):
                                        ins = e.matmul(b3[:, ct * 64:(ct + 1) * 64], lhsT=gl[:, ct * 128:(ct + 1) * 128], rhs=w2s[:], start=True, stop=True)
                                    return ins
                                p.op("pe", mmv, reads=[R("w2s"), R("gl")], writes=[rb3])
                                p.op("dve", lambda e, g=g, b3=b3: e.tensor_copy(out=Vcmp[:, g, :, 0:64], in_=b3[:, 0:128].rearrange("p (c d) -> p c d", c=2)),
                                     reads=[rb3], writes=[R("VcmpV%d" % g)])
                    p.barrier()
            with contextlib.ExitStack() as st3:
                def sb(name, shape, dt):
                    return st3.enter_context(nc.sbuf_tensor(tag + name, list(shape), dt))
                wout = sb("wout", [128, 8, D], BF16)
                for q in range(2):
                    src = self.attn_w_out[j, q * 512:(q + 1) * 512, :].rearrange("(k p) m -> p k m", p=128)
                    p.op("pool", lambda e, src=src, q=q: e.dma_start(out=wout[:, q * 4:(q + 1) * 4, :], in_=src), writes=[R("wout%d" % q)], dma=self.st_prep)
                r_wout = [R("wout0"), R("wout1")]
                QX = [sb("QX%d" % b_, [128, 4, 128], BF16) for b_ in range(2)]
                nbw = sb("nbw", [128, 128], F32)
                p.op("pool", lambda e: e.memset(nbw[:], 0.0), writes=[R("nbw")])
                selb = sb("selb", [128, 32, 64], F32)
                p.op("sp", lambda e: e.dma_start(out=selb[:], in_=self.selb_in), writes=[R("selb")], dma=self.st_const)
                qt = [sb("qt%d" % b, [64, 16, 256], BF16) for b in range(2)]
                Eb = [sb("E%d" % b, [128, 4, 128], BF16) for b in range(4)]
                SBANKS = (0, 1, 6)
                maskC = sb("maskC", [128, 4, 128], BF16)
                maskP = sb("maskP", [128, 4, 128], BF16)
                p.op("pool", lambda e: e.memset(maskC[:], 1.0), writes=[R("maskC")])
                p.op("pool", lambda e: e.memset(maskP[:], 1.0), writes=[R("maskP")])
                p.op("pool", lambda e: e.affine_select(out=maskC[:], in_=maskC[:], pattern=[[0, 4], [1, 128]], compare_op=ALU.is_ge, fill=0.0, base=0, channel_multiplier=-1),
                     reads=[R("maskC")], writes=[R("maskC")])
                p.op("pool", lambda e: e.affine_select(out=maskP[:], in_=maskP[:], pattern=[[0, 4], [-1, 128]], compare_op=ALU.is_ge, fill=0.0, base=-1, channel_multiplier=1),
                     reads=[R("maskP")], writes=[R("maskP")])
                ot = sb("ot", [128, D], BF16)
                accb = sb("accb", [128, 4, 64], F32)
                imp = sb("imp", [128, 64], F32)
                sc2 = sb("sc2", [128, 64], F32)
                m8 = sb("m8", [128, 8], F32)
                den = sb("den", [128, 4], F32)
                oT = sb("oT", [128, 8, 128], BF16)
                xres_t = sb("xres_t", [128, D], F32)
                xo = sb("xo", [128, D], F32)
                ec = [0]
                sc_ = [0]

                def score_exp(i, lhsT, lres, rhs, rres, mask, extra=None):
                    sb_i = SBANKS[sc_[0] % 3]
                    sc_[0] += 1
                    bank, rb = self.pbank[sb_i], self.r_pb[sb_i]
                    eb = ec[0] % 4
                    ec[0] += 1

                    def mm(e, bank=bank):
                        ins = e.matmul(bank[:, :].rearrange("p (r q) -> p r q", r=4), lhsT=lhsT, rhs=rhs, start=True, stop=(extra is None))
                        if extra is not None:
                            ins = e.matmul(bank[:, :].rearrange("p (r q) -> p r q", r=4), lhsT=extra[0], rhs=extra[1], start=False, stop=True)
                        return ins
                    rr = list(lres) + list(rres) + (list(extra[2]) if extra is not None else [])
                    p.op("pe", mm, reads=rr, writes=[rb])
                    E = Eb[eb]
                    rE = R("E%d" % eb)
                    p.op("act", lambda e, E=E, bank=bank: e.activation(out=E[:].rearrange("p r q -> p (r q)"), in_=bank[:, :], func=AF.Exp, scale=0.125),
                         reads=[rb], writes=[rE])
                    if mask is CAUSAL or mask is PREV:
                        mt_, rm_ = (maskC, R("maskC")) if mask is CAUSAL else (maskP, R("maskP"))
                        p.op("dve", lambda e, E=E, mt_=mt_: e.tensor_tensor(out=E[:], in0=E[:], in1=mt_[:], op=ALU.mult), reads=[rE, rm_], writes=[rE])
                    elif mask is not None:
                        base, cm, stepq = mask
                        p.op("pool", lambda e, E=E, base=base, cm=cm, stepq=stepq: e.affine_select(
                            out=E[:], in_=E[:], pattern=[[0, 4], [stepq, 128]], compare_op=ALU.is_ge, fill=0.0, base=base, channel_multiplier=cm),
                            reads=[rE], writes=[rE])
                    return E, rE

                def pv(E, rE, vrhs, vres, ncols, first, last):
                    def mm(e):
                        for r in range(4):
                            ins = e.matmul(self.pbank[2 + r][:, 0:ncols], lhsT=E[:, r, :], rhs=vrhs, start=first, stop=last)
                        return ins
                    p.op("pe", mm, reads=[rE] + list(vres), writes=[self.r_pb[2 + r] for r in range(4)])

                CAUSAL = (0, -1, 1)
                PREV = (-1, 1, -1)
                den2 = [den, sb("denB", [128, 4], F32)]
                bc = [0]

                def pv2(E, rE, vrhs, vres, ncols, first, last, par):
                    off = par * 256

                    def mm(e):
                        for r in range(4):
                            ins = e.matmul(self.pbank[2 + r][:, off:off + ncols], lhsT=E[:, r, :], rhs=vrhs, start=first, stop=last)
                        return ins
                    p.op("pe", mm, reads=[rE] + list(vres), writes=[R("O%d_%d" % (r, par)) for r in range(4)] + [self.r_pb[2 + r] for r in range(4)])

                def load_q(i0):
                    qb2 = (i0 // 2) % 2
                    blk = i0 // 4
                    src = self.qT[:, :, i0 * 128:i0 * 128 + 256].rearrange("h d t -> d h t")
                    p.op("sp", lambda e, src=src, qb2=qb2: e.dma_start(out=qt[qb2][:], in_=src),
                         reads=[self.R("qT_%d_%d" % (h, blk)) for h in range(16)], writes=[R("qt%d" % qb2)], dma=self.st_x)
                otB = [ot, sb("ot1", [128, D], BF16)]
                accbG = [accb, sb("accb1", [128, 4, 64], F32)]
                impG = [imp, sb("imp1", [128, 64], F32)]
                atails = []
                load_q(0)
                for i in range(32):
                    qb_ = (i // 2) % 2
                    if i % 2 == 0 and i + 2 < 32:
                        load_q(i + 2)
                    qsl = slice((i % 2) * 128, (i % 2 + 1) * 128)
                    rq = [R("qt%d" % qb_)]
                    ctxs = []

                    def make_g(g, i=i, qb_=qb_, qsl=qsl, rq=rq):
                        ot = otB[i % 2]
                        rot = R("ot%d" % (i % 2))
                        accb = accbG[g]
                        racc = R("accb%d" % g)
                        imp = impG[g]
                        rimp = R("imp%d" % g)
                        qa4 = qt[qb_][:, g * 4:(g + 1) * 4, qsl]
                        qb4 = qt[qb_][:, 8 + g * 4:8 + (g + 1) * 4, qsl]
                        gsl = gates[:, i, g * 12:(g + 1) * 12].rearrange("p (r b) -> p r b", b=3)
                        qxp = (2 * i + g) % 2
                        if use_slc:
                            p.op("pool", lambda e, qxp=qxp, qb4=qb4: e.tensor_copy(out=QX[qxp][0:64, :, :], in_=qb4), reads=rq, writes=[R("QXq%d" % qxp)])
                        tiles = []
                        kts = [kt for kt in (i - 1, i) if kt >= 0]
                        for n, kt in enumerate(kts):
                            tiles.append(("swa", kT[:, g, kt * 128:(kt + 1) * 128], [R("kT%d" % g)], qa4, CAUSAL if kt == i else PREV, None,
                                          Vall[:, kt, g, :], [R("Vall"), R("Vones")], 65, n == 0, n == len(kts) - 1))
                        nct = 1 if i < 16 else 2
                        if use_cmp or use_slc:
                            for ct in range(nct):
                                tiles.append(("cmp", kcmpT[:, g, ct * 128:(ct + 1) * 128], [R("kcmpT%d" % g)], qb4, (128 * i - 2048 * ct - 31, -16, 1), None,
                                              Vcmp[:, g, ct, :], [R("VcmpV%d" % g), R("VcmpW%d" % g), R("Vcmp1")], 129, ct == 0, ct == nct - 1))
                        if use_win:
                            kts = [kt for kt in range(i - 4, i + 1) if kt >= 0]
                            for n, kt in enumerate(kts):
                                mk = CAUSAL if kt == i else (PREV if kt == i - 4 else None)
                                tiles.append(("win", kT[:, 4 + g, kt * 128:(kt + 1) * 128], [R("kT%d" % (4 + g))], qb4, mk, None,
                                              Vall[:, kt, 4 + g, :], [R("Vall"), R("Vones")], 65, n == 0, n == len(kts) - 1))
                        if use_slc:
                            for kt in range(i + 1):
                                tiles.append(("slc", kx[:, g, kt * 128:(kt + 1) * 128], [R("kxK%d" % g), R("kxE"), R("QXq%d" % qxp), R("QXn%d" % qxp)], QX[qxp][:], CAUSAL if kt == i else None, None,
                                              Vall[:, kt, 2 + g, :], [R("Vall"), R("Vones")], 65, kt == 0, kt == i))
                        branches = [br for br in ("swa", "cmp", "win", "slc") if any(t[0] == br for t in tiles)]
                        last_nsa = [br for br in branches if br != "swa" and br != "cmp"]
                        last_nsa = last_nsa[-1] if last_nsa else "cmp"

                        def finish(br, par):
                            dn = den2[par]
                            rdn = R("den%d" % par)
                            rO = [R("O%d_%d" % (r, par)) for r in range(4)]
                            off = par * 256
                            O = [self.pbank[2 + r] for r in range(4)]
                            if br == "swa":
                                for r in range(4):
                                    h = g * 4 + r
                                    p.op("dve", lambda e, r=r, h=h: e.tensor_tensor(out=dn[:, r:r + 1], in0=O[r][:, off + 64:off + 65], in1=esink[:, h:h + 1], op=ALU.add),
                                         reads=[rO[r], R("esink")], writes=[rdn])
                                p.op("dve", lambda e: e.reciprocal(out=dn[:], in_=dn[:]), reads=[rdn], writes=[rdn])
                                for r in range(4):
                                    h = g * 4 + r
                                    p.op("dve", lambda e, r=r, h=h: e.tensor_scalar(out=ot[:, h * 64:(h + 1) * 64], in0=O[r][:, off:off + 64], scalar1=dn[:, r:r + 1], scalar2=None, op0=ALU.mult),
                                         reads=[rO[r], rdn], writes=[rot])
                                return
                            if br == "cmp":
                                for r in range(4):
                                    p.op("dve", lambda e, r=r: e.tensor_scalar(out=dn[:, r:r + 1], in0=O[r][:, off + 64:off + 65], scalar1=1e-30, scalar2=None, op0=ALU.max),
                                         reads=[rO[r]], writes=[rdn])
                                p.op("dve", lambda e: e.reciprocal(out=dn[:], in_=dn[:]), reads=[rdn], writes=[rdn])
                                for r in range(4):
                                    if r == 0:
                                        p.op("dve", lambda e, r=r: e.tensor_scalar(out=imp[:], in0=O[r][:, off + 65:off + 129], scalar1=dn[:, r:r + 1], scalar2=None, op0=ALU.mult),
                                             reads=[rO[r], rdn], writes=[rimp])
                                    else:
                                        p.op("dve", lambda e, r=r: e.scalar_tensor_tensor(out=imp[:], in0=O[r][:, off + 65:off + 129], scalar=dn[:, r:r + 1], in1=imp[:], op0=ALU.mult, op1=ALU.add),
                                             reads=[rO[r], rdn, rimp], writes=[rimp])
                                if use_slc:
                                    p.op("dve", lambda e, i=i: e.tensor_tensor(out=imp[:], in0=imp[:], in1=selb[:, i, :], op=ALU.add), reads=[rimp, R("selb")], writes=[rimp])
                                    p.op("dve", lambda e: e.max(out=m8[:], in_=imp[:]), reads=[rimp], writes=[R("m8")])
                                    p.op("dve", lambda e: e.match_replace(out=sc2[:], in_to_replace=m8[:], in_values=imp[:], imm_value=-3.0e38), reads=[rimp, R("m8")], writes=[R("sc2")])
                                    p.op("dve", lambda e: e.max(out=m8[:], in_=sc2[:]), reads=[R("sc2"), R("m8")], writes=[R("m8")])
                                    p.op("dve", lambda e: e.tensor_scalar(out=nbw[:, 64:128], in0=imp[:], scalar1=m8[:, 7:8], scalar2=-BIG, op0=ALU.is_lt, op1=ALU.mult),
                                         reads=[rimp, R("m8"), R("nbw")], writes=[R("nbw")])
                                    sb_i = SBANKS[sc_[0] % 3]
                                    sc_[0] += 1
                                    bank, rb = self.pbank[sb_i], self.r_pb[sb_i]
                                    p.op("pe", lambda e, bank=bank: e.transpose(out=bank[:, 0:128], in_=nbw[:], identity=self.ident_f[:]), reads=[R("nbw"), self.R("ident")], writes=[rb])
                                    p.op("dve", lambda e, bank=bank, qxp=qxp: e.tensor_copy(out=QX[qxp][64:128, :, :], in_=bank[64:128, 0:128].rearrange("p (o q) -> p o q", o=1).to_broadcast([64, 4, 128])),
                                         reads=[rb], writes=[R("QXn%d" % qxp)])
                                p.op("dve", lambda e, gsl=gsl: e.tensor_tensor(out=dn[:], in0=dn[:], in1=gsl[:, :, 0], op=ALU.mult), reads=[rdn, R("gates")], writes=[rdn])
                                for r in range(4):
                                    h = 8 + g * 4 + r
                                    if not use_cmp:
                                        p.op("dve", lambda e, r=r: e.memset(accb[:, r, :], 0.0), reads=[racc], writes=[racc])
                                    elif last_nsa == "cmp":
                                        p.op("dve", lambda e, r=r, h=h: e.tensor_scalar(out=ot[:, h * 64:(h + 1) * 64], in0=O[r][:, off:off + 64], scalar1=dn[:, r:r + 1], scalar2=None, op0=ALU.mult),
                                             reads=[rO[r], rdn], writes=[rot])
                                    else:
                                        p.op("dve", lambda e, r=r: e.tensor_scalar(out=accb[:, r, :], in0=O[r][:, off:off + 64], scalar1=dn[:, r:r + 1], scalar2=None, op0=ALU.mult),
                                             reads=[rO[r], rdn], writes=[racc])
                                return
                            gi = 2 if br == "win" else 1
                            for r in range(4):
                                p.op("dve", lambda e, r=r: e.tensor_copy(out=dn[:, r:r + 1], in_=O[r][:, off + 64:off + 65]), reads=[rO[r]], writes=[rdn])
                            p.op("dve", lambda e: e.reciprocal(out=dn[:], in_=dn[:]), reads=[rdn], writes=[rdn])
                            p.op("dve", lambda e, gsl=gsl, gi=gi: e.tensor_tensor(out=dn[:], in0=dn[:], in1=gsl[:, :, gi], op=ALU.mult), reads=[rdn, R("gates")], writes=[rdn])
                            for r in range(4):
                                h = 8 + g * 4 + r
                                if br == last_nsa:
                                    p.op("dve", lambda e, r=r, h=h: e.scalar_tensor_tensor(out=ot[:, h * 64:(h + 1) * 64], in0=O[r][:, off:off + 64], scalar=dn[:, r:r + 1], in1=accb[:, r, :], op0=ALU.mult, op1=ALU.add),
                                         reads=[rO[r], rdn, racc], writes=[rot])
                                else:
                                    p.op("dve", lambda e, r=r: e.scalar_tensor_tensor(out=accb[:, r, :], in0=O[r][:, off:off + 64], scalar=dn[:, r:r + 1], in1=accb[:, r, :], op0=ALU.mult, op1=ALU.add),
                                         reads=[rO[r], rdn, racc], writes=[racc])

                        par_of = {}
                        for br in branches:
                            par_of[br] = 0
                        return tiles, finish, par_of

                    for g in range(2):
                        ctxs.append(make_g(g))
                    merged = []
                    for brs in (("swa", "cmp"), ("win",), ("slc",)):
                        for gi in range(2):
                            for br in brs:
                                merged += [(gi, tl) for tl in ctxs[gi][0] if tl[0] == br]
                    queue = []
                    LOOK = 2

                    def pop():
                        pE, prE, gi, ptl = queue.pop(0)
                        pv2(pE, prE, ptl[6], ptl[7], ptl[8], ptl[9], ptl[10], 0)
                        if ptl[10]:
                            ctxs[gi][1](ptl[0], 0)
                    ntl = 0
                    for gi, tl in merged:
                        br, lhsT, lres, rhs, mask, extra, vrhs, vres, ncols, first, last = tl
                        if br == "slc" and first:
                            while any(qq[3][0] == "cmp" and qq[2] == gi for qq in queue):
                                pop()
                        E, rE = score_exp(i, lhsT, lres, rhs, rq, mask, extra=extra)
                        queue.append((E, rE, gi, tl))
                        while len(queue) > LOOK:
                            pop()
                        ntl += 1
                        if ntl == 4:
                            while atails:
                                atails.pop(0)()
                    while queue:
                        pop()

                    def atail(i=i):
                        ot = otB[i % 2]
                        rot = R("ot%d" % (i % 2))
                        p.op("sp", lambda e, i=i: e.dma_start(out=xres_t[:], in_=xsrc[i * 128:(i + 1) * 128, :]),
                             reads=[self.R("xres")], writes=[R("xres_t")], dma=self.st_x)
                        for half in range(2):
                            rp = self.r_ptr[1]

                            def tr(e, half=half):
                                for q in range(4):
                                    kc = half * 4 + q
                                    ins = e.transpose(out=self.ptrh[1][:, q * 128:(q + 1) * 128], in_=ot[:, kc * 128:(kc + 1) * 128], identity=self.ident_b[:])
                                return ins
                            p.op("pe", tr, reads=[rot, self.R("ident_b")], writes=[rp])
                            dst = oT[:, half * 4:(half + 1) * 4, :]
                            srcp = self.ptrh[1][:, :].rearrange("p (q m) -> p q m", q=4)
                            p.op("act", lambda e, dst=dst, srcp=srcp: e.copy(out=dst, in_=srcp), reads=[rp], writes=[R("oT%d" % half)])
                        for ch in range(2):
                            bo, rbo = self.pbank[7], self.r_pb[7]

                            def mmo(e, ch=ch, bo=bo):
                                for k in range(8):
                                    ins = e.matmul(bo[:, :], lhsT=oT[:, k, :], rhs=wout[:, k, ch * 512:(ch + 1) * 512], start=(k == 0), stop=(k == 7))
                                return ins
                            p.op("pe", mmo, reads=[R("oT0"), R("oT1")] + r_wout, writes=[rbo])
                            p.op("dve", lambda e, ch=ch, bo=bo: e.tensor_tensor(out=xo[:, ch * 512:(ch + 1) * 512], in0=bo[:, :], in1=xres_t[:, ch * 512:(ch + 1) * 512], op=ALU.add),
                                 reads=[rbo, R("xres_t")], writes=[R("xo")])
                        if "ot" in dbg:
                            p.op("dve", lambda e: e.tensor_copy(out=xo[:], in_=ot[:]), reads=[rot, R("xo")], writes=[R("xo")])
                        p.op("sp", lambda e, i=i: e.dma_start(out=self.xres[i * 128:(i + 1) * 128, :], in_=xo[:]),
                             reads=[R("xo")], writes=[self.R("xres_w%d" % (i % 4))], dma=self.st_o)
                    atails.append(atail)
                while atails:
                    atails.pop(0)()
            self.phase_barrier()
        self.x_src = self.xres

    def final_norm(self):
        p = self.p
        nc = self.nc
        xsrc = self.x_src
        with contextlib.ExitStack() as st:
            def sb(name, shape, dt):
                return st.enter_context(nc.sbuf_tensor(name, list(shape), dt))
            self.load_gain(DEPTH * 3)
            bufs = [(sb("fn_x%d" % b, [128, D], F32), sb("fn_sq%d" % b, [128, D], BF16), sb("fn_ss%d" % b, [128, 1], F32),
                     sb("fn_rs%d" % b, [128, 1], F32), sb("fn_o%d" % b, [128, D], F32)) for b in range(2)]
            for tt in range(S // 128):
                b = tt % 2
                xt, sq, ss, rs, ot = bufs[b]
                rx, rss, rrs, ro = [self.R("fn_%s%d" % (n, b)) for n in ("x", "ss", "rs", "o")]
                p.op("sp", lambda e, xt=xt, tt=tt: e.dma_start(out=xt[:], in_=xsrc[tt * 128:(tt + 1) * 128, :]),
                     reads=[self.R("xres")], writes=[rx], dma=self.st_x)
                p.op("act", lambda e, xt=xt, sq=sq, ss=ss: e.activation(out=sq[:], in_=xt[:], func=AF.Square, accum_out=ss[:]),
                     reads=[rx], writes=[self.R("fn_sq%d" % b), rss])
                p.op("act", lambda e, ss=ss, rs=rs: e.activation(out=rs[:], in_=ss[:], func=AF.Sqrt, scale=1.0 / D, bias=self.epsb[:]),
                     reads=[rss, self.R("epsb")], writes=[rrs])
                p.op("dve", lambda e, rs=rs: e.reciprocal(out=rs[:], in_=rs[:]), reads=[rrs], writes=[rrs])
                p.op("dve", lambda e, xt=xt, ot=ot, rs=rs: e.scalar_tensor_tensor(out=ot[:], in0=xt[:], scalar=rs[:], in1=self.gbc[:], op0=ALU.mult, op1=ALU.mult),
                     reads=[rx, rrs, self.R("gbc")], writes=[ro])
                p.op("sp", lambda e, ot=ot, tt=tt: e.dma_start(out=self.out[tt * 128:(tt + 1) * 128, :], in_=ot[:]),
                     reads=[ro], writes=[self.R("out_w%d" % (tt % 4))], dma=self.st_o)
            self.final_wait()

    def copy_out(self):
        p = self.p
        nc = self.nc
        xsrc = self.x_src
        with contextlib.ExitStack() as st:
            bufs = [st.enter_context(nc.sbuf_tensor("co%d" % b, [128, D], F32)) for b in range(2)]
            for tt in range(S // 128):
                b = tt % 2
                rx = self.R("co%d" % b)
                p.op("sp", lambda e, b=b, tt=tt: e.dma_start(out=bufs[b][:], in_=xsrc[tt * 128:(tt + 1) * 128, :]),
                     reads=[self.R("xres")], writes=[rx], dma=self.st_x)
                p.op("sp", lambda e, b=b, tt=tt: e.dma_start(out=self.out[tt * 128:(tt + 1) * 128, :], in_=bufs[b][:]),
                     reads=[rx], writes=[self.R("out_w%d" % (tt % 4))], dma=self.st_o)
            self.final_wait()

    def final_wait(self):
        p = self.p
        sems = self._store_waits()

        def fn(e, sems=sems):
            for sem, val in sems:
                e.wait_ge(sem, val)
            return e.nop()
        p.op("sp", fn, reads=[self.R("out_w%d" % k) for k in range(4)], writes=[self.R("done")])


def full_plan():
    def prep(l):
        out = [("prep_ffn", l, 0)]
        out.append(("prep_attn", l // 2) if l % 2 == 0 else ("prep_ssm", l // 2))
        out.append(("prep_ffn", l, 1))
        return out
    plan = prep(0)
    for l in range(DEPTH):
        if l + 1 < DEPTH:
            plan += prep(l + 1)
        plan.append(("ffn", l, 0))
        plan.append(("attn", l) if l % 2 == 0 else ("ssd", l))
        plan.append(("ffn", l, 1))
    plan.append(("final",))
    return plan


_CACHE = {}


def attn_w_layout(w):
    heads = [(h * 64) for h in range(8)] + [512, 576] + [768 + h * 64 for h in range(8)] + [1280, 1344] + [1536, 1600] + [1792, 1856]
    cols = []
    for c0 in heads:
        cols += list(range(c0, c0 + 64)) + list(range(c0 + 32, c0 + 64)) + list(range(c0, c0 + 32))
    cols += list(range(1408, 1536))
    cols += list(range(640, 768)) + list(range(1664, 1792)) + list(range(1920, 2048)) + list(range(2048, 2072))
    assert len(cols) == NAW
    return np.ascontiguousarray(w[:, :, np.asarray(cols)])


def _rope_tables():
    inv = (1.0 / (np.float32(10000.0) ** (np.arange(0, 64, 2, dtype=np.float32) / np.float32(64)))).astype(np.float32)
    ang = (np.arange(S, dtype=np.float32)[:, None] * inv[None, :]).astype(np.float32)
    c = np.cos(ang).astype(np.float32).T
    s_ = np.sin(ang).astype(np.float32).T
    return np.ascontiguousarray(np.stack([np.concatenate([c, c], 0), np.concatenate([-s_, s_], 0)], 0))


def _wsel():
    n_cmp = (S - 32) // 16 + 1
    cs = np.arange(n_cmp) * 16
    ss = np.arange(S // 64) * 64
    ov = np.minimum(cs[:, None] + 32, ss[None, :] + 64) - np.maximum(cs[:, None], ss[None, :])
    w = np.zeros((256, 64), np.float32)
    w[:n_cmp] = np.clip(ov, 0, None) / 32.0
    return np.ascontiguousarray(w.reshape(2, 128, 64).transpose(1, 0, 2))


def _selb():
    t = np.arange(S)
    cur = (t // 64)[:, None]
    jj = np.arange(64)[None, :]
    valid = jj <= cur
    forced = valid & ((jj == 0) | (jj == cur) | (jj == cur - 1))
    b = np.where(forced, 1e4, 0.0) - np.where(valid, 0.0, 1e4)
    return np.ascontiguousarray(b.astype(np.float32).reshape(32, 128, 64).transpose(1, 0, 2))


ROPE = _rope_tables()
WSEL = _wsel()
SELB = _selb()
_ii = np.arange(128)
TRI = np.stack([(_ii[:, None] <= _ii[None, :]), (_ii[:, None] > _ii[None, :]), np.ones((128, 128), bool)]).astype(np.float32)


def run_plan(plan, inputs, n_cores=8, trace=False):
    key = repr(plan)
    if key not in _CACHE:
        _CACHE[key] = Builder(plan).build()
    nc = _CACHE[key]
    x = np.ascontiguousarray(inputs["x"], dtype=np.float32)
    gains = np.concatenate([np.asarray(inputs["norm_gains"], np.float32).reshape(DEPTH * 3, D),
                            np.asarray(inputs["final_norm"], np.float32).reshape(1, D)], axis=0)
    common = {
        "gains": np.ascontiguousarray(gains),
        "ffn_w_gate": np.ascontiguousarray(inputs["ffn_w_gate"], dtype=np.float32),
        "ffn_w_up": np.ascontiguousarray(inputs["ffn_w_up"], dtype=np.float32),
        "ffn_w_down": np.ascontiguousarray(inputs["ffn_w_down"], dtype=np.float32),
        "ident": np.eye(128, dtype=np.float32),
        "tri": TRI,
        "ssm_w_in": np.ascontiguousarray(inputs["ssm_w_in"], dtype=np.float32),
        "ssm_w_out": np.ascontiguousarray(inputs["ssm_w_out"], dtype=np.float32),
        "ssm_cw": np.ascontiguousarray(np.asarray(inputs["ssm_conv_w"], np.float32).transpose(0, 2, 1).reshape(2, 32, 128, 4).transpose(0, 2, 1, 3)),
        "ssm_cb": np.ascontiguousarray(np.asarray(inputs["ssm_conv_b"], np.float32).reshape(2, 32, 128).transpose(0, 2, 1)),
        "ssm_vec": np.ascontiguousarray(np.stack([np.asarray(inputs["ssm_dt_bias"], np.float32), np.asarray(inputs["ssm_a_log"], np.float32),
                                                  np.asarray(inputs["ssm_d"], np.float32)], axis=1).reshape(2, 96)),
        "ssm_norm": np.ascontiguousarray(inputs["ssm_norm"], dtype=np.float32),
        "attn_wr": attn_w_layout(np.asarray(inputs["attn_w_in"], np.float32)),
        "attn_w_out": np.ascontiguousarray(inputs["attn_w_out"], dtype=np.float32),
        "attn_sinks": np.ascontiguousarray(inputs["attn_sinks"], dtype=np.float32),
        "rope": ROPE,
        "cmp_w1": np.ascontiguousarray(np.stack([np.asarray(inputs["cmp_k_w1"], np.float32), np.asarray(inputs["cmp_v_w1"], np.float32)], axis=1)),
        "cmp_w2": np.ascontiguousarray(np.stack([np.asarray(inputs["cmp_k_w2"], np.float32), np.asarray(inputs["cmp_v_w2"], np.float32)], axis=1)),
        "cmp_posT": np.ascontiguousarray(np.stack([np.asarray(inputs["cmp_k_pos"], np.float32).transpose(0, 2, 1),
                                                   np.asarray(inputs["cmp_v_pos"], np.float32).transpose(0, 2, 1)], axis=1)),
        "wsel": WSEL,
        "selb": SELB,
    }
    in_maps = []
    for c in range(n_cores):
        m = dict(common)
        m["x"] = x[c % 4]
        in_maps.append(m)
    res = run_bass_kernel_spmd(nc, in_maps, core_ids=list(range(n_cores)), trace=trace)
    out = np.stack([res.results[c % n_cores]["out"] for c in range(4)], axis=0)
    return out, res


def kernel(**inputs):
    out, _ = run_plan(full_plan(), inputs)
    return out.astype(np.float32)
```

```python
import contextlib
import numpy as np
import concourse.bass as bass
import concourse.mybir as mybir
from concourse.bass_utils import run_bass_kernel_spmd

F32 = mybir.dt.float32
BF16 = mybir.dt.bfloat16
AF = mybir.ActivationFunctionType
ALU = mybir.AluOpType
AX = mybir.AxisListType

D = 1024
S = 4096
DEPTH = 4
DFF = 2816
NFC = DFF // 128
EPS = 1e-6
NAW = 3608


class Res:
    __slots__ = ("name", "w", "r")

    def __init__(self, name):
        self.name = name
        self.w = None
        self.r = []


class Op:
    __slots__ = ("eng", "fn", "waits", "inc", "dma", "dsem", "dval", "cnt", "pre")


class Prog:
    ENGS = ("pe", "act", "dve", "pool", "sp")

    def __init__(self, nc, stack):
        self.nc = nc
        self.stack = stack
        self.ops = {e: [] for e in self.ENGS}
        self.esem = {e: stack.enter_context(nc.semaphore("es_" + e)) for e in self.ENGS}
        self.nsem = 5
        self.streams = []

    def new_sem(self, name):
        self.nsem += 1
        return self.stack.enter_context(self.nc.semaphore(name))

    def op(self, eng, fn, reads=(), writes=(), dma=None):
        o = Op()
        o.eng = eng
        o.fn = fn
        o.inc = False
        o.dma = dma
        o.cnt = 0
        o.pre = None
        o.dsem = None
        o.dval = 0
        deps = []
        seen = set()

        def add(d, raw):
            if d is None or id(d) in seen:
                return
            if d.dma is None and d.eng == eng:
                if eng == "pe" or not raw:
                    return
            seen.add(id(d))
            deps.append(d)

        for r in reads:
            add(r.w, True)
        for w in writes:
            add(w.w, False)
            for rr in w.r:
                add(rr, False)
        for d in deps:
            if d.dma is None:
                d.inc = True
        o.waits = deps
        if dma is not None:
            sem, val, pre = dma.next()
            o.dsem, o.dval, o.pre = sem, val, pre
            dma.ops.append(o)
        for r in reads:
            r.r.append(o)
        for w in writes:
            w.w = o
            w.r = []
        self.ops[eng].append(o)
        return o

    def barrier(self):
        deps = []
        for e in self.ENGS:
            for o in reversed(self.ops[e]):
                if o.dma is None:
                    deps.append(o)
                    break
        for st in self.streams:
            deps.extend(st.ops[-st.R:])
        for e in self.ENGS:
            o = Op()
            o.eng = e
            o.fn = lambda eng: eng.nop()
            o.inc = False
            o.dma = None
            o.cnt = 0
            o.pre = None
            o.dsem = None
            o.dval = 0
            o.waits = [d for d in deps if not (d.dma is None and d.eng == e)]
            for d in o.waits:
                if d.dma is None:
                    d.inc = True
            self.ops[e].append(o)

    def emit(self):
        nc = self.nc
        for e in self.ENGS:
            c = 0
            for o in self.ops[e]:
                if o.dma is None and o.inc:
                    c += 1
                o.cnt = c
        self.counts = {e: (len(self.ops[e]), self.ops[e][-1].cnt if self.ops[e] else 0) for e in self.ENGS}

        def body_for(ename):
            def body(eng):
                seen = {}

                def wait(sem, val):
                    k = id(sem)
                    if seen.get(k, 0) >= val:
                        return
                    seen[k] = val
                    eng.wait_ge(sem, val)

                for o in self.ops[ename]:
                    for d in o.waits:
                        if d.dma is not None:
                            wait(d.dsem, d.dval)
                        else:
                            wait(self.esem[d.eng], d.cnt)
                    if o.pre is not None and o.pre[1] > 0:
                        wait(o.pre[0], o.pre[1])
                    ins = o.fn(eng)
                    if o.dma is not None:
                        ins.then_inc(o.dsem, 16)
                    elif o.inc:
                        ins.then_inc(self.esem[ename], 1)
            return body

        with nc.Block() as block:
            block.tensor(body_for("pe"))
            block.scalar(body_for("act"))
            block.vector(body_for("dve"))
            block.gpsimd(body_for("pool"))
            block.sync(body_for("sp"))


class DmaStream:
    def __init__(self, prog, name, R):
        self.sems = [prog.new_sem("%s%d" % (name, i)) for i in range(R)]
        self.R = R
        self.k = 0
        self.ops = []
        prog.streams.append(self)

    def next(self):
        k = self.k
        self.k += 1
        sem = self.sems[k % self.R]
        return sem, 16 * (k // self.R + 1), (sem, 16 * (k // self.R))


class Builder:
    def __init__(self, plan):
        self.plan = plan
        self.nc = bass.Bass("TRN2", target_bir_lowering=False)
        self.stack = contextlib.ExitStack()
        self.res_cache = {}

    def R(self, name):
        r = self.res_cache.get(name)
        if r is None:
            r = self.res_cache[name] = Res(name)
        return r

    def dram_in(self, name, shape, dt=F32):
        return self.nc.dram_tensor(name, list(shape), dt, kind="ExternalInput").ap()

    def dram_out(self, name, shape, dt=F32):
        return self.nc.dram_tensor(name, list(shape), dt, kind="ExternalOutput").ap()

    def dram_tmp(self, name, shape, dt):
        return self.nc.dram_tensor(name, list(shape), dt).ap()

    def sb(self, name, shape, dt):
        return self.stack.enter_context(self.nc.sbuf_tensor(name, list(shape), dt))

    def ps(self, name, shape, dt):
        return self.stack.enter_context(self.nc.psum_tensor(name, list(shape), dt))

    def build(self):
        nc = self.nc
        with self.stack:
            self.p = Prog(nc, self.stack)
            self._build()
            self.p.emit()
        return nc

    def _build(self):
        p = self.p
        plan = self.plan
        self.x_in = self.dram_in("x", [S, D])
        self.gains = self.dram_in("gains", [DEPTH * 3 + 1, D])
        self.wg = self.dram_in("ffn_w_gate", [DEPTH, 2, D, DFF])
        self.wu = self.dram_in("ffn_w_up", [DEPTH, 2, D, DFF])
        self.wd = self.dram_in("ffn_w_down", [DEPTH, 2, DFF, D])
        self.ident_in = self.dram_in("ident", [128, 128])
        self.tri_in = self.dram_in("tri", [3, 128, 128])
        self.ssm_w_in = self.dram_in("ssm_w_in", [2, D, 6176])
        self.ssm_w_out = self.dram_in("ssm_w_out", [2, 2048, D])
        self.ssm_cw = self.dram_in("ssm_cw", [2, 128, 32, 4])
        self.ssm_cb = self.dram_in("ssm_cb", [2, 128, 32])
        self.ssm_vec = self.dram_in("ssm_vec", [2, 96])
        self.ssm_norm = self.dram_in("ssm_norm", [2, 2048])
        self.swin = self.dram_tmp("swin", [2, D, 6176], BF16)
        self.attn_wr = self.dram_in("attn_wr", [2, D, NAW])
        self.attn_w_out = self.dram_in("attn_w_out", [2, D, D])
        self.sinks_in = self.dram_in("attn_sinks", [2, 8])
        self.rope_in = self.dram_in("rope", [2, 64, S])
        self.cmp_w1 = self.dram_in("cmp_w1", [2, 2, 2048, 128])
        self.cmp_w2 = self.dram_in("cmp_w2", [2, 2, 128, 64])
        self.cmp_posT = self.dram_in("cmp_posT", [2, 2, 64, 32])
        self.wsel_in = self.dram_in("wsel", [128, 2, 64])
        self.selb_in = self.dram_in("selb", [128, 32, 64])
        self.awin = self.dram_tmp("awin", [2, D, NAW], BF16)
        self.qT = self.dram_tmp("qT", [16, 64, S], BF16)
        self.out = self.dram_out("out", [S, D])
        self.xres = self.dram_tmp("xres", [S, D], F32)
        self.wgt = self.dram_tmp("wgt", [DEPTH, 2, 6, 128, 8, 512], BF16)
        self.wut = self.dram_tmp("wut", [DEPTH, 2, 6, 128, 8, 512], BF16)
        self.wdt = self.dram_tmp("wdt", [DEPTH, 2, DFF, D], BF16)

        self.st_const = DmaStream(p, "dc", 1)
        self.st_prep = DmaStream(p, "dp", 4)
        self.st_x = DmaStream(p, "dx", 4)
        self.st_w = DmaStream(p, "dw", 4)
        self.st_o = DmaStream(p, "do", 4)

        self.ident_f = self.sb("ident_f", [128, 128], F32)
        self.ident_b = self.sb("ident_b", [128, 128], BF16)
        self.gbc = self.sb("gbc", [128, D], F32)
        self.epsb = self.sb("epsb", [128, 1], F32)
        r_ident = self.R("ident")
        p.op("sp", lambda e: e.dma_start(out=self.ident_f[:], in_=self.ident_in), writes=[r_ident], dma=self.st_const)
        p.op("dve", lambda e: e.tensor_copy(out=self.ident_b[:], in_=self.ident_f[:]), reads=[r_ident], writes=[self.R("ident_b")])
        p.op("dve", lambda e: e.memset(self.epsb[:], EPS), writes=[self.R("epsb")])

        self.pbank = [self.ps("pb%d" % i, [128, 512], F32) for i in range(8)]
        self.ptrh = [self.pbank[6 + i][:, :].bitcast(BF16)[:, 0:512] for i in range(2)]
        self.r_pb = [self.R("pb%d" % i) for i in range(8)]
        self.r_ptr = [self.r_pb[6], self.r_pb[7]]

        self.x_src = self.x_in
        for ph in plan:
            kind = ph[0]
            if kind == "prep_ffn":
                self.prep_ffn(ph[1], ph[2])
            elif kind == "ffn":
                self.dbg_stage = ph[3] if len(ph) > 3 else 99
                self.ffn(ph[1], ph[2])
            elif kind == "prep_ssm":
                self.prep_ssm(ph[1])
            elif kind == "ssd":
                self.ssd(ph[1])
            elif kind == "prep_attn":
                self.prep_attn(ph[1])
            elif kind == "attn":
                self.attn_dbg = ph[2] if len(ph) > 2 else None
                self.attn(ph[1])
            elif kind == "final":
                self.final_norm()
            elif kind == "copy_out":
                self.copy_out()
            else:
                raise ValueError(kind)

    def load_gain(self, row):
        p = self.p
        src = self.gains[row:row + 1, :].broadcast_to([128, D])
        p.op("sp", lambda e: e.dma_start(out=self.gbc[:], in_=src), writes=[self.R("gbc")], dma=self.st_const)

    def prep_ffn(self, l, i):
        p = self.p
        for (src, dst, nm) in ((self.wg, self.wgt, "g"), (self.wu, self.wut, "u")):
            for blk in range(6):
                w = 512 if blk < 5 else 256
                s_ap = src[l, i, :, blk * 512:blk * 512 + w].rearrange("(kc p) m -> p kc m", p=128)
                d_ap = dst[l, i, blk, :, :, 0:w]
                p.op("pool", lambda e, s_ap=s_ap, d_ap=d_ap: e.dma_start(out=d_ap, in_=s_ap),
                     writes=[self.R("wt_%s_%d_%d_%d" % (nm, l, i, blk))], dma=self.st_prep)
        for q in range(4):
            rows = DFF // 4
            s_ap = self.wd[l, i, q * rows:(q + 1) * rows, :]
            d_ap = self.wdt[l, i, q * rows:(q + 1) * rows, :]
            p.op("pool", lambda e, s_ap=s_ap, d_ap=d_ap: e.dma_start(out=d_ap, in_=s_ap),
                 writes=[self.R("wt_d_%d_%d_%d" % (l, i, q))], dma=self.st_prep)

    def norm_transpose(self, xsrc, t0, ntile, hT, r_hT, xt_bufs, tag):
        p = self.p
        for j in range(ntile):
            tt = t0 + j
            b = j % 2
            xt, hb, sq, ss, rs = xt_bufs[b]
            rx = self.R("%s_xt%d" % (tag, b))
            rh = self.R("%s_hb%d" % (tag, b))
            rss = self.R("%s_ss%d" % (tag, b))
            rsq = self.R("%s_sq%d" % (tag, b))
            p.op("sp", lambda e, xt=xt, tt=tt: e.dma_start(out=xt[:], in_=xsrc[tt * 128:(tt + 1) * 128, :]),
                 reads=[self.R("xres")], writes=[rx], dma=self.st_x)
            p.op("act", lambda e, xt=xt, sq=sq, ss=ss: e.activation(out=sq[:], in_=xt[:], func=AF.Square, accum_out=ss[:]),
                 reads=[rx], writes=[rsq, rss])
            p.op("act", lambda e, ss=ss, rs=rs: e.activation(out=rs[:], in_=ss[:], func=AF.Sqrt, scale=1.0 / D, bias=self.epsb[:]),
                 reads=[rss, self.R("epsb")], writes=[self.R("%s_rs%d" % (tag, b))])
            p.op("dve", lambda e, rs=rs: e.reciprocal(out=rs[:], in_=rs[:]),
                 reads=[self.R("%s_rs%d" % (tag, b))], writes=[self.R("%s_rs%d" % (tag, b))])
            p.op("dve", lambda e, xt=xt, hb=hb, rs=rs: e.scalar_tensor_tensor(out=hb[:], in0=xt[:], scalar=rs[:], in1=self.gbc[:], op0=ALU.mult, op1=ALU.mult),
                 reads=[rx, self.R("%s_rs%d" % (tag, b)), self.R("gbc")], writes=[rh])
            for half in range(2):
                rp = self.r_ptr[half]

                def tr(e, hb=hb, half=half):
                    ins = None
                    for q in range(4):
                        kc = half * 4 + q
                        ins = e.transpose(out=self.ptrh[half][:, q * 128:(q + 1) * 128],
                                          in_=hb[:, kc * 128:(kc + 1) * 128], identity=self.ident_b[:])
                    return ins
                p.op("pe", tr, reads=[rh, self.R("ident_b")], writes=[rp])
                dst = hT[:, half * 4:(half + 1) * 4, j * 128:(j + 1) * 128]
                srcp = self.ptrh[half][:, :].rearrange("p (q m) -> p q m", q=4)
                eng = "act" if half == 0 else "dve"
                if eng == "act":
                    p.op("act", lambda e, dst=dst, srcp=srcp: e.copy(out=dst, in_=srcp), reads=[rp], writes=[r_hT[j][half]])
                else:
                    p.op("dve", lambda e, dst=dst, srcp=srcp: e.tensor_copy(out=dst, in_=srcp), reads=[rp], writes=[r_hT[j][half]])

    def ffn(self, l, i):
        p = self.p
        nc = self.nc
        TB = 1024
        NTB = S // TB
        xsrc = self.x_src
        with contextlib.ExitStack() as st:
            def sb(name, shape, dt):
                return st.enter_context(nc.sbuf_tensor(name, list(shape), dt))
            tag = "f%d%d" % (l, i)
            wd_sb = sb(tag + "wd", [128, NFC, D], BF16)
            aT = sb(tag + "aT", [128, NFC, TB], BF16)
            hT = sb(tag + "hT", [128, 8, TB], BF16)
            wgu = [(sb(tag + "wg%d" % b, [128, 8, 512], BF16), sb(tag + "wu%d" % b, [128, 8, 512], BF16)) for b in range(2)]
            xt_bufs = [(sb(tag + "xt%d" % b, [128, D], F32), sb(tag + "hb%d" % b, [128, D], BF16),
                        sb(tag + "sq%d" % b, [128, D], BF16), sb(tag + "ss%d" % b, [128, 1], F32),
                        sb(tag + "rs%d" % b, [128, 1], F32)) for b in range(2)]
            sg = [sb(tag + "sg%d" % b, [128, 512], F32) for b in range(2)]
            xo = [sb(tag + "xo%d" % b, [128, D], F32) for b in range(2)]
            r_wd = [self.R(tag + "wd0"), self.R(tag + "wd1")]
            r_aT = self.R(tag + "aT")
            r_hT = [[self.R(tag + "hT%d_%d" % (jj, hh)) for hh in range(2)] for jj in range(TB // 128)]
            r_wg = [self.R(tag + "wgs%d" % b) for b in range(2)]
            r_wu = [self.R(tag + "wus%d" % b) for b in range(2)]
            r_sg = [self.R(tag + "sg%d" % b) for b in range(2)]
            r_xo = [self.R(tag + "xo%d" % b) for b in range(2)]

            self.load_gain(l * 3 + (0 if i == 0 else 2))
            for q in range(2):
                fa, fb = q * 11, (q + 1) * 11
                src = self.wdt[l, i, fa * 128:fb * 128, :].rearrange("(fc p) m -> p fc m", p=128)
                p.op("sp", lambda e, src=src, fa=fa, fb=fb: e.dma_start(out=wd_sb[:, fa:fb, :], in_=src),
                     reads=[self.R("wt_d_%d_%d_%d" % (l, i, 2 * q)), self.R("wt_d_%d_%d_%d" % (l, i, 2 * q + 1))], writes=[r_wd[q]], dma=self.st_w)

            wcount = 0
            for tb in range(NTB):
                t0 = tb * (TB // 128)
                self.norm_transpose(xsrc, t0, TB // 128, hT, r_hT, xt_bufs, tag)
                if self.dbg_stage <= 1:
                    continue
                for blk in range(6):
                    w = 512 if blk < 5 else 256
                    b = wcount % 2
                    wcount += 1
                    wgs, wus = wgu[b]
                    p.op("sp", lambda e, wgs=wgs, blk=blk, w=w: e.dma_start(out=wgs[:, :, 0:w], in_=self.wgt[l, i, blk, :, :, 0:w]),
                         reads=[self.R("wt_g_%d_%d_%d" % (l, i, blk))], writes=[r_wg[b]], dma=self.st_w)
                    p.op("sp", lambda e, wus=wus, blk=blk, w=w: e.dma_start(out=wus[:, :, 0:w], in_=self.wut[l, i, blk, :, :, 0:w]),
                         reads=[self.R("wt_u_%d_%d_%d" % (l, i, blk))], writes=[r_wu[b]], dma=self.st_w)
                    for m in range(w // 128):
                        fc = blk * 4 + m
                        for half in range(TB // 512):
                            pg = (fc * 2 + half) % 2
                            bg, bu = self.pbank[pg * 2], self.pbank[pg * 2 + 1]
                            rg, ru = self.r_pb[pg * 2], self.r_pb[pg * 2 + 1]

                            def mm(e, wt, bank, m=m, half=half):
                                ins = None
                                for kc in range(8):
                                    ins = e.matmul(bank[:, :], lhsT=wt[:, kc, m * 128:(m + 1) * 128],
                                                   rhs=hT[:, kc, half * 512:(half + 1) * 512],
                                                   start=(kc == 0), stop=(kc == 7))
                                return ins
                            hres = [r_hT[half * 4 + jj][hh] for jj in range(4) for hh in range(2)]
                            p.op("pe", lambda e, wgs=wgs, bg=bg, mm=mm: mm(e, wgs, bg), reads=[r_wg[b]] + hres, writes=[rg])
                            p.op("pe", lambda e, wus=wus, bu=bu, mm=mm: mm(e, wus, bu), reads=[r_wu[b]] + hres, writes=[ru])
                            sgb = sg[pg]
                            p.op("act", lambda e, sgb=sgb, bg=bg: e.activation(out=sgb[:], in_=bg[:, :], func=AF.Silu),
                                 reads=[rg], writes=[r_sg[pg]])
                            dst = aT[:, fc, half * 512:(half + 1) * 512]
                            p.op("dve", lambda e, dst=dst, sgb=sgb, bu=bu: e.tensor_tensor(out=dst, in0=sgb[:], in1=bu[:, :], op=ALU.mult),
                                 reads=[r_sg[pg], ru], writes=[r_aT])
                if self.dbg_stage <= 2:
                    continue
                for j in range(TB // 128):
                    tt = t0 + j
                    xb = j % 2
                    xt = xt_bufs[xb][0]
                    rx = self.R("%s_xt%d" % (tag, xb))
                    p.op("sp", lambda e, xt=xt, tt=tt: e.dma_start(out=xt[:], in_=xsrc[tt * 128:(tt + 1) * 128, :]),
                         reads=[self.R("xres")], writes=[rx], dma=self.st_x)
                    for ch in range(2):
                        pb = 4 + (j * 2 + ch) % 2
                        bank, rb = self.pbank[pb], self.r_pb[pb]

                        def mmd(e, bank=bank, j=j, ch=ch):
                            ins = None
                            for fc in range(NFC):
                                ins = e.matmul(bank[:, :], lhsT=aT[:, fc, j * 128:(j + 1) * 128],
                                               rhs=wd_sb[:, fc, ch * 512:(ch + 1) * 512],
                                               start=(fc == 0), stop=(fc == NFC - 1))
                            return ins
                        p.op("pe", mmd, reads=[r_aT] + r_wd, writes=[rb])
                        xob = xo[xb]
                        p.op("dve", lambda e, xob=xob, bank=bank, xt=xt, ch=ch: e.scalar_tensor_tensor(
                            out=xob[:, ch * 512:(ch + 1) * 512], in0=bank[:, :], scalar=0.5, in1=xt[:, ch * 512:(ch + 1) * 512],
                            op0=ALU.mult, op1=ALU.add), reads=[rb, rx], writes=[r_xo[xb]])
                    p.op("sp", lambda e, xob=xob, tt=tt: e.dma_start(out=self.xres[tt * 128:(tt + 1) * 128, :], in_=xob[:]),
                         reads=[r_xo[xb]], writes=[self.R("xres_w%d" % (tt % 4))], dma=self.st_o)
            self.phase_barrier()
        self.x_src = self.xres

    def _store_waits(self):
        st = self.st_o
        return [(st.sems[idx % st.R], 16 * (idx // st.R + 1)) for idx in range(max(0, st.k - st.R), st.k)]

    def phase_barrier(self):
        p = self.p
        sems = self._store_waits()

        def fn(e, sems=sems):
            for sem, val in sems:
                e.wait_ge(sem, val)
            return e.nop()
        p.op("sp", fn, reads=[self.R("xres_w%d" % k) for k in range(4)], writes=[self.R("xres")])
        p.barrier()

    def prep_ssm(self, j):
        p = self.p
        for (c0, c1) in ((0, 2048), (2048, 4096), (4096, 6144), (6144, 6176)):
            for hh in range(2):
                s_ap = self.ssm_w_in[j, hh * 512:(hh + 1) * 512, c0:c1]
                d_ap = self.swin[j, hh * 512:(hh + 1) * 512, c0:c1]
                p.op("pool", lambda e, s_ap=s_ap, d_ap=d_ap: e.dma_start(out=d_ap, in_=s_ap),
                     writes=[self.R("swin%d_%d_%d" % (j, c0, hh))], dma=self.st_prep)

    def swin_res(self, j, c0):
        base = (c0 // 2048) * 2048 if c0 < 6144 else 6144
        return [self.R("swin%d_%d_%d" % (j, base, hh)) for hh in range(2)]

    def ssd(self, l):
        p = self.p
        nc = self.nc
        j = l // 2
        TBK = 256
        NBLK = S // TBK
        NT = TBK // 128
        xsrc = self.x_src
        with contextlib.ExitStack() as st:
            def sb(name, shape, dt):
                return st.enter_context(nc.sbuf_tensor(tag + name, list(shape), dt))
            tag = "s%d" % l
            R = lambda n: self.R(tag + n)
            L1 = sb("L1", [128, 128], F32)
            L2 = sb("L2", [128, 128], F32)
            ONES = sb("ONES", [128, 128], F32)
            cw = sb("cw", [128, 32, 4], F32)
            cb = sb("cb", [128, 32], F32)
            vec = sb("vec", [128, 3, 32], F32)
            a_bc = sb("a_bc", [128, 32], F32)
            Dbc = sb("Dbc", [128, 32, 1], F32)
            nw = sb("nw", [128, 2048], F32)
            wout = sb("wout", [128, 16, 1024], BF16)
            wdt = sb("wdt", [128, 8, 32], BF16)
            halo = sb("halo", [128, 32, 3], F32)
            state = sb("state", [128, 2048], F32)
            state_bf = sb("state_bf", [128, 2048], BF16)
            hT = sb("hT", [128, 8, TBK], BF16)
            wx = [sb("wx%d" % b, [128, 8, 512], BF16) for b in range(2)]
            xin = [sb("xin%d" % b, [128, TBK + 3], F32) for b in range(3)]
            acc = [sb("acc%d" % b, [128, TBK], F32) for b in range(3)]
            xsb = [sb("xsb%d" % b, [128, TBK], BF16) for b in range(2)]
            BT = sb("BT", [128, 8, TBK], BF16)
            CT = sb("CT", [128, 8, TBK], BF16)
            Btok = sb("Btok", [128, NT, 1024], BF16)
            xs_tok = sb("xs_tok", [128, NT, 2048], BF16)
            sz = sb("sz", [128, NT, 2048], BF16)
            xt_bufs = [(sb("xt%d" % b, [128, D], F32), sb("hb%d" % b, [128, D], BF16),
                        sb("sq%d" % b, [128, D], BF16), sb("ss%d" % b, [128, 1], F32),
                        sb("rs%d" % b, [128, 1], F32)) for b in range(2)]
            dtv = sb("dtv", [128, 32], F32)
            dt3 = sb("dt3", [128, 32, 1], F32)
            da = sb("da", [128, 32], F32)
            eall = sb("eall", [128, 96], F32)
            ea3 = sb("ea3", [128, 32, 1], F32)
            w23 = sb("w23", [128, 32, 1], F32)
            xdt = sb("xdt", [128, 2048], BF16)
            xdec = sb("xdec", [128, 2048], BF16)
            ybuf = sb("ybuf", [128, 2048], F32)
            t3 = sb("t3", [128, 2048], F32)
            gnb = sb("gnb", [128, 2048], BF16)
            gnT = sb("gnT", [128, 16, 128], BF16)
            Eg = [sb("Eg%d" % b, [128, 4, 128], F32) for b in range(2)]
            MT = [sb("MT%d" % b, [128, 4, 128], BF16) for b in range(2)]
            GTm = [sb("GTm%d" % b, [128, 1, 128], F32) for b in range(3)]
            ybufD = [sb("ybufD%d" % b, [128, 256], F32) for b in range(2)]
            Ada4 = [sb("Ada4_%d" % b, [128, 4, 128], F32) for b in range(2)]
            ssg = sb("ssg", [128, 8], F32)
            rsg = sb("rsg", [128, 8, 1], F32)
            xo = sb("xo", [128, D], F32)
            xres_t = sb("xres_t", [128, D], F32)

            p.op("sp", lambda e: e.dma_start(out=L1[:], in_=self.tri_in[0]), writes=[R("L1")], dma=self.st_const)
            p.op("sp", lambda e: e.dma_start(out=L2[:], in_=self.tri_in[1]), writes=[R("L2")], dma=self.st_const)
            p.op("sp", lambda e: e.dma_start(out=ONES[:], in_=self.tri_in[2]), writes=[R("ONES")], dma=self.st_const)
            p.op("sp", lambda e: e.dma_start(out=cw[:], in_=self.ssm_cw[j]), writes=[R("cw")], dma=self.st_const)
            p.op("sp", lambda e: e.dma_start(out=cb[:], in_=self.ssm_cb[j]), writes=[R("cb")], dma=self.st_const)
            p.op("sp", lambda e: e.dma_start(out=vec[:].rearrange("p a b -> p (a b)"),
                                             in_=self.ssm_vec[j:j + 1, :].broadcast_to([128, 96])), writes=[R("vec")], dma=self.st_const)
            p.op("sp", lambda e: e.dma_start(out=nw[:], in_=self.ssm_norm[j:j + 1, :].broadcast_to([128, 2048])), writes=[R("nw")], dma=self.st_const)
            for q in range(4):
                src = self.ssm_w_out[j, q * 512:(q + 1) * 512, :].rearrange("(k p) m -> p k m", p=128)
                p.op("pool", lambda e, src=src, q=q: e.dma_start(out=wout[:, q * 4:(q + 1) * 4, :], in_=src), writes=[R("wout%d" % q)], dma=self.st_prep)
            r_wout = [R("wout%d" % q) for q in range(4)]
            p.op("sp", lambda e: e.dma_start(out=wdt[:], in_=self.swin[j, :, 6144:6176].rearrange("(k p) m -> p k m", p=128)),
                 reads=self.swin_res(j, 6144), writes=[R("wdt")], dma=self.st_const)
            p.op("act", lambda e: e.activation(out=a_bc[:], in_=vec[:, 1, :], func=AF.Exp), reads=[R("vec")], writes=[R("a_bc")])
            p.op("dve", lambda e: e.tensor_scalar(out=a_bc[:], in0=a_bc[:], scalar1=-1.0, scalar2=None, op0=ALU.mult), reads=[R("a_bc")], writes=[R("a_bc")])
            p.op("dve", lambda e: e.tensor_copy(out=Dbc[:, :, 0], in_=vec[:, 2, :]), reads=[R("vec")], writes=[R("Dbc")])
            p.op("pool", lambda e: e.memset(halo[:], 0.0), writes=[R("halo%d" % cc_) for cc_ in range(32)])
            p.op("pool", lambda e: e.memset(state[:], 0.0), writes=[R("state")])
            p.op("pool", lambda e: e.memset(state_bf[:], 0.0), writes=[R("state_bf")])
            self.load_gain(l * 3 + 1)

            r_hT = [[R("hT%d_%d" % (jj, hh)) for hh in range(2)] for jj in range(NT)]
            hres_all = [r_hT[jj][hh] for jj in range(NT) for hh in range(2)]
            wxc = 0
            cvc = 0
            pendB = []
            pendB2 = []
            tails = []
            for blk in range(NBLK):
                t0 = blk * NT
                self.norm_transpose(xsrc, t0, NT, hT, r_hT, xt_bufs, tag)
                for wgI in range(8):
                    b = wxc % 2
                    wxc += 1
                    c0 = 2048 + wgI * 512
                    src = self.swin[j, :, c0:c0 + 512].rearrange("(k p) m -> p k m", p=128)
                    p.op("sp", lambda e, src=src, b=b: e.dma_start(out=wx[b][:], in_=src),
                         reads=self.swin_res(j, c0), writes=[R("wx%d" % b)], dma=self.st_w)
                    for m in range(4):
                        cc = wgI * 4 + m
                        pb = cc % 2
                        bank, rb = self.pbank[pb], self.r_pb[pb]
                        while len(pendB) >= 2:
                            pendB.pop(0)()

                        def mm(e, b=b, m=m, bank=bank):
                            for kc in range(8):
                                ins = e.matmul(bank[:, 0:TBK], lhsT=wx[b][:, kc, m * 128:(m + 1) * 128], rhs=hT[:, kc, :],
                                               start=(kc == 0), stop=(kc == 7))
                            return ins
                        p.op("pe", mm, reads=[R("wx%d" % b)] + hres_all, writes=[rb])
                        while len(pendB2) >= 2:
                            pendB2.pop(0)()
                        cbuf = cvc % 3
                        cvc += 1
                        xi, ac = xin[cbuf], acc[cbuf]
                        rxi, rac = R("xin%d" % cbuf), R("acc%d" % cbuf)
                        rhalo = R("halo%d" % cc)
                        p.op("pool", lambda e, xi=xi, cc=cc: e.tensor_copy(out=xi[:, 0:3], in_=halo[:, cc, :]), reads=[rhalo], writes=[rxi])
                        p.op("act", lambda e, xi=xi, bank=bank: e.copy(out=xi[:, 3:3 + TBK], in_=bank[:, 0:TBK]), reads=[rb], writes=[rxi])
                        p.op("pool", lambda e, xi=xi, cc=cc: e.tensor_copy(out=halo[:, cc, :], in_=xi[:, TBK:TBK + 3]), reads=[rxi], writes=[rhalo])
                        p.op("act", lambda e, xi=xi, ac=ac, cc=cc: e.activation(out=ac[:], in_=xi[:, 0:TBK], func=AF.Identity, scale=cw[:, cc, 0:1], bias=cb[:, cc:cc + 1]),
                             reads=[rxi, R("cw"), R("cb")], writes=[rac])
                        for w in range(1, 4):
                            p.op("dve", lambda e, xi=xi, ac=ac, cc=cc, w=w: e.scalar_tensor_tensor(out=ac[:], in0=xi[:, w:w + TBK], scalar=cw[:, cc, w:w + 1], in1=ac[:],
                                                                                                 op0=ALU.mult, op1=ALU.add), reads=[rxi, rac, R("cw")], writes=[rac])

                        def stageB1(cc=cc, ac=ac, rac=rac, cbuf=cbuf):
                            if cc < 16:
                                xb_, rxb = xsb[cbuf % 2], R("xsb%d" % (cbuf % 2))
                                p.op("act", lambda e, ac=ac, xb_=xb_: e.activation(out=xb_[:], in_=ac[:], func=AF.Silu), reads=[rac], writes=[rxb])
                            elif cc < 24:
                                g = cc - 16
                                p.op("act", lambda e, ac=ac, g=g: e.activation(out=BT[:, g, :], in_=ac[:], func=AF.Silu), reads=[rac], writes=[R("BT%d" % g)])
                            else:
                                g = cc - 24
                                p.op("act", lambda e, ac=ac, g=g: e.activation(out=CT[:, g, :], in_=ac[:], func=AF.Silu), reads=[rac], writes=[R("CT%d" % g)])

                        def stageB2(cc=cc, cbuf=cbuf):
                            if cc < 16:
                                xb_, rxb = xsb[cbuf % 2], R("xsb%d" % (cbuf % 2))
                                hp = cc % 2
                                rp = self.r_ptr[hp]

                                def tr(e, xb_=xb_, hp=hp):
                                    for q in range(NT):
                                        ins = e.transpose(out=self.ptrh[hp][:, q * 128:(q + 1) * 128], in_=xb_[:, q * 128:(q + 1) * 128], identity=self.ident_b[:])
                                    return ins
                                p.op("pe", tr, reads=[rxb, self.R("ident_b")], writes=[rp])
                                dst = xs_tok[:, :, cc * 128:(cc + 1) * 128]
                                srcp = self.ptrh[hp][:, 0:NT * 128].rearrange("p (q m) -> p q m", q=NT)
                                p.op("dve", lambda e, dst=dst, srcp=srcp: e.tensor_copy(out=dst, in_=srcp), reads=[rp], writes=[R("xs_tok")])
                            elif cc < 24:
                                g = cc - 16
                                hp = cc % 2
                                rp = self.r_ptr[hp]

                                def tr(e, g=g, hp=hp):
                                    for q in range(NT):
                                        ins = e.transpose(out=self.ptrh[hp][:, q * 128:(q + 1) * 128], in_=BT[:, g, q * 128:(q + 1) * 128], identity=self.ident_b[:])
                                    return ins
                                p.op("pe", tr, reads=[R("BT%d" % g), self.R("ident_b")], writes=[rp])
                                dst = Btok[:, :, g * 128:(g + 1) * 128]
                                srcp = self.ptrh[hp][:, 0:NT * 128].rearrange("p (q m) -> p q m", q=NT)
                                p.op("dve", lambda e, dst=dst, srcp=srcp: e.tensor_copy(out=dst, in_=srcp), reads=[rp], writes=[R("Btok")])
                        pendB.append(stageB1)
                        pendB2.append(stageB2)
                        if cc == 3:
                            while tails:
                                tails.pop(0)()
                while pendB:
                    pendB.pop(0)()
                while pendB2:
                    pendB2.pop(0)()
                for zc in range(4):
                    b = wxc % 2
                    wxc += 1
                    c0 = zc * 512
                    src = self.swin[j, :, c0:c0 + 512].rearrange("(k p) m -> p k m", p=128)
                    p.op("sp", lambda e, src=src, b=b: e.dma_start(out=wx[b][:], in_=src),
                         reads=self.swin_res(j, c0), writes=[R("wx%d" % b)], dma=self.st_w)
                    for q in range(NT):
                        pb = (zc * NT + q) % 2
                        bank, rb = self.pbank[pb], self.r_pb[pb]

                        def mmz(e, b=b, q=q, bank=bank):
                            for kc in range(8):
                                ins = e.matmul(bank[:, :], lhsT=hT[:, kc, q * 128:(q + 1) * 128], rhs=wx[b][:, kc, :], start=(kc == 0), stop=(kc == 7))
                            return ins
                        p.op("pe", mmz, reads=[R("wx%d" % b)] + r_hT[q], writes=[rb])
                        p.op("act", lambda e, q=q, zc=zc, bank=bank: e.activation(out=sz[:, q, zc * 512:(zc + 1) * 512], in_=bank[:, :], func=AF.Silu),
                             reads=[rb], writes=[R("sz%d" % q)])
                for q in range(NT):
                    tt = t0 + q
                    tsl = slice(q * 128, (q + 1) * 128)
                    b0, rb0 = self.pbank[0], self.r_pb[0]

                    def mmdt(e, q=q):
                        for kc in range(8):
                            ins = e.matmul(b0[:, 0:32], lhsT=hT[:, kc, q * 128:(q + 1) * 128], rhs=wdt[:, kc, :], start=(kc == 0), stop=(kc == 7))
                        return ins
                    p.op("pe", mmdt, reads=[R("wdt")] + r_hT[q], writes=[rb0])
                    p.op("dve", lambda e: e.tensor_tensor(out=dtv[:], in0=b0[:, 0:32], in1=vec[:, 0, :], op=ALU.add), reads=[rb0, R("vec")], writes=[R("dtv")])
                    p.op("act", lambda e: e.activation(out=dtv[:], in_=dtv[:], func=AF.Exp), reads=[R("dtv")], writes=[R("dtv")])
                    p.op("act", lambda e: e.activation(out=dt3[:, :, 0], in_=dtv[:], func=AF.Ln, bias=1.0), reads=[R("dtv")], writes=[R("dt3")])
                    p.op("dve", lambda e: e.tensor_tensor(out=da[:], in0=dt3[:, :, 0], in1=a_bc[:], op=ALU.mult), reads=[R("dt3"), R("a_bc")], writes=[R("da")])

                    def mmcs(e):
                        e.matmul(b0[:, 32:64], lhsT=L1[:], rhs=da[:], start=True, stop=True)
                        e.matmul(b0[:, 64:96], lhsT=L2[:], rhs=da[:], start=True, stop=True)
                        return e.matmul(b0[:, 96:128], lhsT=ONES[:], rhs=da[:], start=True, stop=True)
                    p.op("pe", mmcs, reads=[R("da"), R("L1"), R("L2"), R("ONES")], writes=[rb0])
                    p.op("act", lambda e: e.activation(out=eall[:], in_=b0[:, 32:128], func=AF.Exp), reads=[rb0], writes=[R("eall")])
                    p.op("dve", lambda e: e.tensor_copy(out=ea3[:, :, 0], in_=eall[:, 0:32]), reads=[R("eall")], writes=[R("ea3")])
                    p.op("dve", lambda e: e.tensor_tensor(out=w23[:, :, 0], in0=dt3[:, :, 0], in1=eall[:, 32:64], op=ALU.mult), reads=[R("dt3"), R("eall")], writes=[R("w23")])
                    xs3 = xs_tok[:, q, :].rearrange("p (h d) -> p h d", h=32)
                    p.op("dve", lambda e, xs3=xs3: e.tensor_tensor(out=xdt[:].rearrange("p (h d) -> p h d", h=32), in0=xs3, in1=dt3[:].to_broadcast([128, 32, 64]), op=ALU.mult),
                         reads=[R("xs_tok"), R("dt3")], writes=[R("xdt")])
                    p.op("pool", lambda e, xs3=xs3: e.tensor_tensor(out=xdec[:].rearrange("p (h d) -> p h d", h=32), in0=xs3, in1=w23[:].to_broadcast([128, 32, 64]), op=ALU.mult),
                         reads=[R("xs_tok"), R("w23")], writes=[R("xdec")])
                    p.op("pool", lambda e, xs3=xs3: e.tensor_tensor(out=t3[:].rearrange("p (h d) -> p h d", h=32), in0=xs3, in1=Dbc[:].to_broadcast([128, 32, 64]), op=ALU.mult),
                         reads=[R("xs_tok"), R("Dbc")], writes=[R("t3")])
                    def S0(g, tsl=tsl):
                        gb, g3 = g % 2, g % 3
                        b1, rb1 = self.pbank[1], self.r_pb[1]
                        gcol = slice((g % 4) * 128, (g % 4 + 1) * 128)
                        p.op("pe", lambda e, g=g, gcol=gcol, tsl=tsl: e.matmul(b1[:, gcol], lhsT=BT[:, g, tsl], rhs=CT[:, g, tsl], start=True, stop=True),
                             reads=[R("BT%d" % g), R("CT%d" % g)], writes=[rb1])
                        p.op("dve", lambda e, g3=g3, gcol=gcol: e.tensor_tensor(out=GTm[g3][:, 0, :], in0=b1[:, gcol], in1=L1[:], op=ALU.mult),
                             reads=[rb1, R("L1")], writes=[R("GTm%d" % g3)])
                        for r in range(4):
                            h = 4 * g + r
                            p.op("act", lambda e, gb=gb, r=r, h=h: e.activation(out=Ada4[gb][:, r, :], in_=L2[:], func=AF.Copy, scale=da[:, h:h + 1]),
                                 reads=[R("L2"), R("da")], writes=[R("Ada4_%d" % gb)])

                    def S1(g):
                        gb = g % 2
                        bs, rbs = self.pbank[2 + gb], self.r_pb[2 + gb]

                        def mmseg(e, gb=gb, bs=bs):
                            for r in range(4):
                                ins = e.matmul(bs[:, r * 128:(r + 1) * 128], lhsT=Ada4[gb][:, r, :], rhs=L1[:], start=True, stop=True)
                            return ins
                        p.op("pe", mmseg, reads=[R("Ada4_%d" % gb), R("L1")], writes=[rbs])
                        p.op("act", lambda e, gb=gb, bs=bs: e.activation(out=Eg[gb][:].rearrange("p r l -> p (r l)"), in_=bs[:, :], func=AF.Exp),
                             reads=[rbs], writes=[R("Eg%d" % gb)])

                    def S2(g):
                        gb, g3 = g % 2, g % 3
                        p.op("dve", lambda e, gb=gb, g3=g3: e.tensor_tensor(out=MT[gb][:], in0=Eg[gb][:], in1=GTm[g3][:].to_broadcast([128, 4, 128]), op=ALU.mult),
                             reads=[R("Eg%d" % gb), R("GTm%d" % g3)], writes=[R("MT%d" % gb)])

                    def S3(g, tsl=tsl, q=q):
                        gb = g % 2
                        by, rby = self.pbank[4 + gb], self.r_pb[4 + gb]

                        def mmy(e, g=g, gb=gb, by=by, tsl=tsl):
                            for r in range(4):
                                h = 4 * g + r
                                e.matmul(by[:, r * 64:(r + 1) * 64], lhsT=MT[gb][:, r, :], rhs=xdt[:, h * 64:(h + 1) * 64], start=True, stop=True)
                            return e.matmul(by[:, 256:512], lhsT=CT[:, g, tsl], rhs=state_bf[:, g * 256:(g + 1) * 256], start=True, stop=True)
                        p.op("pe", mmy, reads=[R("MT%d" % gb), R("xdt"), R("CT%d" % g), R("state_bf")], writes=[rby])
                        p.op("pe", lambda e, g=g, q=q: e.matmul(b0[:, 256:512], lhsT=Btok[:, q, g * 128:(g + 1) * 128], rhs=xdec[:, g * 256:(g + 1) * 256], start=True, stop=True),
                             reads=[R("Btok"), R("xdec")], writes=[rb0])
                        for r in range(4):
                            h = 4 * g + r
                            p.op("dve", lambda e, h=h, r=r: e.scalar_tensor_tensor(out=state[:, h * 64:(h + 1) * 64], in0=state[:, h * 64:(h + 1) * 64],
                                                                                     scalar=eall[:, 64 + h:65 + h], in1=b0[:, 256 + r * 64:256 + (r + 1) * 64],
                                                                                     op0=ALU.mult, op1=ALU.add), reads=[R("state"), R("eall"), rb0], writes=[R("state")])
                        p.op("act", lambda e, gb=gb, by=by: e.copy(out=ybufD[gb][:], in_=by[:, 0:256]), reads=[rby], writes=[R("ybufD%d" % gb)])
                        yg = ybuf[:, g * 256:(g + 1) * 256].rearrange("p (r d) -> p r d", r=4)
                        p.op("dve", lambda e, yg=yg, by=by, g=g: e.tensor_tensor(out=yg, in0=by[:, 256:512].rearrange("p (r d) -> p r d", r=4),
                                                                               in1=ea3[:, 4 * g:4 * g + 4, :].to_broadcast([128, 4, 64]), op=ALU.mult),
                             reads=[rby, R("ea3")], writes=[R("ybuf")])
                        p.op("pool", lambda e, g=g, gb=gb: e.tensor_tensor(out=ybuf[:, g * 256:(g + 1) * 256], in0=ybuf[:, g * 256:(g + 1) * 256], in1=ybufD[gb][:], op=ALU.add),
                             reads=[R("ybuf"), R("ybufD%d" % gb)], writes=[R("ybuf")])

                    for k in range(11):
                        if k < 8:
                            S0(k)
                        if 0 <= k - 1 < 8:
                            S1(k - 1)
                        if 0 <= k - 2 < 8:
                            S2(k - 2)
                        if 0 <= k - 3 < 8:
                            S3(k - 3)
                        if k == 1:
                            while tails:
                                tails.pop(0)()
                    p.op("act", lambda e: e.copy(out=state_bf[:], in_=state[:]), reads=[R("state")], writes=[R("state_bf")])
                    p.op("pool", lambda e: e.tensor_tensor(out=ybuf[:], in0=ybuf[:], in1=t3[:], op=ALU.add), reads=[R("ybuf"), R("t3")], writes=[R("ybuf")])
                    p.op("pool", lambda e, q=q: e.tensor_tensor(out=ybuf[:], in0=ybuf[:], in1=sz[:, q, :], op=ALU.mult), reads=[R("ybuf"), R("sz%d" % q)], writes=[R("ybuf")])
                    for g in range(8):
                        p.op("act", lambda e, g=g: e.activation(out=gnb[:, g * 256:(g + 1) * 256], in_=ybuf[:, g * 256:(g + 1) * 256], func=AF.Square, accum_out=ssg[:, g:g + 1]),
                             reads=[R("ybuf")], writes=[R("gnb"), R("ssg")])
                    p.op("act", lambda e: e.activation(out=rsg[:, :, 0], in_=ssg[:], func=AF.Sqrt, scale=1.0 / 256, bias=self.epsb[:]), reads=[R("ssg"), self.R("epsb")], writes=[R("rsg")])
                    p.op("dve", lambda e: e.reciprocal(out=rsg[:, :, 0], in_=rsg[:, :, 0]), reads=[R("rsg")], writes=[R("rsg")])
                    p.op("dve", lambda e: e.tensor_tensor(out=ybuf[:].rearrange("p (g d) -> p g d", g=8), in0=ybuf[:].rearrange("p (g d) -> p g d", g=8),
                                                          in1=rsg[:].to_broadcast([128, 8, 256]), op=ALU.mult), reads=[R("ybuf"), R("rsg")], writes=[R("ybuf")])
                    p.op("pool", lambda e: e.tensor_tensor(out=gnb[:], in0=ybuf[:], in1=nw[:], op=ALU.mult), reads=[R("ybuf"), R("nw"), R("gnb")], writes=[R("gnb")])
                    def tail(tt=tt):
                        for qq in range(4):
                            hp = qq % 2
                            rp = self.r_ptr[hp]

                            def trg(e, qq=qq, hp=hp):
                                for m in range(4):
                                    k = qq * 4 + m
                                    ins = e.transpose(out=self.ptrh[hp][:, m * 128:(m + 1) * 128], in_=gnb[:, k * 128:(k + 1) * 128], identity=self.ident_b[:])
                                return ins
                            p.op("pe", trg, reads=[R("gnb"), self.R("ident_b")], writes=[rp])
                            dst = gnT[:, qq * 4:(qq + 1) * 4, :]
                            srcp = self.ptrh[hp][:, :].rearrange("p (q m) -> p q m", q=4)
                            if hp == 0:
                                p.op("act", lambda e, dst=dst, srcp=srcp: e.copy(out=dst, in_=srcp), reads=[rp], writes=[R("gnT%d" % qq)])
                            else:
                                p.op("dve", lambda e, dst=dst, srcp=srcp: e.tensor_copy(out=dst, in_=srcp), reads=[rp], writes=[R("gnT%d" % qq)])
                        p.op("sp", lambda e, tt=tt: e.dma_start(out=xres_t[:], in_=xsrc[tt * 128:(tt + 1) * 128, :]),
                             reads=[self.R("xres")], writes=[R("xres_t")], dma=self.st_x)
                        for ch in range(2):
                            bo, rbo = self.pbank[6 + ch], self.r_pb[6 + ch]

                            def mmo(e, ch=ch, bo=bo):
                                for k in range(16):
                                    ins = e.matmul(bo[:, :], lhsT=gnT[:, k, :], rhs=wout[:, k, ch * 512:(ch + 1) * 512], start=(k == 0), stop=(k == 15))
                                return ins
                            p.op("pe", mmo, reads=[R("gnT%d" % qq) for qq in range(4)] + r_wout, writes=[rbo])
                            p.op("dve", lambda e, ch=ch, bo=bo: e.tensor_tensor(out=xo[:, ch * 512:(ch + 1) * 512], in0=bo[:, :], in1=xres_t[:, ch * 512:(ch + 1) * 512], op=ALU.add),
                                 reads=[rbo, R("xres_t")], writes=[R("xo")])
                        p.op("sp", lambda e, tt=tt: e.dma_start(out=self.xres[tt * 128:(tt + 1) * 128, :], in_=xo[:]),
                             reads=[R("xo")], writes=[self.R("xres_w%d" % (tt % 4))], dma=self.st_o)
                    tails.append(tail)
            while tails:
                tails.pop(0)()
            self.phase_barrier()
        self.x_src = self.xres

    def prep_attn(self, j):
        p = self.p
        for (c0, c1) in ((0, 2048), (2048, NAW)):
            for hh in range(2):
                s_ap = self.attn_wr[j, hh * 512:(hh + 1) * 512, c0:c1]
                d_ap = self.awin[j, hh * 512:(hh + 1) * 512, c0:c1]
                p.op("pool", lambda e, s_ap=s_ap, d_ap=d_ap: e.dma_start(out=d_ap, in_=s_ap),
                     writes=[self.R("awin%d_%d_%d" % (j, c0, hh))], dma=self.st_prep)

    def awin_res(self, j, c0, c1):
        out = []
        for base in (0, 2048):
            hi = 2048 if base == 0 else NAW
            if c0 < hi and c1 > base:
                out += [self.R("awin%d_%d_%d" % (j, base, hh)) for hh in range(2)]
        return out

    def attn(self, l):
        p = self.p
        nc = self.nc
        j = l // 2
        xsrc = self.x_src
        tag = "a%d" % l
        R = lambda n: self.R(tag + n)
        BIG = 30000.0
        dbg = self.attn_dbg or ""
        use_cmp = ("nocmp" not in dbg)
        use_slc = ("noslc" not in dbg)
        use_win = ("nowin" not in dbg)
        with contextlib.ExitStack() as st_long:
            def sbl(name, shape, dt):
                return st_long.enter_context(nc.sbuf_tensor(tag + name, list(shape), dt))
            kT = sbl("kT", [64, 6, S], BF16)
            kx = sbl("kx", [128, 2, S], BF16)
            Vall = sbl("V", [128, 32, 6, 65], BF16)
            gates = sbl("gates", [128, 32, 24], F32)
            kcmpT = sbl("kcmpT", [64, 2, 256], BF16)
            Vcmp = sbl("Vcmp", [128, 2, 2, 129], BF16)
            esink = sbl("esink", [128, 8], F32)
            p.op("pool", lambda e: e.memset(Vall[:, :, :, 64:65], 1.0), writes=[R("Vones")])
            p.op("pool", lambda e: e.memset(kx[64:128, :, :], 1.0), writes=[R("kxE")])
            for g_ in range(2):
                p.op("pool", lambda e, g_=g_: e.affine_select(out=kx[64:128, g_, :].rearrange("p (b m) -> p b m", b=64), in_=kx[64:128, g_, :].rearrange("p (b m) -> p b m", b=64),
                                                             pattern=[[-1, 64], [0, 64]], compare_op=ALU.is_equal, fill=0.0, base=0, channel_multiplier=1),
                     reads=[R("kxE")], writes=[R("kxE")])
            p.op("sp", lambda e: e.dma_start(out=esink[:], in_=self.sinks_in[j:j + 1, :].broadcast_to([128, 8])), writes=[R("esink")], dma=self.st_const)
            p.op("act", lambda e: e.activation(out=esink[:], in_=esink[:], func=AF.Exp), reads=[R("esink")], writes=[R("esink")])
            self.load_gain(l * 3 + 1)

            with contextlib.ExitStack() as st_x:
                kcT = st_x.enter_context(nc.sbuf_tensor(tag + "kcT", [64, 2, S], BF16))
                vcT = st_x.enter_context(nc.sbuf_tensor(tag + "vcT", [128, S], BF16))
                with contextlib.ExitStack() as st1:
                    def sb(name, shape, dt):
                        return st1.enter_context(nc.sbuf_tensor(tag + name, list(shape), dt))
                    TBK = 512
                    NT = 4
                    hT = sb("hT", [128, 8, TBK], BF16)
                    wb = [sb("wb%d" % b, [128, 8, 512], BF16) for b in range(2)]
                    cosb = sb("cosb", [128, TBK], F32)
                    sinb = sb("sinb", [128, TBK], F32)
                    t1 = [sb("t1_%d" % b, [128, TBK], F32) for b in range(2)]
                    t2 = [sb("t2_%d" % b, [128, TBK], F32) for b in range(2)]
                    qst = [sb("qst%d" % b, [128, TBK], BF16) for b in range(2)]
                    xt_bufs = [(sb("xt%d" % b, [128, D], F32), sb("hb%d" % b, [128, D], BF16),
                                sb("sq%d" % b, [128, D], BF16), sb("ss%d" % b, [128, 1], F32),
                                sb("rs%d" % b, [128, 1], F32)) for b in range(2)]
                    r_hT = [[R("hT%d_%d" % (jj, hh)) for hh in range(2)] for jj in range(NT)]
                    hres_all = [r_hT[jj][hh] for jj in range(NT) for hh in range(2)]
                    wc = 0
                    hc = 0
                    for blk in range(S // TBK):
                        t0 = blk * NT
                        csl = slice(blk * TBK, (blk + 1) * TBK)
                        self.norm_transpose(xsrc, t0, NT, hT, r_hT, xt_bufs, tag)
                        for hh_ in range(2):
                            p.op("sp", lambda e, csl=csl, hh_=hh_: e.dma_start(out=cosb[hh_ * 64:(hh_ + 1) * 64, :], in_=self.rope_in[0, :, csl]), writes=[R("cosb%d" % hh_)], dma=self.st_x)
                            p.op("sp", lambda e, csl=csl, hh_=hh_: e.dma_start(out=sinb[hh_ * 64:(hh_ + 1) * 64, :], in_=self.rope_in[1, :, csl]), writes=[R("sinb%d" % hh_)], dma=self.st_x)
                        for wgI in range(6):
                            b = wc % 2
                            wc += 1
                            c0 = wgI * 512
                            src = self.awin[j, :, c0:c0 + 512].rearrange("(k p) m -> p k m", p=128)
                            p.op("sp", lambda e, src=src, b=b: e.dma_start(out=wb[b][:], in_=src),
                                 reads=self.awin_res(j, c0, c0 + 512), writes=[R("wb%d" % b)], dma=self.st_w)
                            for m in range(2):
                                pi = wgI * 2 + m
                                hb_ = hc % 2
                                hc += 1
                                bA, rA = self.pbank[hb_ * 2], self.r_pb[hb_ * 2]
                                bB, rB = self.pbank[hb_ * 2 + 1], self.r_pb[hb_ * 2 + 1]

                                def mm(e, bank, off, b=b, m=m):
                                    for kc in range(8):
                                        ins = e.matmul(bank[:, :], lhsT=wb[b][:, kc, m * 256 + off:m * 256 + off + 128], rhs=hT[:, kc, :],
                                                       start=(kc == 0), stop=(kc == 7))
                                    return ins
                                p.op("pe", lambda e, mm=mm, bA=bA: mm(e, bA, 0), reads=[R("wb%d" % b)] + hres_all, writes=[rA])
                                p.op("pe", lambda e, mm=mm, bB=bB: mm(e, bB, 128), reads=[R("wb%d" % b)] + hres_all, writes=[rB])
                                p.op("dve", lambda e, hb_=hb_, bA=bA: e.tensor_tensor(out=t1[hb_][:], in0=bA[:, :], in1=cosb[:], op=ALU.mult),
                                     reads=[rA, R("cosb0"), R("cosb1")], writes=[R("t1_%d" % hb_)])
                                p.op("dve", lambda e, hb_=hb_, bB=bB: e.tensor_tensor(out=t2[hb_][:], in0=bB[:, :], in1=sinb[:], op=ALU.mult),
                                     reads=[rB, R("sinb0"), R("sinb1")], writes=[R("t2_%d" % hb_)])
                                if pi < 8:
                                    if pi < 2:
                                        dst, rd = kT[:, pi, csl], R("kT%d" % pi)
                                    elif pi < 4:
                                        dst, rd = kcT[:, pi - 2, csl], R("kcT%d" % (pi - 2))
                                    elif pi < 6:
                                        dst, rd = kx[0:64, pi - 4, csl], R("kxK%d" % (pi - 4))
                                    else:
                                        dst, rd = kT[:, 4 + pi - 6, csl], R("kT%d" % (4 + pi - 6))
                                    p.op("pool", lambda e, hb_=hb_, dst=dst: e.tensor_tensor(out=dst, in0=t1[hb_][0:64, :], in1=t2[hb_][0:64, :], op=ALU.add),
                                         reads=[R("t1_%d" % hb_), R("t2_%d" % hb_)], writes=[rd])
                                    p.op("pool", lambda e, hb_=hb_: e.tensor_tensor(out=qst[hb_][64:128, :], in0=t1[hb_][64:128, :], in1=t2[hb_][64:128, :], op=ALU.add),
                                         reads=[R("t1_%d" % hb_), R("t2_%d" % hb_)], writes=[R("qst%d" % hb_)])
                                    p.op("sp", lambda e, hb_=hb_, pi=pi, csl=csl: e.dma_start(out=self.qT[pi, :, csl], in_=qst[hb_][64:128, :]),
                                         reads=[R("qst%d" % hb_)], writes=[self.R("qT_%d_%d" % (pi, blk))], dma=self.st_o)
                                else:
                                    qh = 8 + 2 * (pi - 8)
                                    p.op("pool", lambda e, hb_=hb_: e.tensor_tensor(out=qst[hb_][:], in0=t1[hb_][:], in1=t2[hb_][:], op=ALU.add),
                                         reads=[R("t1_%d" % hb_), R("t2_%d" % hb_)], writes=[R("qst%d" % hb_)])
                                    p.op("sp", lambda e, hb_=hb_, qh=qh, csl=csl: e.dma_start(out=self.qT[qh:qh + 2, :, csl].rearrange("h d t -> (h d) t"), in_=qst[hb_][:]),
                                         reads=[R("qst%d" % hb_)], writes=[self.R("qT_%d_%d" % (qh, blk)), self.R("qT_%d_%d" % (qh + 1, blk))], dma=self.st_o)
                        b = wc % 2
                        wc += 1
                        src = self.awin[j, :, 3072:3608].rearrange("(k p) m -> p k m", p=128)
                        src_vc = self.awin[j, :, 3072:3200].rearrange("(k p) m -> p k m", p=128)
                        src_tm = self.awin[j, :, 3200:3608].rearrange("(k p) m -> p k m", p=128)
                        b2 = wc % 2
                        wc += 1
                        p.op("sp", lambda e, b=b, src_vc=src_vc: e.dma_start(out=wb[b][:, :, 0:128], in_=src_vc),
                             reads=self.awin_res(j, 3072, 3200), writes=[R("wb%d" % b)], dma=self.st_w)
                        p.op("sp", lambda e, b2=b2, src_tm=src_tm: e.dma_start(out=wb[b2][:, :, 0:408], in_=src_tm),
                             reads=self.awin_res(j, 3200, 3608), writes=[R("wb%d" % b2)], dma=self.st_w)
                        bA, rA = self.pbank[4], self.r_pb[4]

                        def mmvc(e, b=b, bA=bA):
                            for kc in range(8):
                                ins = e.matmul(bA[:, :], lhsT=wb[b][:, kc, 0:128], rhs=hT[:, kc, :], start=(kc == 0), stop=(kc == 7))
                            return ins
                        p.op("pe", mmvc, reads=[R("wb%d" % b)] + hres_all, writes=[rA])
                        p.op("act", lambda e, bA=bA, csl=csl: e.copy(out=vcT[:, csl], in_=bA[:, :]), reads=[rA], writes=[R("vcT")])
                        for q in range(NT):
                            tt = t0 + q
                            bT, rT = self.pbank[5], self.r_pb[5]

                            def mmtm(e, q=q, b2=b2, bT=bT):
                                for kc in range(8):
                                    ins = e.matmul(bT[:, 0:408], lhsT=hT[:, kc, q * 128:(q + 1) * 128], rhs=wb[b2][:, kc, 0:408], start=(kc == 0), stop=(kc == 7))
                                return ins
                            p.op("pe", mmtm, reads=[R("wb%d" % b2)] + r_hT[q], writes=[rT])
                            p.op("dve", lambda e, tt=tt, bT=bT: e.tensor_copy(out=Vall[:, tt, :, 0:64], in_=bT[:, 0:384].rearrange("p (a d) -> p a d", a=6)),
                                 reads=[rT], writes=[R("Vall")])
                            p.op("act", lambda e, tt=tt, bT=bT: e.activation(out=gates[:, tt, :], in_=bT[:, 384:408], func=AF.Sigmoid),
                                 reads=[rT], writes=[R("gates")])
                    p.barrier()
                with contextlib.ExitStack() as st2:
                    def sb(name, shape, dt):
                        return st2.enter_context(nc.sbuf_tensor(tag + name, list(shape), dt))
                    w1s = sb("w1s", [64, 32, 128], BF16)
                    w2s = sb("w2s", [128, 64], BF16)
                    posf = sb("posf", [64, 32], F32)
                    posb_ = sb("posb", [64, 32, 2], BF16)
                    pbias = sb("pbias", [128, 1], F32)
                    u = sb("u", [128, 256], F32)
                    u2 = sb("u2", [128, 256], F32)
                    sg_ = sb("sgm", [128, 256], F32)
                    gl = sb("gl", [128, 256], BF16)
                    wself = sb("wself", [128, 2, 64], F32)
                    p.op("sp", lambda e: e.dma_start(out=wself[:], in_=self.wsel_in), writes=[R("wself")], dma=self.st_const)
                    p.op("dve", lambda e: e.memset(u[:], 0.0), writes=[R("u")])
                    p.op("pool", lambda e: e.memset(Vcmp[:, :, :, 64:65], 1.0), writes=[R("Vcmp1")])
                    for g in range(2):
                        p.op("dve", lambda e, g=g: e.tensor_copy(out=Vcmp[:, g, :, 65:129], in_=wself[:]), reads=[R("wself")], writes=[R("VcmpW%d" % g)])
                    for kv in range(2):
                        w1_in = self.cmp_w1[j, kv].rearrange("(pp d) h -> d pp h", d=64)
                        p.op("pool", lambda e, w1_in=w1_in: e.dma_start(out=w1s[:], in_=w1_in), writes=[R("w1s")], dma=self.st_prep)
                        p.op("pool", lambda e, kv=kv: e.dma_start(out=w2s[:], in_=self.cmp_w2[j, kv]), writes=[R("w2s")], dma=self.st_prep)
                        p.op("sp", lambda e, kv=kv: e.dma_start(out=posf[:], in_=self.cmp_posT[j, kv]), writes=[R("posf")], dma=self.st_const)
                        p.op("dve", lambda e: e.tensor_copy(out=posb_[:], in_=posf[:].rearrange("p (a o) -> p a o", o=1).to_broadcast([64, 32, 2])), reads=[R("posf")], writes=[R("posb")])
                        b0, rb0 = self.pbank[0], self.r_pb[0]

                        def mmb(e):
                            for pp in range(32):
                                ins = e.matmul(b0[:, 0:2], lhsT=w1s[:, pp, :], rhs=posb_[:, pp, :], start=(pp == 0), stop=(pp == 31))
                            return ins
                        p.op("pe", mmb, reads=[R("w1s"), R("posb")], writes=[rb0])
                        p.op("dve", lambda e: e.tensor_copy(out=pbias[:], in_=b0[:, 0:1]), reads=[rb0], writes=[R("pbias")])
                        for g in range(2):
                            b1, rb1 = self.pbank[1 + g], self.r_pb[1 + g]
                            if kv == 0:
                                srcT = kcT[:, g, :]
                                rsrc = R("kcT%d" % g)
                            else:
                                srcT = vcT[g * 64:(g + 1) * 64, :]
                                rsrc = R("vcT")

                            def mmh(e, srcT=srcT, b1=b1, g=g):
                                for pp in range(32):
                                    ins = e.matmul(b1[:, 0:255], lhsT=w1s[g * 64 * kv:g * 64 * kv + 64, pp, :] if False else w1s[:, pp, :],
                                                   rhs=srcT[:, pp:pp + 16 * 254 + 1:16], start=(pp == 0), stop=(pp == 31))
                                return ins
                            if kv == 1 and g == 1:
                                vtmp = sb("vtmp", [64, S], BF16)
                                p.op("sp", lambda e, vtmp=vtmp: e.dma_start(out=vtmp[:], in_=vcT[64:128, :]), reads=[R("vcT")], writes=[R("vtmp")], dma=self.st_x)
                                srcT2 = vtmp[:, :]

                                def mmh(e, srcT2=srcT2, b1=b1):
                                    for pp in range(32):
                                        ins = e.matmul(b1[:, 0:255], lhsT=w1s[:, pp, :], rhs=srcT2[:, pp:pp + 16 * 254 + 1:16], start=(pp == 0), stop=(pp == 31))
                                    return ins
                                rsrc = R("vtmp")
                            p.op("pe", mmh, reads=[R("w1s"), rsrc], writes=[rb1])
                            p.op("act", lambda e, b1=b1: e.activation(out=u[:, 0:255], in_=b1[:, 0:255], func=AF.Identity, bias=pbias[:]),
                                 reads=[rb1, R("pbias"), R("u")], writes=[R("u")])
                            p.op("dve", lambda e: e.tensor_tensor(out=u2[:], in0=u[:], in1=u[:], op=ALU.mult), reads=[R("u")], writes=[R("u2")])
                            p.op("dve", lambda e: e.tensor_scalar(out=u2[:], in0=u2[:], scalar1=0.044715, scalar2=1.0, op0=ALU.mult, op1=ALU.add), reads=[R("u2")], writes=[R("u2")])
                            p.op("dve", lambda e: e.tensor_tensor(out=u2[:], in0=u2[:], in1=u[:], op=ALU.mult), reads=[R("u2"), R("u")], writes=[R("u2")])
                            p.op("act", lambda e: e.activation(out=sg_[:], in_=u2[:], func=AF.Sigmoid, scale=1.5957691216057308), reads=[R("u2")], writes=[R("sgm")])
                            p.op("dve", lambda e: e.tensor_tensor(out=gl[:], in0=u[:], in1=sg_[:], op=ALU.mult), reads=[R("u"), R("sgm")], writes=[R("gl")])
                            b3, rb3 = self.pbank[3], self.r_pb[3]
                            if kv == 0:
                                p.op("pe", lambda e, b3=b3: e.matmul(b3[0:64, 0:256], lhsT=w2s[:], rhs=gl[:], start=True, stop=True), reads=[R("w2s"), R("gl")], writes=[rb3])
                                p.op("dve", lambda e, g=g, b3=b3: e.tensor_copy(out=kcmpT[:, g, :], in_=b3[0:64, 0:256]), reads=[rb3], writes=[R("kcmpT%d" % g)])
                            else:
                                def mmv(e, b3=b3):
                                    for ct in range(2):
                                        ins = e.matmul(b3[:, ct * 64:(ct + 1) * 64], lhsT=gl[:, ct * 128:(ct + 1) * 128], rhs=w2s[:], start=True, stop=True)
                                    return ins
                                p.op("pe", mmv, reads=[R("w2s"), R("gl")], writes=[rb3])
                                p.op("dve", lambda e, g=g, b3=b3: e.tensor_copy(out=Vcmp[:, g, :, 0:64], in_=b3[:, 0:128].rearrange("p (c d) -> p c d", c=2)),
                                     reads=[rb3], writes=[R("VcmpV%d" % g)])
                    p.barrier()
            with contextlib.ExitStack() as st3:
                def sb(name, shape, dt):
                    return st3.enter_context(nc.sbuf_tensor(tag + name, list(shape), dt))
                wout = sb("wout", [128, 8, D], BF16)
                for q in range(2):
                    src = self.attn_w_out[j, q * 512:(q + 1) * 512, :].rearrange("(k p) m -> p k m", p=128)
                    p.op("pool", lambda e, src=src, q=q: e.dma_start(out=wout[:, q * 4:(q + 1) * 4, :], in_=src), writes=[R("wout%d" % q)], dma=self.st_prep)
                r_wout = [R("wout0"), R("wout1")]
                QX = [sb("QX%d" % b_, [128, 4, 128], BF16) for b_ in range(2)]
                nbw = sb("nbw", [128, 128], F32)
                p.op("pool", lambda e: e.memset(nbw[:], 0.0), writes=[R("nbw")])
                selb = sb("selb", [128, 32, 64], F32)
                p.op("sp", lambda e: e.dma_start(out=selb[:], in_=self.selb_in), writes=[R("selb")], dma=self.st_const)
                qt = [sb("qt%d" % b, [64, 16, 256], BF16) for b in range(2)]
                Eb = [sb("E%d" % b, [128, 4, 128], BF16) for b in range(4)]
                SBANKS = (0, 1, 6)
                maskC = sb("maskC", [128, 4, 128], BF16)
                maskP = sb("maskP", [128, 4, 128], BF16)
                p.op("pool", lambda e: e.memset(maskC[:], 1.0), writes=[R("maskC")])
                p.op("pool", lambda e: e.memset(maskP[:], 1.0), writes=[R("maskP")])
                p.op("pool", lambda e: e.affine_select(out=maskC[:], in_=maskC[:], pattern=[[0, 4], [1, 128]], compare_op=ALU.is_ge, fill=0.0, base=0, channel_multiplier=-1),
                     reads=[R("maskC")], writes=[R("maskC")])
                p.op("pool", lambda e: e.affine_select(out=maskP[:], in_=maskP[:], pattern=[[0, 4], [-1, 128]], compare_op=ALU.is_ge, fill=0.0, base=-1, channel_multiplier=1),
                     reads=[R("maskP")], writes=[R("maskP")])
                ot = sb("ot", [128, D], BF16)
                accb = sb("accb", [128, 4, 64], F32)
                imp = sb("imp", [128, 64], F32)
                sc2 = sb("sc2", [128, 64], F32)
                m8 = sb("m8", [128, 8], F32)
                den = sb("den", [128, 4], F32)
                oT = sb("oT", [128, 8, 128], BF16)
                xres_t = sb("xres_t", [128, D], F32)
                xo = sb("xo", [128, D], F32)
                ec = [0]
                sc_ = [0]

                def score_exp(i, lhsT, lres, rhs, rres, mask, extra=None):
                    sb_i = SBANKS[sc_[0] % 3]
                    sc_[0] += 1
                    bank, rb = self.pbank[sb_i], self.r_pb[sb_i]
                    eb = ec[0] % 4
                    ec[0] += 1

                    def mm(e, bank=bank):
                        ins = e.matmul(bank[:, :].rearrange("p (r q) -> p r q", r=4), lhsT=lhsT, rhs=rhs, start=True, stop=(extra is None))
                        if extra is not None:
                            ins = e.matmul(bank[:, :].rearrange("p (r q) -> p r q", r=4), lhsT=extra[0], rhs=extra[1], start=False, stop=True)
                        return ins
                    rr = list(lres) + list(rres) + (list(extra[2]) if extra is not None else [])
                    p.op("pe", mm, reads=rr, writes=[rb])
                    E = Eb[eb]
                    rE = R("E%d" % eb)
                    p.op("act", lambda e, E=E, bank=bank: e.activation(out=E[:].rearrange("p r q -> p (r q)"), in_=bank[:, :], func=AF.Exp, scale=0.125),
                         reads=[rb], writes=[rE])
                    if mask is CAUSAL or mask is PREV:
                        mt_, rm_ = (maskC, R("maskC")) if mask is CAUSAL else (maskP, R("maskP"))
                        p.op("dve", lambda e, E=E, mt_=mt_: e.tensor_tensor(out=E[:], in0=E[:], in1=mt_[:], op=ALU.mult), reads=[rE, rm_], writes=[rE])
                    elif mask is not None:
                        base, cm, stepq = mask
                        p.op("pool", lambda e, E=E, base=base, cm=cm, stepq=stepq: e.affine_select(
                            out=E[:], in_=E[:], pattern=[[0, 4], [stepq, 128]], compare_op=ALU.is_ge, fill=0.0, base=base, channel_multiplier=cm),
                            reads=[rE], writes=[rE])
                    return E, rE

                def pv(E, rE, vrhs, vres, ncols, first, last):
                    def mm(e):
                        for r in range(4):
                            ins = e.matmul(self.pbank[2 + r][:, 0:ncols], lhsT=E[:, r, :], rhs=vrhs, start=first, stop=last)
                        return ins
                    p.op("pe", mm, reads=[rE] + list(vres), writes=[self.r_pb[2 + r] for r in range(4)])

                CAUSAL = (0, -1, 1)
                PREV = (-1, 1, -1)
                den2 = [den, sb("denB", [128, 4], F32)]
                bc = [0]

                def pv2(E, rE, vrhs, vres, ncols, first, last, par):
                    off = par * 256

                    def mm(e):
                        for r in range(4):
                            ins = e.matmul(self.pbank[2 + r][:, off:off + ncols], lhsT=E[:, r, :], rhs=vrhs, start=first, stop=last)
                        return ins
                    p.op("pe", mm, reads=[rE] + list(vres), writes=[R("O%d_%d" % (r, par)) for r in range(4)] + [self.r_pb[2 + r] for r in range(4)])

                def load_q(i0):
                    qb2 = (i0 // 2) % 2
                    blk = i0 // 4
                    src = self.qT[:, :, i0 * 128:i0 * 128 + 256].rearrange("h d t -> d h t")
                    p.op("sp", lambda e, src=src, qb2=qb2: e.dma_start(out=qt[qb2][:], in_=src),
                         reads=[self.R("qT_%d_%d" % (h, blk)) for h in range(16)], writes=[R("qt%d" % qb2)], dma=self.st_x)
                otB = [ot, sb("ot1", [128, D], BF16)]
                accbG = [accb, sb("accb1", [128, 4, 64], F32)]
                impG = [imp, sb("imp1", [128, 64], F32)]
                atails = []
                load_q(0)
                for i in range(32):
                    qb_ = (i // 2) % 2
                    if i % 2 == 0 and i + 2 < 32:
                        load_q(i + 2)
                    qsl = slice((i % 2) * 128, (i % 2 + 1) * 128)
                    rq = [R("qt%d" % qb_)]
                    ctxs = []

                    def make_g(g, i=i, qb_=qb_, qsl=qsl, rq=rq):
                        ot = otB[i % 2]
                        rot = R("ot%d" % (i % 2))
                        accb = accbG[g]
                        racc = R("accb%d" % g)
                        imp = impG[g]
                        rimp = R("imp%d" % g)
                        qa4 = qt[qb_][:, g * 4:(g + 1) * 4, qsl]
                        qb4 = qt[qb_][:, 8 + g * 4:8 + (g + 1) * 4, qsl]
                        gsl = gates[:, i, g * 12:(g + 1) * 12].rearrange("p (r b) -> p r b", b=3)
                        qxp = (2 * i + g) % 2
                        if use_slc:
                            p.op("pool", lambda e, qxp=qxp, qb4=qb4: e.tensor_copy(out=QX[qxp][0:64, :, :], in_=qb4), reads=rq, writes=[R("QXq%d" % qxp)])
                        tiles = []
                        kts = [kt for kt in (i - 1, i) if kt >= 0]
                        for n, kt in enumerate(kts):
                            tiles.append(("swa", kT[:, g, kt * 128:(kt + 1) * 128], [R("kT%d" % g)], qa4, CAUSAL if kt == i else PREV, None,
                                          Vall[:, kt, g, :], [R("Vall"), R("Vones")], 65, n == 0, n == len(kts) - 1))
                        nct = 1 if i < 16 else 2
                        if use_cmp or use_slc:
                            for ct in range(nct):
                                tiles.append(("cmp", kcmpT[:, g, ct * 128:(ct + 1) * 128], [R("kcmpT%d" % g)], qb4, (128 * i - 2048 * ct - 31, -16, 1), None,
                                              Vcmp[:, g, ct, :], [R("VcmpV%d" % g), R("VcmpW%d" % g), R("Vcmp1")], 129, ct == 0, ct == nct - 1))
                        if use_win:
                            kts = [kt for kt in range(i - 4, i + 1) if kt >= 0]
                            for n, kt in enumerate(kts):
                                mk = CAUSAL if kt == i else (PREV if kt == i - 4 else None)
                                tiles.append(("win", kT[:, 4 + g, kt * 128:(kt + 1) * 128], [R("kT%d" % (4 + g))], qb4, mk, None,
                                              Vall[:, kt, 4 + g, :], [R("Vall"), R("Vones")], 65, n == 0, n == len(kts) - 1))
                        if use_slc:
                            for kt in range(i + 1):
                                tiles.append(("slc", kx[:, g, kt * 128:(kt + 1) * 128], [R("kxK%d" % g), R("kxE"), R("QXq%d" % qxp), R("QXn%d" % qxp)], QX[qxp][:], CAUSAL if kt == i else None, None,
                                              Vall[:, kt, 2 + g, :], [R("Vall"), R("Vones")], 65, kt == 0, kt == i))
                        branches = [br for br in ("swa", "cmp", "win", "slc") if any(t[0] == br for t in tiles)]
                        last_nsa = [br for br in branches if br != "swa" and br != "cmp"]
                        last_nsa = last_nsa[-1] if last_nsa else "cmp"

                        def finish(br, par):
                            dn = den2[par]
                            rdn = R("den%d" % par)
                            rO = [R("O%d_%d" % (r, par)) for r in range(4)]
                            off = par * 256
                            O = [self.pbank[2 + r] for r in range(4)]
                            if br == "swa":
                                for r in range(4):
                                    h = g * 4 + r
                                    p.op("dve", lambda e, r=r, h=h: e.tensor_tensor(out=dn[:, r:r + 1], in0=O[r][:, off + 64:off + 65], in1=esink[:, h:h + 1], op=ALU.add),
                                         reads=[rO[r], R("esink")], writes=[rdn])
                                p.op("dve", lambda e: e.reciprocal(out=dn[:], in_=dn[:]), reads=[rdn], writes=[rdn])
                                for r in range(4):
                                    h = g * 4 + r
                                    p.op("dve", lambda e, r=r, h=h: e.tensor_scalar(out=ot[:, h * 64:(h + 1) * 64], in0=O[r][:, off:off + 64], scalar1=dn[:, r:r + 1], scalar2=None, op0=ALU.mult),
                                         reads=[rO[r], rdn], writes=[rot])
                                return
                            if br == "cmp":
                                for r in range(4):
                                    p.op("dve", lambda e, r=r: e.tensor_scalar(out=dn[:, r:r + 1], in0=O[r][:, off + 64:off + 65], scalar1=1e-30, scalar2=None, op0=ALU.max),
                                         reads=[rO[r]], writes=[rdn])
                                p.op("dve", lambda e: e.reciprocal(out=dn[:], in_=dn[:]), reads=[rdn], writes=[rdn])
                                for r in range(4):
                                    if r == 0:
                                        p.op("dve", lambda e, r=r: e.tensor_scalar(out=imp[:], in0=O[r][:, off + 65:off + 129], scalar1=dn[:, r:r + 1], scalar2=None, op0=ALU.mult),
                                             reads=[rO[r], rdn], writes=[rimp])
                                    else:
                                        p.op("dve", lambda e, r=r: e.scalar_tensor_tensor(out=imp[:], in0=O[r][:, off + 65:off + 129], scalar=dn[:, r:r + 1], in1=imp[:], op0=ALU.mult, op1=ALU.add),
                                             reads=[rO[r], rdn, rimp], writes=[rimp])
                                if use_slc:
                                    p.op("dve", lambda e, i=i: e.tensor_tensor(out=imp[:], in0=imp[:], in1=selb[:, i, :], op=ALU.add), reads=[rimp, R("selb")], writes=[rimp])
                                    p.op("dve", lambda e: e.max(out=m8[:], in_=imp[:]), reads=[rimp], writes=[R("m8")])
                                    p.op("dve", lambda e: e.match_replace(out=sc2[:], in_to_replace=m8[:], in_values=imp[:], imm_value=-3.0e38), reads=[rimp, R("m8")], writes=[R("sc2")])
                                    p.op("dve", lambda e: e.max(out=m8[:], in_=sc2[:]), reads=[R("sc2"), R("m8")], writes=[R("m8")])
                                    p.op("dve", lambda e: e.tensor_scalar(out=nbw[:, 64:128], in0=imp[:], scalar1=m8[:, 7:8], scalar2=-BIG, op0=ALU.is_lt, op1=ALU.mult),
                                         reads=[rimp, R("m8"), R("nbw")], writes=[R("nbw")])
                                    sb_i = SBANKS[sc_[0] % 3]
                                    sc_[0] += 1
                                    bank, rb = self.pbank[sb_i], self.r_pb[sb_i]
                                    p.op("pe", lambda e, bank=bank: e.transpose(out=bank[:, 0:128], in_=nbw[:], identity=self.ident_f[:]), reads=[R("nbw"), self.R("ident")], writes=[rb])
                                    p.op("dve", lambda e, bank=bank, qxp=qxp: e.tensor_copy(out=QX[qxp][64:128, :, :], in_=bank[64:128, 0:128].rearrange("p (o q) -> p o q", o=1).to_broadcast([64, 4, 128])),
                                         reads=[rb], writes=[R("QXn%d" % qxp)])
                                p.op("dve", lambda e, gsl=gsl: e.tensor_tensor(out=dn[:], in0=dn[:], in1=gsl[:, :, 0], op=ALU.mult), reads=[rdn, R("gates")], writes=[rdn])
                                for r in range(4):
                                    h = 8 + g * 4 + r
                                    if not use_cmp:
                                        p.op("dve", lambda e, r=r: e.memset(accb[:, r, :], 0.0), reads=[racc], writes=[racc])
                                    elif last_nsa == "cmp":
                                        p.op("dve", lambda e, r=r, h=h: e.tensor_scalar(out=ot[:, h * 64:(h + 1) * 64], in0=O[r][:, off:off + 64], scalar1=dn[:, r:r + 1], scalar2=None, op0=ALU.mult),
                                             reads=[rO[r], rdn], writes=[rot])
                                    else:
                                        p.op("dve", lambda e, r=r: e.tensor_scalar(out=accb[:, r, :], in0=O[r][:, off:off + 64], scalar1=dn[:, r:r + 1], scalar2=None, op0=ALU.mult),
                                             reads=[rO[r], rdn], writes=[racc])
                                return
                            gi = 2 if br == "win" else 1
                            for r in range(4):
                                p.op("dve", lambda e, r=r: e.tensor_copy(out=dn[:, r:r + 1], in_=O[r][:, off + 64:off + 65]), reads=[rO[r]], writes=[rdn])
                            p.op("dve", lambda e: e.reciprocal(out=dn[:], in_=dn[:]), reads=[rdn], writes=[rdn])
                            p.op("dve", lambda e, gsl=gsl, gi=gi: e.tensor_tensor(out=dn[:], in0=dn[:], in1=gsl[:, :, gi], op=ALU.mult), reads=[rdn, R("gates")], writes=[rdn])
                            for r in range(4):
                                h = 8 + g * 4 + r
                                if br == last_nsa:
                                    p.op("dve", lambda e, r=r, h=h: e.scalar_tensor_tensor(out=ot[:, h * 64:(h + 1) * 64], in0=O[r][:, off:off + 64], scalar=dn[:, r:r + 1], in1=accb[:, r, :], op0=ALU.mult, op1=ALU.add),
                                         reads=[rO[r], rdn, racc], writes=[rot])
                                else:
                                    p.op("dve", lambda e, r=r: e.scalar_tensor_tensor(out=accb[:, r, :], in0=O[r][:, off:off + 64], scalar=dn[:, r:r + 1], in1=accb[:, r, :], op0=ALU.mult, op1=ALU.add),
                                         reads=[rO[r], rdn, racc], writes=[racc])

                        par_of = {}
                        for br in branches:
                            par_of[br] = 0
                        return tiles, finish, par_of

                    for g in range(2):
                        ctxs.append(make_g(g))
                    merged = []
                    for brs in (("swa", "cmp"), ("win",), ("slc",)):
                        for gi in range(2):
                            for br in brs:
                                merged += [(gi, tl) for tl in ctxs[gi][0] if tl[0] == br]
                    queue = []
                    LOOK = 2

                    def pop():
                        pE, prE, gi, ptl = queue.pop(0)
                        pv2(pE, prE, ptl[6], ptl[7], ptl[8], ptl[9], ptl[10], 0)
                        if ptl[10]:
                            ctxs[gi][1](ptl[0], 0)
                    ntl = 0
                    for gi, tl in merged:
                        br, lhsT, lres, rhs, mask, extra, vrhs, vres, ncols, first, last = tl
                        if br == "slc" and first:
                            while any(qq[3][0] == "cmp" and qq[2] == gi for qq in queue):
                                pop()
                        E, rE = score_exp(i, lhsT, lres, rhs, rq, mask, extra=extra)
                        queue.append((E, rE, gi, tl))
                        while len(queue) > LOOK:
                            pop()
                        ntl += 1
                        if ntl == 4:
                            while atails:
                                atails.pop(0)()
                    while queue:
                        pop()

                    def atail(i=i):
                        ot = otB[i % 2]
                        rot = R("ot%d" % (i % 2))
                        p.op("sp", lambda e, i=i: e.dma_start(out=xres_t[:], in_=xsrc[i * 128:(i + 1) * 128, :]),
                             reads=[self.R("xres")], writes=[R("xres_t")], dma=self.st_x)
                        for half in range(2):
                            rp = self.r_ptr[1]

                            def tr(e, half=half):
                                for q in range(4):
                                    kc = half * 4 + q
                                    ins = e.transpose(out=self.ptrh[1][:, q * 128:(q + 1) * 128], in_=ot[:, kc * 128:(kc + 1) * 128], identity=self.ident_b[:])
                                return ins
                            p.op("pe", tr, reads=[rot, self.R("ident_b")], writes=[rp])
                            dst = oT[:, half * 4:(half + 1) * 4, :]
                            srcp = self.ptrh[1][:, :].rearrange("p (q m) -> p q m", q=4)
                            p.op("act", lambda e, dst=dst, srcp=srcp: e.copy(out=dst, in_=srcp), reads=[rp], writes=[R("oT%d" % half)])
                        for ch in range(2):
                            bo, rbo = self.pbank[7], self.r_pb[7]

                            def mmo(e, ch=ch, bo=bo):
                                for k in range(8):
                                    ins = e.matmul(bo[:, :], lhsT=oT[:, k, :], rhs=wout[:, k, ch * 512:(ch + 1) * 512], start=(k == 0), stop=(k == 7))
                                return ins
                            p.op("pe", mmo, reads=[R("oT0"), R("oT1")] + r_wout, writes=[rbo])
                            p.op("dve", lambda e, ch=ch, bo=bo: e.tensor_tensor(out=xo[:, ch * 512:(ch + 1) * 512], in0=bo[:, :], in1=xres_t[:, ch * 512:(ch + 1) * 512], op=ALU.add),
                                 reads=[rbo, R("xres_t")], writes=[R("xo")])
                        if "ot" in dbg:
                            p.op("dve", lambda e: e.tensor_copy(out=xo[:], in_=ot[:]), reads=[rot, R("xo")], writes=[R("xo")])
                        p.op("sp", lambda e, i=i: e.dma_start(out=self.xres[i * 128:(i + 1) * 128, :], in_=xo[:]),
                             reads=[R("xo")], writes=[self.R("xres_w%d" % (i % 4))], dma=self.st_o)
                    atails.append(atail)
                while atails:
                    atails.pop(0)()
            self.phase_barrier()
        self.x_src = self.xres

    def final_norm(self):
        p = self.p
        nc = self.nc
        xsrc = self.x_src
        with contextlib.ExitStack() as st:
            def sb(name, shape, dt):
                return st.enter_context(nc.sbuf_tensor(name, list(shape), dt))
            self.load_gain(DEPTH * 3)
            bufs = [(sb("fn_x%d" % b, [128, D], F32), sb("fn_sq%d" % b, [128, D], BF16), sb("fn_ss%d" % b, [128, 1], F32),
                     sb("fn_rs%d" % b, [128, 1], F32), sb("fn_o%d" % b, [128, D], F32)) for b in range(2)]
            for tt in range(S // 128):
                b = tt % 2
                xt, sq, ss, rs, ot = bufs[b]
                rx, rss, rrs, ro = [self.R("fn_%s%d" % (n, b)) for n in ("x", "ss", "rs", "o")]
                p.op("sp", lambda e, xt=xt, tt=tt: e.dma_start(out=xt[:], in_=xsrc[tt * 128:(tt + 1) * 128, :]),
                     reads=[self.R("xres")], writes=[rx], dma=self.st_x)
                p.op("act", lambda e, xt=xt, sq=sq, ss=ss: e.activation(out=sq[:], in_=xt[:], func=AF.Square, accum_out=ss[:]),
                     reads=[rx], writes=[self.R("fn_sq%d" % b), rss])
                p.op("act", lambda e, ss=ss, rs=rs: e.activation(out=rs[:], in_=ss[:], func=AF.Sqrt, scale=1.0 / D, bias=self.epsb[:]),
                     reads=[rss, self.R("epsb")], writes=[rrs])
                p.op("dve", lambda e, rs=rs: e.reciprocal(out=rs[:], in_=rs[:]), reads=[rrs], writes=[rrs])
                p.op("dve", lambda e, xt=xt, ot=ot, rs=rs: e.scalar_tensor_tensor(out=ot[:], in0=xt[:], scalar=rs[:], in1=self.gbc[:], op0=ALU.mult, op1=ALU.mult),
                     reads=[rx, rrs, self.R("gbc")], writes=[ro])
                p.op("sp", lambda e, ot=ot, tt=tt: e.dma_start(out=self.out[tt * 128:(tt + 1) * 128, :], in_=ot[:]),
                     reads=[ro], writes=[self.R("out_w%d" % (tt % 4))], dma=self.st_o)
            self.final_wait()

    def copy_out(self):
        p = self.p
        nc = self.nc
        xsrc = self.x_src
        with contextlib.ExitStack() as st:
            bufs = [st.enter_context(nc.sbuf_tensor("co%d" % b, [128, D], F32)) for b in range(2)]
            for tt in range(S // 128):
                b = tt % 2
                rx = self.R("co%d" % b)
                p.op("sp", lambda e, b=b, tt=tt: e.dma_start(out=bufs[b][:], in_=xsrc[tt * 128:(tt + 1) * 128, :]),
                     reads=[self.R("xres")], writes=[rx], dma=self.st_x)
                p.op("sp", lambda e, b=b, tt=tt: e.dma_start(out=self.out[tt * 128:(tt + 1) * 128, :], in_=bufs[b][:]),
                     reads=[rx], writes=[self.R("out_w%d" % (tt % 4))], dma=self.st_o)
            self.final_wait()

    def final_wait(self):
        p = self.p
        sems = self._store_waits()

        def fn(e, sems=sems):
            for sem, val in sems:
                e.wait_ge(sem, val)
            return e.nop()
        p.op("sp", fn, reads=[self.R("out_w%d" % k) for k in range(4)], writes=[self.R("done")])


def full_plan():
    def prep(l):
        out = [("prep_ffn", l, 0)]
        out.append(("prep_attn", l // 2) if l % 2 == 0 else ("prep_ssm", l // 2))
        out.append(("prep_ffn", l, 1))
        return out
    plan = prep(0)
    for l in range(DEPTH):
        if l + 1 < DEPTH:
            plan += prep(l + 1)
        plan.append(("ffn", l, 0))
        plan.append(("attn", l) if l % 2 == 0 else ("ssd", l))
        plan.append(("ffn", l, 1))
    plan.append(("final",))
    return plan


_CACHE = {}


def attn_w_layout(w):
    kh = [512, 576, 1280, 1344, 1536, 1600, 1792, 1856]
    qh = [h * 64 for h in range(8)] + [768 + h * 64 for h in range(8)]
    pairs = [(kh[i], qh[i]) for i in range(8)] + [(qh[8 + 2 * m], qh[9 + 2 * m]) for m in range(4)]
    cols = []
    for (ca, cb_) in pairs:
        cols += list(range(ca, ca + 64)) + list(range(cb_, cb_ + 64))
        cols += list(range(ca + 32, ca + 64)) + list(range(ca, ca + 32)) + list(range(cb_ + 32, cb_ + 64)) + list(range(cb_, cb_ + 32))
    cols += list(range(1408, 1536))
    cols += list(range(640, 768)) + list(range(1664, 1792)) + list(range(1920, 2048)) + list(range(2048, 2072))
    assert len(cols) == NAW
    return np.ascontiguousarray(w[:, :, np.asarray(cols)])


def _rope_tables():
    inv = (1.0 / (np.float32(10000.0) ** (np.arange(0, 64, 2, dtype=np.float32) / np.float32(64)))).astype(np.float32)
    ang = (np.arange(S, dtype=np.float32)[:, None] * inv[None, :]).astype(np.float32)
    c = np.cos(ang).astype(np.float32).T
    s_ = np.sin(ang).astype(np.float32).T
    return np.ascontiguousarray(np.stack([np.concatenate([c, c], 0), np.concatenate([-s_, s_], 0)], 0))


def _wsel():
    n_cmp = (S - 32) // 16 + 1
    cs = np.arange(n_cmp) * 16
    ss = np.arange(S // 64) * 64
    ov = np.minimum(cs[:, None] + 32, ss[None, :] + 64) - np.maximum(cs[:, None], ss[None, :])
    w = np.zeros((256, 64), np.float32)
    w[:n_cmp] = np.clip(ov, 0, None) / 32.0
    return np.ascontiguousarray(w.reshape(2, 128, 64).transpose(1, 0, 2))


def _selb():
    t = np.arange(S)
    cur = (t // 64)[:, None]
    jj = np.arange(64)[None, :]
    valid = jj <= cur
    forced = valid & ((jj == 0) | (jj == cur) | (jj == cur - 1))
    b = np.where(forced, 1e4, 0.0) - np.where(valid, 0.0, 1e4)
    return np.ascontiguousarray(b.astype(np.float32).reshape(32, 128, 64).transpose(1, 0, 2))


ROPE = _rope_tables()
WSEL = _wsel()
SELB = _selb()
_ii = np.arange(128)
TRI = np.stack([(_ii[:, None] <= _ii[None, :]), (_ii[:, None] > _ii[None, :]), np.ones((128, 128), bool)]).astype(np.float32)


def run_plan(plan, inputs, n_cores=8, trace=False):
    key = repr(plan)
    if key not in _CACHE:
        _CACHE[key] = Builder(plan).build()
    nc = _CACHE[key]
    x = np.ascontiguousarray(inputs["x"], dtype=np.float32)
    gains = np.concatenate([np.asarray(inputs["norm_gains"], np.float32).reshape(DEPTH * 3, D),
                            np.asarray(inputs["final_norm"], np.float32).reshape(1, D)], axis=0)
    common = {
        "gains": np.ascontiguousarray(gains),
        "ffn_w_gate": np.ascontiguousarray(inputs["ffn_w_gate"], dtype=np.float32),
        "ffn_w_up": np.ascontiguousarray(inputs["ffn_w_up"], dtype=np.float32),
        "ffn_w_down": np.ascontiguousarray(inputs["ffn_w_down"], dtype=np.float32),
        "ident": np.eye(128, dtype=np.float32),
        "tri": TRI,
        "ssm_w_in": np.ascontiguousarray(inputs["ssm_w_in"], dtype=np.float32),
        "ssm_w_out": np.ascontiguousarray(inputs["ssm_w_out"], dtype=np.float32),
        "ssm_cw": np.ascontiguousarray(np.asarray(inputs["ssm_conv_w"], np.float32).transpose(0, 2, 1).reshape(2, 32, 128, 4).transpose(0, 2, 1, 3)),
        "ssm_cb": np.ascontiguousarray(np.asarray(inputs["ssm_conv_b"], np.float32).reshape(2, 32, 128).transpose(0, 2, 1)),
        "ssm_vec": np.ascontiguousarray(np.stack([np.asarray(inputs["ssm_dt_bias"], np.float32), np.asarray(inputs["ssm_a_log"], np.float32),
                                                  np.asarray(inputs["ssm_d"], np.float32)], axis=1).reshape(2, 96)),
        "ssm_norm": np.ascontiguousarray(inputs["ssm_norm"], dtype=np.float32),
        "attn_wr": attn_w_layout(np.asarray(inputs["attn_w_in"], np.float32)),
        "attn_w_out": np.ascontiguousarray(inputs["attn_w_out"], dtype=np.float32),
        "attn_sinks": np.ascontiguousarray(inputs["attn_sinks"], dtype=np.float32),
        "rope": ROPE,
        "cmp_w1": np.ascontiguousarray(np.stack([np.asarray(inputs["cmp_k_w1"], np.float32), np.asarray(inputs["cmp_v_w1"], np.float32)], axis=1)),
        "cmp_w2": np.ascontiguousarray(np.stack([np.asarray(inputs["cmp_k_w2"], np.float32), np.asarray(inputs["cmp_v_w2"], np.float32)], axis=1)),
        "cmp_posT": np.ascontiguousarray(np.stack([np.asarray(inputs["cmp_k_pos"], np.float32).transpose(0, 2, 1),
                                                   np.asarray(inputs["cmp_v_pos"], np.float32).transpose(0, 2, 1)], axis=1)),
        "wsel": WSEL,
        "selb": SELB,
    }
    in_maps = []
    for c in range(n_cores):
        m = dict(common)
        m["x"] = x[c % 4]
        in_maps.append(m)
    res = run_bass_kernel_spmd(nc, in_maps, core_ids=list(range(n_cores)), trace=trace)
    out = np.stack([res.results[c % n_cores]["out"] for c in range(4)], axis=0)
    return out, res


def kernel(**inputs):
    out, _ = run_plan(full_plan(), inputs)
    return out.astype(np.float32)
```

```python
import contextlib
import numpy as np
import concourse.bass as bass
import concourse.mybir as mybir
from concourse.bass_utils import run_bass_kernel_spmd

F32 = mybir.dt.float32
BF16 = mybir.dt.bfloat16
AF = mybir.ActivationFunctionType
ALU = mybir.AluOpType
AX = mybir.AxisListType

D = 1024
S = 4096
DEPTH = 4
DFF = 2816
NFC = DFF // 128
EPS = 1e-6
NAW = 3608


class Res:
    __slots__ = ("name", "w", "r")

    def __init__(self, name):
        self.name = name
        self.w = None
        self.r = []


class Op:
    __slots__ = ("eng", "fn", "waits", "inc", "dma", "dsem", "dval", "cnt", "pre")


class Prog:
    ENGS = ("pe", "act", "dve", "pool", "sp")

    def __init__(self, nc, stack):
        self.nc = nc
        self.stack = stack
        self.ops = {e: [] for e in self.ENGS}
        self.esem = {e: stack.enter_context(nc.semaphore("es_" + e)) for e in self.ENGS}
        self.nsem = 5
        self.streams = []

    def new_sem(self, name):
        self.nsem += 1
        return self.stack.enter_context(self.nc.semaphore(name))

    def op(self, eng, fn, reads=(), writes=(), dma=None):
        o = Op()
        o.eng = eng
        o.fn = fn
        o.inc = False
        o.dma = dma
        o.cnt = 0
        o.pre = None
        o.dsem = None
        o.dval = 0
        deps = []
        seen = set()

        def add(d, raw):
            if d is None or id(d) in seen:
                return
            if d.dma is None and d.eng == eng:
                if eng == "pe" or not raw:
                    return
            seen.add(id(d))
            deps.append(d)

        for r in reads:
            add(r.w, True)
        for w in writes:
            add(w.w, False)
            for rr in w.r:
                add(rr, False)
        for d in deps:
            if d.dma is None:
                d.inc = True
        o.waits = deps
        if dma is not None:
            sem, val, pre = dma.next()
            o.dsem, o.dval, o.pre = sem, val, pre
            dma.ops.append(o)
        for r in reads:
            r.r.append(o)
        for w in writes:
            w.w = o
            w.r = []
        self.ops[eng].append(o)
        return o

    def barrier(self):
        deps = []
        for e in self.ENGS:
            for o in reversed(self.ops[e]):
                if o.dma is None:
                    deps.append(o)
                    break
        for st in self.streams:
            deps.extend(st.ops[-st.R:])
        for e in self.ENGS:
            o = Op()
            o.eng = e
            o.fn = lambda eng: eng.nop()
            o.inc = False
            o.dma = None
            o.cnt = 0
            o.pre = None
            o.dsem = None
            o.dval = 0
            o.waits = [d for d in deps if not (d.dma is None and d.eng == e)]
            for d in o.waits:
                if d.dma is None:
                    d.inc = True
            self.ops[e].append(o)

    def emit(self):
        nc = self.nc
        for e in self.ENGS:
            c = 0
            for o in self.ops[e]:
                if o.dma is None and o.inc:
                    c += 1
                o.cnt = c
        self.counts = {e: (len(self.ops[e]), self.ops[e][-1].cnt if self.ops[e] else 0) for e in self.ENGS}

        def body_for(ename):
            def body(eng):
                seen = {}

                def wait(sem, val):
                    k = id(sem)
                    if seen.get(k, 0) >= val:
                        return
                    seen[k] = val
                    eng.wait_ge(sem, val)

                for o in self.ops[ename]:
                    for d in o.waits:
                        if d.dma is not None:
                            wait(d.dsem, d.dval)
                        else:
                            wait(self.esem[d.eng], d.cnt)
                    if o.pre is not None and o.pre[1] > 0:
                        wait(o.pre[0], o.pre[1])
                    ins = o.fn(eng)
                    if o.dma is not None:
                        ins.then_inc(o.dsem, 16)
                    elif o.inc:
                        ins.then_inc(self.esem[ename], 1)
            return body

        with nc.Block() as block:
            block.tensor(body_for("pe"))
            block.scalar(body_for("act"))
            block.vector(body_for("dve"))
            block.gpsimd(body_for("pool"))
            block.sync(body_for("sp"))


class DmaStream:
    def __init__(self, prog, name, R):
        self.sems = [prog.new_sem("%s%d" % (name, i)) for i in range(R)]
        self.R = R
        self.k = 0
        self.ops = []
        prog.streams.append(self)

    def next(self):
        k = self.k
        self.k += 1
        sem = self.sems[k % self.R]
        return sem, 16 * (k // self.R + 1), (sem, 16 * (k // self.R))


class Builder:
    def __init__(self, plan):
        self.plan = plan
        self.nc = bass.Bass("TRN2", target_bir_lowering=False)
        self.stack = contextlib.ExitStack()
        self.res_cache = {}

    def R(self, name):
        r = self.res_cache.get(name)
        if r is None:
            r = self.res_cache[name] = Res(name)
        return r

    def dram_in(self, name, shape, dt=F32):
        return self.nc.dram_tensor(name, list(shape), dt, kind="ExternalInput").ap()

    def dram_out(self, name, shape, dt=F32):
        return self.nc.dram_tensor(name, list(shape), dt, kind="ExternalOutput").ap()

    def dram_tmp(self, name, shape, dt):
        return self.nc.dram_tensor(name, list(shape), dt).ap()

    def sb(self, name, shape, dt):
        return self.stack.enter_context(self.nc.sbuf_tensor(name, list(shape), dt))

    def ps(self, name, shape, dt):
        return self.stack.enter_context(self.nc.psum_tensor(name, list(shape), dt))

    def build(self):
        nc = self.nc
        with self.stack:
            self.p = Prog(nc, self.stack)
            self._build()
            self.p.emit()
        return nc

    def _build(self):
        p = self.p
        plan = self.plan
        self.x_in = self.dram_in("x", [S, D])
        self.gains = self.dram_in("gains", [DEPTH * 3 + 1, D])
        self.wg = self.dram_in("ffn_w_gate", [DEPTH, 2, D, DFF])
        self.wu = self.dram_in("ffn_w_up", [DEPTH, 2, D, DFF])
        self.wd = self.dram_in("ffn_w_down", [DEPTH, 2, DFF, D])
        self.ident_in = self.dram_in("ident", [128, 128])
        self.tri_in = self.dram_in("tri", [3, 128, 128])
        self.ssm_w_in = self.dram_in("ssm_w_in", [2, D, 6176])
        self.ssm_w_out = self.dram_in("ssm_w_out", [2, 2048, D])
        self.ssm_cw = self.dram_in("ssm_cw", [2, 128, 32, 4])
        self.ssm_cb = self.dram_in("ssm_cb", [2, 128, 32])
        self.ssm_vec = self.dram_in("ssm_vec", [2, 96])
        self.ssm_norm = self.dram_in("ssm_norm", [2, 2048])
        self.swin = self.dram_tmp("swin", [2, D, 6176], BF16)
        self.attn_wr = self.dram_in("attn_wr", [2, D, NAW])
        self.attn_w_out = self.dram_in("attn_w_out", [2, D, D])
        self.sinks_in = self.dram_in("attn_sinks", [2, 8])
        self.rope_in = self.dram_in("rope", [2, 64, S])
        self.cmp_w1 = self.dram_in("cmp_w1", [2, 2, 2048, 128])
        self.cmp_w2 = self.dram_in("cmp_w2", [2, 2, 128, 64])
        self.cmp_posT = self.dram_in("cmp_posT", [2, 2, 64, 32])
        self.wsel_in = self.dram_in("wsel", [128, 2, 64])
        self.selb_in = self.dram_in("selb", [128, 32, 64])
        self.awin = self.dram_tmp("awin", [2, D, NAW], BF16)
        self.qT = self.dram_tmp("qT", [16, 64, S], BF16)
        self.out = self.dram_out("out", [S, D])
        self.xres = self.dram_tmp("xres", [S, D], F32)
        self.wgt = self.dram_tmp("wgt", [DEPTH, 2, 6, 128, 8, 512], BF16)
        self.wut = self.dram_tmp("wut", [DEPTH, 2, 6, 128, 8, 512], BF16)
        self.wdt = self.dram_tmp("wdt", [DEPTH, 2, DFF, D], BF16)

        self.st_const = DmaStream(p, "dc", 1)
        self.st_prep = DmaStream(p, "dp", 4)
        self.st_x = DmaStream(p, "dx", 4)
        self.st_w = DmaStream(p, "dw", 4)
        self.st_o = DmaStream(p, "do", 4)

        self.ident_f = self.sb("ident_f", [128, 128], F32)
        self.ident_b = self.sb("ident_b", [128, 128], BF16)
        self.gbc = self.sb("gbc", [128, D], F32)
        self.epsb = self.sb("epsb", [128, 1], F32)
        r_ident = self.R("ident")
        p.op("sp", lambda e: e.dma_start(out=self.ident_f[:], in_=self.ident_in), writes=[r_ident], dma=self.st_const)
        p.op("dve", lambda e: e.tensor_copy(out=self.ident_b[:], in_=self.ident_f[:]), reads=[r_ident], writes=[self.R("ident_b")])
        p.op("dve", lambda e: e.memset(self.epsb[:], EPS), writes=[self.R("epsb")])

        self.pbank = [self.ps("pb%d" % i, [128, 512], F32) for i in range(8)]
        self.ptrh = [self.pbank[6 + i][:, :].bitcast(BF16)[:, 0:512] for i in range(2)]
        self.r_pb = [self.R("pb%d" % i) for i in range(8)]
        self.r_ptr = [self.r_pb[6], self.r_pb[7]]

        self.x_src = self.x_in
        for ph in plan:
            kind = ph[0]
            if kind == "prep_ffn":
                self.prep_ffn(ph[1], ph[2])
            elif kind == "ffn":
                self.dbg_stage = ph[3] if len(ph) > 3 else 99
                self.ffn(ph[1], ph[2])
            elif kind == "prep_ssm":
                self.prep_ssm(ph[1])
            elif kind == "ssd":
                self.ssd(ph[1])
            elif kind == "prep_attn":
                self.prep_attn(ph[1])
            elif kind == "attn":
                self.attn_dbg = ph[2] if len(ph) > 2 else None
                self.attn(ph[1])
            elif kind == "final":
                self.final_norm()
            elif kind == "copy_out":
                self.copy_out()
            else:
                raise ValueError(kind)

    def load_gain(self, row):
        p = self.p
        src = self.gains[row:row + 1, :].broadcast_to([128, D])
        p.op("sp", lambda e: e.dma_start(out=self.gbc[:], in_=src), writes=[self.R("gbc")], dma=self.st_const)

    def prep_ffn(self, l, i):
        p = self.p
        for (src, dst, nm) in ((self.wg, self.wgt, "g"), (self.wu, self.wut, "u")):
            for blk in range(6):
                w = 512 if blk < 5 else 256
                s_ap = src[l, i, :, blk * 512:blk * 512 + w].rearrange("(kc p) m -> p kc m", p=128)
                d_ap = dst[l, i, blk, :, :, 0:w]
                p.op("pool", lambda e, s_ap=s_ap, d_ap=d_ap: e.dma_start(out=d_ap, in_=s_ap),
                     writes=[self.R("wt_%s_%d_%d_%d" % (nm, l, i, blk))], dma=self.st_prep)
        for q in range(4):
            rows = DFF // 4
            s_ap = self.wd[l, i, q * rows:(q + 1) * rows, :]
            d_ap = self.wdt[l, i, q * rows:(q + 1) * rows, :]
            p.op("pool", lambda e, s_ap=s_ap, d_ap=d_ap: e.dma_start(out=d_ap, in_=s_ap),
                 writes=[self.R("wt_d_%d_%d_%d" % (l, i, q))], dma=self.st_prep)

    def norm_transpose(self, xsrc, t0, ntile, hT, r_hT, xt_bufs, tag):
        p = self.p
        for j in range(ntile):
            tt = t0 + j
            b = j % 2
            xt, hb, sq, ss, rs = xt_bufs[b]
            rx = self.R("%s_xt%d" % (tag, b))
            rh = self.R("%s_hb%d" % (tag, b))
            rss = self.R("%s_ss%d" % (tag, b))
            rsq = self.R("%s_sq%d" % (tag, b))
            p.op("sp", lambda e, xt=xt, tt=tt: e.dma_start(out=xt[:], in_=xsrc[tt * 128:(tt + 1) * 128, :]),
                 reads=[self.R("xres")], writes=[rx], dma=self.st_x)
            p.op("act", lambda e, xt=xt, sq=sq, ss=ss: e.activation(out=sq[:], in_=xt[:], func=AF.Square, accum_out=ss[:]),
                 reads=[rx], writes=[rsq, rss])
            p.op("act", lambda e, ss=ss, rs=rs: e.activation(out=rs[:], in_=ss[:], func=AF.Sqrt, scale=1.0 / D, bias=self.epsb[:]),
                 reads=[rss, self.R("epsb")], writes=[self.R("%s_rs%d" % (tag, b))])
            p.op("dve", lambda e, rs=rs: e.reciprocal(out=rs[:], in_=rs[:]),
                 reads=[self.R("%s_rs%d" % (tag, b))], writes=[self.R("%s_rs%d" % (tag, b))])
            p.op("dve", lambda e, xt=xt, hb=hb, rs=rs: e.scalar_tensor_tensor(out=hb[:], in0=xt[:], scalar=rs[:], in1=self.gbc[:], op0=ALU.mult, op1=ALU.mult),
                 reads=[rx, self.R("%s_rs%d" % (tag, b)), self.R("gbc")], writes=[rh])
            for half in range(2):
                rp = self.r_ptr[half]

                def tr(e, hb=hb, half=half):
                    ins = None
                    for q in range(4):
                        kc = half * 4 + q
                        ins = e.transpose(out=self.ptrh[half][:, q * 128:(q + 1) * 128],
                                          in_=hb[:, kc * 128:(kc + 1) * 128], identity=self.ident_b[:])
                    return ins
                p.op("pe", tr, reads=[rh, self.R("ident_b")], writes=[rp])
                dst = hT[:, half * 4:(half + 1) * 4, j * 128:(j + 1) * 128]
                srcp = self.ptrh[half][:, :].rearrange("p (q m) -> p q m", q=4)
                eng = "act" if half == 0 else "dve"
                if eng == "act":
                    p.op("act", lambda e, dst=dst, srcp=srcp: e.copy(out=dst, in_=srcp), reads=[rp], writes=[r_hT[j][half]])
                else:
                    p.op("dve", lambda e, dst=dst, srcp=srcp: e.tensor_copy(out=dst, in_=srcp), reads=[rp], writes=[r_hT[j][half]])

    def ffn(self, l, i):
        p = self.p
        nc = self.nc
        TB = 1024
        NTB = S // TB
        xsrc = self.x_src
        with contextlib.ExitStack() as st:
            def sb(name, shape, dt):
                return st.enter_context(nc.sbuf_tensor(name, list(shape), dt))
            tag = "f%d%d" % (l, i)
            wd_sb = sb(tag + "wd", [128, NFC, D], BF16)
            aT = sb(tag + "aT", [128, NFC, TB], BF16)
            hT = sb(tag + "hT", [128, 8, TB], BF16)
            wgu = [(sb(tag + "wg%d" % b, [128, 8, 512], BF16), sb(tag + "wu%d" % b, [128, 8, 512], BF16)) for b in range(2)]
            xt_bufs = [(sb(tag + "xt%d" % b, [128, D], F32), sb(tag + "hb%d" % b, [128, D], BF16),
                        sb(tag + "sq%d" % b, [128, D], BF16), sb(tag + "ss%d" % b, [128, 1], F32),
                        sb(tag + "rs%d" % b, [128, 1], F32)) for b in range(2)]
            sg = [sb(tag + "sg%d" % b, [128, 512], F32) for b in range(2)]
            xo = [sb(tag + "xo%d" % b, [128, D], F32) for b in range(2)]
            r_wd = [self.R(tag + "wd0"), self.R(tag + "wd1")]
            r_aT = self.R(tag + "aT")
            r_hT = [[self.R(tag + "hT%d_%d" % (jj, hh)) for hh in range(2)] for jj in range(TB // 128)]
            r_wg = [self.R(tag + "wgs%d" % b) for b in range(2)]
            r_wu = [self.R(tag + "wus%d" % b) for b in range(2)]
            r_sg = [self.R(tag + "sg%d" % b) for b in range(2)]
            r_xo = [self.R(tag + "xo%d" % b) for b in range(2)]

            self.load_gain(l * 3 + (0 if i == 0 else 2))
            for q in range(2):
                fa, fb = q * 11, (q + 1) * 11
                src = self.wdt[l, i, fa * 128:fb * 128, :].rearrange("(fc p) m -> p fc m", p=128)
                p.op("sp", lambda e, src=src, fa=fa, fb=fb: e.dma_start(out=wd_sb[:, fa:fb, :], in_=src),
                     reads=[self.R("wt_d_%d_%d_%d" % (l, i, 2 * q)), self.R("wt_d_%d_%d_%d" % (l, i, 2 * q + 1))], writes=[r_wd[q]], dma=self.st_w)

            wcount = 0
            for tb in range(NTB):
                t0 = tb * (TB // 128)
                self.norm_transpose(xsrc, t0, TB // 128, hT, r_hT, xt_bufs, tag)
                if self.dbg_stage <= 1:
                    continue
                for blk in range(6):
                    w = 512 if blk < 5 else 256
                    b = wcount % 2
                    wcount += 1
                    wgs, wus = wgu[b]
                    p.op("sp", lambda e, wgs=wgs, blk=blk, w=w: e.dma_start(out=wgs[:, :, 0:w], in_=self.wgt[l, i, blk, :, :, 0:w]),
                         reads=[self.R("wt_g_%d_%d_%d" % (l, i, blk))], writes=[r_wg[b]], dma=self.st_w)
                    p.op("sp", lambda e, wus=wus, blk=blk, w=w: e.dma_start(out=wus[:, :, 0:w], in_=self.wut[l, i, blk, :, :, 0:w]),
                         reads=[self.R("wt_u_%d_%d_%d" % (l, i, blk))], writes=[r_wu[b]], dma=self.st_w)
                    for m in range(w // 128):
                        fc = blk * 4 + m
                        for half in range(TB // 512):
                            pg = (fc * 2 + half) % 2
                            bg, bu = self.pbank[pg * 2], self.pbank[pg * 2 + 1]
                            rg, ru = self.r_pb[pg * 2], self.r_pb[pg * 2 + 1]

                            def mm(e, wt, bank, m=m, half=half):
                                ins = None
                                for kc in range(8):
                                    ins = e.matmul(bank[:, :], lhsT=wt[:, kc, m * 128:(m + 1) * 128],
                                                   rhs=hT[:, kc, half * 512:(half + 1) * 512],
                                                   start=(kc == 0), stop=(kc == 7))
                                return ins
                            hres = [r_hT[half * 4 + jj][hh] for jj in range(4) for hh in range(2)]
                            p.op("pe", lambda e, wgs=wgs, bg=bg, mm=mm: mm(e, wgs, bg), reads=[r_wg[b]] + hres, writes=[rg])
                            p.op("pe", lambda e, wus=wus, bu=bu, mm=mm: mm(e, wus, bu), reads=[r_wu[b]] + hres, writes=[ru])
                            sgb = sg[pg]
                            p.op("act", lambda e, sgb=sgb, bg=bg: e.activation(out=sgb[:], in_=bg[:, :], func=AF.Silu),
                                 reads=[rg], writes=[r_sg[pg]])
                            dst = aT[:, fc, half * 512:(half + 1) * 512]
                            p.op("dve", lambda e, dst=dst, sgb=sgb, bu=bu: e.tensor_tensor(out=dst, in0=sgb[:], in1=bu[:, :], op=ALU.mult),
                                 reads=[r_sg[pg], ru], writes=[r_aT])
                if self.dbg_stage <= 2:
                    continue
                for j in range(TB // 128):
                    tt = t0 + j
                    xb = j % 2
                    xt = xt_bufs[xb][0]
                    rx = self.R("%s_xt%d" % (tag, xb))
                    p.op("sp", lambda e, xt=xt, tt=tt: e.dma_start(out=xt[:], in_=xsrc[tt * 128:(tt + 1) * 128, :]),
                         reads=[self.R("xres")], writes=[rx], dma=self.st_x)
                    for ch in range(2):
                        pb = 4 + (j * 2 + ch) % 2
                        bank, rb = self.pbank[pb], self.r_pb[pb]

                        def mmd(e, bank=bank, j=j, ch=ch):
                            ins = None
                            for fc in range(NFC):
                                ins = e.matmul(bank[:, :], lhsT=aT[:, fc, j * 128:(j + 1) * 128],
                                               rhs=wd_sb[:, fc, ch * 512:(ch + 1) * 512],
                                               start=(fc == 0), stop=(fc == NFC - 1))
                            return ins
                        p.op("pe", mmd, reads=[r_aT] + r_wd, writes=[rb])
                        xob = xo[xb]
                        p.op("dve", lambda e, xob=xob, bank=bank, xt=xt, ch=ch: e.scalar_tensor_tensor(
                            out=xob[:, ch * 512:(ch + 1) * 512], in0=bank[:, :], scalar=0.5, in1=xt[:, ch * 512:(ch + 1) * 512],
                            op0=ALU.mult, op1=ALU.add), reads=[rb, rx], writes=[r_xo[xb]])
                    p.op("sp", lambda e, xob=xob, tt=tt: e.dma_start(out=self.xres[tt * 128:(tt + 1) * 128, :], in_=xob[:]),
                         reads=[r_xo[xb]], writes=[self.R("xres_w%d" % (tt % 4))], dma=self.st_o)
            self.phase_barrier()
        self.x_src = self.xres

    def _store_waits(self):
        st = self.st_o
        return [(st.sems[idx % st.R], 16 * (idx // st.R + 1)) for idx in range(max(0, st.k - st.R), st.k)]

    def phase_barrier(self):
        p = self.p
        sems = self._store_waits()

        def fn(e, sems=sems):
            for sem, val in sems:
                e.wait_ge(sem, val)
            return e.nop()
        p.op("sp", fn, reads=[self.R("xres_w%d" % k) for k in range(4)], writes=[self.R("xres")])
        p.barrier()

    def prep_ssm(self, j):
        p = self.p
        for (c0, c1) in ((0, 2048), (2048, 4096), (4096, 6144), (6144, 6176)):
            for hh in range(2):
                s_ap = self.ssm_w_in[j, hh * 512:(hh + 1) * 512, c0:c1]
                d_ap = self.swin[j, hh * 512:(hh + 1) * 512, c0:c1]
                p.op("pool", lambda e, s_ap=s_ap, d_ap=d_ap: e.dma_start(out=d_ap, in_=s_ap),
                     writes=[self.R("swin%d_%d_%d" % (j, c0, hh))], dma=self.st_prep)

    def swin_res(self, j, c0):
        base = (c0 // 2048) * 2048 if c0 < 6144 else 6144
        return [self.R("swin%d_%d_%d" % (j, base, hh)) for hh in range(2)]

    def ssd(self, l):
        p = self.p
        nc = self.nc
        j = l // 2
        TBK = 256
        NBLK = S // TBK
        NT = TBK // 128
        xsrc = self.x_src
        with contextlib.ExitStack() as st:
            def sb(name, shape, dt):
                return st.enter_context(nc.sbuf_tensor(tag + name, list(shape), dt))
            tag = "s%d" % l
            R = lambda n: self.R(tag + n)
            L1 = sb("L1", [128, 128], F32)
            L2 = sb("L2", [128, 128], F32)
            ONES = sb("ONES", [128, 128], F32)
            cw = sb("cw", [128, 32, 4], F32)
            cb = sb("cb", [128, 32], F32)
            vec = sb("vec", [128, 3, 32], F32)
            a_bc = sb("a_bc", [128, 32], F32)
            Dbc = sb("Dbc", [128, 32, 1], F32)
            nw = sb("nw", [128, 2048], F32)
            wout = sb("wout", [128, 16, 1024], BF16)
            wdt = sb("wdt", [128, 8, 32], BF16)
            halo = sb("halo", [128, 32, 3], F32)
            state = sb("state", [128, 2048], F32)
            state_bf = sb("state_bf", [128, 2048], BF16)
            hT = sb("hT", [128, 8, TBK], BF16)
            wx = [sb("wx%d" % b, [128, 8, 512], BF16) for b in range(2)]
            xin = [sb("xin%d" % b, [128, TBK + 3], F32) for b in range(3)]
            acc = [sb("acc%d" % b, [128, TBK], F32) for b in range(3)]
            xsb = [sb("xsb%d" % b, [128, TBK], BF16) for b in range(2)]
            BT = sb("BT", [128, 8, TBK], BF16)
            CT = sb("CT", [128, 8, TBK], BF16)
            Btok = sb("Btok", [128, NT, 1024], BF16)
            xs_tok = sb("xs_tok", [128, NT, 2048], BF16)
            sz = sb("sz", [128, NT, 2048], BF16)
            xt_bufs = [(sb("xt%d" % b, [128, D], F32), sb("hb%d" % b, [128, D], BF16),
                        sb("sq%d" % b, [128, D], BF16), sb("ss%d" % b, [128, 1], F32),
                        sb("rs%d" % b, [128, 1], F32)) for b in range(2)]
            dtv = sb("dtv", [128, 32], F32)
            dt3 = sb("dt3", [128, 32, 1], F32)
            da = sb("da", [128, 32], F32)
            eall = sb("eall", [128, 96], F32)
            ea3 = sb("ea3", [128, 32, 1], F32)
            w23 = sb("w23", [128, 32, 1], F32)
            xdt = sb("xdt", [128, 2048], BF16)
            xdec = sb("xdec", [128, 2048], BF16)
            ybuf = sb("ybuf", [128, 2048], F32)
            t3 = sb("t3", [128, 2048], F32)
            gnb = sb("gnb", [128, 2048], BF16)
            gnT = sb("gnT", [128, 16, 128], BF16)
            Eg = [sb("Eg%d" % b, [128, 4, 128], F32) for b in range(2)]
            MT = [sb("MT%d" % b, [128, 4, 128], BF16) for b in range(2)]
            GTm = [sb("GTm%d" % b, [128, 1, 128], F32) for b in range(3)]
            ybufD = [sb("ybufD%d" % b, [128, 256], F32) for b in range(2)]
            Ada4 = [sb("Ada4_%d" % b, [128, 4, 128], F32) for b in range(2)]
            ssg = sb("ssg", [128, 8], F32)
            rsg = sb("rsg", [128, 8, 1], F32)
            xo = sb("xo", [128, D], F32)
            xres_t = sb("xres_t", [128, D], F32)

            p.op("sp", lambda e: e.dma_start(out=L1[:], in_=self.tri_in[0]), writes=[R("L1")], dma=self.st_const)
            p.op("sp", lambda e: e.dma_start(out=L2[:], in_=self.tri_in[1]), writes=[R("L2")], dma=self.st_const)
            p.op("sp", lambda e: e.dma_start(out=ONES[:], in_=self.tri_in[2]), writes=[R("ONES")], dma=self.st_const)
            p.op("sp", lambda e: e.dma_start(out=cw[:], in_=self.ssm_cw[j]), writes=[R("cw")], dma=self.st_const)
            p.op("sp", lambda e: e.dma_start(out=cb[:], in_=self.ssm_cb[j]), writes=[R("cb")], dma=self.st_const)
            p.op("sp", lambda e: e.dma_start(out=vec[:].rearrange("p a b -> p (a b)"),
                                             in_=self.ssm_vec[j:j + 1, :].broadcast_to([128, 96])), writes=[R("vec")], dma=self.st_const)
            p.op("sp", lambda e: e.dma_start(out=nw[:], in_=self.ssm_norm[j:j + 1, :].broadcast_to([128, 2048])), writes=[R("nw")], dma=self.st_const)
            for q in range(4):
                src = self.ssm_w_out[j, q * 512:(q + 1) * 512, :].rearrange("(k p) m -> p k m", p=128)
                p.op("pool", lambda e, src=src, q=q: e.dma_start(out=wout[:, q * 4:(q + 1) * 4, :], in_=src), writes=[R("wout%d" % q)], dma=self.st_prep)
            r_wout = [R("wout%d" % q) for q in range(4)]
            p.op("sp", lambda e: e.dma_start(out=wdt[:], in_=self.swin[j, :, 6144:6176].rearrange("(k p) m -> p k m", p=128)),
                 reads=self.swin_res(j, 6144), writes=[R("wdt")], dma=self.st_const)
            p.op("act", lambda e: e.activation(out=a_bc[:], in_=vec[:, 1, :], func=AF.Exp), reads=[R("vec")], writes=[R("a_bc")])
            p.op("dve", lambda e: e.tensor_scalar(out=a_bc[:], in0=a_bc[:], scalar1=-1.0, scalar2=None, op0=ALU.mult), reads=[R("a_bc")], writes=[R("a_bc")])
            p.op("dve", lambda e: e.tensor_copy(out=Dbc[:, :, 0], in_=vec[:, 2, :]), reads=[R("vec")], writes=[R("Dbc")])
            p.op("pool", lambda e: e.memset(halo[:], 0.0), writes=[R("halo%d" % cc_) for cc_ in range(32)])
            p.op("pool", lambda e: e.memset(state[:], 0.0), writes=[R("state")])
            p.op("pool", lambda e: e.memset(state_bf[:], 0.0), writes=[R("state_bf")])
            self.load_gain(l * 3 + 1)

            r_hT = [[R("hT%d_%d" % (jj, hh)) for hh in range(2)] for jj in range(NT)]
            hres_all = [r_hT[jj][hh] for jj in range(NT) for hh in range(2)]
            wxc = 0
            cvc = 0
            pendB = []
            pendB2 = []
            tails = []
            for blk in range(NBLK):
                t0 = blk * NT
                self.norm_transpose(xsrc, t0, NT, hT, r_hT, xt_bufs, tag)
                for wgI in range(8):
                    b = wxc % 2
                    wxc += 1
                    c0 = 2048 + wgI * 512
                    src = self.swin[j, :, c0:c0 + 512].rearrange("(k p) m -> p k m", p=128)
                    p.op("sp", lambda e, src=src, b=b: e.dma_start(out=wx[b][:], in_=src),
                         reads=self.swin_res(j, c0), writes=[R("wx%d" % b)], dma=self.st_w)
                    for m in range(4):
                        cc = wgI * 4 + m
                        pb = cc % 2
                        bank, rb = self.pbank[pb], self.r_pb[pb]
                        while len(pendB) >= 2:
                            pendB.pop(0)()

                        def mm(e, b=b, m=m, bank=bank):
                            for kc in range(8):
                                ins = e.matmul(bank[:, 0:TBK], lhsT=wx[b][:, kc, m * 128:(m + 1) * 128], rhs=hT[:, kc, :],
                                               start=(kc == 0), stop=(kc == 7))
                            return ins
                        p.op("pe", mm, reads=[R("wx%d" % b)] + hres_all, writes=[rb])
                        while len(pendB2) >= 2:
                            pendB2.pop(0)()
                        cbuf = cvc % 3
                        cvc += 1
                        xi, ac = xin[cbuf], acc[cbuf]
                        rxi, rac = R("xin%d" % cbuf), R("acc%d" % cbuf)
                        rhalo = R("halo%d" % cc)
                        p.op("pool", lambda e, xi=xi, cc=cc: e.tensor_copy(out=xi[:, 0:3], in_=halo[:, cc, :]), reads=[rhalo], writes=[rxi])
                        p.op("act", lambda e, xi=xi, bank=bank: e.copy(out=xi[:, 3:3 + TBK], in_=bank[:, 0:TBK]), reads=[rb], writes=[rxi])
                        p.op("pool", lambda e, xi=xi, cc=cc: e.tensor_copy(out=halo[:, cc, :], in_=xi[:, TBK:TBK + 3]), reads=[rxi], writes=[rhalo])
                        p.op("act", lambda e, xi=xi, ac=ac, cc=cc: e.activation(out=ac[:], in_=xi[:, 0:TBK], func=AF.Identity, scale=cw[:, cc, 0:1], bias=cb[:, cc:cc + 1]),
                             reads=[rxi, R("cw"), R("cb")], writes=[rac])
                        for w in range(1, 4):
                            p.op("dve", lambda e, xi=xi, ac=ac, cc=cc, w=w: e.scalar_tensor_tensor(out=ac[:], in0=xi[:, w:w + TBK], scalar=cw[:, cc, w:w + 1], in1=ac[:],
                                                                                                 op0=ALU.mult, op1=ALU.add), reads=[rxi, rac, R("cw")], writes=[rac])

                        def stageB1(cc=cc, ac=ac, rac=rac, cbuf=cbuf):
                            if cc < 16:
                                xb_, rxb = xsb[cbuf % 2], R("xsb%d" % (cbuf % 2))
                                p.op("act", lambda e, ac=ac, xb_=xb_: e.activation(out=xb_[:], in_=ac[:], func=AF.Silu), reads=[rac], writes=[rxb])
                            elif cc < 24:
                                g = cc - 16
                                p.op("act", lambda e, ac=ac, g=g: e.activation(out=BT[:, g, :], in_=ac[:], func=AF.Silu), reads=[rac], writes=[R("BT%d" % g)])
                            else:
                                g = cc - 24
                                p.op("act", lambda e, ac=ac, g=g: e.activation(out=CT[:, g, :], in_=ac[:], func=AF.Silu), reads=[rac], writes=[R("CT%d" % g)])

                        def stageB2(cc=cc, cbuf=cbuf):
                            if cc < 16:
                                xb_, rxb = xsb[cbuf % 2], R("xsb%d" % (cbuf % 2))
                                hp = cc % 2
                                rp = self.r_ptr[hp]

                                def tr(e, xb_=xb_, hp=hp):
                                    for q in range(NT):
                                        ins = e.transpose(out=self.ptrh[hp][:, q * 128:(q + 1) * 128], in_=xb_[:, q * 128:(q + 1) * 128], identity=self.ident_b[:])
                                    return ins
                                p.op("pe", tr, reads=[rxb, self.R("ident_b")], writes=[rp])
                                dst = xs_tok[:, :, cc * 128:(cc + 1) * 128]
                                srcp = self.ptrh[hp][:, 0:NT * 128].rearrange("p (q m) -> p q m", q=NT)
                                p.op("dve", lambda e, dst=dst, srcp=srcp: e.tensor_copy(out=dst, in_=srcp), reads=[rp], writes=[R("xs_tok")])
                            elif cc < 24:
                                g = cc - 16
                                hp = cc % 2
                                rp = self.r_ptr[hp]

                                def tr(e, g=g, hp=hp):
                                    for q in range(NT):
                                        ins = e.transpose(out=self.ptrh[hp][:, q * 128:(q + 1) * 128], in_=BT[:, g, q * 128:(q + 1) * 128], identity=self.ident_b[:])
                                    return ins
                                p.op("pe", tr, reads=[R("BT%d" % g), self.R("ident_b")], writes=[rp])
                                dst = Btok[:, :, g * 128:(g + 1) * 128]
                                srcp = self.ptrh[hp][:, 0:NT * 128].rearrange("p (q m) -> p q m", q=NT)
                                p.op("dve", lambda e, dst=dst, srcp=srcp: e.tensor_copy(out=dst, in_=srcp), reads=[rp], writes=[R("Btok")])
                        pendB.append(stageB1)
                        pendB2.append(stageB2)
                        if cc == 3:
                            while tails:
                                tails.pop(0)()
                while pendB:
                    pendB.pop(0)()
                while pendB2:
                    pendB2.pop(0)()
                for zc in range(4):
                    b = wxc % 2
                    wxc += 1
                    c0 = zc * 512
                    src = self.swin[j, :, c0:c0 + 512].rearrange("(k p) m -> p k m", p=128)
                    p.op("sp", lambda e, src=src, b=b: e.dma_start(out=wx[b][:], in_=src),
                         reads=self.swin_res(j, c0), writes=[R("wx%d" % b)], dma=self.st_w)
                    for q in range(NT):
                        pb = (zc * NT + q) % 2
                        bank, rb = self.pbank[pb], self.r_pb[pb]

                        def mmz(e, b=b, q=q, bank=bank):
                            for kc in range(8):
                                ins = e.matmul(bank[:, :], lhsT=hT[:, kc, q * 128:(q + 1) * 128], rhs=wx[b][:, kc, :], start=(kc == 0), stop=(kc == 7))
                            return ins
                        p.op("pe", mmz, reads=[R("wx%d" % b)] + r_hT[q], writes=[rb])
                        p.op("act", lambda e, q=q, zc=zc, bank=bank: e.activation(out=sz[:, q, zc * 512:(zc + 1) * 512], in_=bank[:, :], func=AF.Silu),
                             reads=[rb], writes=[R("sz%d" % q)])
                for q in range(NT):
                    tt = t0 + q
                    tsl = slice(q * 128, (q + 1) * 128)
                    b0, rb0 = self.pbank[0], self.r_pb[0]

                    def mmdt(e, q=q):
                        for kc in range(8):
                            ins = e.matmul(b0[:, 0:32], lhsT=hT[:, kc, q * 128:(q + 1) * 128], rhs=wdt[:, kc, :], start=(kc == 0), stop=(kc == 7))
                        return ins
                    p.op("pe", mmdt, reads=[R("wdt")] + r_hT[q], writes=[rb0])
                    p.op("dve", lambda e: e.tensor_tensor(out=dtv[:], in0=b0[:, 0:32], in1=vec[:, 0, :], op=ALU.add), reads=[rb0, R("vec")], writes=[R("dtv")])
                    p.op("act", lambda e: e.activation(out=dtv[:], in_=dtv[:], func=AF.Exp), reads=[R("dtv")], writes=[R("dtv")])
                    p.op("act", lambda e: e.activation(out=dt3[:, :, 0], in_=dtv[:], func=AF.Ln, bias=1.0), reads=[R("dtv")], writes=[R("dt3")])
                    p.op("dve", lambda e: e.tensor_tensor(out=da[:], in0=dt3[:, :, 0], in1=a_bc[:], op=ALU.mult), reads=[R("dt3"), R("a_bc")], writes=[R("da")])

                    def mmcs(e):
                        e.matmul(b0[:, 32:64], lhsT=L1[:], rhs=da[:], start=True, stop=True)
                        e.matmul(b0[:, 64:96], lhsT=L2[:], rhs=da[:], start=True, stop=True)
                        return e.matmul(b0[:, 96:128], lhsT=ONES[:], rhs=da[:], start=True, stop=True)
                    p.op("pe", mmcs, reads=[R("da"), R("L1"), R("L2"), R("ONES")], writes=[rb0])
                    p.op("act", lambda e: e.activation(out=eall[:], in_=b0[:, 32:128], func=AF.Exp), reads=[rb0], writes=[R("eall")])
                    p.op("dve", lambda e: e.tensor_copy(out=ea3[:, :, 0], in_=eall[:, 0:32]), reads=[R("eall")], writes=[R("ea3")])
                    p.op("dve", lambda e: e.tensor_tensor(out=w23[:, :, 0], in0=dt3[:, :, 0], in1=eall[:, 32:64], op=ALU.mult), reads=[R("dt3"), R("eall")], writes=[R("w23")])
                    xs3 = xs_tok[:, q, :].rearrange("p (h d) -> p h d", h=32)
                    p.op("dve", lambda e, xs3=xs3: e.tensor_tensor(out=xdt[:].rearrange("p (h d) -> p h d", h=32), in0=xs3, in1=dt3[:].to_broadcast([128, 32, 64]), op=ALU.mult),
                         reads=[R("xs_tok"), R("dt3")], writes=[R("xdt")])
                    p.op("pool", lambda e, xs3=xs3: e.tensor_tensor(out=xdec[:].rearrange("p (h d) -> p h d", h=32), in0=xs3, in1=w23[:].to_broadcast([128, 32, 64]), op=ALU.mult),
                         reads=[R("xs_tok"), R("w23")], writes=[R("xdec")])
                    p.op("pool", lambda e, xs3=xs3: e.tensor_tensor(out=t3[:].rearrange("p (h d) -> p h d", h=32), in0=xs3, in1=Dbc[:].to_broadcast([128, 32, 64]), op=ALU.mult),
                         reads=[R("xs_tok"), R("Dbc")], writes=[R("t3")])
                    def S0(g, tsl=tsl):
                        gb, g3 = g % 2, g % 3
                        b1, rb1 = self.pbank[1], self.r_pb[1]
                        gcol = slice((g % 4) * 128, (g % 4 + 1) * 128)
                        p.op("pe", lambda e, g=g, gcol=gcol, tsl=tsl: e.matmul(b1[:, gcol], lhsT=BT[:, g, tsl], rhs=CT[:, g, tsl], start=True, stop=True),
                             reads=[R("BT%d" % g), R("CT%d" % g)], writes=[rb1])
                        p.op("dve", lambda e, g3=g3, gcol=gcol: e.tensor_tensor(out=GTm[g3][:, 0, :], in0=b1[:, gcol], in1=L1[:], op=ALU.mult),
                             reads=[rb1, R("L1")], writes=[R("GTm%d" % g3)])
                        for r in range(4):
                            h = 4 * g + r
                            p.op("act", lambda e, gb=gb, r=r, h=h: e.activation(out=Ada4[gb][:, r, :], in_=L2[:], func=AF.Copy, scale=da[:, h:h + 1]),
                                 reads=[R("L2"), R("da")], writes=[R("Ada4_%d" % gb)])

                    def S1(g):
                        gb = g % 2
                        bs, rbs = self.pbank[2 + gb], self.r_pb[2 + gb]

                        def mmseg(e, gb=gb, bs=bs):
                            for r in range(4):
                                ins = e.matmul(bs[:, r * 128:(r + 1) * 128], lhsT=Ada4[gb][:, r, :], rhs=L1[:], start=True, stop=True)
                            return ins
                        p.op("pe", mmseg, reads=[R("Ada4_%d" % gb), R("L1")], writes=[rbs])
                        p.op("act", lambda e, gb=gb, bs=bs: e.activation(out=Eg[gb][:].rearrange("p r l -> p (r l)"), in_=bs[:, :], func=AF.Exp),
                             reads=[rbs], writes=[R("Eg%d" % gb)])

                    def S2(g):
                        gb, g3 = g % 2, g % 3
                        p.op("dve", lambda e, gb=gb, g3=g3: e.tensor_tensor(out=MT[gb][:], in0=Eg[gb][:], in1=GTm[g3][:].to_broadcast([128, 4, 128]), op=ALU.mult),
                             reads=[R("Eg%d" % gb), R("GTm%d" % g3)], writes=[R("MT%d" % gb)])

                    def S3(g, tsl=tsl, q=q):
                        gb = g % 2
                        by, rby = self.pbank[4 + gb], self.r_pb[4 + gb]

                        def mmy(e, g=g, gb=gb, by=by, tsl=tsl):
                            for r in range(4):
                                h = 4 * g + r
                                e.matmul(by[:, r * 64:(r + 1) * 64], lhsT=MT[gb][:, r, :], rhs=xdt[:, h * 64:(h + 1) * 64], start=True, stop=True)
                            return e.matmul(by[:, 256:512], lhsT=CT[:, g, tsl], rhs=state_bf[:, g * 256:(g + 1) * 256], start=True, stop=True)
                        p.op("pe", mmy, reads=[R("MT%d" % gb), R("xdt"), R("CT%d" % g), R("state_bf")], writes=[rby])
                        p.op("pe", lambda e, g=g, q=q: e.matmul(b0[:, 256:512], lhsT=Btok[:, q, g * 128:(g + 1) * 128], rhs=xdec[:, g * 256:(g + 1) * 256], start=True, stop=True),
                             reads=[R("Btok"), R("xdec")], writes=[rb0])
                        for r in range(4):
                            h = 4 * g + r
                            p.op("dve", lambda e, h=h, r=r: e.scalar_tensor_tensor(out=state[:, h * 64:(h + 1) * 64], in0=state[:, h * 64:(h + 1) * 64],
                                                                                     scalar=eall[:, 64 + h:65 + h], in1=b0[:, 256 + r * 64:256 + (r + 1) * 64],
                                                                                     op0=ALU.mult, op1=ALU.add), reads=[R("state"), R("eall"), rb0], writes=[R("state")])
                        p.op("act", lambda e, gb=gb, by=by: e.copy(out=ybufD[gb][:], in_=by[:, 0:256]), reads=[rby], writes=[R("ybufD%d" % gb)])
                        yg = ybuf[:, g * 256:(g + 1) * 256].rearrange("p (r d) -> p r d", r=4)
                        p.op("dve", lambda e, yg=yg, by=by, g=g: e.tensor_tensor(out=yg, in0=by[:, 256:512].rearrange("p (r d) -> p r d", r=4),
                                                                               in1=ea3[:, 4 * g:4 * g + 4, :].to_broadcast([128, 4, 64]), op=ALU.mult),
                             reads=[rby, R("ea3")], writes=[R("ybuf")])
                        p.op("pool", lambda e, g=g, gb=gb: e.tensor_tensor(out=ybuf[:, g * 256:(g + 1) * 256], in0=ybuf[:, g * 256:(g + 1) * 256], in1=ybufD[gb][:], op=ALU.add),
                             reads=[R("ybuf"), R("ybufD%d" % gb)], writes=[R("ybuf")])

                    for k in range(11):
                        if k < 8:
                            S0(k)
                        if 0 <= k - 1 < 8:
                            S1(k - 1)
                        if 0 <= k - 2 < 8:
                            S2(k - 2)
                        if 0 <= k - 3 < 8:
                            S3(k - 3)
                        if k == 1:
                            while tails:
                                tails.pop(0)()
                    p.op("act", lambda e: e.copy(out=state_bf[:], in_=state[:]), reads=[R("state")], writes=[R("state_bf")])
                    p.op("pool", lambda e: e.tensor_tensor(out=ybuf[:], in0=ybuf[:], in1=t3[:], op=ALU.add), reads=[R("ybuf"), R("t3")], writes=[R("ybuf")])
                    p.op("pool", lambda e, q=q: e.tensor_tensor(out=ybuf[:], in0=ybuf[:], in1=sz[:, q, :], op=ALU.mult), reads=[R("ybuf"), R("sz%d" % q)], writes=[R("ybuf")])
                    for g in range(8):
                        p.op("act", lambda e, g=g: e.activation(out=gnb[:, g * 256:(g + 1) * 256], in_=ybuf[:, g * 256:(g + 1) * 256], func=AF.Square, accum_out=ssg[:, g:g + 1]),
                             reads=[R("ybuf")], writes=[R("gnb"), R("ssg")])
                    p.op("act", lambda e: e.activation(out=rsg[:, :, 0], in_=ssg[:], func=AF.Sqrt, scale=1.0 / 256, bias=self.epsb[:]), reads=[R("ssg"), self.R("epsb")], writes=[R("rsg")])
                    p.op("dve", lambda e: e.reciprocal(out=rsg[:, :, 0], in_=rsg[:, :, 0]), reads=[R("rsg")], writes=[R("rsg")])
                    p.op("dve", lambda e: e.tensor_tensor(out=ybuf[:].rearrange("p (g d) -> p g d", g=8), in0=ybuf[:].rearrange("p (g d) -> p g d", g=8),
                                                          in1=rsg[:].to_broadcast([128, 8, 256]), op=ALU.mult), reads=[R("ybuf"), R("rsg")], writes=[R("ybuf")])
                    p.op("pool", lambda e: e.tensor_tensor(out=gnb[:], in0=ybuf[:], in1=nw[:], op=ALU.mult), reads=[R("ybuf"), R("nw"), R("gnb")], writes=[R("gnb")])
                    def tail(tt=tt):
                        for qq in range(4):
                            hp = qq % 2
                            rp = self.r_ptr[hp]

                            def trg(e, qq=qq, hp=hp):
                                for m in range(4):
                                    k = qq * 4 + m
                                    ins = e.transpose(out=self.ptrh[hp][:, m * 128:(m + 1) * 128], in_=gnb[:, k * 128:(k + 1) * 128], identity=self.ident_b[:])
                                return ins
                            p.op("pe", trg, reads=[R("gnb"), self.R("ident_b")], writes=[rp])
                            dst = gnT[:, qq * 4:(qq + 1) * 4, :]
                            srcp = self.ptrh[hp][:, :].rearrange("p (q m) -> p q m", q=4)
                            if hp == 0:
                                p.op("act", lambda e, dst=dst, srcp=srcp: e.copy(out=dst, in_=srcp), reads=[rp], writes=[R("gnT%d" % qq)])
                            else:
                                p.op("dve", lambda e, dst=dst, srcp=srcp: e.tensor_copy(out=dst, in_=srcp), reads=[rp], writes=[R("gnT%d" % qq)])
                        p.op("sp", lambda e, tt=tt: e.dma_start(out=xres_t[:], in_=xsrc[tt * 128:(tt + 1) * 128, :]),
                             reads=[self.R("xres")], writes=[R("xres_t")], dma=self.st_x)
                        for ch in range(2):
                            bo, rbo = self.pbank[6 + ch], self.r_pb[6 + ch]

                            def mmo(e, ch=ch, bo=bo):
                                for k in range(16):
                                    ins = e.matmul(bo[:, :], lhsT=gnT[:, k, :], rhs=wout[:, k, ch * 512:(ch + 1) * 512], start=(k == 0), stop=(k == 15))
                                return ins
                            p.op("pe", mmo, reads=[R("gnT%d" % qq) for qq in range(4)] + r_wout, writes=[rbo])
                            p.op("dve", lambda e, ch=ch, bo=bo: e.tensor_tensor(out=xo[:, ch * 512:(ch + 1) * 512], in0=bo[:, :], in1=xres_t[:, ch * 512:(ch + 1) * 512], op=ALU.add),
                                 reads=[rbo, R("xres_t")], writes=[R("xo")])
                        p.op("sp", lambda e, tt=tt: e.dma_start(out=self.xres[tt * 128:(tt + 1) * 128, :], in_=xo[:]),
                             reads=[R("xo")], writes=[self.R("xres_w%d" % (tt % 4))], dma=self.st_o)
                    tails.append(tail)
            while tails:
                tails.pop(0)()
            self.phase_barrier()
        self.x_src = self.xres

    def prep_attn(self, j):
        p = self.p
        for (c0, c1) in ((0, 2048), (2048, NAW)):
            for hh in range(2):
                s_ap = self.attn_wr[j, hh * 512:(hh + 1) * 512, c0:c1]
                d_ap = self.awin[j, hh * 512:(hh + 1) * 512, c0:c1]
                p.op("pool", lambda e, s_ap=s_ap, d_ap=d_ap: e.dma_start(out=d_ap, in_=s_ap),
                     writes=[self.R("awin%d_%d_%d" % (j, c0, hh))], dma=self.st_prep)

    def awin_res(self, j, c0, c1):
        out = []
        for base in (0, 2048):
            hi = 2048 if base == 0 else NAW
            if c0 < hi and c1 > base:
                out += [self.R("awin%d_%d_%d" % (j, base, hh)) for hh in range(2)]
        return out

    def attn(self, l):
        p = self.p
        nc = self.nc
        j = l // 2
        xsrc = self.x_src
        tag = "a%d" % l
        R = lambda n: self.R(tag + n)
        BIG = 30000.0
        dbg = self.attn_dbg or ""
        use_cmp = ("nocmp" not in dbg)
        use_slc = ("noslc" not in dbg)
        use_win = ("nowin" not in dbg)
        with contextlib.ExitStack() as st_long:
            def sbl(name, shape, dt):
                return st_long.enter_context(nc.sbuf_tensor(tag + name, list(shape), dt))
            kT = sbl("kT", [64, 6, S], BF16)
            kx = sbl("kx", [128, 2, S], BF16)
            Vall = sbl("V", [128, 32, 6, 65], BF16)
            gates = sbl("gates", [128, 32, 24], F32)
            kcmpT = sbl("kcmpT", [64, 2, 256], BF16)
            Vcmp = sbl("Vcmp", [128, 2, 2, 129], BF16)
            esink = sbl("esink", [128, 8], F32)
            p.op("pool", lambda e: e.memset(Vall[:, :, :, 64:65], 1.0), writes=[R("Vones")])
            p.op("pool", lambda e: e.memset(kx[64:128, :, :], 1.0), writes=[R("kxE")])
            for g_ in range(2):
                p.op("pool", lambda e, g_=g_: e.affine_select(out=kx[64:128, g_, :].rearrange("p (b m) -> p b m", b=64), in_=kx[64:128, g_, :].rearrange("p (b m) -> p b m", b=64),
                                                             pattern=[[-1, 64], [0, 64]], compare_op=ALU.is_equal, fill=0.0, base=0, channel_multiplier=1),
                     reads=[R("kxE")], writes=[R("kxE")])
            p.op("sp", lambda e: e.dma_start(out=esink[:], in_=self.sinks_in[j:j + 1, :].broadcast_to([128, 8])), writes=[R("esink")], dma=self.st_const)
            p.op("act", lambda e: e.activation(out=esink[:], in_=esink[:], func=AF.Exp), reads=[R("esink")], writes=[R("esink")])
            self.load_gain(l * 3 + 1)

            with contextlib.ExitStack() as st_x:
                kcT = st_x.enter_context(nc.sbuf_tensor(tag + "kcT", [64, 2, S], BF16))
                vcT = st_x.enter_context(nc.sbuf_tensor(tag + "vcT", [128, S], BF16))
                with contextlib.ExitStack() as st1:
                    def sb(name, shape, dt):
                        return st1.enter_context(nc.sbuf_tensor(tag + name, list(shape), dt))
                    TBK = 512
                    NT = 4
                    hT = sb("hT", [128, 8, TBK], BF16)
                    wb = [sb("wb%d" % b, [128, 8, 512], BF16) for b in range(2)]
                    cosb = sb("cosb", [128, TBK], F32)
                    sinb = sb("sinb", [128, TBK], F32)
                    t1 = [sb("t1_%d" % b, [128, TBK], F32) for b in range(2)]
                    t2 = [sb("t2_%d" % b, [128, TBK], F32) for b in range(2)]
                    qst = [sb("qst%d" % b, [128, TBK], BF16) for b in range(2)]
                    xt_bufs = [(sb("xt%d" % b, [128, D], F32), sb("hb%d" % b, [128, D], BF16),
                                sb("sq%d" % b, [128, D], BF16), sb("ss%d" % b, [128, 1], F32),
                                sb("rs%d" % b, [128, 1], F32)) for b in range(2)]
                    r_hT = [[R("hT%d_%d" % (jj, hh)) for hh in range(2)] for jj in range(NT)]
                    hres_all = [r_hT[jj][hh] for jj in range(NT) for hh in range(2)]
                    wc = 0
                    hc = 0
                    for blk in range(S // TBK):
                        t0 = blk * NT
                        csl = slice(blk * TBK, (blk + 1) * TBK)
                        self.norm_transpose(xsrc, t0, NT, hT, r_hT, xt_bufs, tag)
                        for hh_ in range(2):
                            p.op("sp", lambda e, csl=csl, hh_=hh_: e.dma_start(out=cosb[hh_ * 64:(hh_ + 1) * 64, :], in_=self.rope_in[0, :, csl]), writes=[R("cosb%d" % hh_)], dma=self.st_x)
                            p.op("sp", lambda e, csl=csl, hh_=hh_: e.dma_start(out=sinb[hh_ * 64:(hh_ + 1) * 64, :], in_=self.rope_in[1, :, csl]), writes=[R("sinb%d" % hh_)], dma=self.st_x)
                        for wgI in range(6):
                            b = wc % 2
                            wc += 1
                            c0 = wgI * 512
                            src = self.awin[j, :, c0:c0 + 512].rearrange("(k p) m -> p k m", p=128)
                            p.op("sp", lambda e, src=src, b=b: e.dma_start(out=wb[b][:], in_=src),
                                 reads=self.awin_res(j, c0, c0 + 512), writes=[R("wb%d" % b)], dma=self.st_w)
                            for m in range(2):
                                pi = wgI * 2 + m
                                hb_ = hc % 2
                                hc += 1
                                bA, rA = self.pbank[hb_ * 2], self.r_pb[hb_ * 2]
                                bB, rB = self.pbank[hb_ * 2 + 1], self.r_pb[hb_ * 2 + 1]

                                def mm(e, bank, off, b=b, m=m):
                                    for kc in range(8):
                                        ins = e.matmul(bank[:, :], lhsT=wb[b][:, kc, m * 256 + off:m * 256 + off + 128], rhs=hT[:, kc, :],
                                                       start=(kc == 0), stop=(kc == 7))
                                    return ins
                                p.op("pe", lambda e, mm=mm, bA=bA: mm(e, bA, 0), reads=[R("wb%d" % b)] + hres_all, writes=[rA])
                                p.op("pe", lambda e, mm=mm, bB=bB: mm(e, bB, 128), reads=[R("wb%d" % b)] + hres_all, writes=[rB])
                                p.op("dve", lambda e, hb_=hb_, bA=bA: e.tensor_tensor(out=t1[hb_][:], in0=bA[:, :], in1=cosb[:], op=ALU.mult),
                                     reads=[rA, R("cosb0"), R("cosb1")], writes=[R("t1_%d" % hb_)])
                                p.op("dve", lambda e, hb_=hb_, bB=bB: e.tensor_tensor(out=t2[hb_][:], in0=bB[:, :], in1=sinb[:], op=ALU.mult),
                                     reads=[rB, R("sinb0"), R("sinb1")], writes=[R("t2_%d" % hb_)])
                                if pi < 8:
                                    if pi < 2:
                                        dst, rd = kT[:, pi, csl], R("kT%d" % pi)
                                    elif pi < 4:
                                        dst, rd = kcT[:, pi - 2, csl], R("kcT%d" % (pi - 2))
                                    elif pi < 6:
                                        dst, rd = kx[0:64, pi - 4, csl], R("kxK%d" % (pi - 4))
                                    else:
                                        dst, rd = kT[:, 4 + pi - 6, csl], R("kT%d" % (4 + pi - 6))
                                    p.op("pool", lambda e, hb_=hb_, dst=dst: e.tensor_tensor(out=dst, in0=t1[hb_][0:64, :], in1=t2[hb_][0:64, :], op=ALU.add),
                                         reads=[R("t1_%d" % hb_), R("t2_%d" % hb_)], writes=[rd])
                                    p.op("pool", lambda e, hb_=hb_: e.tensor_tensor(out=qst[hb_][64:128, :], in0=t1[hb_][64:128, :], in1=t2[hb_][64:128, :], op=ALU.add),
                                         reads=[R("t1_%d" % hb_), R("t2_%d" % hb_)], writes=[R("qst%d" % hb_)])
                                    p.op("sp", lambda e, hb_=hb_, pi=pi, csl=csl: e.dma_start(out=self.qT[pi, :, csl], in_=qst[hb_][64:128, :]),
                                         reads=[R("qst%d" % hb_)], writes=[self.R("qT_%d_%d" % (pi, blk))], dma=self.st_o)
                                else:
                                    qh = 8 + 2 * (pi - 8)
                                    p.op("pool", lambda e, hb_=hb_: e.tensor_tensor(out=qst[hb_][:], in0=t1[hb_][:], in1=t2[hb_][:], op=ALU.add),
                                         reads=[R("t1_%d" % hb_), R("t2_%d" % hb_)], writes=[R("qst%d" % hb_)])
                                    p.op("sp", lambda e, hb_=hb_, qh=qh, csl=csl: e.dma_start(out=self.qT[qh:qh + 2, :, csl].rearrange("h d t -> (h d) t"), in_=qst[hb_][:]),
                                         reads=[R("qst%d" % hb_)], writes=[self.R("qT_%d_%d" % (qh, blk)), self.R("qT_%d_%d" % (qh + 1, blk))], dma=self.st_o)
                        b = wc % 2
                        wc += 1
                        src = self.awin[j, :, 3072:3608].rearrange("(k p) m -> p k m", p=128)
                        src_vc = self.awin[j, :, 3072:3200].rearrange("(k p) m -> p k m", p=128)
                        src_tm = self.awin[j, :, 3200:3608].rearrange("(k p) m -> p k m", p=128)
                        b2 = wc % 2
                        wc += 1
                        p.op("sp", lambda e, b=b, src_vc=src_vc: e.dma_start(out=wb[b][:, :, 0:128], in_=src_vc),
                             reads=self.awin_res(j, 3072, 3200), writes=[R("wb%d" % b)], dma=self.st_w)
                        p.op("sp", lambda e, b2=b2, src_tm=src_tm: e.dma_start(out=wb[b2][:, :, 0:408], in_=src_tm),
                             reads=self.awin_res(j, 3200, 3608), writes=[R("wb%d" % b2)], dma=self.st_w)
                        bA, rA = self.pbank[4], self.r_pb[4]

                        def mmvc(e, b=b, bA=bA):
                            for kc in range(8):
                                ins = e.matmul(bA[:, :], lhsT=wb[b][:, kc, 0:128], rhs=hT[:, kc, :], start=(kc == 0), stop=(kc == 7))
                            return ins
                        p.op("pe", mmvc, reads=[R("wb%d" % b)] + hres_all, writes=[rA])
                        p.op("act", lambda e, bA=bA, csl=csl: e.copy(out=vcT[:, csl], in_=bA[:, :]), reads=[rA], writes=[R("vcT")])
                        for q in range(NT):
                            tt = t0 + q
                            bT, rT = self.pbank[5], self.r_pb[5]

                            def mmtm(e, q=q, b2=b2, bT=bT):
                                for kc in range(8):
                                    ins = e.matmul(bT[:, 0:408], lhsT=hT[:, kc, q * 128:(q + 1) * 128], rhs=wb[b2][:, kc, 0:408], start=(kc == 0), stop=(kc == 7))
                                return ins
                            p.op("pe", mmtm, reads=[R("wb%d" % b2)] + r_hT[q], writes=[rT])
                            p.op("dve", lambda e, tt=tt, bT=bT: e.tensor_copy(out=Vall[:, tt, :, 0:64], in_=bT[:, 0:384].rearrange("p (a d) -> p a d", a=6)),
                                 reads=[rT], writes=[R("Vall")])
                            p.op("act", lambda e, tt=tt, bT=bT: e.activation(out=gates[:, tt, :], in_=bT[:, 384:408], func=AF.Sigmoid),
                                 reads=[rT], writes=[R("gates")])
                    p.barrier()
                with contextlib.ExitStack() as st2:
                    def sb(name, shape, dt):
                        return st2.enter_context(nc.sbuf_tensor(tag + name, list(shape), dt))
                    w1s = sb("w1s", [64, 32, 128], BF16)
                    w2s = sb("w2s", [128, 64], BF16)
                    posf = sb("posf", [64, 32], F32)
                    posb_ = sb("posb", [64, 32, 2], BF16)
                    pbias = sb("pbias", [128, 1], F32)
                    u = sb("u", [128, 256], F32)
                    u2 = sb("u2", [128, 256], F32)
                    sg_ = sb("sgm", [128, 256], F32)
                    gl = sb("gl", [128, 256], BF16)
                    wself = sb("wself", [128, 2, 64], F32)
                    p.op("sp", lambda e: e.dma_start(out=wself[:], in_=self.wsel_in), writes=[R("wself")], dma=self.st_const)
                    p.op("dve", lambda e: e.memset(u[:], 0.0), writes=[R("u")])
                    p.op("pool", lambda e: e.memset(Vcmp[:, :, :, 64:65], 1.0), writes=[R("Vcmp1")])
                    for g in range(2):
                        p.op("dve", lambda e, g=g: e.tensor_copy(out=Vcmp[:, g, :, 65:129], in_=wself[:]), reads=[R("wself")], writes=[R("VcmpW%d" % g)])
                    for kv in range(2):
                        w1_in = self.cmp_w1[j, kv].rearrange("(pp d) h -> d pp h", d=64)
                        p.op("pool", lambda e, w1_in=w1_in: e.dma_start(out=w1s[:], in_=w1_in), writes=[R("w1s")], dma=self.st_prep)
                        p.op("pool", lambda e, kv=kv: e.dma_start(out=w2s[:], in_=self.cmp_w2[j, kv]), writes=[R("w2s")], dma=self.st_prep)
                        p.op("sp", lambda e, kv=kv: e.dma_start(out=posf[:], in_=self.cmp_posT[j, kv]), writes=[R("posf")], dma=self.st_const)
                        p.op("dve", lambda e: e.tensor_copy(out=posb_[:], in_=posf[:].rearrange("p (a o) -> p a o", o=1).to_broadcast([64, 32, 2])), reads=[R("posf")], writes=[R("posb")])
                        b0, rb0 = self.pbank[0], self.r_pb[0]

                        def mmb(e):
                            for pp in range(32):
                                ins = e.matmul(b0[:, 0:2], lhsT=w1s[:, pp, :], rhs=posb_[:, pp, :], start=(pp == 0), stop=(pp == 31))
                            return ins
                        p.op("pe", mmb, reads=[R("w1s"), R("posb")], writes=[rb0])
                        p.op("dve", lambda e: e.tensor_copy(out=pbias[:], in_=b0[:, 0:1]), reads=[rb0], writes=[R("pbias")])
                        for g in range(2):
                            b1, rb1 = self.pbank[1 + g], self.r_pb[1 + g]
                            if kv == 0:
                                srcT = kcT[:, g, :]
                                rsrc = R("kcT%d" % g)
                            else:
                                srcT = vcT[g * 64:(g + 1) * 64, :]
                                rsrc = R("vcT")

                            def mmh(e, srcT=srcT, b1=b1, g=g):
                                for pp in range(32):
                                    ins = e.matmul(b1[:, 0:255], lhsT=w1s[g * 64 * kv:g * 64 * kv + 64, pp, :] if False else w1s[:, pp, :],
                                                   rhs=srcT[:, pp:pp + 16 * 254 + 1:16], start=(pp == 0), stop=(pp == 31))
                                return ins
                            if kv == 1 and g == 1:
                                vtmp = sb("vtmp", [64, S], BF16)
                                p.op("sp", lambda e, vtmp=vtmp: e.dma_start(out=vtmp[:], in_=vcT[64:128, :]), reads=[R("vcT")], writes=[R("vtmp")], dma=self.st_x)
                                srcT2 = vtmp[:, :]

                                def mmh(e, srcT2=srcT2, b1=b1):
                                    for pp in range(32):
                                        ins = e.matmul(b1[:, 0:255], lhsT=w1s[:, pp, :], rhs=srcT2[:, pp:pp + 16 * 254 + 1:16], start=(pp == 0), stop=(pp == 31))
                                    return ins
                                rsrc = R("vtmp")
                            p.op("pe", mmh, reads=[R("w1s"), rsrc], writes=[rb1])
                            p.op("act", lambda e, b1=b1: e.activation(out=u[:, 0:255], in_=b1[:, 0:255], func=AF.Identity, bias=pbias[:]),
                                 reads=[rb1, R("pbias"), R("u")], writes=[R("u")])
                            p.op("dve", lambda e: e.tensor_tensor(out=u2[:], in0=u[:], in1=u[:], op=ALU.mult), reads=[R("u")], writes=[R("u2")])
                            p.op("dve", lambda e: e.tensor_scalar(out=u2[:], in0=u2[:], scalar1=0.044715, scalar2=1.0, op0=ALU.mult, op1=ALU.add), reads=[R("u2")], writes=[R("u2")])
                            p.op("dve", lambda e: e.tensor_tensor(out=u2[:], in0=u2[:], in1=u[:], op=ALU.mult), reads=[R("u2"), R("u")], writes=[R("u2")])
                            p.op("act", lambda e: e.activation(out=sg_[:], in_=u2[:], func=AF.Sigmoid, scale=1.5957691216057308), reads=[R("u2")], writes=[R("sgm")])
                            p.op("dve", lambda e: e.tensor_tensor(out=gl[:], in0=u[:], in1=sg_[:], op=ALU.mult), reads=[R("u"), R("sgm")], writes=[R("gl")])
                            b3, rb3 = self.pbank[3], self.r_pb[3]
                            if kv == 0:
                                p.op("pe", lambda e, b3=b3: e.matmul(b3[0:64, 0:256], lhsT=w2s[:], rhs=gl[:], start=True, stop=True), reads=[R("w2s"), R("gl")], writes=[rb3])
                                p.op("dve", lambda e, g=g, b3=b3: e.tensor_copy(out=kcmpT[:, g, :], in_=b3[0:64, 0:256]), reads=[rb3], writes=[R("kcmpT%d" % g)])
                            else:
                                def mmv(e, b3=b3):
                                    for ct in range(2):
                                        ins = e.matmul(b3[:, ct * 64:(ct + 1) * 64], lhsT=gl[:, ct * 128:(ct + 1) * 128], rhs=w2s[:], start=True, stop=True)
                                    return ins
                                p.op("pe", mmv, reads=[R("w2s"), R("gl")], writes=[rb3])
                                p.op("dve", lambda e, g=g, b3=b3: e.tensor_copy(out=Vcmp[:, g, :, 0:64], in_=b3[:, 0:128].rearrange("p (c d) -> p c d", c=2)),
                                     reads=[rb3], writes=[R("VcmpV%d" % g)])
                    p.barrier()
            with contextlib.ExitStack() as st3:
                def sb(name, shape, dt):
                    return st3.enter_context(nc.sbuf_tensor(tag + name, list(shape), dt))
                wout = sb("wout", [128, 8, D], BF16)
                for q in range(2):
                    src = self.attn_w_out[j, q * 512:(q + 1) * 512, :].rearrange("(k p) m -> p k m", p=128)
                    p.op("pool", lambda e, src=src, q=q: e.dma_start(out=wout[:, q * 4:(q + 1) * 4, :], in_=src), writes=[R("wout%d" % q)], dma=self.st_prep)
                r_wout = [R("wout0"), R("wout1")]
                QX = [sb("QX%d" % b_, [128, 4, 128], BF16) for b_ in range(2)]
                nbw = sb("nbw", [128, 128], F32)
                p.op("pool", lambda e: e.memset(nbw[:], 0.0), writes=[R("nbw")])
                selb = sb("selb", [128, 32, 64], F32)
                p.op("sp", lambda e: e.dma_start(out=selb[:], in_=self.selb_in), writes=[R("selb")], dma=self.st_const)
                qt = [sb("qt%d" % b, [64, 16, 256], BF16) for b in range(2)]
                Eb = [sb("E%d" % b, [128, 4, 128], BF16) for b in range(4)]
                SBANKS = (0, 1, 6)
                maskC = sb("maskC", [128, 4, 128], BF16)
                maskP = sb("maskP", [128, 4, 128], BF16)
                p.op("pool", lambda e: e.memset(maskC[:], 1.0), writes=[R("maskC")])
                p.op("pool", lambda e: e.memset(maskP[:], 1.0), writes=[R("maskP")])
                p.op("pool", lambda e: e.affine_select(out=maskC[:], in_=maskC[:], pattern=[[0, 4], [1, 128]], compare_op=ALU.is_ge, fill=0.0, base=0, channel_multiplier=-1),
                     reads=[R("maskC")], writes=[R("maskC")])
                p.op("pool", lambda e: e.affine_select(out=maskP[:], in_=maskP[:], pattern=[[0, 4], [-1, 128]], compare_op=ALU.is_ge, fill=0.0, base=-1, channel_multiplier=1),
                     reads=[R("maskP")], writes=[R("maskP")])
                ot = sb("ot", [128, D], BF16)
                accb = sb("accb", [128, 4, 64], F32)
                imp = sb("imp", [128, 64], F32)
                sc2 = sb("sc2", [128, 64], F32)
                m8 = sb("m8", [128, 8], F32)
                den = sb("den", [128, 4], F32)
                oT = sb("oT", [128, 8, 128], BF16)
                xres_t = sb("xres_t", [128, D], F32)
                xo = sb("xo", [128, D], F32)
                ec = [0]
                sc_ = [0]

                def score_exp(i, lhsT, lres, rhs, rres, mask, extra=None):
                    sb_i = SBANKS[sc_[0] % 3]
                    sc_[0] += 1
                    bank, rb = self.pbank[sb_i], self.r_pb[sb_i]
                    eb = ec[0] % 4
                    ec[0] += 1

                    def mm(e, bank=bank):
                        ins = e.matmul(bank[:, :].rearrange("p (r q) -> p r q", r=4), lhsT=lhsT, rhs=rhs, start=True, stop=(extra is None))
                        if extra is not None:
                            ins = e.matmul(bank[:, :].rearrange("p (r q) -> p r q", r=4), lhsT=extra[0], rhs=extra[1], start=False, stop=True)
                        return ins
                    rr = list(lres) + list(rres) + (list(extra[2]) if extra is not None else [])
                    p.op("pe", mm, reads=rr, writes=[rb])
                    E = Eb[eb]
                    rE = R("E%d" % eb)
                    p.op("act", lambda e, E=E, bank=bank: e.activation(out=E[:].rearrange("p r q -> p (r q)"), in_=bank[:, :], func=AF.Exp, scale=0.125),
                         reads=[rb], writes=[rE])
                    if mask is CAUSAL or mask is PREV:
                        mt_, rm_ = (maskC, R("maskC")) if mask is CAUSAL else (maskP, R("maskP"))
                        p.op("dve", lambda e, E=E, mt_=mt_: e.tensor_tensor(out=E[:], in0=E[:], in1=mt_[:], op=ALU.mult), reads=[rE, rm_], writes=[rE])
                    elif mask is not None:
                        base, cm, stepq = mask
                        p.op("pool", lambda e, E=E, base=base, cm=cm, stepq=stepq: e.affine_select(
                            out=E[:], in_=E[:], pattern=[[0, 4], [stepq, 128]], compare_op=ALU.is_ge, fill=0.0, base=base, channel_multiplier=cm),
                            reads=[rE], writes=[rE])
                    return E, rE

                def pv(E, rE, vrhs, vres, ncols, first, last):
                    def mm(e):
                        for r in range(4):
                            ins = e.matmul(self.pbank[2 + r][:, 0:ncols], lhsT=E[:, r, :], rhs=vrhs, start=first, stop=last)
                        return ins
                    p.op("pe", mm, reads=[rE] + list(vres), writes=[self.r_pb[2 + r] for r in range(4)])

                CAUSAL = (0, -1, 1)
                PREV = (-1, 1, -1)
                den2 = [den, sb("denB", [128, 4], F32)]
                bc = [0]

                def pv2(E, rE, vrhs, vres, ncols, first, last, par):
                    off = par * 256

                    def mm(e):
                        for r in range(4):
                            ins = e.matmul(self.pbank[2 + r][:, off:off + ncols], lhsT=E[:, r, :], rhs=vrhs, start=first, stop=last)
                        return ins
                    p.op("pe", mm, reads=[rE] + list(vres), writes=[R("O%d_%d" % (r, par)) for r in range(4)] + [self.r_pb[2 + r] for r in range(4)])

                def load_q(i0):
                    qb2 = (i0 // 2) % 2
                    blk = i0 // 4
                    src = self.qT[:, :, i0 * 128:i0 * 128 + 256].rearrange("h d t -> d h t")
                    p.op("sp", lambda e, src=src, qb2=qb2: e.dma_start(out=qt[qb2][:], in_=src),
                         reads=[self.R("qT_%d_%d" % (h, blk)) for h in range(16)], writes=[R("qt%d" % qb2)], dma=self.st_x)
                otB = [ot, sb("ot1", [128, D], BF16)]
                accbG = [accb, sb("accb1", [128, 4, 64], F32)]
                impG = [imp, sb("imp1", [128, 64], F32)]
                atails = []
                load_q(0)
                for i in range(32):
                    qb_ = (i // 2) % 2
                    if i % 2 == 0 and i + 2 < 32:
                        load_q(i + 2)
                    qsl = slice((i % 2) * 128, (i % 2 + 1) * 128)
                    rq = [R("qt%d" % qb_)]
                    ctxs = []

                    def make_g(g, i=i, qb_=qb_, qsl=qsl, rq=rq):
                        ot = otB[i % 2]
                        rot = R("ot%d" % (i % 2))
                        accb = accbG[g]
                        racc = R("accb%d" % g)
                        imp = impG[g]
                        rimp = R("imp%d" % g)
                        qa4 = qt[qb_][:, g * 4:(g + 1) * 4, qsl]
                        qb4 = qt[qb_][:, 8 + g * 4:8 + (g + 1) * 4, qsl]
                        gsl = gates[:, i, g * 12:(g + 1) * 12].rearrange("p (r b) -> p r b", b=3)
                        qxp = (2 * i + g) % 2
                        if use_slc:
                            p.op("pool", lambda e, qxp=qxp, qb4=qb4: e.tensor_copy(out=QX[qxp][0:64, :, :], in_=qb4), reads=rq, writes=[R("QXq%d" % qxp)])
                        tiles = []
                        kts = [kt for kt in (i - 1, i) if kt >= 0]
                        for n, kt in enumerate(kts):
                            tiles.append(("swa", kT[:, g, kt * 128:(kt + 1) * 128], [R("kT%d" % g)], qa4, CAUSAL if kt == i else PREV, None,
                                          Vall[:, kt, g, :], [R("Vall"), R("Vones")], 65, n == 0, n == len(kts) - 1))
                        nct = 1 if i < 16 else 2
                        if use_cmp or use_slc:
                            for ct in range(nct):
                                tiles.append(("cmp", kcmpT[:, g, ct * 128:(ct + 1) * 128], [R("kcmpT%d" % g)], qb4, (128 * i - 2048 * ct - 31, -16, 1), None,
                                              Vcmp[:, g, ct, :], [R("VcmpV%d" % g), R("VcmpW%d" % g), R("Vcmp1")], 129, ct == 0, ct == nct - 1))
                        if use_win:
                            kts = [kt for kt in range(i - 4, i + 1) if kt >= 0]
                            for n, kt in enumerate(kts):
                                mk = CAUSAL if kt == i else (PREV if kt == i - 4 else None)
                                tiles.append(("win", kT[:, 4 + g, kt * 128:(kt + 1) * 128], [R("kT%d" % (4 + g))], qb4, mk, None,
                                              Vall[:, kt, 4 + g, :], [R("Vall"), R("Vones")], 65, n == 0, n == len(kts) - 1))
                        if use_slc:
                            for kt in range(i + 1):
                                tiles.append(("slc", kx[:, g, kt * 128:(kt + 1) * 128], [R("kxK%d" % g), R("kxE"), R("QXq%d" % qxp), R("QXn%d" % qxp)], QX[qxp][:], CAUSAL if kt == i else None, None,
                                              Vall[:, kt, 2 + g, :], [R("Vall"), R("Vones")], 65, kt == 0, kt == i))
                        branches = [br for br in ("swa", "cmp", "win", "slc") if any(t[0] == br for t in tiles)]
                        last_nsa = [br for br in branches if br != "swa" and br != "cmp"]
                        last_nsa = last_nsa[-1] if last_nsa else "cmp"

                        def finish(br, par):
                            dn = den2[par]
                            rdn = R("den%d" % par)
                            rO = [R("O%d_%d" % (r, par)) for r in range(4)]
                            off = par * 256
                            O = [self.pbank[2 + r] for r in range(4)]
                            if br == "swa":
                                for r in range(4):
                                    h = g * 4 + r
                                    p.op("dve", lambda e, r=r, h=h: e.tensor_tensor(out=dn[:, r:r + 1], in0=O[r][:, off + 64:off + 65], in1=esink[:, h:h + 1], op=ALU.add),
                                         reads=[rO[r], R("esink")], writes=[rdn])
                                p.op("dve", lambda e: e.reciprocal(out=dn[:], in_=dn[:]), reads=[rdn], writes=[rdn])
                                for r in range(4):
                                    h = g * 4 + r
                                    p.op("dve", lambda e, r=r, h=h: e.tensor_scalar(out=ot[:, h * 64:(h + 1) * 64], in0=O[r][:, off:off + 64], scalar1=dn[:, r:r + 1], scalar2=None, op0=ALU.mult),
                                         reads=[rO[r], rdn], writes=[rot])
                                return
                            if br == "cmp":
                                for r in range(4):
                                    p.op("dve", lambda e, r=r: e.tensor_scalar(out=dn[:, r:r + 1], in0=O[r][:, off + 64:off + 65], scalar1=1e-30, scalar2=None, op0=ALU.max),
                                         reads=[rO[r]], writes=[rdn])
                                p.op("dve", lambda e: e.reciprocal(out=dn[:], in_=dn[:]), reads=[rdn], writes=[rdn])
                                for r in range(4):
                                    if r == 0:
                                        p.op("dve", lambda e, r=r: e.tensor_scalar(out=imp[:], in0=O[r][:, off + 65:off + 129], scalar1=dn[:, r:r + 1], scalar2=None, op0=ALU.mult),
                                             reads=[rO[r], rdn], writes=[rimp])
                                    else:
                                        p.op("dve", lambda e, r=r: e.scalar_tensor_tensor(out=imp[:], in0=O[r][:, off + 65:off + 129], scalar=dn[:, r:r + 1], in1=imp[:], op0=ALU.mult, op1=ALU.add),
                                             reads=[rO[r], rdn, rimp], writes=[rimp])
                                if use_slc:
                                    p.op("dve", lambda e, i=i: e.tensor_tensor(out=imp[:], in0=imp[:], in1=selb[:, i, :], op=ALU.add), reads=[rimp, R("selb")], writes=[rimp])
                                    p.op("dve", lambda e: e.max(out=m8[:], in_=imp[:]), reads=[rimp], writes=[R("m8")])
                                    p.op("dve", lambda e: e.match_replace(out=sc2[:], in_to_replace=m8[:], in_values=imp[:], imm_value=-3.0e38), reads=[rimp, R("m8")], writes=[R("sc2")])
                                    p.op("dve", lambda e: e.max(out=m8[:], in_=sc2[:]), reads=[R("sc2"), R("m8")], writes=[R("m8")])
                                    p.op("dve", lambda e: e.tensor_scalar(out=nbw[:, 64:128], in0=imp[:], scalar1=m8[:, 7:8], scalar2=-BIG, op0=ALU.is_lt, op1=ALU.mult),
                                         reads=[rimp, R("m8"), R("nbw")], writes=[R("nbw")])
                                    sb_i = SBANKS[sc_[0] % 3]
                                    sc_[0] += 1
                                    bank, rb = self.pbank[sb_i], self.r_pb[sb_i]
                                    p.op("pe", lambda e, bank=bank: e.transpose(out=bank[:, 0:128], in_=nbw[:], identity=self.ident_f[:]), reads=[R("nbw"), self.R("ident")], writes=[rb])
                                    p.op("dve", lambda e, bank=bank, qxp=qxp: e.tensor_copy(out=QX[qxp][64:128, :, :], in_=bank[64:128, 0:128].rearrange("p (o q) -> p o q", o=1).to_broadcast([64, 4, 128])),
                                         reads=[rb], writes=[R("QXn%d" % qxp)])
                                p.op("dve", lambda e, gsl=gsl: e.tensor_tensor(out=dn[:], in0=dn[:], in1=gsl[:, :, 0], op=ALU.mult), reads=[rdn, R("gates")], writes=[rdn])
                                for r in range(4):
                                    h = 8 + g * 4 + r
                                    if not use_cmp:
                                        p.op("dve", lambda e, r=r: e.memset(accb[:, r, :], 0.0), reads=[racc], writes=[racc])
                                    elif last_nsa == "cmp":
                                        p.op("dve", lambda e, r=r, h=h: e.tensor_scalar(out=ot[:, h * 64:(h + 1) * 64], in0=O[r][:, off:off + 64], scalar1=dn[:, r:r + 1], scalar2=None, op0=ALU.mult),
                                             reads=[rO[r], rdn], writes=[rot])
                                    else:
                                        p.op("dve", lambda e, r=r: e.tensor_scalar(out=accb[:, r, :], in0=O[r][:, off:off + 64], scalar1=dn[:, r:r + 1], scalar2=None, op0=ALU.mult),
                                             reads=[rO[r], rdn], writes=[racc])
                                return
                            gi = 2 if br == "win" else 1
                            for r in range(4):
                                p.op("dve", lambda e, r=r: e.tensor_copy(out=dn[:, r:r + 1], in_=O[r][:, off + 64:off + 65]), reads=[rO[r]], writes=[rdn])
                            p.op("dve", lambda e: e.reciprocal(out=dn[:], in_=dn[:]), reads=[rdn], writes=[rdn])
                            p.op("dve", lambda e, gsl=gsl, gi=gi: e.tensor_tensor(out=dn[:], in0=dn[:], in1=gsl[:, :, gi], op=ALU.mult), reads=[rdn, R("gates")], writes=[rdn])
                            for r in range(4):
                                h = 8 + g * 4 + r
                                if br == last_nsa:
                                    p.op("dve", lambda e, r=r, h=h: e.scalar_tensor_tensor(out=ot[:, h * 64:(h + 1) * 64], in0=O[r][:, off:off + 64], scalar=dn[:, r:r + 1], in1=accb[:, r, :], op0=ALU.mult, op1=ALU.add),
                                         reads=[rO[r], rdn, racc], writes=[rot])
                                else:
                                    p.op("dve", lambda e, r=r: e.scalar_tensor_tensor(out=accb[:, r, :], in0=O[r][:, off:off + 64], scalar=dn[:, r:r + 1], in1=accb[:, r, :], op0=ALU.mult, op1=ALU.add),
                                         reads=[rO[r], rdn, racc], writes=[racc])

                        par_of = {}
                        for br in branches:
                            par_of[br] = 0
                        return tiles, finish, par_of

                    for g in range(2):
                        ctxs.append(make_g(g))
                    merged = []
                    for brs in (("swa", "cmp"), ("win",), ("slc",)):
                        for gi in range(2):
                            for br in brs:
                                merged += [(gi, tl) for tl in ctxs[gi][0] if tl[0] == br]
                    queue = []
                    LOOK = 2

                    def pop():
                        pE, prE, gi, ptl = queue.pop(0)
                        pv2(pE, prE, ptl[6], ptl[7], ptl[8], ptl[9], ptl[10], 0)
                        if ptl[10]:
                            ctxs[gi][1](ptl[0], 0)
                    ntl = 0
                    for gi, tl in merged:
                        br, lhsT, lres, rhs, mask, extra, vrhs, vres, ncols, first, last = tl
                        if br == "slc" and first:
                            while any(qq[3][0] == "cmp" and qq[2] == gi for qq in queue):
                                pop()
                        E, rE = score_exp(i, lhsT, lres, rhs, rq, mask, extra=extra)
                        queue.append((E, rE, gi, tl))
                        while len(queue) > LOOK:
                            pop()
                        ntl += 1
                        if ntl == 4:
                            while atails:
                                atails.pop(0)()
                    while queue:
                        pop()

                    def atail(i=i):
                        ot = otB[i % 2]
                        rot = R("ot%d" % (i % 2))
                        p.op("sp", lambda e, i=i: e.dma_start(out=xres_t[:], in_=xsrc[i * 128:(i + 1) * 128, :]),
                             reads=[self.R("xres")], writes=[R("xres_t")], dma=self.st_x)
                        for half in range(2):
                            rp = self.r_ptr[1]

                            def tr(e, half=half):
                                for q in range(4):
                                    kc = half * 4 + q
                                    ins = e.transpose(out=self.ptrh[1][:, q * 128:(q + 1) * 128], in_=ot[:, kc * 128:(kc + 1) * 128], identity=self.ident_b[:])
                                return ins
                            p.op("pe", tr, reads=[rot, self.R("ident_b")], writes=[rp])
                            dst = oT[:, half * 4:(half + 1) * 4, :]
                            srcp = self.ptrh[1][:, :].rearrange("p (q m) -> p q m", q=4)
                            p.op("act", lambda e, dst=dst, srcp=srcp: e.copy(out=dst, in_=srcp), reads=[rp], writes=[R("oT%d" % half)])
                        for ch in range(2):
                            bo, rbo = self.pbank[7], self.r_pb[7]

                            def mmo(e, ch=ch, bo=bo):
                                for k in range(8):
                                    ins = e.matmul(bo[:, :], lhsT=oT[:, k, :], rhs=wout[:, k, ch * 512:(ch + 1) * 512], start=(k == 0), stop=(k == 7))
                                return ins
                            p.op("pe", mmo, reads=[R("oT0"), R("oT1")] + r_wout, writes=[rbo])
                            p.op("dve", lambda e, ch=ch, bo=bo: e.tensor_tensor(out=xo[:, ch * 512:(ch + 1) * 512], in0=bo[:, :], in1=xres_t[:, ch * 512:(ch + 1) * 512], op=ALU.add),
                                 reads=[rbo, R("xres_t")], writes=[R("xo")])
                        if "ot" in dbg:
                            p.op("dve", lambda e: e.tensor_copy(out=xo[:], in_=ot[:]), reads=[rot, R("xo")], writes=[R("xo")])
                        p.op("sp", lambda e, i=i: e.dma_start(out=self.xres[i * 128:(i + 1) * 128, :], in_=xo[:]),
                             reads=[R("xo")], writes=[self.R("xres_w%d" % (i % 4))], dma=self.st_o)
                    atails.append(atail)
                while atails:
                    atails.pop(0)()
            self.phase_barrier()
        self.x_src = self.xres

    def final_norm(self):
        p = self.p
        nc = self.nc
        xsrc = self.x_src
        with contextlib.ExitStack() as st:
            def sb(name, shape, dt):
                return st.enter_context(nc.sbuf_tensor(name, list(shape), dt))
            self.load_gain(DEPTH * 3)
            bufs = [(sb("fn_x%d" % b, [128, D], F32), sb("fn_sq%d" % b, [128, D], BF16), sb("fn_ss%d" % b, [128, 1], F32),
                     sb("fn_rs%d" % b, [128, 1], F32), sb("fn_o%d" % b, [128, D], F32)) for b in range(2)]
            for tt in range(S // 128):
                b = tt % 2
                xt, sq, ss, rs, ot = bufs[b]
                rx, rss, rrs, ro = [self.R("fn_%s%d" % (n, b)) for n in ("x", "ss", "rs", "o")]
                p.op("sp", lambda e, xt=xt, tt=tt: e.dma_start(out=xt[:], in_=xsrc[tt * 128:(tt + 1) * 128, :]),
                     reads=[self.R("xres")], writes=[rx], dma=self.st_x)
                p.op("act", lambda e, xt=xt, sq=sq, ss=ss: e.activation(out=sq[:], in_=xt[:], func=AF.Square, accum_out=ss[:]),
                     reads=[rx], writes=[self.R("fn_sq%d" % b), rss])
                p.op("act", lambda e, ss=ss, rs=rs: e.activation(out=rs[:], in_=ss[:], func=AF.Sqrt, scale=1.0 / D, bias=self.epsb[:]),
                     reads=[rss, self.R("epsb")], writes=[rrs])
                p.op("dve", lambda e, rs=rs: e.reciprocal(out=rs[:], in_=rs[:]), reads=[rrs], writes=[rrs])
                p.op("dve", lambda e, xt=xt, ot=ot, rs=rs: e.scalar_tensor_tensor(out=ot[:], in0=xt[:], scalar=rs[:], in1=self.gbc[:], op0=ALU.mult, op1=ALU.mult),
                     reads=[rx, rrs, self.R("gbc")], writes=[ro])
                p.op("sp", lambda e, ot=ot, tt=tt: e.dma_start(out=self.out[tt * 128:(tt + 1) * 128, :], in_=ot[:]),
                     reads=[ro], writes=[self.R("out_w%d" % (tt % 4))], dma=self.st_o)
            self.final_wait()

    def copy_out(self):
        p = self.p
        nc = self.nc
        xsrc = self.x_src
        with contextlib.ExitStack() as st:
            bufs = [st.enter_context(nc.sbuf_tensor("co%d" % b, [128, D], F32)) for b in range(2)]
            for tt in range(S // 128):
                b = tt % 2
                rx = self.R("co%d" % b)
                p.op("sp", lambda e, b=b, tt=tt: e.dma_start(out=bufs[b][:], in_=xsrc[tt * 128:(tt + 1) * 128, :]),
                     reads=[self.R("xres")], writes=[rx], dma=self.st_x)
                p.op("sp", lambda e, b=b, tt=tt: e.dma_start(out=self.out[tt * 128:(tt + 1) * 128, :], in_=bufs[b][:]),
                     reads=[rx], writes=[self.R("out_w%d" % (tt % 4))], dma=self.st_o)
            self.final_wait()

    def final_wait(self):
        p = self.p
        sems = self._store_waits()

        def fn(e, sems=sems):
            for sem, val in sems:
                e.wait_ge(sem, val)
            return e.nop()
        p.op("sp", fn, reads=[self.R("out_w%d" % k) for k in range(4)], writes=[self.R("done")])


def full_plan():
    def prep(l):
        out = [("prep_ffn", l, 0)]
        out.append(("prep_attn", l // 2) if l % 2 == 0 else ("prep_ssm", l // 2))
        out.append(("prep_ffn", l, 1))
        return out
    plan = prep(0)
    for l in range(DEPTH):
        if l + 1 < DEPTH:
            plan += prep(l + 1)
        plan.append(("ffn", l, 0))
        plan.append(("attn", l) if l % 2 == 0 else ("ssd", l))
        plan.append(("ffn", l, 1))
    plan.append(("final",))
    return plan


_CACHE = {}


def attn_w_layout(w):
    kh = [512, 576, 1280, 1344, 1536, 1600, 1792, 1856]
    qh = [h * 64 for h in range(8)] + [768 + h * 64 for h in range(8)]
    pairs = [(kh[i], qh[i]) for i in range(8)] + [(qh[8 + 2 * m], qh[9 + 2 * m]) for m in range(4)]
    cols = []
    for (ca, cb_) in pairs:
        cols += list(range(ca, ca + 64)) + list(range(cb_, cb_ + 64))
        cols += list(range(ca + 32, ca + 64)) + list(range(ca, ca + 32)) + list(range(cb_ + 32, cb_ + 64)) + list(range(cb_, cb_ + 32))
    cols += list(range(1408, 1536))
    cols += list(range(640, 768)) + list(range(1664, 1792)) + list(range(1920, 2048)) + list(range(2048, 2072))
    assert len(cols) == NAW
    return np.ascontiguousarray(w[:, :, np.asarray(cols)])


def _rope_tables():
    inv = (1.0 / (np.float32(10000.0) ** (np.arange(0, 64, 2, dtype=np.float32) / np.float32(64)))).astype(np.float32)
    ang = (np.arange(S, dtype=np.float32)[:, None] * inv[None, :]).astype(np.float32)
    c = np.cos(ang).astype(np.float32).T
    s_ = np.sin(ang).astype(np.float32).T
    return np.ascontiguousarray(np.stack([np.concatenate([c, c], 0), np.concatenate([-s_, s_], 0)], 0))


def _wsel():
    n_cmp = (S - 32) // 16 + 1
    cs = np.arange(n_cmp) * 16
    ss = np.arange(S // 64) * 64
    ov = np.minimum(cs[:, None] + 32, ss[None, :] + 64) - np.maximum(cs[:, None], ss[None, :])
    w = np.zeros((256, 64), np.float32)
    w[:n_cmp] = np.clip(ov, 0, None) / 32.0
    return np.ascontiguousarray(w.reshape(2, 128, 64).transpose(1, 0, 2))


def _selb():
    t = np.arange(S)
    cur = (t // 64)[:, None]
    jj = np.arange(64)[None, :]
    valid = jj <= cur
    forced = valid & ((jj == 0) | (jj == cur) | (jj == cur - 1))
    b = np.where(forced, 1e4, 0.0) - np.where(valid, 0.0, 1e4)
    return np.ascontiguousarray(b.astype(np.float32).reshape(32, 128, 64).transpose(1, 0, 2))


ROPE = _rope_tables()
WSEL = _wsel()
SELB = _selb()
_ii = np.arange(128)
TRI = np.stack([(_ii[:, None] <= _ii[None, :]), (_ii[:, None] > _ii[None, :]), np.ones((128, 128), bool)]).astype(np.float32)


def run_plan(plan, inputs, n_cores=8, trace=False):
    key = repr(plan)
    if key not in _CACHE:
        _CACHE[key] = Builder(plan).build()
    nc = _CACHE[key]
    x = np.ascontiguousarray(inputs["x"], dtype=np.float32)
    gains = np.concatenate([np.asarray(inputs["norm_gains"], np.float32).reshape(DEPTH * 3, D),
                            np.asarray(inputs["final_norm"], np.float32).reshape(1, D)], axis=0)
    common = {
        "gains": np.ascontiguousarray(gains),
        "ffn_w_gate": np.ascontiguousarray(inputs["ffn_w_gate"], dtype=np.float32),
        "ffn_w_up": np.ascontiguousarray(inputs["ffn_w_up"], dtype=np.float32),
        "ffn_w_down": np.ascontiguousarray(inputs["ffn_w_down"], dtype=np.float32),
        "ident": np.eye(128, dtype=np.float32),
        "tri": TRI,
        "ssm_w_in": np.ascontiguousarray(inputs["ssm_w_in"], dtype=np.float32),
        "ssm_w_out": np.ascontiguousarray(inputs["ssm_w_out"], dtype=np.float32),
        "ssm_cw": np.ascontiguousarray(np.asarray(inputs["ssm_conv_w"], np.float32).transpose(0, 2, 1).reshape(2, 32, 128, 4).transpose(0, 2, 1, 3)),
        "ssm_cb": np.ascontiguousarray(np.asarray(inputs["ssm_conv_b"], np.float32).reshape(2, 32, 128).transpose(0, 2, 1)),
        "ssm_vec": np.ascontiguousarray(np.stack([np.asarray(inputs["ssm_dt_bias"], np.float32), np.asarray(inputs["ssm_a_log"], np.float32),
                                                  np.asarray(inputs["ssm_d"], np.float32)], axis=1).reshape(2, 96)),
        "ssm_norm": np.ascontiguousarray(inputs["ssm_norm"], dtype=np.float32),
        "attn_wr": attn_w_layout(np.asarray(inputs["attn_w_in"], np.float32)),
        "attn_w_out": np.ascontiguousarray(inputs["attn_w_out"], dtype=np.float32),
        "attn_sinks": np.ascontiguousarray(inputs["attn_sinks"], dtype=np.float32),
        "rope": ROPE,
        "cmp_w1": np.ascontiguousarray(np.stack([np.asarray(inputs["cmp_k_w1"], np.float32), np.asarray(inputs["cmp_v_w1"], np.float32)], axis=1)),
        "cmp_w2": np.ascontiguousarray(np.stack([np.asarray(inputs["cmp_k_w2"], np.float32), np.asarray(inputs["cmp_v_w2"], np.float32)], axis=1)),
        "cmp_posT": np.ascontiguousarray(np.stack([np.asarray(inputs["cmp_k_pos"], np.float32).transpose(0, 2, 1),
                                                   np.asarray(inputs["cmp_v_pos"], np.float32).transpose(0, 2, 1)], axis=1)),
        "wsel": WSEL,
        "selb": SELB,
    }
    work = [0, 1, 4, 5] if n_cores == 8 else [c % n_cores for c in range(4)]
    in_maps = []
    for c in range(n_cores):
        m = dict(common)
        if c in work:
            m["x"] = x[work.index(c)]
        else:
            m["x"] = np.zeros_like(x[0])
        in_maps.append(m)
    res = run_bass_kernel_spmd(nc, in_maps, core_ids=list(range(n_cores)), trace=trace)
    out = np.stack([res.results[work[b]]["out"] for b in range(4)], axis=0)
    return out, res


def kernel(**inputs):
    out, _ = run_plan(full_plan(), inputs)
    return out.astype(np.float32)
```
